# Optimizing a Trainium2 kernel written in Bass

```python
import math
import jax, jax.numpy as jnp
from jax import lax
import numpy as np

D_MODEL = 1024
BATCH = 4
SEQ = 8192
DEPTH = 4
DEC_BATCH = 1
DEC_SEQ = 16384
PAST_LEN = 128

N_EVEN = (DEPTH + 1) // 2
N_ODD = DEPTH // 2
EPS = 1e-6
F32 = jnp.float32

S5_WIDTH = D_MODEL // 2
S5_GROUP = 16
S5_GROUPS = S5_WIDTH // S5_GROUP
S5_STATE = 64
DT_MIN = 1e-3
DT_MAX = 1e-1

RW_WIDTH = D_MODEL // 2
RW_HEAD = 64
RW_HEADS = RW_WIDTH // RW_HEAD
RW_LORA_W = 64
RW_LORA_A = 64
RW_LN_EPS = 64e-5
RW_SHIFTED = 3 * RW_WIDTH + RW_LORA_W + RW_LORA_A

EVEN_IN = 2 * S5_WIDTH + RW_SHIFTED + RW_WIDTH
EVEN_MIX = S5_WIDTH + RW_WIDTH

AT_HEADS = 16
AT_KV_HEADS = 4
AT_GROUP = AT_HEADS // AT_KV_HEADS
AT_HEAD_DIM = D_MODEL // AT_HEADS
AT_WINDOW = 128
AT_BLOCK = 128
AT_WIDTH = AT_HEADS * AT_HEAD_DIM
AT_KV_WIDTH = AT_KV_HEADS * AT_HEAD_DIM
ODD_IN = 2 * AT_WIDTH + 2 * AT_KV_WIDTH
NEG_INF = -1e30

kernel_name = "hybrid_s5_rwkv7_swa_encoder"


def rms_norm(x, g):
    xf = x.astype(F32)
    y = xf * lax.rsqrt(jnp.mean(xf * xf, axis=-1, keepdims=True) + EPS)
    return (y * g.astype(F32)).astype(x.dtype)


def _complex_affine_op(c1, c2):
    a1r, a1i, b1r, b1i = c1
    a2r, a2i, b2r, b2i = c2
    return (a2r * a1r - a2i * a1i,
            a2r * a1i + a2i * a1r,
            a2r * b1r - a2i * b1i + b2r,
            a2r * b1i + a2i * b1r + b2i)


def s5_direction(u, a_re, a_im, log_dt, b_re, b_im, c_re, c_im, reverse):
    dt = jnp.exp(log_dt.astype(F32))[:, None]
    ar = a_re.astype(F32)
    ai = a_im.astype(F32)
    mag = jnp.exp(ar * dt)
    lr = mag * jnp.cos(ai * dt)
    li = mag * jnp.sin(ai * dt)
    den = ar * ar + ai * ai
    nr = lr - 1.0
    qr = (nr * ar + li * ai) / den
    qi = (li * ar - nr * ai) / den
    br = b_re.astype(F32)
    bi = b_im.astype(F32)
    bbr = qr[..., None] * br - qi[..., None] * bi
    bbi = qr[..., None] * bi + qi[..., None] * br
    hr = jnp.einsum('blgc,gnc->blgn', u, bbr)
    hi = jnp.einsum('blgc,gnc->blgn', u, bbi)
    shp = hr.shape
    _, _, hr, hi = lax.associative_scan(
        _complex_affine_op,
        (jnp.broadcast_to(lr, shp), jnp.broadcast_to(li, shp), hr, hi),
        reverse=reverse, axis=1)
    return (jnp.einsum('blgn,gcn->blgc', hr, c_re.astype(F32))
            - jnp.einsum('blgn,gcn->blgc', hi, c_im.astype(F32)))


def s5_branch(u, z, p):
    B, L, _ = u.shape
    uf = u.astype(F32)
    ug = uf.reshape(B, L, S5_GROUPS, S5_GROUP)
    y = (s5_direction(ug, p['s5_a_re'][0], p['s5_a_im'][0], p['s5_log_dt'][0],
                      p['s5_b_re'][0], p['s5_b_im'][0], p['s5_c_re'][0], p['s5_c_im'][0], False)
         + s5_direction(ug, p['s5_a_re'][1], p['s5_a_im'][1], p['s5_log_dt'][1],
                        p['s5_b_re'][1], p['s5_b_im'][1], p['s5_c_re'][1], p['s5_c_im'][1], True))
    y = y.reshape(B, L, S5_WIDTH) + p['s5_d'].astype(F32) * uf
    y = jax.nn.gelu(y)
    y = y * jax.nn.sigmoid(y @ p['s5_glu_w'].astype(F32) + p['s5_glu_b'].astype(F32))
    return y * jax.nn.silu(z.astype(F32))


def centred_shift(x, mu):
    xp = jnp.pad(x, ((0, 0), (1, 1), (0, 0)))
    nb = 0.5 * (xp[:, :-2] + xp[:, 2:])
    return x + mu * (nb - x)


def rwkv_scan(r, w, k, v, kk, b, reverse):
    Bsz, H, N = r.shape[1:]

    def step(S, inp):
        r_t, w_t, k_t, v_t, kk_t, b_t = inp
        sa = -jnp.einsum('bhvk,bhk->bhv', S, kk_t)
        S = (S * w_t[:, :, None, :] + sa[..., None] * b_t[:, :, None, :]
             + v_t[..., None] * k_t[:, :, None, :])
        return S, jnp.einsum('bhvk,bhk->bhv', S, r_t)

    S0 = jnp.zeros((Bsz, H, N, N), F32)
    _, y = lax.scan(step, S0, (r, w, k, v, kk, b), reverse=reverse)
    return y


def rwkv_branch(h, z, p):
    B, L, _ = h.shape
    h = centred_shift(h.astype(F32), p['rw_mu'].astype(F32))
    r, k, v, lw, la = jnp.split(h, [RW_WIDTH, 2 * RW_WIDTH, 3 * RW_WIDTH,
                                    3 * RW_WIDTH + RW_LORA_W], axis=-1)
    a = jax.nn.sigmoid(p['rw_a0'].astype(F32) + la @ p['rw_a_up'].astype(F32))
    tw = jnp.tanh(lw)

    def decay(d):
        wl = -jax.nn.softplus(-(p['rw_w0'][d].astype(F32) + tw @ p['rw_w_up'][d].astype(F32))) - 0.5
        return jnp.exp(-jnp.exp(wl))

    heads = lambda t: t.reshape(B, L, RW_HEADS, RW_HEAD)
    kk = heads(k * p['rw_k_k'].astype(F32))
    kk = kk / jnp.maximum(jnp.sqrt(jnp.sum(kk * kk, axis=-1, keepdims=True)), 1e-12)
    k = k * (1.0 + (a - 1.0) * p['rw_k_a'].astype(F32))
    rh, kh, vh, ah = heads(r), heads(k), heads(v), heads(a)
    tm = lambda t: jnp.swapaxes(t, 0, 1)
    R, K, V, KK, BB = tm(rh), tm(kh), tm(vh), tm(kk), tm(kk * ah)
    y = (rwkv_scan(R, tm(heads(decay(0))), K, V, KK, BB, False)
         + rwkv_scan(R, tm(heads(decay(1))), K, V, KK, BB, True))
    y = tm(y)
    mean = jnp.mean(y, axis=-1, keepdims=True)
    var = jnp.mean(jnp.square(y - mean), axis=-1, keepdims=True)
    y = ((y - mean) * lax.rsqrt(var + RW_LN_EPS)).reshape(B, L, RW_WIDTH)
    y = y * p['rw_ln_g'].astype(F32) + p['rw_ln_b'].astype(F32)
    bonus = jnp.sum(rh * kh * p['rw_r_k'].astype(F32), axis=-1, keepdims=True) * vh
    y = y + bonus.reshape(B, L, RW_WIDTH)
    return y * jax.nn.silu(z.astype(F32))


def alibi_slopes():
    return jnp.exp2(-8.0 * jnp.arange(1, AT_HEADS + 1, dtype=F32) / AT_HEADS)


def rms_head(t, g):
    tf = t.astype(F32)
    return tf * lax.rsqrt(jnp.mean(tf * tf, axis=-1, keepdims=True) + EPS) * g.astype(F32)


def attn_branch(h, p):
    B, L, _ = h.shape
    q, k, v, z = jnp.split(h, [AT_WIDTH, AT_WIDTH + AT_KV_WIDTH,
                               AT_WIDTH + 2 * AT_KV_WIDTH], axis=-1)
    q = rms_head(q.reshape(B, L, AT_HEADS, AT_HEAD_DIM), p['at_q_norm']) * (AT_HEAD_DIM ** -0.5)
    k = rms_head(k.reshape(B, L, AT_KV_HEADS, AT_HEAD_DIM), p['at_k_norm'])
    v = v.reshape(B, L, AT_KV_HEADS, AT_HEAD_DIM).astype(F32)
    nb = L // AT_BLOCK
    qb = q.reshape(B, nb, AT_BLOCK, AT_KV_HEADS, AT_GROUP, AT_HEAD_DIM).transpose(1, 0, 2, 3, 4, 5)

    def band(t):
        tp = jnp.pad(t, ((0, 0), (AT_BLOCK, AT_BLOCK), (0, 0), (0, 0)))
        tp = tp.reshape(B, nb + 2, AT_BLOCK, AT_KV_HEADS, AT_HEAD_DIM)
        w = jnp.concatenate([tp[:, :-2], tp[:, 1:-1], tp[:, 2:]], axis=2)
        return jnp.swapaxes(w, 0, 1)

    kb, vb = band(k), band(v)
    rel = AT_BLOCK + jnp.arange(AT_BLOCK)[:, None] - jnp.arange(3 * AT_BLOCK)[None, :]
    in_win = jnp.abs(rel) <= AT_WINDOW
    slopes = alibi_slopes().reshape(AT_KV_HEADS, AT_GROUP)
    bias = -slopes[:, :, None, None] * jnp.abs(rel).astype(F32)
    sink = p['at_sink'].astype(F32).reshape(1, AT_KV_HEADS, AT_GROUP, 1)

    def block(args):
        i, qi, ki, vi = args
        s_pos = (i - 1) * AT_BLOCK + jnp.arange(3 * AT_BLOCK)
        valid = in_win & ((s_pos >= 0) & (s_pos < L))[None, :]
        s = jnp.einsum('bqkgd,bskd->bkgqs', qi, ki) + bias
        s = jnp.where(valid, s, NEG_INF)
        m = jnp.maximum(jnp.max(s, axis=-1), sink)
        pr = jnp.exp(s - m[..., None])
        den = jnp.sum(pr, axis=-1) + jnp.exp(sink - m)
        o = jnp.einsum('bkgqs,bskd->bqkgd', pr, vi)
        return o / jnp.transpose(den, (0, 3, 1, 2))[..., None]

    o = lax.map(block, (jnp.arange(nb), qb, kb, vb))
    o = o.transpose(1, 0, 2, 3, 4, 5).reshape(B, L, AT_WIDTH)
    return o * jax.nn.silu(z.astype(F32))


def even_layer(x, p):
    h = rms_norm(x, p['norm'])
    proj = h @ p['w_in']
    u, z_s5, h_rw, z_rw = jnp.split(
        proj, [S5_WIDTH, 2 * S5_WIDTH, 2 * S5_WIDTH + RW_SHIFTED], axis=-1)
    ya = s5_branch(u, z_s5, p)
    yb = rwkv_branch(h_rw, z_rw, p)
    y = jnp.concatenate([ya, yb], axis=-1).astype(x.dtype) @ p['w_out']
    return x + y.astype(x.dtype)


def odd_layer(x, p):
    h = rms_norm(x, p['norm'])
    o = attn_branch(h @ p['w_in'], p)
    return x + (o.astype(x.dtype) @ p['w_out']).astype(x.dtype)


def setup_inputs(seed: int = 0) -> dict:
    key = jax.random.key(seed)
    ks = iter(jax.random.split(key, 40))
    nrm = lambda shape, scale: scale * jax.random.normal(next(ks), shape, F32)
    NE, NO, G, N = N_EVEN, N_ODD, S5_GROUPS, S5_STATE
    a_im_base = jnp.pi * jnp.arange(N, dtype=F32)
    return {
        'x_prompt': nrm((BATCH, SEQ, D_MODEL), 1.0),
        'x_sample': nrm((DEC_BATCH, DEC_SEQ, D_MODEL), 1.0),
        'ev_norm': 1.0 + nrm((NE, D_MODEL), 0.02),
        'ev_w_in': nrm((NE, D_MODEL, EVEN_IN), D_MODEL ** -0.5),
        's5_a_re': -0.5 + nrm((NE, 2, G, N), 0.01),
        's5_a_im': a_im_base + nrm((NE, 2, G, N), 0.01),
        's5_log_dt': jax.random.uniform(next(ks), (NE, 2, G), F32,
                                        math.log(DT_MIN), math.log(DT_MAX)),
        's5_b_re': nrm((NE, 2, G, N, S5_GROUP), (2 * S5_GROUP) ** -0.5),
        's5_b_im': nrm((NE, 2, G, N, S5_GROUP), (2 * S5_GROUP) ** -0.5),
        's5_c_re': nrm((NE, 2, G, S5_GROUP, N), (2 * N) ** -0.5),
        's5_c_im': nrm((NE, 2, G, S5_GROUP, N), (2 * N) ** -0.5),
        's5_d': nrm((NE, S5_WIDTH), 1.0),
        's5_glu_w': nrm((NE, S5_WIDTH, S5_WIDTH), S5_WIDTH ** -0.5),
        's5_glu_b': nrm((NE, S5_WIDTH), 0.01),
        'rw_mu': jax.random.uniform(next(ks), (NE, RW_SHIFTED), F32),
        'rw_w0': jax.random.uniform(next(ks), (NE, 2, RW_WIDTH), F32, -6.0, -1.0),
        'rw_w_up': nrm((NE, 2, RW_LORA_W, RW_WIDTH), 0.05),
        'rw_a0': nrm((NE, RW_WIDTH), 0.1),
        'rw_a_up': nrm((NE, RW_LORA_A, RW_WIDTH), 0.05),
        'rw_k_k': 0.85 + nrm((NE, RW_WIDTH), 0.02),
        'rw_k_a': 1.0 + nrm((NE, RW_WIDTH), 0.02),
        'rw_r_k': nrm((NE, RW_HEADS, RW_HEAD), 0.1),
        'rw_ln_g': 1.0 + nrm((NE, RW_WIDTH), 0.02),
        'rw_ln_b': nrm((NE, RW_WIDTH), 0.01),
        'ev_w_out': nrm((NE, EVEN_MIX, D_MODEL), EVEN_MIX ** -0.5),
        'od_norm': 1.0 + nrm((NO, D_MODEL), 0.02),
        'od_w_in': nrm((NO, D_MODEL, ODD_IN), D_MODEL ** -0.5),
        'at_q_norm': 1.0 + nrm((NO, AT_HEAD_DIM), 0.02),
        'at_k_norm': 1.0 + nrm((NO, AT_HEAD_DIM), 0.02),
        'at_sink': nrm((NO, AT_HEADS), 1.0),
        'od_w_out': nrm((NO, AT_WIDTH, D_MODEL), AT_WIDTH ** -0.5),
    }


def reference(x_prompt, x_sample, ev_norm, ev_w_in, s5_a_re, s5_a_im, s5_log_dt,
              s5_b_re, s5_b_im, s5_c_re, s5_c_im, s5_d, s5_glu_w, s5_glu_b,
              rw_mu, rw_w0, rw_w_up, rw_a0, rw_a_up, rw_k_k, rw_k_a, rw_r_k,
              rw_ln_g, rw_ln_b, ev_w_out, od_norm, od_w_in, at_q_norm, at_k_norm,
              at_sink, od_w_out):
    ev = {'norm': ev_norm, 'w_in': ev_w_in, 's5_a_re': s5_a_re, 's5_a_im': s5_a_im,
          's5_log_dt': s5_log_dt, 's5_b_re': s5_b_re, 's5_b_im': s5_b_im,
          's5_c_re': s5_c_re, 's5_c_im': s5_c_im, 's5_d': s5_d, 's5_glu_w': s5_glu_w,
          's5_glu_b': s5_glu_b, 'rw_mu': rw_mu, 'rw_w0': rw_w0, 'rw_w_up': rw_w_up,
          'rw_a0': rw_a0, 'rw_a_up': rw_a_up, 'rw_k_k': rw_k_k, 'rw_k_a': rw_k_a,
          'rw_r_k': rw_r_k, 'rw_ln_g': rw_ln_g, 'rw_ln_b': rw_ln_b, 'w_out': ev_w_out}
    od = {'norm': od_norm, 'w_in': od_w_in, 'at_q_norm': at_q_norm,
          'at_k_norm': at_k_norm, 'at_sink': at_sink, 'w_out': od_w_out}

    def trunk(x):
        for layer in range(DEPTH):
            j = layer // 2
            if layer % 2 == 0:
                x = even_layer(x, {n: a[j] for n, a in ev.items()})
            else:
                x = odd_layer(x, {n: a[j] for n, a in od.items()})
        return x

    y_prompt = trunk(x_prompt)
    y_sample = trunk(x_sample)
    return (y_prompt, y_sample)
```

```python
import numpy as np
from contextlib import ExitStack
import concourse.bass as bass
import concourse.mybir as mybir
from concourse.bass_utils import run_bass_kernel_spmd

F32 = mybir.dt.float32
BF16 = mybir.dt.bfloat16
AF = mybir.ActivationFunctionType
ALU = mybir.AluOpType
AX = mybir.AxisListType

N_DMA_SLOTS = 24
D = 1024
EPS = 1e-6
NEG = -30000.0


import types


def _snap(fn):
    if fn.__closure__ is None:
        return fn
    cells = tuple(types.CellType(c.cell_contents) for c in fn.__closure__)
    return types.FunctionType(fn.__code__, fn.__globals__, fn.__name__, fn.__defaults__, cells)


class T:
    __slots__ = ("t", "name", "writers", "readers", "war")

    def __init__(self, t, name=""):
        self.t = t
        self.name = name
        self.writers = []
        self.readers = []
        self.war = []

    def __getitem__(self, idx):
        return self.t[idx]


class Prog:
    ENGS = ("pe", "act", "dve", "pool", "sp")

    def __init__(self, nc, stack):
        self.nc = nc
        self.stack = stack
        self.gstack = stack
        self.ops = {e: [] for e in self.ENGS}
        self.cnt = {e: 0 for e in self.ENGS}
        self.seen = {e: {} for e in self.ENGS}
        self.sems = {e: stack.enter_context(nc.semaphore("s_" + e)) for e in self.ENGS}
        self.dma_sems = [stack.enter_context(nc.semaphore("s_dma%d" % i)) for i in range(N_DMA_SLOTS)]
        self.dma_n = 0
        self.same_engine_sync = True
        self._uid = 0

    def sb(self, shape, dt=F32, name=None):
        self._uid += 1
        name = "sb%d" % self._uid
        return T(self.stack.enter_context(self.nc.sbuf_tensor(name, list(shape), dt)), name)

    def ps(self, shape, dt=F32):
        self._uid += 1
        name = "ps%d" % self._uid
        return T(self.stack.enter_context(self.nc.psum_tensor(name, list(shape), dt)), name)

    def dram(self, name, shape, dt=F32):
        return self.nc.dram_tensor(name, list(shape), dt, kind="Internal")

    def _need(self, eng, dep, waits):
        key, val, deng = dep
        if deng == eng and (eng == "pe" or not self.same_engine_sync):
            return
        if self.seen[eng].get(key, -1) >= val:
            return
        self.seen[eng][key] = val
        waits.append((key, val))

    def _deps(self, eng, reads, writes, accum):
        waits = []
        for t in reads:
            for w in t.writers:
                self._need(eng, w, waits)
        for t in writes:
            if not accum:
                for w in t.writers:
                    self._need(eng, w, waits)
            else:
                for w in t.war:
                    self._need(eng, w, waits)
            for r in t.readers:
                self._need(eng, r, waits)
        return waits

    def _commit(self, tok, reads, writes, accum):
        for t in reads:
            t.readers.append(tok)
        for t in writes:
            if accum:
                t.writers.append(tok)
                t.war = t.war + t.readers
            else:
                t.war = t.writers + t.readers
                t.writers = [tok]
            t.readers = []

    def _sem(self, key):
        return self.sems[key] if isinstance(key, str) else self.dma_sems[key]

    def op(self, eng, fn, reads=(), writes=(), accum=False):
        import os
        if eng == "pool" and os.environ.get("NOPOOL"):
            eng = "dve"
        kmax = int(os.environ.get("KMAX", "0"))
        if kmax and sum(self.cnt.values()) >= kmax:
            return None
        waits = self._deps(eng, reads, writes, accum)
        self.cnt[eng] += 1
        tok = (eng, self.cnt[eng], eng)
        self._commit(tok, reads, writes, accum)
        self.ops[eng].append((waits, _snap(fn), (eng, 1)))
        return tok

    def dma(self, out_ap, in_ap, reads=(), writes=(), q="sp", accum=False):
        waits = self._deps(q, reads, writes, accum)
        i = self.dma_n
        self.dma_n += 1
        slot = i % N_DMA_SLOTS
        val = 16 * (i // N_DMA_SLOTS + 1)
        if i >= N_DMA_SLOTS and self.seen[q].get(slot, -1) < val - 16:
            self.seen[q][slot] = val - 16
            waits.append((slot, val - 16))
        tok = (slot, val, "dma")
        self._commit(tok, reads, writes, accum)

        def fn(e, out_ap=out_ap, in_ap=in_ap):
            return e.dma_start(out=out_ap, in_=in_ap)
        self.ops[q].append((waits, fn, (slot, 16)))
        return tok

    def barrier(self):
        for e in self.ENGS:
            waits = []
            for o in self.ENGS:
                if o != e and self.cnt[o] > 0 and self.seen[e].get(o, -1) < self.cnt[o]:
                    self.seen[e][o] = self.cnt[o]
                    waits.append((o, self.cnt[o]))
            n = self.dma_n
            for slot in range(min(n, N_DMA_SLOTS)):
                last_i = ((n - 1 - slot) // N_DMA_SLOTS) * N_DMA_SLOTS + slot
                v = 16 * (last_i // N_DMA_SLOTS + 1)
                if self.seen[e].get(slot, -1) < v:
                    self.seen[e][slot] = v
                    waits.append((slot, v))
            if waits:
                self.ops[e].append((waits, None, None))

    def emit(self):
        nc = self.nc
        self.barrier()
        block = self.gstack.enter_context(nc.Block())
        prog = self

        def run(engname, e):
            for waits, fn, inc in prog.ops[engname]:
                for key, val in waits:
                    e.wait_ge(prog._sem(key), val)
                if fn is not None:
                    fn(e).then_inc(prog._sem(inc[0]), inc[1])

        @block.tensor
        def _(e):
            run("pe", e)

        @block.scalar
        def _(e):
            run("act", e)

        @block.vector
        def _(e):
            run("dve", e)

        @block.gpsimd
        def _(e):
            run("pool", e)

        @block.sync
        def _(e):
            run("sp", e)


class Ctx:
    pass


def rr(P, C, key, engs):
    C.rr[key] = C.rr.get(key, -1) + 1
    return engs[C.rr[key] % len(engs)]


def copy_op(P, eng, out_ap, in_ap, reads, writes, accum=False):
    if eng == "act":
        P.op("act", lambda e: e.activation(out=out_ap, in_=in_ap, func=AF.Copy), reads=reads, writes=writes, accum=accum)
    elif eng == "dve":
        P.op("dve", lambda e: e.tensor_copy(out=out_ap, in_=in_ap), reads=reads, writes=writes, accum=accum)
    else:
        P.op("pool", lambda e: e.tensor_copy(out=out_ap, in_=in_ap), reads=reads, writes=writes, accum=accum)


def load_weight_bf16(P, C, dst, src_ap_fn, nchunk, ncols, stage):
    for c in range(nchunk):
        s = stage[c % len(stage)]
        P.dma(s[:, 0:ncols], src_ap_fn(c), reads=[], writes=[s])
        eng = ("act", "dve", "pool")[c % 3]
        copy_op(P, eng, dst[:, c, :], s[:, 0:ncols], [s], [dst], accum=True)


def rmsnorm_T(P, C, xt, gn, hT, S):
    junk, ss, ss2, rs, h = S.junk, S.ss, S.ss2, S.rs, S.h
    P.op("act", lambda e: e.activation(out=junk[:], in_=xt[:], func=AF.Square, accum_out=ss[:]),
         reads=[xt], writes=[junk, ss])
    P.op("act", lambda e: e.activation(out=ss2[:], in_=ss[:], func=AF.Sqrt, scale=1.0 / D, bias=EPS),
         reads=[ss], writes=[ss2])
    P.op("dve", lambda e: e.reciprocal(out=rs[:], in_=ss2[:]), reads=[ss2], writes=[rs])
    P.op("dve", lambda e: e.scalar_tensor_tensor(out=h[:], in0=xt[:], scalar=rs[:, 0:1], in1=gn[:],
                                                 op0=ALU.mult, op1=ALU.mult), reads=[xt, rs, gn], writes=[h])
    transpose_to(P, C, h, 8, hT)


def transpose_to(P, C, src, nblk, dst, src_off=0):
    for g0 in range(0, nblk, 4):
        n = min(4, nblk - g0)
        bank = C.gbank()
        for c in range(n):
            P.op("pe", lambda e, c=c, bank=bank, g0=g0: e.transpose(
                out=bank[:, c * 128:(c + 1) * 128],
                in_=src[:, src_off + (g0 + c) * 128: src_off + (g0 + c + 1) * 128], identity=C.ident[:]),
                reads=[src, C.ident], writes=[bank], accum=(c > 0))
        eng = rr(P, C, "tev", ("act", "dve"))
        copy_op(P, eng, dst[:, g0:g0 + n, :], bank[:, 0:n * 128].rearrange("p (a b) -> p a b", a=n),
                [bank], [dst], accum=(g0 > 0))


def transpose_heads(P, C, src, nheads, dst, src_off=0):
    for g0 in range(0, nheads, 4):
        n = min(4, nheads - g0)
        bank = C.gbank()
        for c in range(n):
            P.op("pe", lambda e, c=c, bank=bank, g0=g0: e.transpose(
                out=bank[0:64, c * 128:(c + 1) * 128],
                in_=src[:, src_off + (g0 + c) * 64: src_off + (g0 + c + 1) * 64], identity=C.ident[:]),
                reads=[src, C.ident], writes=[bank], accum=(c > 0))
        eng = rr(P, C, "tev", ("act", "dve"))
        copy_op(P, eng, dst[:, g0:g0 + n, :], bank[0:64, 0:n * 128].rearrange("p (a b) -> p a b", a=n),
                [bank], [dst], accum=(g0 > 0))


def matmul_group(P, C, bank, ncols, hT, W, col0, nk=8, out_off=0):
    for c in range(nk):
        P.op("pe", lambda e, c=c: e.matmul(bank[:, out_off:out_off + ncols], lhsT=hT[:, c, :],
                                           rhs=W[:, c, col0:col0 + ncols], start=(c == 0), stop=(c == nk - 1)),
             reads=[hT, W], writes=[bank], accum=(c > 0))


def odd_layer(P, C, l, x_src, x_dst, xs_tr, xd_tr):
    NT = C.NT
    st = ExitStack()
    P.stack = st
    I = C.inp
    wq = P.sb([128, 8, 2560], BF16)
    wo = P.sb([128, 8, 1024], BF16)
    stage = [P.sb([128, 2560]), P.sb([128, 2560])]
    load_weight_bf16(P, C, wq, lambda c: I["od_w_in"][l, c * 128:(c + 1) * 128, :], 8, 2560, stage)
    load_weight_bf16(P, C, wo, lambda c: I["od_w_out"][l, c * 128:(c + 1) * 128, :], 8, 1024, stage)
    gn = P.sb([128, 1024]); gq = P.sb([128, 64]); gk = P.sb([128, 64]); esink = P.sb([128, 16])
    P.dma(gn[:], I["od_norm_rep"][l], writes=[gn])
    P.dma(gq[:], I["qg_rep"][l], writes=[gq])
    P.dma(gk[:], I["kg_rep"][l], writes=[gk])
    P.dma(esink[:], I["sink_rep"][l], writes=[esink])
    P.op("act", lambda e: e.activation(out=esink[:], in_=esink[:], func=AF.Exp), reads=[esink], writes=[esink])
    biasT = P.sb([128, 12, 512])
    for j in range(4):
        P.dma(biasT[:, 3 * j:3 * j + 3, :], I["alibi"][j].rearrange("r s q -> s r q"), writes=[biasT], accum=True)

    S = Ctx()
    S.junk = P.sb([128, 1024]); S.ss = P.sb([128, 1]); S.ss2 = P.sb([128, 1]); S.rs = P.sb([128, 1]); S.h = P.sb([128, 1024])
    hT = P.sb([128, 8, 128], BF16)
    xring = [P.sb([128, 1024]) for _ in range(3)]
    qf = P.sb([128, 1024]); qsq = P.sb([128, 1024]); qss = P.sb([128, 16]); qr = P.sb([128, 16]); qn = P.sb([128, 1024])
    kf = P.sb([128, 256]); ksq = P.sb([128, 256]); kss = P.sb([128, 4]); kr = P.sb([128, 4]); kn = P.sb([128, 256])
    QT = [P.sb([64, 16, 128], BF16) for _ in range(3)]
    KT = [P.sb([64, 4, 128], BF16) for _ in range(4)]
    V = [P.sb([128, 4, 72], BF16) for _ in range(4)]
    for v in V:
        P.op("pool", lambda e, v=v: e.memset(v[:], 1.0), writes=[v])
    sz = [P.sb([128, 1024]) for _ in range(3)]
    sring = [P.sb([128, 512]) for _ in range(3)]
    pr = [P.sb([128, 512], BF16) for _ in range(6)]
    den = P.sb([128, 4]); rden = P.sb([128, 4])
    o = P.sb([128, 1024]); og = P.sb([128, 1024]); ogT = P.sb([128, 8, 128], BF16)
    xo = [P.sb([128, 1024]) for _ in range(2)]

    def rms_heads(src, sq, ssum, rinv, nh, g, outs):
        P.op("pool", lambda e: e.tensor_tensor(out=sq[:], in0=src[:], in1=src[:], op=ALU.mult), reads=[src], writes=[sq])
        P.op("dve", lambda e: e.tensor_reduce(out=ssum[:], in_=sq[:].rearrange("p (h d) -> p h d", h=nh), axis=AX.X, op=ALU.add),
             reads=[sq], writes=[ssum])
        P.op("act", lambda e: e.activation(out=ssum[:], in_=ssum[:], func=AF.Sqrt, scale=1.0 / 64, bias=EPS),
             reads=[ssum], writes=[ssum])
        P.op("dve", lambda e: e.reciprocal(out=rinv[:], in_=ssum[:]), reads=[ssum], writes=[rinv])
        P.op("dve", lambda e: e.tensor_tensor(out=sq[:].rearrange("p (h d) -> p h d", h=nh),
                                              in0=src[:].rearrange("p (h d) -> p h d", h=nh),
                                              in1=rinv[:].unsqueeze(2).to_broadcast([128, nh, 64]), op=ALU.mult),
             reads=[src, rinv], writes=[sq])
        for oi, (oap, ot) in enumerate(outs):
            P.op("pool", lambda e, oap=oap: e.tensor_tensor(out=oap, in0=sq[:].rearrange("p (h d) -> p h d", h=nh),
                                                            in1=g[:].unsqueeze(1).to_broadcast([128, nh, 64]), op=ALU.mult),
                 reads=[sq, g], writes=[ot], accum=(oi > 0))

    def stage1(j):
        xt = xring[j % 3]
        P.dma(xt[:], x_src[j * 128:(j + 1) * 128, :], reads=[xs_tr[j]], writes=[xt])
        rmsnorm_T(P, C, xt, gn, hT, S)
        for g in range(2):
            bank = C.gbank()
            matmul_group(P, C, bank, 512, hT, wq, g * 512)
            copy_op(P, rr(P, C, "qev", ("act", "dve")), qf[:, g * 512:(g + 1) * 512], bank[:], [bank], [qf], accum=(g > 0))
        rms_heads(qf, qsq, qss, qr, 16, gq, [(qn[:].rearrange("p (h d) -> p h d", h=16), qn)])
        transpose_heads(P, C, qn, 16, QT[j % 3])
        bank = C.gbank()
        matmul_group(P, C, bank, 512, hT, wq, 1024)
        copy_op(P, "dve", kf[:], bank[:, 0:256], [bank], [kf])
        Vt = V[j % 4]
        copy_op(P, "dve", Vt[:, :, 0:64], bank[:, 256:512].rearrange("p (h d) -> p h d", h=4), [bank], [Vt])
        rms_heads(kf, ksq, kss, kr, 4, gk, [(kn[:].rearrange("p (h d) -> p h d", h=4), kn)])
        transpose_heads(P, C, kn, 4, KT[j % 4])
        for g in range(2):
            bank = C.gbank()
            matmul_group(P, C, bank, 512, hT, wq, 1536 + g * 512)
            szt = sz[j % 3]
            P.op("act", lambda e, bank=bank, g=g, szt=szt: e.activation(out=szt[:, g * 512:(g + 1) * 512], in_=bank[:], func=AF.Silu),
                 reads=[bank], writes=[szt], accum=(g > 0))

    def stage2(i):
        half = NT // 2
        for jkv in range(4):
            blocks = [b for b in (i - 1, i, i + 1) if 0 <= b < NT]
            for b in blocks:
                rel = b - i + 1
                bank = C.sbank[rel]
                for hl in range(4):
                    hq = 4 * jkv + hl
                    P.op("pe", lambda e, bank=bank, hl=hl, b=b, hq=hq: e.matmul(
                        bank[:, hl * 128:(hl + 1) * 128], lhsT=KT[b % 4][:, jkv, :],
                        rhs=QT[i % 3][:, hq, :], start=True, stop=True),
                        reads=[KT[b % 4], QT[i % 3]], writes=[bank], accum=(hl > 0))
                s_t = sring[rel]
                P.op("dve", lambda e, bank=bank, s_t=s_t, rel=rel: e.scalar_tensor_tensor(
                    out=s_t[:], in0=bank[:], scalar=0.125, in1=biasT[:, 3 * jkv + rel, :], op0=ALU.mult, op1=ALU.add),
                    reads=[bank, biasT], writes=[s_t])
                if (i == half - 1 and b == i + 1) or (i == half and b == i - 1):
                    P.op("dve", lambda e, s_t=s_t: e.tensor_scalar(out=s_t[:], in0=s_t[:], scalar1=C.flags[:, 1:2], scalar2=None,
                                                                   op0=ALU.add), reads=[s_t, C.flags], writes=[s_t])
                pt = pr[(jkv % 2) * 3 + rel]
                P.op("act", lambda e, pt=pt, s_t=s_t: e.activation(out=pt[:], in_=s_t[:], func=AF.Exp), reads=[s_t], writes=[pt])
            pvb = C.pbank[jkv % 2]
            for hl in range(4):
                for bi, b in enumerate(blocks):
                    rel = b - i + 1
                    pt = pr[(jkv % 2) * 3 + rel]
                    P.op("pe", lambda e, pt=pt, hl=hl, b=b, bi=bi: e.matmul(
                        pvb[:, hl * 65:(hl + 1) * 65], lhsT=pt[:, hl * 128:(hl + 1) * 128], rhs=V[b % 4][:, jkv, 0:65],
                        start=(bi == 0), stop=(bi == len(blocks) - 1)),
                        reads=[pt, V[b % 4]], writes=[pvb], accum=not (hl == 0 and bi == 0))
            pv3 = pvb[:, 0:260].rearrange("p (h d) -> p h d", h=4)
            P.op("dve", lambda e, pv3=pv3: e.tensor_tensor(out=den[:], in0=pv3[:, :, 64], in1=esink[:, 4 * jkv:4 * jkv + 4], op=ALU.add),
                 reads=[pvb, esink], writes=[den])
            P.op("dve", lambda e: e.reciprocal(out=rden[:], in_=den[:]), reads=[den], writes=[rden])
            P.op("dve", lambda e, pv3=pv3: e.tensor_tensor(
                out=o[:, jkv * 256:(jkv + 1) * 256].rearrange("p (h d) -> p h d", h=4), in0=pv3[:, :, 0:64],
                in1=rden[:].unsqueeze(2).to_broadcast([128, 4, 64]), op=ALU.mult),
                reads=[pvb, rden], writes=[o], accum=(jkv > 0))
        P.op("pool", lambda e: e.tensor_tensor(out=og[:], in0=o[:], in1=sz[i % 3][:], op=ALU.mult), reads=[o, sz[i % 3]], writes=[og])
        transpose_to(P, C, og, 8, ogT)
        xot = xo[i % 2]
        for g in range(2):
            bank = C.gbank()
            matmul_group(P, C, bank, 512, ogT, wo, g * 512)
            P.op("dve", lambda e, bank=bank, g=g: e.tensor_tensor(out=xot[:, g * 512:(g + 1) * 512], in0=bank[:],
                                                                  in1=xring[i % 3][:, g * 512:(g + 1) * 512], op=ALU.add),
                 reads=[bank, xring[i % 3]], writes=[xot], accum=(g > 0))
        P.dma(x_dst[i * 128:(i + 1) * 128, :], xot[:], reads=[xot], writes=[xd_tr[i]])

    import os
    dbg = int(os.environ.get("KDBG", "9"))
    for t in range(NT + 1):
        if t < NT and dbg >= 1:
            stage1(t)
        if t >= 1 and dbg >= 2:
            stage2(t - 1)
    P.barrier()
    st.close()
    P.stack = P.gstack


INPUT_SHAPES = {
    "od_w_in": [2, 1024, 2560], "od_w_out": [2, 1024, 1024], "od_norm_rep": [2, 128, 1024],
    "qg_rep": [2, 128, 64], "kg_rep": [2, 128, 64], "sink_rep": [2, 128, 16],
    "alibi": [4, 3, 128, 512], "ident": [128, 128], "flags": [128, 4],
    "ev_w_in": [2, 1024, 3200], "ev_norm_rep": [2, 128, 1024], "ev_w_out": [2, 1024, 1024],
    "s5_ar_row": [2, 2, 128, 2048], "s5_ai_row": [2, 2, 128, 2048], "s5_dt_row": [2, 2, 128, 2048],
    "s5_ar_col": [2, 2, 128, 16], "s5_ai_col": [2, 2, 128, 16], "s5_dt_col": [2, 2, 128, 16],
    "s5_b_col": [2, 2, 2, 128, 16, 16], "s5_c_col": [2, 2, 2, 128, 16, 16],
    "s5_d_col": [2, 128, 4], "glu_b_col": [2, 128, 4], "s5_glu_w": [2, 512, 512],
    "iota_col": [128, 2], "iota_row": [2, 128, 128], "tri": [2, 128, 128], "triE": [2, 128, 128],
    "rw_mu_rep": [2, 128, 1664], "rw_w0_rep": [2, 2, 128, 512], "rw_a0_rep": [2, 128, 512], "rw_k_k_rep": [2, 128, 512],
    "rw_k_a_rep": [2, 128, 512], "rw_r_k_rep": [2, 128, 512], "rw_ln_g_rep": [2, 128, 512], "rw_ln_b_rep": [2, 128, 512],
    "rw_w_up": [2, 2, 64, 512], "rw_a_up": [2, 64, 512],
}


def alibi_tables():
    slopes = np.exp2(-8.0 * np.arange(1, 17, dtype=np.float32) / 16).astype(np.float32)
    s = np.arange(128)[:, None]
    t = np.arange(128)[None, :]
    out = np.zeros((4, 3, 128, 4, 128), np.float32)
    for rel in range(3):
        d = np.abs(t - (s + (rel - 1) * 128)).astype(np.float32)
        for j in range(4):
            for hl in range(4):
                out[j, rel, :, hl, :] = np.where(d <= 128, -slopes[4 * j + hl] * d, NEG)
    return out.reshape(4, 3, 128, 512)


def host_layout(inputs, layers):
    f = lambda a: np.ascontiguousarray(np.asarray(a, np.float32))
    rep = lambda a: f(np.broadcast_to(np.asarray(a)[:, None, :], (a.shape[0], 128, a.shape[1])))
    m = {}
    m["od_w_in"] = f(inputs["od_w_in"]); m["od_w_out"] = f(inputs["od_w_out"])
    m["od_norm_rep"] = rep(inputs["od_norm"]); m["qg_rep"] = rep(inputs["at_q_norm"]); m["kg_rep"] = rep(inputs["at_k_norm"])
    m["sink_rep"] = rep(inputs["at_sink"])
    m["alibi"] = alibi_tables(); m["ident"] = np.eye(128, dtype=np.float32)
    m["ev_w_in"] = f(inputs["ev_w_in"]); m["ev_w_out"] = f(inputs["ev_w_out"]); m["ev_norm_rep"] = rep(inputs["ev_norm"])
    NE = 2
    rowrep = lambda a: f(np.broadcast_to(a.reshape(NE, 2, 1, 2048), (NE, 2, 128, 2048)))
    m["s5_ar_row"] = rowrep(np.asarray(inputs["s5_a_re"])); m["s5_ai_row"] = rowrep(np.asarray(inputs["s5_a_im"]))
    m["s5_dt_row"] = rowrep(np.repeat(np.asarray(inputs["s5_log_dt"])[..., None], 64, axis=-1))
    col = lambda a: f(a.reshape(NE, 2, 16, 128).transpose(0, 1, 3, 2))
    m["s5_ar_col"] = col(np.asarray(inputs["s5_a_re"])); m["s5_ai_col"] = col(np.asarray(inputs["s5_a_im"]))
    m["s5_dt_col"] = col(np.repeat(np.asarray(inputs["s5_log_dt"])[..., None], 64, axis=-1))
    bcol = lambda a: np.asarray(a).reshape(NE, 2, 16, 2, 64, 16).transpose(0, 1, 3, 4, 2, 5).reshape(NE, 2, 128, 16, 16)
    m["s5_b_col"] = f(np.stack([bcol(inputs["s5_b_re"]), bcol(inputs["s5_b_im"])], axis=2))
    ccol = lambda a: np.asarray(a).reshape(NE, 2, 16, 2, 16, 64).transpose(0, 1, 3, 5, 2, 4).reshape(NE, 2, 128, 16, 16)
    m["s5_c_col"] = f(np.stack([ccol(inputs["s5_c_re"]), ccol(inputs["s5_c_im"])], axis=2))
    c4 = lambda a: f(np.asarray(a).reshape(NE, 4, 128).transpose(0, 2, 1))
    m["s5_d_col"] = c4(inputs["s5_d"]); m["glu_b_col"] = c4(inputs["s5_glu_b"]); m["s5_glu_w"] = f(inputs["s5_glu_w"])
    ar = np.arange(128, dtype=np.float32)
    m["iota_col"] = f(np.stack([ar + 1, 128 - ar], axis=1))
    m["iota_row"] = f(np.stack([np.broadcast_to(ar + 1, (128, 128)), np.broadcast_to(128 - ar, (128, 128))]))
    s_, t_ = np.arange(128)[:, None], np.arange(128)[None, :]
    m["tri"] = f(np.stack([(s_ <= t_), (s_ >= t_)]).astype(np.float32))
    m["triE"] = f(np.stack([(s_ < t_), (s_ > t_)]).astype(np.float32))
    for k in ("rw_mu", "rw_a0", "rw_k_k", "rw_k_a", "rw_ln_g", "rw_ln_b"):
        m[k + "_rep"] = rep(np.asarray(inputs[k]))
    m["rw_r_k_rep"] = rep(np.asarray(inputs["rw_r_k"]).reshape(NE, 512))
    w0 = np.asarray(inputs["rw_w0"])
    m["rw_w0_rep"] = f(np.broadcast_to(w0[:, :, None, :], (NE, 2, 128, 512)))
    m["rw_w_up"] = f(inputs["rw_w_up"]); m["rw_a_up"] = f(inputs["rw_a_up"])
    return m


def build_program(NT, layers, debug=False):
    nc = bass.Bass("TRN2", target_bir_lowering=False)
    NTOK = NT * 128
    gst = ExitStack()
    P = Prog(nc, gst)
    C = Ctx()
    C.NT = NT
    C.rr = {}
    C.inp = {k: nc.dram_tensor(k, shp, F32, kind="ExternalInput") for k, shp in INPUT_SHAPES.items()}
    xin = nc.dram_tensor("xin", [NTOK, D], F32, kind="ExternalInput")
    xout = nc.dram_tensor("xout", [NTOK, D], F32, kind="ExternalOutput")
    xa = P.dram("xa", [NTOK, D]); xb = P.dram("xb", [NTOK, D])
    banks = [P.ps([128, 512]) for _ in range(8)]
    C.gb = banks[0:3]; C.sbank = banks[3:6]; C.pbank = banks[6:8]
    C.gi = 0

    def gbank():
        C.gi += 1
        return C.gb[C.gi % len(C.gb)]
    C.gbank = gbank
    C.banks = banks
    C.ident = P.sb([128, 128]); C.flags = P.sb([128, 4])
    C.iota_col = P.sb([128, 2]); C.iota_row = [P.sb([128, 128]) for _ in range(2)]; C.tri = [P.sb([128, 128]) for _ in range(2)]
    C.zero_col = P.sb([128, 2]); C.ones_col = P.sb([128, 2]); C.triE = [P.sb([128, 128]) for _ in range(2)]
    P.op("pool", lambda e: e.memset(C.zero_col[:], 0.0), writes=[C.zero_col])
    P.op("pool", lambda e: e.memset(C.ones_col[:], 1.0), writes=[C.ones_col])
    for d in range(2):
        P.dma(C.triE[d][:], C.inp["triE"][d], writes=[C.triE[d]])
    P.dma(C.iota_col[:], C.inp["iota_col"][:, :], writes=[C.iota_col])
    for d in range(2):
        P.dma(C.iota_row[d][:], C.inp["iota_row"][d], writes=[C.iota_row[d]])
        P.dma(C.tri[d][:], C.inp["tri"][d], writes=[C.tri[d]])
    dbgk = "ExternalOutput" if debug else "Internal"
    C.proj = nc.dram_tensor("proj", [NTOK, 3200], F32, kind=dbgk)
    C.ys5T = nc.dram_tensor("ys5T", [512, NTOK], F32, kind="Internal")
    C.mixT = nc.dram_tensor("mixT", [1024, NTOK], BF16, kind=dbgk)
    C.hrw = C.proj
    C.yrw = nc.dram_tensor("yrw", [NTOK, 512], F32, kind="Internal")
    C.yrw_tr = [T(None) for _ in range(NT)]
    C.proj_tr = [T(None) for _ in range(NT)]; C.ys_tr = [T(None) for _ in range(NT)]; C.mix_tr = [T(None) for _ in range(NT)]
    P.dma(C.ident[:], C.inp["ident"][:, :], writes=[C.ident])
    P.dma(C.flags[:], C.inp["flags"][:, :], writes=[C.flags])
    bufs = [xin] + [(xa, xb)[i % 2] for i in range(len(layers) - 1)] + [xout]
    trs = [[T(None) for _ in range(NT)] for _ in range(len(layers) + 1)]
    for li, (kind, l) in enumerate(layers):
        if kind == "odd":
            odd_layer(P, C, l, bufs[li], bufs[li + 1], trs[li], trs[li + 1])
        else:
            even_layer(P, C, l, bufs[li], bufs[li + 1], trs[li], trs[li + 1])
    P.emit()
    return nc, gst


MAGIC = 12582912.0
TWO_PI = 2.0 * np.pi


def round_frac(P, eng, out, in_, tmp):
    (o_ap, o_t), (i_ap, i_t), (t_ap, t_t) = out, in_, tmp
    P.op(eng, lambda e: e.tensor_scalar(out=t_ap, in0=i_ap, scalar1=MAGIC, scalar2=MAGIC, op0=ALU.add, op1=ALU.subtract),
         reads=[i_t], writes=[t_t])
    P.op(eng, lambda e: e.tensor_tensor(out=o_ap, in0=i_ap, in1=t_ap, op=ALU.subtract), reads=[i_t, t_t], writes=[o_t])


def even_phaseA(P, C, l, x_src, xs_tr):
    NT = C.NT
    st = ExitStack(); P.stack = st
    I = C.inp
    w = P.sb([128, 8, 3200], BF16)
    stage = [P.sb([128, 3200]), P.sb([128, 3200])]
    load_weight_bf16(P, C, w, lambda c: I["ev_w_in"][l, c * 128:(c + 1) * 128, :], 8, 3200, stage)
    gn = P.sb([128, 1024])
    P.dma(gn[:], I["ev_norm_rep"][l], writes=[gn])
    S = Ctx()
    S.junk = P.sb([128, 1024]); S.ss = P.sb([128, 1]); S.ss2 = P.sb([128, 1]); S.rs = P.sb([128, 1]); S.h = P.sb([128, 1024])
    hT = P.sb([128, 8, 128], BF16)
    xring = [P.sb([128, 1024]) for _ in range(2)]
    for j in range(NT):
        xt = xring[j % 2]
        P.dma(xt[:], x_src[j * 128:(j + 1) * 128, :], reads=[xs_tr[j]], writes=[xt])
        rmsnorm_T(P, C, xt, gn, hT, S)
        pst = stage[j % 2]
        for g in range(7):
            ncol = 512 if g < 6 else 128
            bank = C.gbank()
            matmul_group(P, C, bank, ncol, hT, w, g * 512)
            copy_op(P, rr(P, C, "pev", ("act", "dve")), pst[:, g * 512:g * 512 + ncol], bank[:, 0:ncol], [bank], [pst], accum=(g > 0))
        P.dma(C.proj[j * 128:(j + 1) * 128, :], pst[:], reads=[pst], writes=[C.proj_tr[j]])
    P.barrier()
    st.close(); P.stack = P.gstack


def s5_tables(P, C, l, d, K):
    I = C.inp
    W = K.work
    a_r, a_i, dtr, t0, t1, t2 = W[0], W[1], W[2], W[3], W[4], W[5]

    def build(shape_is_row, ar_src, ai_src, dt_src, steps_fn, sign, out_re, out_im):
        P.dma(a_r[:], ar_src, writes=[a_r]); P.dma(a_i[:], ai_src, writes=[a_i]); P.dma(dtr[:], dt_src, writes=[dtr])
        P.op("act", lambda e: e.activation(out=dtr[:], in_=dtr[:], func=AF.Exp), reads=[dtr], writes=[dtr])
        P.op("dve", lambda e: e.tensor_tensor(out=a_r[:], in0=a_r[:], in1=dtr[:], op=ALU.mult), reads=[a_r, dtr], writes=[a_r])
        P.op("dve", lambda e: e.scalar_tensor_tensor(out=a_i[:], in0=a_i[:], scalar=1.0 / TWO_PI, in1=dtr[:], op0=ALU.mult, op1=ALU.mult),
             reads=[a_i, dtr], writes=[a_i])
        round_frac(P, "dve", (a_i[:], a_i), (a_i[:], a_i), (t0[:], t0))
        steps_fn(a_r, a_i)
        P.op("act", lambda e: e.activation(out=t1[:], in_=a_r[:], func=AF.Exp, scale=float(sign)), reads=[a_r], writes=[t1])
        round_frac(P, "dve", (t0[:], t0), (a_i[:], a_i), (t2[:], t2))
        P.op("act", lambda e: e.activation(out=t0[:], in_=t0[:], func=AF.Sin, scale=TWO_PI), reads=[t0], writes=[t0])
        P.op("dve", lambda e: e.tensor_scalar(out=a_i[:], in0=a_i[:], scalar1=0.25, scalar2=None, op0=ALU.add), reads=[a_i], writes=[a_i])
        round_frac(P, "dve", (a_i[:], a_i), (a_i[:], a_i), (t2[:], t2))
        P.op("act", lambda e: e.activation(out=a_i[:], in_=a_i[:], func=AF.Sin, scale=TWO_PI), reads=[a_i], writes=[a_i])
        P.op("dve", lambda e: e.tensor_tensor(out=out_re[:].rearrange("p a b -> p (a b)"), in0=t1[:], in1=a_i[:], op=ALU.mult),
             reads=[t1, a_i], writes=[out_re])
        P.op("dve", lambda e: e.scalar_tensor_tensor(out=out_im[:].rearrange("p a b -> p (a b)"), in0=t1[:], scalar=float(sign), in1=t0[:],
                                                     op0=ALU.mult, op1=ALU.mult), reads=[t1, t0], writes=[out_im])

    def steps_row(a_r, a_i):
        for t in (a_r, a_i):
            P.op("dve", lambda e, t=t: e.tensor_scalar(out=t[:], in0=t[:], scalar1=C.iota_col[:, d:d + 1], scalar2=None, op0=ALU.mult),
                 reads=[t, C.iota_col], writes=[t])
    build(True, I["s5_ar_row"][l, d], I["s5_ai_row"][l, d], I["s5_dt_row"][l, d], steps_row, -1, K.Tin_re, K.Tin_im)

    def steps_col(a_r, a_i):
        for t in (a_r, a_i):
            P.op("dve", lambda e, t=t: e.tensor_tensor(out=t[:].rearrange("p (a b) -> p a b", a=16),
                                                       in0=t[:, 0:16].unsqueeze(2).to_broadcast([128, 16, 128]),
                                                       in1=C.iota_row[d][:].unsqueeze(1).to_broadcast([128, 16, 128]), op=ALU.mult),
                 reads=[t, C.iota_row[d]], writes=[t])
    ca, ci, cd = K.col_a, K.col_i, K.col_d

    def build_col():
        P.dma(ca[:], I["s5_ar_col"][l, d], writes=[ca]); P.dma(ci[:], I["s5_ai_col"][l, d], writes=[ci]); P.dma(cd[:], I["s5_dt_col"][l, d], writes=[cd])
        P.op("act", lambda e: e.activation(out=cd[:], in_=cd[:], func=AF.Exp), reads=[cd], writes=[cd])
        P.op("dve", lambda e: e.tensor_tensor(out=K.c_ardt[:], in0=ca[:], in1=cd[:], op=ALU.mult), reads=[ca, cd], writes=[K.c_ardt])
        P.op("dve", lambda e: e.scalar_tensor_tensor(out=K.c_frac[:], in0=ci[:], scalar=1.0 / TWO_PI, in1=cd[:], op0=ALU.mult, op1=ALU.mult),
             reads=[ci, cd], writes=[K.c_frac])
        round_frac(P, "dve", (K.c_frac[:], K.c_frac), (K.c_frac[:], K.c_frac), (K.c_tmp[:], K.c_tmp))
        P.op("dve", lambda e: e.tensor_tensor(out=a_r[:].rearrange("p (a b) -> p a b", a=16),
                                              in0=K.c_ardt[:].unsqueeze(2).to_broadcast([128, 16, 128]),
                                              in1=C.iota_row[d][:].unsqueeze(1).to_broadcast([128, 16, 128]), op=ALU.mult),
             reads=[K.c_ardt, C.iota_row[d]], writes=[a_r])
        P.op("dve", lambda e: e.tensor_tensor(out=a_i[:].rearrange("p (a b) -> p a b", a=16),
                                              in0=K.c_frac[:].unsqueeze(2).to_broadcast([128, 16, 128]),
                                              in1=C.iota_row[d][:].unsqueeze(1).to_broadcast([128, 16, 128]), op=ALU.mult),
             reads=[K.c_frac, C.iota_row[d]], writes=[a_i])
        sign = 1
        P.op("act", lambda e: e.activation(out=t1[:], in_=a_r[:], func=AF.Exp, scale=float(sign)), reads=[a_r], writes=[t1])
        round_frac(P, "dve", (t0[:], t0), (a_i[:], a_i), (t2[:], t2))
        P.op("act", lambda e: e.activation(out=t0[:], in_=t0[:], func=AF.Sin, scale=TWO_PI), reads=[t0], writes=[t0])
        P.op("dve", lambda e: e.tensor_scalar(out=a_i[:], in0=a_i[:], scalar1=0.25, scalar2=None, op0=ALU.add), reads=[a_i], writes=[a_i])
        round_frac(P, "dve", (a_i[:], a_i), (a_i[:], a_i), (t2[:], t2))
        P.op("act", lambda e: e.activation(out=a_i[:], in_=a_i[:], func=AF.Sin, scale=TWO_PI), reads=[a_i], writes=[a_i])
        P.op("dve", lambda e: e.tensor_tensor(out=K.Tout_re[:].rearrange("p a b -> p (a b)"), in0=t1[:], in1=a_i[:], op=ALU.mult),
             reads=[t1, a_i], writes=[K.Tout_re])
        P.op("dve", lambda e: e.tensor_tensor(out=K.Tout_im[:].rearrange("p a b -> p (a b)"), in0=t1[:], in1=t0[:], op=ALU.mult),
             reads=[t1, t0], writes=[K.Tout_im])
    build_col()

    s1, c1, m1, nr, dn, q_r, q_i, u0, u1 = [K.small[i] for i in range(9)]
    P.op("act", lambda e: e.activation(out=m1[:], in_=K.c_ardt[:], func=AF.Exp), reads=[K.c_ardt], writes=[m1])
    P.op("act", lambda e: e.activation(out=s1[:], in_=K.c_frac[:], func=AF.Sin, scale=TWO_PI), reads=[K.c_frac], writes=[s1])
    P.op("dve", lambda e: e.tensor_scalar(out=u0[:], in0=K.c_frac[:], scalar1=0.25, scalar2=None, op0=ALU.add), reads=[K.c_frac], writes=[u0])
    round_frac(P, "dve", (u0[:], u0), (u0[:], u0), (u1[:], u1))
    P.op("act", lambda e: e.activation(out=c1[:], in_=u0[:], func=AF.Sin, scale=TWO_PI), reads=[u0], writes=[c1])
    tt = lambda o, a, b, op, eng="dve": P.op(eng, lambda e: e.tensor_tensor(out=o[:], in0=a[:], in1=b[:], op=op), reads=[a, b], writes=[o])
    tt(c1, c1, m1, ALU.mult)
    tt(s1, s1, m1, ALU.mult)
    P.op("dve", lambda e: e.tensor_scalar(out=nr[:], in0=c1[:], scalar1=-1.0, scalar2=None, op0=ALU.add), reads=[c1], writes=[nr])
    tt(dn, ca, ca, ALU.mult); tt(u0, ci, ci, ALU.mult); tt(dn, dn, u0, ALU.add)
    P.op("dve", lambda e: e.reciprocal(out=dn[:], in_=dn[:]), reads=[dn], writes=[dn])
    tt(u0, nr, ca, ALU.mult); tt(u1, s1, ci, ALU.mult); tt(u0, u0, u1, ALU.add); tt(q_r, u0, dn, ALU.mult)
    tt(u0, s1, ca, ALU.mult); tt(u1, nr, ci, ALU.mult); tt(u0, u0, u1, ALU.subtract); tt(q_i, u0, dn, ALU.mult)
    bre, bim, bbr, bbi, tb = K.bre, K.bim, K.bbr, K.bbi, K.tb
    for (dst, ri) in ((bre, 0), (bim, 1)):
        P.op("pool", lambda e, dst=dst: e.memset(dst[:], 0.0), writes=[dst])
        P.dma(dst[0:64, :, 0:16], I["s5_b_col"][l, d, ri, 0:64], writes=[dst])
        P.dma(dst[64:128, :, 16:32], I["s5_b_col"][l, d, ri, 64:128], writes=[dst])
    bc = lambda q: q[:].unsqueeze(2).to_broadcast([128, 16, 32])
    P.op("dve", lambda e: e.tensor_tensor(out=bbr[:], in0=bre[:], in1=bc(q_r), op=ALU.mult), reads=[bre, q_r], writes=[bbr])
    P.op("dve", lambda e: e.tensor_tensor(out=tb[:], in0=bim[:], in1=bc(q_i), op=ALU.mult), reads=[bim, q_i], writes=[tb])
    tt(bbr, bbr, tb, ALU.subtract)
    P.op("dve", lambda e: e.tensor_tensor(out=bbi[:], in0=bim[:], in1=bc(q_r), op=ALU.mult), reads=[bim, q_r], writes=[bbi])
    P.op("dve", lambda e: e.tensor_tensor(out=tb[:], in0=bre[:], in1=bc(q_i), op=ALU.mult), reads=[bre, q_i], writes=[tb])
    tt(bbi, bbi, tb, ALU.add)
    zp = K.zp
    for z in zp:
        P.op("pool", lambda e, z=z: e.memset(z[:], 0.0), writes=[z])
    for (src, dst) in ((bbr, K.BT_re), (bbi, K.BT_im)):
        for ch in range(4):
            bank = C.gbank()
            for pl in range(4):
                copy_op(P, "dve", zp[pl][:, 32 * pl:32 * pl + 32], src[:, 4 * ch + pl, :], [src], [zp[pl]])
                P.op("pe", lambda e, bank=bank, pl=pl: e.transpose(out=bank[:, pl * 128:(pl + 1) * 128], in_=zp[pl][:], identity=C.ident[:]),
                     reads=[zp[pl], C.ident], writes=[bank], accum=(pl > 0))
            copy_op(P, "act", dst[:, ch, :], bank[:], [bank], [dst], accum=(ch > 0))
    for (dst, ri) in ((K.Cre, 0), (K.Cimn, 1)):
        P.op("pool", lambda e, dst=dst: e.memset(dst[:], 0.0), writes=[dst])
        P.dma(dst[0:64, :, 32:48], I["s5_c_col"][l, d, ri, 0:64], writes=[dst])
        P.dma(dst[64:128, :, 48:64], I["s5_c_col"][l, d, ri, 64:128], writes=[dst])
    P.op("dve", lambda e: e.tensor_scalar(out=K.Cimn[:], in0=K.Cimn[:], scalar1=-1.0, scalar2=None, op0=ALU.mult), reads=[K.Cimn], writes=[K.Cimn])


def s5_pass(P, C, l, d):
    NT = C.NT
    half = NT // 2
    st = ExitStack(); P.stack = st
    I = C.inp
    K = Ctx()
    K.Tin_re = P.sb([128, 4, 512]); K.Tin_im = P.sb([128, 4, 512])
    K.Tout_re = P.sb([128, 16, 128]); K.Tout_im = P.sb([128, 16, 128])
    K.BT_re = P.sb([128, 4, 512]); K.BT_im = P.sb([128, 4, 512])
    K.Cre = P.sb([128, 16, 64]); K.Cimn = P.sb([128, 16, 64])
    st2 = ExitStack(); P.stack = st2
    K.work = [P.sb([128, 2048]) for _ in range(6)]
    K.col_a = P.sb([128, 16]); K.col_i = P.sb([128, 16]); K.col_d = P.sb([128, 16])
    K.c_ardt = P.sb([128, 16]); K.c_frac = P.sb([128, 16]); K.c_tmp = P.sb([128, 16])
    K.small = [P.sb([128, 16]) for _ in range(9)]
    K.bre = P.sb([128, 16, 32]); K.bim = P.sb([128, 16, 32]); K.bbr = P.sb([128, 16, 32]); K.bbi = P.sb([128, 16, 32]); K.tb = P.sb([128, 16, 32])
    K.zp = [P.sb([128, 128]) for _ in range(4)]
    s5_tables(P, C, l, d, K)
    P.barrier()
    st2.close(); P.stack = st
    tri = C.tri[d]
    zero = C.zero_col
    uring = [P.sb([128, 512]) for _ in range(2)]
    uT = [P.sb([128, 4, 128]) for _ in range(2)]
    g_re = [P.sb([128, 512]) for _ in range(2)]; g_im = [P.sb([128, 512]) for _ in range(2)]
    ta = [P.sb([128, 512]) for _ in range(2)]; tb = [P.sb([128, 512]) for _ in range(2)]
    hre = [[P.sb([128, 128]) for _ in range(16)] for _ in range(2)]
    him = [[P.sb([128, 128]) for _ in range(16)] for _ in range(2)]
    r1 = [P.sb([128, 128]) for _ in range(2)]; r2 = [P.sb([128, 128]) for _ in range(2)]
    cc = [[P.sb([128, 2]) for _ in range(16)] for _ in range(1)][0]
    ysb = [P.sb([128, 4, 128]) for _ in range(2)]
    ysT = C.ys5T.ap().rearrange("(k p) t -> p k t", p=128)
    bk = C.banks
    if d == 1:
        dcol = P.sb([128, 4]); gbcol = P.sb([128, 4])
        P.dma(dcol[:], I["s5_d_col"][l], writes=[dcol]); P.dma(gbcol[:], I["glu_b_col"][l], writes=[gbcol])
        wg = P.sb([128, 4, 512], BF16)
        wgs = [P.sb([128, 512]), P.sb([128, 512])]
        load_weight_bf16(P, C, wg, lambda c: I["s5_glu_w"][l, c * 128:(c + 1) * 128, :], 4, 512, wgs)
        yf = [P.sb([128, 4, 128]) for _ in range(2)]
        zt = [P.sb([128, 512]) for _ in range(2)]
        szT = P.sb([128, 4, 128])
        yv = P.sb([128, 4, 128]); x2 = P.sb([128, 4, 128]); sg = P.sb([128, 4, 128]); yg = P.sb([128, 4, 128]); ygb = P.sb([128, 4, 128], BF16)
        gs = P.sb([128, 4, 128]); ya = [P.sb([128, 4, 128], BF16) for _ in range(2)]
        mixT = C.mixT.ap().rearrange("(k p) t -> p k t", p=128)

    order = list(range(NT)) if d == 0 else list(range(NT - 1, -1, -1))
    last = 127 if d == 0 else 0
    for it, i in enumerate(order):
        par = it % 2
        ut = uring[par]
        P.dma(ut[:], C.proj[i * 128:(i + 1) * 128, 0:512], reads=[C.proj_tr[i]], writes=[ut])
        if d == 1:
            P.dma(yf[par][:], ysT[:, :, i * 128:(i + 1) * 128], reads=[C.ys_tr[i]], writes=[yf[par]])
            P.dma(zt[par][:], C.proj[i * 128:(i + 1) * 128, 512:1024], reads=[C.proj_tr[i]], writes=[zt[par]])
        uTt = uT[par]
        for c in range(4):
            P.op("pe", lambda e, c=c: e.transpose(out=bk[0][:, c * 128:(c + 1) * 128], in_=ut[:, c * 128:(c + 1) * 128], identity=C.ident[:]),
                 reads=[ut, C.ident], writes=[bk[0]], accum=(c > 0))
        copy_op(P, "act", uTt[:].rearrange("p a b -> p (a b)"), bk[0][:], [bk[0]], [uTt])
        first_of_seq = (it == 0) or (d == 0 and i == half) or (d == 1 and i == half - 1)
        for ch in range(4):
            cp = ch % 2
            P.op("pe", lambda e: e.matmul(bk[1][:], lhsT=uTt[:, ch, :], rhs=K.BT_re[:, ch, :], start=True, stop=True),
                 reads=[uTt, K.BT_re], writes=[bk[1]])
            P.op("pe", lambda e: e.matmul(bk[2][:], lhsT=uTt[:, ch, :], rhs=K.BT_im[:, ch, :], start=True, stop=True),
                 reads=[uTt, K.BT_im], writes=[bk[2]])
            Tr = K.Tin_re[:, ch, :]; Ti = K.Tin_im[:, ch, :]
            gr, gi, a_, b_ = g_re[cp], g_im[cp], ta[cp], tb[cp]
            P.op("dve", lambda e: e.tensor_tensor(out=a_[:], in0=bk[1][:], in1=Tr, op=ALU.mult), reads=[bk[1], K.Tin_re], writes=[a_])
            P.op("dve", lambda e: e.tensor_tensor(out=b_[:], in0=bk[2][:], in1=Ti, op=ALU.mult), reads=[bk[2], K.Tin_im], writes=[b_])
            P.op("pool", lambda e: e.tensor_tensor(out=gr[:], in0=a_[:], in1=b_[:], op=ALU.subtract), reads=[a_, b_], writes=[gr])
            P.op("dve", lambda e: e.tensor_tensor(out=a_[:], in0=bk[1][:], in1=Ti, op=ALU.mult), reads=[bk[1], K.Tin_im], writes=[a_])
            P.op("dve", lambda e: e.tensor_tensor(out=b_[:], in0=bk[2][:], in1=Tr, op=ALU.mult), reads=[bk[2], K.Tin_re], writes=[b_])
            P.op("pool", lambda e: e.tensor_tensor(out=gi[:], in0=a_[:], in1=b_[:], op=ALU.add), reads=[a_, b_], writes=[gi])
            for pl in range(4):
                P.op("pe", lambda e, pl=pl: e.matmul(bk[3][:, pl * 128:(pl + 1) * 128], lhsT=gr[:, pl * 128:(pl + 1) * 128], rhs=tri[:],
                                                     start=True, stop=True), reads=[gr, tri], writes=[bk[3]], accum=(pl > 0))
            for pl in range(4):
                P.op("pe", lambda e, pl=pl: e.matmul(bk[4][:, pl * 128:(pl + 1) * 128], lhsT=gi[:, pl * 128:(pl + 1) * 128], rhs=tri[:],
                                                     start=True, stop=True), reads=[gi, tri], writes=[bk[4]], accum=(pl > 0))
            for pl in (0, 1, 3, 2):
                pp = 4 * ch + pl
                hr_prev, hi_prev = hre[1 - par][pp], him[1 - par][pp]
                hr, hi = hre[par][pp], him[par][pp]
                if it == 0:
                    cr, ci_, crt = zero[:, 0:1], zero[:, 0:1], [zero]
                elif first_of_seq:
                    cct = cc[pp]
                    P.op("dve", lambda e: e.tensor_scalar(out=cct[:, 0:1], in0=hr_prev[:, last:last + 1], scalar1=C.flags[:, 0:1], scalar2=None,
                                                          op0=ALU.mult), reads=[hr_prev, C.flags], writes=[cct])
                    P.op("dve", lambda e: e.tensor_scalar(out=cct[:, 1:2], in0=hi_prev[:, last:last + 1], scalar1=C.flags[:, 0:1], scalar2=None,
                                                          op0=ALU.mult), reads=[hi_prev, C.flags], writes=[cct], accum=True)
                    cr, ci_, crt = cct[:, 0:1], cct[:, 1:2], [cct]
                else:
                    cr, ci_, crt = hr_prev[:, last:last + 1], hi_prev[:, last:last + 1], [hr_prev, hi_prev]
                Gr = bk[3][:, pl * 128:(pl + 1) * 128]; Gi = bk[4][:, pl * 128:(pl + 1) * 128]
                Tor = K.Tout_re[:, pp, :]; Toi = K.Tout_im[:, pp, :]
                q1, q2 = r1[pl % 2], r2[pl % 2]
                P.op("dve", lambda e: e.scalar_tensor_tensor(out=q1[:], in0=Gr, scalar=cr, in1=Tor, op0=ALU.add, op1=ALU.mult),
                     reads=[bk[3], K.Tout_re] + crt, writes=[q1])
                P.op("dve", lambda e: e.scalar_tensor_tensor(out=q2[:], in0=Gi, scalar=ci_, in1=Toi, op0=ALU.add, op1=ALU.mult),
                     reads=[bk[4], K.Tout_im] + crt, writes=[q2])
                P.op("pool", lambda e: e.tensor_tensor(out=hr[:], in0=q1[:], in1=q2[:], op=ALU.subtract), reads=[q1, q2], writes=[hr])
                P.op("dve", lambda e: e.scalar_tensor_tensor(out=q1[:], in0=Gr, scalar=cr, in1=Toi, op0=ALU.add, op1=ALU.mult),
                     reads=[bk[3], K.Tout_im] + crt, writes=[q1])
                P.op("dve", lambda e: e.scalar_tensor_tensor(out=q2[:], in0=Gi, scalar=ci_, in1=Tor, op0=ALU.add, op1=ALU.mult),
                     reads=[bk[4], K.Tout_re] + crt, writes=[q2])
                P.op("pool", lambda e: e.tensor_tensor(out=hi[:], in0=q1[:], in1=q2[:], op=ALU.add), reads=[q1, q2], writes=[hi])
                if pl == 3:
                    osl, csl, st0 = slice(64, 128), slice(0, 64), True
                elif pl == 2:
                    osl, csl, st0 = slice(64, 96), slice(32, 64), False
                else:
                    osl, csl, st0 = slice(32 * pl, 32 * pl + 32), slice(32, 64), True
                sgc = pl >= 2
                P.op("pe", lambda e: e.matmul(bk[5][osl, ch * 128:(ch + 1) * 128], lhsT=K.Cre[:, pp, csl], rhs=hr[:],
                                              start=st0, stop=False, skip_group_check=sgc), reads=[K.Cre, hr], writes=[bk[5]], accum=not (ch == 0 and pl == 0))
                P.op("pe", lambda e: e.matmul(bk[5][osl, ch * 128:(ch + 1) * 128], lhsT=K.Cimn[:, pp, csl], rhs=hi[:],
                                              start=False, stop=True, skip_group_check=sgc), reads=[K.Cimn, hi], writes=[bk[5]], accum=True)
        if d == 0:
            yt = ysb[par]
            copy_op(P, "act", yt[:].rearrange("p a b -> p (a b)"), bk[5][:], [bk[5]], [yt])
            P.dma(ysT[:, :, i * 128:(i + 1) * 128], yt[:], reads=[yt], writes=[C.ys_tr[i]])
        else:
            f = lambda t: t[:].rearrange("p a b -> p (a b)")
            P.op("dve", lambda e: e.tensor_tensor(out=f(yv), in0=bk[5][:], in1=f(yf[par]), op=ALU.add), reads=[bk[5], yf[par]], writes=[yv])
            for ch in range(4):
                P.op("dve", lambda e, ch=ch: e.scalar_tensor_tensor(out=yv[:, ch, :], in0=uTt[:, ch, :], scalar=dcol[:, ch:ch + 1], in1=yv[:, ch, :],
                                                                    op0=ALU.mult, op1=ALU.add), reads=[uTt, dcol, yv], writes=[yv], accum=True)
            P.op("act", lambda e: e.activation(out=f(yg), in_=f(yv), func=AF.Gelu_apprx_tanh), reads=[yv], writes=[yg])
            copy_op(P, "pool", f(ygb), f(yg), [yg], [ygb])
            for co in range(4):
                for kc in range(4):
                    P.op("pe", lambda e, co=co, kc=kc: e.matmul(bk[6][:, co * 128:(co + 1) * 128], lhsT=wg[:, kc, co * 128:(co + 1) * 128],
                                                                rhs=ygb[:, kc, :], start=(kc == 0), stop=(kc == 3)),
                         reads=[wg, ygb], writes=[bk[6]], accum=not (co == 0 and kc == 0))
            for co in range(4):
                P.op("act", lambda e, co=co: e.activation(out=sg[:, co, :], in_=bk[6][:, co * 128:(co + 1) * 128], func=AF.Sigmoid,
                                                          bias=gbcol[:, co:co + 1]), reads=[bk[6], gbcol], writes=[sg], accum=(co > 0))
            for c in range(4):
                P.op("pe", lambda e, c=c: e.transpose(out=bk[7][:, c * 128:(c + 1) * 128], in_=zt[par][:, c * 128:(c + 1) * 128], identity=C.ident[:]),
                     reads=[zt[par], C.ident], writes=[bk[7]], accum=(c > 0))
            P.op("act", lambda e: e.activation(out=f(szT), in_=bk[7][:], func=AF.Silu), reads=[bk[7]], writes=[szT])
            P.op("dve", lambda e: e.tensor_tensor(out=f(gs), in0=f(yg), in1=f(sg), op=ALU.mult), reads=[yg, sg], writes=[gs])
            P.op("pool", lambda e: e.tensor_tensor(out=f(ya[par]), in0=f(gs), in1=f(szT), op=ALU.mult), reads=[gs, szT], writes=[ya[par]])
            P.dma(mixT[:, 0:4, i * 128:(i + 1) * 128], ya[par][:], reads=[ya[par]], writes=[C.mix_tr[i]])
    P.barrier()
    st.close(); P.stack = P.gstack


def even_layer(P, C, l, x_src, x_dst, xs_tr, xd_tr):
    import os
    dbg = int(os.environ.get("EDBG", "9"))
    even_phaseA(P, C, l, x_src, xs_tr)
    if dbg >= 1:
        s5_pass(P, C, l, 0)
        s5_pass(P, C, l, 1)
    if dbg >= 2:
        rwkv_pass(P, C, l, 0, x_src, x_dst, xs_tr, xd_tr)
        rwkv_pass(P, C, l, 1, x_src, x_dst, xs_tr, xd_tr)


def rwkv_pass(P, C, l, d, x_src, x_dst, xs_tr, xd_tr):
    NT = C.NT
    half = NT // 2
    st = ExitStack(); P.stack = st
    I = C.inp
    bk = C.banks
    f32 = lambda shape: P.sb(shape)
    b16 = lambda shape: P.sb(shape, BF16)
    tt = lambda eng, o, a, b, op, rd, wr, accum=False: P.op(eng, lambda e: e.tensor_tensor(out=o, in0=a, in1=b, op=op), reads=rd, writes=wr, accum=accum)

    mu = f32([128, 1664]); P.dma(mu[:], I["rw_mu_rep"][l], writes=[mu])
    w0 = f32([128, 512]); P.dma(w0[:], I["rw_w0_rep"][l, d], writes=[w0])
    a0 = f32([128, 512]); P.dma(a0[:], I["rw_a0_rep"][l], writes=[a0])
    kkp = f32([128, 512]); P.dma(kkp[:], I["rw_k_k_rep"][l], writes=[kkp])
    kap = f32([128, 512]); P.dma(kap[:], I["rw_k_a_rep"][l], writes=[kap])
    ups = f32([128, 2, 512])
    P.dma(ups[0:64, 0, :], I["rw_w_up"][l, d], writes=[ups])
    P.dma(ups[64:128, 1, :], I["rw_a_up"][l], writes=[ups])
    triI = C.tri[d]; triE = C.triE[d]; triET = C.triE[1 - d]
    eye_b = C.ident
    if d == 1:
        rkp = f32([128, 512]); P.dma(rkp[:], I["rw_r_k_rep"][l], writes=[rkp])
        lng = f32([128, 512]); P.dma(lng[:], I["rw_ln_g_rep"][l], writes=[lng])
        lnb = f32([128, 512]); P.dma(lnb[:], I["rw_ln_b_rep"][l], writes=[lnb])
        wo = b16([128, 8, 1024])
        wst = [f32([128, 1024]), f32([128, 1024])]
        load_weight_bf16(P, C, wo, lambda c: I["ev_w_out"][l, c * 128:(c + 1) * 128, :], 8, 1024, wst)

    cur = [f32([128, 1664])] * 2; prv = [f32([128, 1664])] * 2; nxt = [f32([128, 1664])] * 2
    hs = f32([128, 1664]); tsum = f32([128, 1664])
    twla = f32([128, 128]); twlaT = f32([128, 128])
    a_t = f32([128, 512]); e2 = f32([128, 512]); tmp = f32([128, 512]); tmp2 = f32([128, 512])
    kkn = f32([128, 512]); pss = f32([128, 8]); prn = f32([128, 8])
    p_t = f32([128, 512]); q_t = f32([128, 512]); kp = f32([128, 512])
    GI = f32([128, 512]); GIinv = f32([128, 512]); GE = f32([128, 512])
    Pd = f32([128, 512]); Qd = f32([128, 512]); Kd = f32([128, 512]); Rd = f32([128, 512])
    Pdb = b16([128, 512]); Qdb = b16([128, 512]); Kdb = b16([128, 512]); Vb = b16([128, 512])
    PR = b16([64, 8, 2, 128]); QTt = b16([64, 8, 128]); KTt = b16([64, 8, 128])
    Bm = [b16([128, 8, 128]) for _ in range(2)]; Am = [b16([128, 8, 128]) for _ in range(2)]; Pm = b16([128, 8, 128])
    MqT = b16([128, 8, 128]); LkT = b16([128, 8, 128]); MkT = b16([128, 8, 128])
    LkV = f32([128, 512]); KV = f32([64, 8, 64])
    Z = f32([64, 8, 64]); Zb = b16([64, 8, 64]); ZK = f32([64, 8, 64]); ZKg = f32([64, 8, 64]); Ztmp = f32([64, 8, 64])
    gcol = f32([64, 8]); onescol = C.ones_col
    rhs_sb = b16([128, 512]); U_sb = b16([128, 512])
    ysb = [f32([128, 512]) for _ in range(2)]
    P.op("pool", lambda e: e.memset(Z[:], 0.0), writes=[Z])
    P.op("pool", lambda e: e.memset(Zb[:], 0.0), writes=[Zb])
    if d == 1:
        yfw = [f32([128, 512]) for _ in range(2)]
        zrw = [f32([128, 512]) for _ in range(2)]
        xres = [f32([128, 1024]) for _ in range(2)]
        mean = f32([128, 8]); var = f32([128, 8]); cent = f32([128, 512]); rkk = f32([128, 512]); bon = f32([128, 8])
        yb = f32([128, 512]); szr = f32([128, 512])
        mixA = [b16([128, 4, 128]) for _ in range(2)]; ybT = b16([128, 4, 128])
        xo = [f32([128, 1024]) for _ in range(2)]
        mixT = C.mixT.ap().rearrange("(k p) t -> p k t", p=128)

    v3 = lambda t, n=8: t[:].rearrange("p (h d) -> p h d", h=n)
    order = list(range(NT)) if d == 0 else list(range(NT - 1, -1, -1))
    last = 127 if d == 0 else 0
    HW = C.hrw
    for it, i in enumerate(order):
        par = it % 2
        r0 = i * 128
        c_, p_, n_ = cur[par], prv[par], nxt[par]
        P.dma(c_[:], HW[r0:r0 + 128, 1024:2688], reads=[C.proj_tr[i]], writes=[c_])
        if i == 0:
            P.op("pool", lambda e: e.memset(p_[:], 0.0), writes=[p_])
            P.dma(p_[1:128, :], HW[r0:r0 + 127, 1024:2688], reads=[C.proj_tr[i]], writes=[p_])
        else:
            P.dma(p_[:], HW[r0 - 1:r0 + 127, 1024:2688], reads=[C.proj_tr[i], C.proj_tr[i - 1]], writes=[p_])
        if i == NT - 1:
            P.op("pool", lambda e: e.memset(n_[:], 0.0), writes=[n_])
            P.dma(n_[0:127, :], HW[r0 + 1:r0 + 128, 1024:2688], reads=[C.proj_tr[i]], writes=[n_])
        else:
            P.dma(n_[:], HW[r0 + 1:r0 + 129, 1024:2688], reads=[C.proj_tr[i], C.proj_tr[i + 1]], writes=[n_])
        if d == 1:
            P.dma(yfw[par][:], C.yrw[r0:r0 + 128, :], reads=[C.yrw_tr[i]], writes=[yfw[par]])
            P.dma(zrw[par][:], HW[r0:r0 + 128, 2688:3200], reads=[C.proj_tr[i]], writes=[zrw[par]])
            P.dma(xres[par][:], x_src[r0:r0 + 128, :], reads=[xs_tr[i]], writes=[xres[par]])
            P.dma(mixA[par][:], mixT[:, 0:4, r0:r0 + 128], reads=[C.mix_tr[i]], writes=[mixA[par]])
        if i == half:
            P.op("dve", lambda e: e.tensor_scalar(out=p_[:], in0=p_[:], scalar1=C.flags[:, 2:3], scalar2=None, op0=ALU.mult), reads=[p_, C.flags], writes=[p_])
        if i == half - 1:
            P.op("dve", lambda e: e.tensor_scalar(out=n_[:], in0=n_[:], scalar1=C.flags[:, 3:4], scalar2=None, op0=ALU.mult), reads=[n_, C.flags], writes=[n_])
        tt("pool", tsum[:], p_[:], n_[:], ALU.add, [p_, n_], [tsum])
        P.op("dve", lambda e: e.scalar_tensor_tensor(out=tsum[:], in0=tsum[:], scalar=0.5, in1=c_[:], op0=ALU.mult, op1=ALU.subtract),
             reads=[tsum, c_], writes=[tsum])
        tt("pool", tsum[:], tsum[:], mu[:], ALU.mult, [tsum, mu], [tsum])
        tt("dve", hs[:], tsum[:], c_[:], ALU.add, [tsum, c_], [hs])
        r_ap, k_ap, v_ap = hs[:, 0:512], hs[:, 512:1024], hs[:, 1024:1536]
        P.op("act", lambda e: e.activation(out=twla[:, 0:64], in_=hs[:, 1536:1600], func=AF.Tanh), reads=[hs], writes=[twla])
        copy_op(P, "dve", twla[:, 64:128], hs[:, 1600:1664], [hs], [twla], accum=True)
        g0 = C.gbank()
        P.op("pe", lambda e: e.transpose(out=g0[:, 0:128], in_=twla[:], identity=C.ident[:]), reads=[twla, C.ident], writes=[g0])
        copy_op(P, "dve", twlaT[:], g0[:, 0:128], [g0], [twlaT])
        g1 = C.gbank()
        P.op("pe", lambda e: e.matmul(g1[:], lhsT=twlaT[64:128, :], rhs=ups[64:128, 1, :], start=True, stop=True), reads=[twlaT, ups], writes=[g1])
        tt("dve", a_t[:], g1[:], a0[:], ALU.add, [g1, a0], [a_t])
        P.op("act", lambda e: e.activation(out=a_t[:], in_=a_t[:], func=AF.Sigmoid), reads=[a_t], writes=[a_t])
        g2 = C.gbank()
        P.op("pe", lambda e: e.matmul(g2[:], lhsT=twlaT[0:64, :], rhs=ups[0:64, 0, :], start=True, stop=True), reads=[twlaT, ups], writes=[g2])
        tt("dve", e2[:], g2[:], w0[:], ALU.add, [g2, w0], [e2])
        P.op("act", lambda e: e.activation(out=e2[:], in_=e2[:], func=AF.Exp, scale=-1.0), reads=[e2], writes=[e2])
        P.op("act", lambda e: e.activation(out=e2[:], in_=e2[:], func=AF.Ln, bias=1.0), reads=[e2], writes=[e2])
        P.op("act", lambda e: e.activation(out=e2[:], in_=e2[:], func=AF.Exp, scale=-1.0, bias=-0.5), reads=[e2], writes=[e2])
        tt("pool", kkn[:], k_ap, kkp[:], ALU.mult, [hs, kkp], [kkn])
        tt("pool", tmp[:], kkn[:], kkn[:], ALU.mult, [kkn], [tmp])
        P.op("dve", lambda e: e.tensor_reduce(out=pss[:], in_=v3(tmp), axis=AX.X, op=ALU.add), reads=[tmp], writes=[pss])
        P.op("act", lambda e: e.activation(out=pss[:], in_=pss[:], func=AF.Sqrt), reads=[pss], writes=[pss])
        P.op("dve", lambda e: e.tensor_scalar(out=pss[:], in0=pss[:], scalar1=1e-12, scalar2=None, op0=ALU.max), reads=[pss], writes=[pss])
        P.op("dve", lambda e: e.reciprocal(out=prn[:], in_=pss[:]), reads=[pss], writes=[prn])
        tt("dve", v3(p_t), v3(kkn), prn[:].unsqueeze(2).to_broadcast([128, 8, 64]), ALU.mult, [kkn, prn], [p_t])
        tt("pool", q_t[:], p_t[:], a_t[:], ALU.mult, [p_t, a_t], [q_t])
        P.op("dve", lambda e: e.scalar_tensor_tensor(out=tmp[:], in0=a_t[:], scalar=-1.0, in1=kap[:], op0=ALU.add, op1=ALU.mult), reads=[a_t, kap], writes=[tmp])
        P.op("dve", lambda e: e.scalar_tensor_tensor(out=kp[:], in0=tmp[:], scalar=1.0, in1=k_ap, op0=ALU.add, op1=ALU.mult), reads=[tmp, hs], writes=[kp])
        copy_op(P, "pool", Vb[:], v_ap, [hs], [Vb])
        gI = C.gbank()
        P.op("pe", lambda e: e.matmul(gI[:], lhsT=triI[:], rhs=e2[:], start=True, stop=True), reads=[triI, e2], writes=[gI])
        P.op("act", lambda e: e.activation(out=GI[:], in_=gI[:], func=AF.Exp, scale=-1.0), reads=[gI], writes=[GI])
        P.op("act", lambda e: e.activation(out=GIinv[:], in_=gI[:], func=AF.Exp), reads=[gI], writes=[GIinv])
        gE = C.gbank()
        P.op("pe", lambda e: e.matmul(gE[:], lhsT=triE[:], rhs=e2[:], start=True, stop=True), reads=[triE, e2], writes=[gE])
        P.op("act", lambda e: e.activation(out=GE[:], in_=gE[:], func=AF.Exp, scale=-1.0), reads=[gE], writes=[GE])
        gT = C.gbank()
        for h in range(8):
            P.op("pe", lambda e, h=h: e.matmul(gT[0:64, h:h + 1], lhsT=e2[:, h * 64:(h + 1) * 64], rhs=onescol[:, 0:1], start=True, stop=True),
                 reads=[e2, onescol], writes=[gT], accum=(h > 0))
        P.op("act", lambda e: e.activation(out=gcol[:], in_=gT[0:64, 0:8], func=AF.Exp, scale=-1.0), reads=[gT], writes=[gcol])
        tt("dve", Pd[:], p_t[:], GE[:], ALU.mult, [p_t, GE], [Pd])
        tt("pool", Qd[:], q_t[:], GIinv[:], ALU.mult, [q_t, GIinv], [Qd])
        tt("dve", Kd[:], kp[:], GIinv[:], ALU.mult, [kp, GIinv], [Kd])
        tt("pool", Rd[:], r_ap, GI[:], ALU.mult, [hs, GI], [Rd])
        copy_op(P, "pool", Pdb[:], Pd[:], [Pd], [Pdb]); copy_op(P, "dve", Qdb[:], Qd[:], [Qd], [Qdb]); copy_op(P, "pool", Kdb[:], Kd[:], [Kd], [Kdb])
        for (src, dstfn, dstt) in ((Pd, lambda h: PR[:, h, 0, :], PR), (Rd, lambda h: PR[:, h, 1, :], PR), (Qd, lambda h: QTt[:, h, :], QTt), (Kd, lambda h: KTt[:, h, :], KTt)):
            for hb in range(2):
                g = C.gbank()
                for hl in range(4):
                    h = 4 * hb + hl
                    P.op("pe", lambda e, hl=hl, h=h, g=g, src=src: e.transpose(out=g[0:64, hl * 128:(hl + 1) * 128], in_=src[:, h * 64:(h + 1) * 64], identity=C.ident[:]),
                         reads=[src, C.ident], writes=[g], accum=(hl > 0))
                if dstt is PR:
                    which = 0 if src is Pd else 1
                    copy_op(P, ("dve", "act")[hb], PR[:, 4 * hb:4 * hb + 4, which, :], g[0:64, :].rearrange("p (a b) -> p a b", a=4), [g], [PR], accum=True)
                else:
                    copy_op(P, ("act", "dve")[hb], dstt[:, 4 * hb:4 * hb + 4, :], g[0:64, :].rearrange("p (a b) -> p a b", a=4), [g], [dstt], accum=(hb > 0))
        Bc, Ac = Bm[0], Am[0]
        for hg in range(4):
            gq = C.gbank(); gk = C.gbank()
            for hh in range(2):
                h = 2 * hg + hh
                P.op("pe", lambda e: e.matmul(gq[:, hh * 256:(hh + 1) * 256], lhsT=QTt[:, h, :],
                                              rhs=PR[:, h, :, :].rearrange("p a b -> p (a b)"), start=True, stop=True),
                     reads=[QTt, PR], writes=[gq], accum=(hh > 0))
                P.op("pe", lambda e: e.matmul(gk[:, hh * 256:(hh + 1) * 256], lhsT=KTt[:, h, :],
                                              rhs=PR[:, h, :, :].rearrange("p a b -> p (a b)"), start=True, stop=True),
                     reads=[KTt, PR], writes=[gk], accum=(hh > 0))
            gq4 = gq[:].rearrange("p (h a b) -> p h a b", h=2, a=2)
            gk4 = gk[:].rearrange("p (h a b) -> p h a b", h=2, a=2)
            hs2 = slice(2 * hg, 2 * hg + 2)
            mE = triE[:].unsqueeze(1).to_broadcast([128, 2, 128]); mI = triI[:].unsqueeze(1).to_broadcast([128, 2, 128])
            tt("dve", Bc[:, hs2, :], gq4[:, :, 0, :], mE, ALU.mult, [gq, triE], [Bc], accum=(hg > 0))
            tt("dve", MqT[:, hs2, :], gq4[:, :, 1, :], mI, ALU.mult, [gq, triI], [MqT], accum=(hg > 0))
            tt("dve", LkT[:, hs2, :], gk4[:, :, 0, :], mE, ALU.mult, [gk, triE], [LkT], accum=(hg > 0))
            tt("dve", MkT[:, hs2, :], gk4[:, :, 1, :], mI, ALU.mult, [gk, triI], [MkT], accum=(hg > 0))
        for hb in range(2):
            g = C.gbank()
            for hl in range(4):
                h = 4 * hb + hl
                P.op("pe", lambda e: e.matmul(g[:, hl * 128:(hl + 1) * 128], lhsT=PR[:, h, 0, :], rhs=QTt[:, h, :],
                                              start=True, stop=True), reads=[PR, QTt], writes=[g], accum=(hl > 0))
            tt("dve", Ac[:, 4 * hb:4 * hb + 4, :], g[:].rearrange("p (h b) -> p h b", h=4), triET[:].unsqueeze(1).to_broadcast([128, 4, 128]), ALU.mult,
               [g, triET], [Ac], accum=(hb > 0))
        P.op("dve", lambda e: e.scalar_tensor_tensor(out=Pm[:], in0=Bc[:], scalar=-1.0, in1=eye_b[:].unsqueeze(1).to_broadcast([128, 8, 128]),
                                                     op0=ALU.mult, op1=ALU.add), reads=[Bc, eye_b], writes=[Pm])
        for lev in range(6):
            Bn, An = Bm[(lev + 1) % 2], Am[(lev + 1) % 2]
            for hb in range(2):
                gA = C.gbank()
                for hl in range(4):
                    h = 4 * hb + hl
                    P.op("pe", lambda e: e.matmul(gA[:, hl * 128:(hl + 1) * 128], lhsT=Bc[:, h, :], rhs=Ac[:, h, :], start=True, stop=True),
                         reads=[Bc, Ac], writes=[gA], accum=(hl > 0))
                copy_op(P, ("act", "dve")[hb], An[:, 4 * hb:4 * hb + 4, :].rearrange("p a b -> p (a b)"), gA[:], [gA], [An], accum=(hb > 0))
                if lev < 5:
                    gB = C.gbank()
                    for hl in range(4):
                        h = 4 * hb + hl
                        P.op("pe", lambda e: e.matmul(gB[:, hl * 128:(hl + 1) * 128], lhsT=Ac[:, h, :], rhs=Bc[:, h, :], start=True, stop=True),
                             reads=[Bc, Ac], writes=[gB], accum=(hl > 0))
                    copy_op(P, ("dve", "act")[hb], Bn[:, 4 * hb:4 * hb + 4, :].rearrange("p a b -> p (a b)"), gB[:], [gB], [Bn], accum=(hb > 0))
            for hb in range(2):
                gP = C.gbank()
                for hl in range(4):
                    h = 4 * hb + hl
                    P.op("pe", lambda e: e.matmul(gP[:, hl * 128:(hl + 1) * 128], lhsT=An[:, h, :], rhs=Pm[:, h, :], start=True, stop=True),
                         reads=[An, Pm], writes=[gP], accum=(hl > 0))
                tt("dve", Pm[:, 4 * hb:4 * hb + 4, :].rearrange("p a b -> p (a b)"), gP[:], Pm[:, 4 * hb:4 * hb + 4, :].rearrange("p a b -> p (a b)"),
                   ALU.add, [gP, Pm], [Pm], accum=True)
            Bc, Ac = Bn, An
        g = C.gbank()
        for h in range(8):
            P.op("pe", lambda e: e.matmul(g[:, h * 64:(h + 1) * 64], lhsT=LkT[:, h, :], rhs=Vb[:, h * 64:(h + 1) * 64], start=True, stop=True),
                 reads=[LkT, Vb], writes=[g], accum=(h > 0))
        copy_op(P, "act", LkV[:], g[:], [g], [LkV])
        g = C.gbank()
        for h in range(8):
            P.op("pe", lambda e: e.matmul(g[0:64, h * 64:(h + 1) * 64], lhsT=Kdb[:, h * 64:(h + 1) * 64], rhs=Vb[:, h * 64:(h + 1) * 64],
                                          start=True, stop=True), reads=[Kdb, Vb], writes=[g], accum=(h > 0))
        copy_op(P, "dve", KV[:].rearrange("p a b -> p (a b)"), g[0:64, :], [g], [KV])
        if it > 0 and ((d == 0 and i == half) or (d == 1 and i == half - 1)):
            P.op("dve", lambda e: e.tensor_scalar(out=Z[:], in0=Z[:], scalar1=C.flags[0:64, 0:1], scalar2=None, op0=ALU.mult), reads=[Z, C.flags], writes=[Z])
            copy_op(P, "dve", Zb[:], Z[:], [Z], [Zb])
        gcb = gcol[:].unsqueeze(2).to_broadcast([64, 8, 64])
        tt("pool", ZK[:], Z[:], KV[:], ALU.add, [Z, KV], [ZK])
        tt("pool", ZKg[:], ZK[:], gcb, ALU.mult, [ZK, gcol], [ZKg])
        gz = bk[3]
        for h in range(8):
            P.op("pe", lambda e: e.matmul(gz[:, h * 64:(h + 1) * 64], lhsT=PR[:, h, 0, :], rhs=Zb[:, h, :], start=True, stop=True),
                 reads=[PR, Zb], writes=[gz], accum=(h > 0))
        tt("dve", rhs_sb[:], gz[:], LkV[:], ALU.add, [gz, LkV], [rhs_sb])
        gu = bk[4]
        for h in range(8):
            P.op("pe", lambda e: e.matmul(gu[:, h * 64:(h + 1) * 64], lhsT=Pm[:, h, :], rhs=rhs_sb[:, h * 64:(h + 1) * 64], start=True, stop=True),
                 reads=[Pm, rhs_sb], writes=[gu], accum=(h > 0))
        P.op("act", lambda e: e.activation(out=U_sb[:], in_=gu[:], func=AF.Copy, scale=-1.0), reads=[gu], writes=[U_sb])
        gy = bk[5]
        for h in range(8):
            osl = gy[:, h * 64:(h + 1) * 64]
            P.op("pe", lambda e: e.matmul(osl, lhsT=PR[:, h, 1, :], rhs=Zb[:, h, :], start=True, stop=False),
                 reads=[PR, Zb], writes=[gy], accum=(h > 0))
            P.op("pe", lambda e: e.matmul(osl, lhsT=MqT[:, h, :], rhs=U_sb[:, h * 64:(h + 1) * 64], start=False, stop=False),
                 reads=[MqT, U_sb], writes=[gy], accum=True)
            P.op("pe", lambda e: e.matmul(osl, lhsT=MkT[:, h, :], rhs=Vb[:, h * 64:(h + 1) * 64], start=False, stop=True),
                 reads=[MkT, Vb], writes=[gy], accum=True)
        gq_ = bk[6]
        for h in range(8):
            P.op("pe", lambda e: e.matmul(gq_[0:64, h * 64:(h + 1) * 64], lhsT=Qdb[:, h * 64:(h + 1) * 64], rhs=U_sb[:, h * 64:(h + 1) * 64],
                                          start=True, stop=True), reads=[Qdb, U_sb], writes=[gq_], accum=(h > 0))
        tt("dve", Ztmp[:], gq_[0:64, :].rearrange("p (a b) -> p a b", a=8), gcb, ALU.mult, [gq_, gcol], [Ztmp])
        tt("dve", Z[:], Ztmp[:], ZKg[:], ALU.add, [Ztmp, ZKg], [Z])
        copy_op(P, "dve", Zb[:], Z[:], [Z], [Zb])
        if d == 0:
            yt = ysb[par]
            copy_op(P, "act", yt[:], gy[:], [gy], [yt])
            P.dma(C.yrw[r0:r0 + 128, :], yt[:], reads=[yt], writes=[C.yrw_tr[i]])
        else:
            y = ysb[par]
            tt("dve", y[:], gy[:], yfw[par][:], ALU.add, [gy, yfw[par]], [y])
            P.op("dve", lambda e: e.tensor_reduce(out=mean[:], in_=v3(y), axis=AX.X, op=ALU.add), reads=[y], writes=[mean])
            P.op("dve", lambda e: e.tensor_scalar(out=mean[:], in0=mean[:], scalar1=1.0 / 64, scalar2=None, op0=ALU.mult), reads=[mean], writes=[mean])
            tt("dve", v3(cent), v3(y), mean[:].unsqueeze(2).to_broadcast([128, 8, 64]), ALU.subtract, [y, mean], [cent])
            tt("pool", tmp2[:], cent[:], cent[:], ALU.mult, [cent], [tmp2])
            P.op("dve", lambda e: e.tensor_reduce(out=var[:], in_=v3(tmp2), axis=AX.X, op=ALU.add), reads=[tmp2], writes=[var])
            P.op("act", lambda e: e.activation(out=var[:], in_=var[:], func=AF.Sqrt, scale=1.0 / 64, bias=64e-5), reads=[var], writes=[var])
            P.op("dve", lambda e: e.reciprocal(out=var[:], in_=var[:]), reads=[var], writes=[var])
            tt("dve", v3(cent), v3(cent), var[:].unsqueeze(2).to_broadcast([128, 8, 64]), ALU.mult, [cent, var], [cent])
            tt("pool", cent[:], cent[:], lng[:], ALU.mult, [cent, lng], [cent])
            tt("pool", cent[:], cent[:], lnb[:], ALU.add, [cent, lnb], [cent])
            tt("pool", rkk[:], r_ap, kp[:], ALU.mult, [hs, kp], [rkk])
            tt("pool", rkk[:], rkk[:], rkp[:], ALU.mult, [rkk, rkp], [rkk])
            P.op("dve", lambda e: e.tensor_reduce(out=bon[:], in_=v3(rkk), axis=AX.X, op=ALU.add), reads=[rkk], writes=[bon])
            tt("dve", v3(rkk), hs[:, 1024:1536].rearrange("p (h d) -> p h d", h=8), bon[:].unsqueeze(2).to_broadcast([128, 8, 64]), ALU.mult, [hs, bon], [rkk])
            tt("pool", cent[:], cent[:], rkk[:], ALU.add, [cent, rkk], [cent])
            P.op("act", lambda e: e.activation(out=szr[:], in_=zrw[par][:], func=AF.Silu), reads=[zrw[par]], writes=[szr])
            tt("dve", yb[:], cent[:], szr[:], ALU.mult, [cent, szr], [yb])
            transpose_to(P, C, yb, 4, ybT)
            xot = xo[par]
            for gcol_i in range(2):
                bank = C.gbank()
                for c in range(8):
                    lhs = mixA[par][:, c, :] if c < 4 else ybT[:, c - 4, :]
                    P.op("pe", lambda e, c=c, lhs=lhs, bank=bank: e.matmul(bank[:], lhsT=lhs, rhs=wo[:, c, gcol_i * 512:(gcol_i + 1) * 512],
                                                                          start=(c == 0), stop=(c == 7)),
                         reads=[mixA[par], ybT, wo], writes=[bank], accum=(c > 0))
                tt("dve", xot[:, gcol_i * 512:(gcol_i + 1) * 512], bank[:], xres[par][:, gcol_i * 512:(gcol_i + 1) * 512], ALU.add,
                   [bank, xres[par]], [xot], accum=(gcol_i > 0))
            P.dma(x_dst[r0:r0 + 128, :], xot[:], reads=[xot], writes=[xd_tr[i]])
    P.barrier()
    st.close(); P.stack = P.gstack


NT_FULL = 128
LAYERS = [("even", 0), ("odd", 0), ("even", 1), ("odd", 1)]


def core_flags(cont):
    fl = np.ones((128, 4), np.float32)
    fl[:, 0] = cont
    fl[:, 1] = (cont - 1.0) * 30000.0
    fl[0, 2] = cont
    fl[127, 3] = cont
    return fl


def kernel(**inputs):
    xp = np.asarray(inputs["x_prompt"], np.float32)
    xs = np.asarray(inputs["x_sample"], np.float32)
    m = host_layout(inputs, LAYERS)
    nc, gst = build_program(NT_FULL, LAYERS)
    streams = [(xs[0], 1.0), (xp[0:2].reshape(16384, D), 0.0), (xp[2:4].reshape(16384, D), 0.0)]
    maps = []
    for c in range(8):
        xin, cont = streams[c] if c < 3 else streams[1 + (c % 2)]
        mm = dict(m)
        mm["flags"] = core_flags(cont)
        mm["xin"] = np.ascontiguousarray(xin)
        maps.append(mm)
    res = run_bass_kernel_spmd(nc, maps, core_ids=list(range(8)))
    y_sample = np.asarray(res.results[0]["xout"], np.float32).reshape(1, 16384, D)
    y_prompt = np.concatenate([np.asarray(res.results[1]["xout"], np.float32).reshape(2, 8192, D),
                               np.asarray(res.results[2]["xout"], np.float32).reshape(2, 8192, D)], axis=0)
    return (y_prompt, y_sample)
```

```python
import numpy as np
from contextlib import ExitStack
import concourse.bass as bass
import concourse.mybir as mybir
from concourse.bass_utils import run_bass_kernel_spmd

F32 = mybir.dt.float32
BF16 = mybir.dt.bfloat16
AF = mybir.ActivationFunctionType
ALU = mybir.AluOpType
AX = mybir.AxisListType

import os as _os
N_DMA_SLOTS = int(_os.environ.get("NSLOTS", "24"))
D = 1024
EPS = 1e-6
NEG = -30000.0


import types


def _snap(fn):
    if fn.__closure__ is None:
        return fn
    cells = tuple(types.CellType(c.cell_contents) for c in fn.__closure__)
    return types.FunctionType(fn.__code__, fn.__globals__, fn.__name__, fn.__defaults__, cells)


class T:
    __slots__ = ("t", "name", "writers", "readers", "war")

    def __init__(self, t, name=""):
        self.t = t
        self.name = name
        self.writers = []
        self.readers = []
        self.war = []

    def __getitem__(self, idx):
        return self.t[idx]


class Prog:
    ENGS = ("pe", "act", "dve", "pool", "sp")

    def __init__(self, nc, stack):
        self.nc = nc
        self.stack = stack
        self.gstack = stack
        self.ops = {e: [] for e in self.ENGS}
        self.cnt = {e: 0 for e in self.ENGS}
        self.seen = {e: {} for e in self.ENGS}
        self.sems = {e: stack.enter_context(nc.semaphore("s_" + e)) for e in self.ENGS}
        self.dma_sems = [stack.enter_context(nc.semaphore("s_dma%d" % i)) for i in range(N_DMA_SLOTS)]
        self.dma_n = 0
        self.cc_n = 0
        self.sems["cc"] = stack.enter_context(nc.semaphore("s_cc"))
        import os
        self.same_engine_sync = not os.environ.get("NOSES")
        self._uid = 0

    def sb(self, shape, dt=F32, name=None):
        self._uid += 1
        name = "sb%d" % self._uid
        return T(self.stack.enter_context(self.nc.sbuf_tensor(name, list(shape), dt)), name)

    def ps(self, shape, dt=F32):
        self._uid += 1
        name = "ps%d" % self._uid
        return T(self.stack.enter_context(self.nc.psum_tensor(name, list(shape), dt)), name)

    def dram(self, name, shape, dt=F32):
        return self.nc.dram_tensor(name, list(shape), dt, kind="Internal")

    def _need(self, eng, dep, waits):
        key, val, deng = dep
        if deng == eng and (eng == "pe" or not self.same_engine_sync):
            return
        if self.seen[eng].get(key, -1) >= val:
            return
        self.seen[eng][key] = val
        waits.append((key, val))

    def _deps(self, eng, reads, writes, accum):
        waits = []
        for t in reads:
            for w in t.writers:
                self._need(eng, w, waits)
        for t in writes:
            if not accum:
                for w in t.writers:
                    self._need(eng, w, waits)
            else:
                for w in t.war:
                    self._need(eng, w, waits)
            for r in t.readers:
                self._need(eng, r, waits)
        return waits

    def _commit(self, tok, reads, writes, accum):
        for t in reads:
            t.readers.append(tok)
        for t in writes:
            if accum:
                t.writers.append(tok)
                t.war = t.war + t.readers
            else:
                t.war = t.writers + t.readers
                t.writers = [tok]
            t.readers = []

    def _sem(self, key):
        return self.sems[key] if isinstance(key, str) else self.dma_sems[key]

    def op(self, eng, fn, reads=(), writes=(), accum=False):
        import os
        if eng == "pool" and os.environ.get("NOPOOL"):
            eng = "dve"
        kmax = int(os.environ.get("KMAX", "0"))
        if kmax and sum(self.cnt.values()) >= kmax:
            return None
        waits = self._deps(eng, reads, writes, accum)
        self.cnt[eng] += 1
        tok = (eng, self.cnt[eng], eng)
        self._commit(tok, reads, writes, accum)
        self.ops[eng].append((waits, _snap(fn), (eng, 1)))
        return tok

    def dma(self, out_ap, in_ap, reads=(), writes=(), q="sp", accum=False):
        waits = self._deps(q, reads, writes, accum)
        i = self.dma_n
        self.dma_n += 1
        slot = i % N_DMA_SLOTS
        val = 16 * (i // N_DMA_SLOTS + 1)
        if i >= N_DMA_SLOTS and self.seen[q].get(slot, -1) < val - 16:
            self.seen[q][slot] = val - 16
            waits.append((slot, val - 16))
        tok = (slot, val, "dma")
        self._commit(tok, reads, writes, accum)

        def fn(e, out_ap=out_ap, in_ap=in_ap):
            return e.dma_start(out=out_ap, in_=in_ap)
        self.ops[q].append((waits, fn, (slot, 16)))
        return tok

    def collective(self, src_ap, dst_ap, reads=(), writes=(), groups=((0, 1), (2, 3), (4, 5), (6, 7))):
        q = "pool"
        waits = self._deps(q, reads, writes, False)
        self.cc_n += 1
        tok = ("cc", self.cc_n, "cc")
        self._commit(tok, reads, writes, False)
        rg = [list(g) for g in groups]

        def fn(e):
            return e.collective_compute("AllGather", ALU.bypass, replica_groups=rg, ins=[src_ap], outs=[dst_ap])
        self.ops[q].append((waits, fn, ("cc", 1)))
        return tok

    def barrier(self):
        for e in self.ENGS:
            waits = []
            for o in self.ENGS:
                if o != e and self.cnt[o] > 0 and self.seen[e].get(o, -1) < self.cnt[o]:
                    self.seen[e][o] = self.cnt[o]
                    waits.append((o, self.cnt[o]))
            if self.cc_n > 0 and self.seen[e].get("cc", -1) < self.cc_n:
                self.seen[e]["cc"] = self.cc_n
                waits.append(("cc", self.cc_n))
            n = self.dma_n
            for slot in range(min(n, N_DMA_SLOTS)):
                last_i = ((n - 1 - slot) // N_DMA_SLOTS) * N_DMA_SLOTS + slot
                v = 16 * (last_i // N_DMA_SLOTS + 1)
                if self.seen[e].get(slot, -1) < v:
                    self.seen[e][slot] = v
                    waits.append((slot, v))
            if waits:
                self.ops[e].append((waits, None, None))

    def emit(self):
        nc = self.nc
        self.barrier()
        block = self.gstack.enter_context(nc.Block())
        prog = self

        def run(engname, e):
            for waits, fn, inc in prog.ops[engname]:
                for key, val in waits:
                    e.wait_ge(prog._sem(key), val)
                if fn is not None:
                    fn(e).then_inc(prog._sem(inc[0]), inc[1])

        @block.tensor
        def _(e):
            run("pe", e)

        @block.scalar
        def _(e):
            run("act", e)

        @block.vector
        def _(e):
            run("dve", e)

        @block.gpsimd
        def _(e):
            run("pool", e)

        @block.sync
        def _(e):
            run("sp", e)


class Ctx:
    pass


def rr(P, C, key, engs):
    C.rr[key] = C.rr.get(key, -1) + 1
    return engs[C.rr[key] % len(engs)]


def copy_op(P, eng, out_ap, in_ap, reads, writes, accum=False):
    if eng == "act":
        P.op("act", lambda e: e.activation(out=out_ap, in_=in_ap, func=AF.Copy), reads=reads, writes=writes, accum=accum)
    elif eng == "dve":
        P.op("dve", lambda e: e.tensor_copy(out=out_ap, in_=in_ap), reads=reads, writes=writes, accum=accum)
    else:
        P.op("pool", lambda e: e.tensor_copy(out=out_ap, in_=in_ap), reads=reads, writes=writes, accum=accum)


def load_weight_bf16(P, C, dst, src_ap_fn, nchunk, ncols, stage):
    for c in range(nchunk):
        s = stage[c % len(stage)]
        P.dma(s[:, 0:ncols], src_ap_fn(c), reads=[], writes=[s])
        eng = ("act", "dve", "pool")[c % 3]
        copy_op(P, eng, dst[:, c, :], s[:, 0:ncols], [s], [dst], accum=True)


def rmsnorm_T(P, C, xt, gn, hT, S):
    junk, ss, ss2, rs, h = S.junk, S.ss, S.ss2, S.rs, S.h
    P.op("act", lambda e: e.activation(out=junk[:], in_=xt[:], func=AF.Square, accum_out=ss[:]),
         reads=[xt], writes=[junk, ss])
    P.op("act", lambda e: e.activation(out=ss2[:], in_=ss[:], func=AF.Sqrt, scale=1.0 / D, bias=EPS),
         reads=[ss], writes=[ss2])
    P.op("dve", lambda e: e.reciprocal(out=rs[:], in_=ss2[:]), reads=[ss2], writes=[rs])
    P.op("dve", lambda e: e.scalar_tensor_tensor(out=h[:], in0=xt[:], scalar=rs[:, 0:1], in1=gn[:],
                                                 op0=ALU.mult, op1=ALU.mult), reads=[xt, rs, gn], writes=[h])
    transpose_to(P, C, h, 8, hT)


def transpose_to(P, C, src, nblk, dst, src_off=0):
    for g0 in range(0, nblk, 4):
        n = min(4, nblk - g0)
        bank = C.gbank()
        for c in range(n):
            P.op("pe", lambda e, c=c, bank=bank, g0=g0: e.transpose(
                out=bank[:, c * 128:(c + 1) * 128],
                in_=src[:, src_off + (g0 + c) * 128: src_off + (g0 + c + 1) * 128], identity=C.ident[:]),
                reads=[src, C.ident], writes=[bank], accum=(c > 0))
        eng = rr(P, C, "tev", ("act", "dve"))
        copy_op(P, eng, dst[:, g0:g0 + n, :], bank[:, 0:n * 128].rearrange("p (a b) -> p a b", a=n),
                [bank], [dst], accum=(g0 > 0))


def transpose_heads(P, C, src, nheads, dst, src_off=0):
    for g0 in range(0, nheads, 4):
        n = min(4, nheads - g0)
        bank = C.gbank()
        for c in range(n):
            P.op("pe", lambda e, c=c, bank=bank, g0=g0: e.transpose(
                out=bank[0:64, c * 128:(c + 1) * 128],
                in_=src[:, src_off + (g0 + c) * 64: src_off + (g0 + c + 1) * 64], identity=C.ident[:]),
                reads=[src, C.ident], writes=[bank], accum=(c > 0))
        eng = rr(P, C, "tev", ("act", "dve"))
        copy_op(P, eng, dst[:, g0:g0 + n, :], bank[0:64, 0:n * 128].rearrange("p (a b) -> p a b", a=n),
                [bank], [dst], accum=(g0 > 0))


def matmul_group(P, C, bank, ncols, hT, W, col0, nk=8, out_off=0):
    for c in range(nk):
        P.op("pe", lambda e, c=c: e.matmul(bank[:, out_off:out_off + ncols], lhsT=hT[:, c, :],
                                           rhs=W[:, c, col0:col0 + ncols], start=(c == 0), stop=(c == nk - 1)),
             reads=[hT, W], writes=[bank], accum=(c > 0))


def odd_layer(P, C, l, x_src, x_dst, xs_tr, xd_tr):
    NT = C.NT
    st = ExitStack()
    P.stack = st
    I = C.inp
    wq = P.sb([128, 8, 2560], BF16)
    wo = P.sb([128, 8, 1024], BF16)
    st2 = ExitStack(); P.stack = st2
    stage = [P.sb([128, 2560]), P.sb([128, 2560])]
    load_weight_bf16(P, C, wq, lambda c: I["od_w_in"][l, c * 128:(c + 1) * 128, :], 8, 2560, stage)
    load_weight_bf16(P, C, wo, lambda c: I["od_w_out"][l, c * 128:(c + 1) * 128, :], 8, 1024, stage)
    P.barrier()
    st2.close(); P.stack = st
    gn = P.sb([128, 1024]); gq = P.sb([128, 64]); gk = P.sb([128, 64]); esink = P.sb([128, 16])
    P.dma(gn[:], I["od_norm_rep"][l], writes=[gn])
    P.dma(gq[:], I["qg_rep"][l], writes=[gq])
    P.dma(gk[:], I["kg_rep"][l], writes=[gk])
    P.dma(esink[:], I["sink_rep"][l], writes=[esink])
    P.op("act", lambda e: e.activation(out=esink[:], in_=esink[:], func=AF.Exp), reads=[esink], writes=[esink])
    biasT = P.sb([128, 16, 512])
    for j in range(4):
        P.dma(biasT[:, 4 * j:4 * j + 4, :], I["alibi"][j].rearrange("r s q -> s r q"), writes=[biasT], accum=True)
    KTh = P.sb([64, 4, 128], BF16); Vh = P.sb([128, 4, 72], BF16)
    kh2 = P.sb([64, 2, 512], BF16); vh2 = P.sb([128, 2, 288], BF16)

    S = Ctx()
    S.junk = P.sb([128, 1024]); S.ss = P.sb([128, 1]); S.ss2 = P.sb([128, 1]); S.rs = P.sb([128, 1]); S.h = P.sb([128, 1024])
    hT = P.sb([128, 8, 128], BF16)
    xring = [P.sb([128, 1024]) for _ in range(3)]
    qf = P.sb([128, 1024]); qsq = P.sb([128, 1024]); qss = P.sb([128, 16]); qr = P.sb([128, 16]); qn = P.sb([128, 1024])
    kf = P.sb([128, 256]); ksq = P.sb([128, 256]); kss = P.sb([128, 4]); kr = P.sb([128, 4]); kn = P.sb([128, 256])
    QT = [P.sb([64, 16, 128], BF16) for _ in range(3)]
    KT = [P.sb([64, 4, 128], BF16) for _ in range(4)]
    V = [P.sb([128, 4, 72], BF16) for _ in range(4)]
    for v in V:
        P.op("pool", lambda e, v=v: e.memset(v[:], 1.0), writes=[v])
    sz = [P.sb([128, 1024]) for _ in range(3)]
    sring = [P.sb([128, 512]) for _ in range(3)]
    pr = [P.sb([128, 512], BF16) for _ in range(6)]
    den = P.sb([128, 4]); rden = P.sb([128, 4])
    o = P.sb([128, 1024]); og = P.sb([128, 1024]); ogT = P.sb([128, 8, 128], BF16)
    xo = [P.sb([128, 1024]) for _ in range(2)]

    def rms_heads(src, sq, ssum, rinv, nh, g, outs):
        P.op("pool", lambda e: e.tensor_tensor(out=sq[:], in0=src[:], in1=src[:], op=ALU.mult), reads=[src], writes=[sq])
        P.op("dve", lambda e: e.tensor_reduce(out=ssum[:], in_=sq[:].rearrange("p (h d) -> p h d", h=nh), axis=AX.X, op=ALU.add),
             reads=[sq], writes=[ssum])
        P.op("act", lambda e: e.activation(out=ssum[:], in_=ssum[:], func=AF.Sqrt, scale=1.0 / 64, bias=EPS),
             reads=[ssum], writes=[ssum])
        P.op("dve", lambda e: e.reciprocal(out=rinv[:], in_=ssum[:]), reads=[ssum], writes=[rinv])
        P.op("dve", lambda e: e.tensor_tensor(out=sq[:].rearrange("p (h d) -> p h d", h=nh),
                                              in0=src[:].rearrange("p (h d) -> p h d", h=nh),
                                              in1=rinv[:].unsqueeze(2).to_broadcast([128, nh, 64]), op=ALU.mult),
             reads=[src, rinv], writes=[sq])
        for oi, (oap, ot) in enumerate(outs):
            P.op("pool", lambda e, oap=oap: e.tensor_tensor(out=oap, in0=sq[:].rearrange("p (h d) -> p h d", h=nh),
                                                            in1=g[:].unsqueeze(1).to_broadcast([128, nh, 64]), op=ALU.mult),
                 reads=[sq, g], writes=[ot], accum=(oi > 0))

    def stage1(j):
        xt = xring[j % 3]
        P.dma(xt[:], x_src[j * 128:(j + 1) * 128, :], reads=[xs_tr[j]], writes=[xt])
        rmsnorm_T(P, C, xt, gn, hT, S)
        for g in range(2):
            bank = C.gbank()
            matmul_group(P, C, bank, 512, hT, wq, g * 512)
            copy_op(P, rr(P, C, "qev", ("act", "dve")), qf[:, g * 512:(g + 1) * 512], bank[:], [bank], [qf], accum=(g > 0))
        rms_heads(qf, qsq, qss, qr, 16, gq, [(qn[:].rearrange("p (h d) -> p h d", h=16), qn)])
        transpose_heads(P, C, qn, 16, QT[j % 3])
        bank = C.gbank()
        matmul_group(P, C, bank, 512, hT, wq, 1024)
        copy_op(P, "dve", kf[:], bank[:, 0:256], [bank], [kf])
        Vt = V[j % 4]
        copy_op(P, "dve", Vt[:, :, 0:64], bank[:, 256:512].rearrange("p (h d) -> p h d", h=4), [bank], [Vt])
        rms_heads(kf, ksq, kss, kr, 4, gk, [(kn[:].rearrange("p (h d) -> p h d", h=4), kn)])
        transpose_heads(P, C, kn, 4, KT[j % 4])
        for g in range(2):
            bank = C.gbank()
            matmul_group(P, C, bank, 512, hT, wq, 1536 + g * 512)
            szt = sz[j % 3]
            P.op("act", lambda e, bank=bank, g=g, szt=szt: e.activation(out=szt[:, g * 512:(g + 1) * 512], in_=bank[:], func=AF.Silu),
                 reads=[bank], writes=[szt], accum=(g > 0))

    def halo_exchange():
        jl = NT - 1
        ktl, vl = KT[jl % 4], V[jl % 4]
        P.dma(C.ksrc[:, :], ktl[:].rearrange("p a b -> p (a b)"), reads=[ktl], writes=[C.ksrc_tr])
        P.dma(C.vsrc[:, :], vl[:].rearrange("p a b -> p (a b)"), reads=[vl], writes=[C.vsrc_tr])
        P.collective(C.ksrc[:, :], C.kdst[:, :], reads=[C.ksrc_tr], writes=[C.kdst_tr])
        P.collective(C.vsrc[:, :], C.vdst[:, :], reads=[C.vsrc_tr], writes=[C.vdst_tr])
        P.dma(kh2[:], C.kdst.ap().rearrange("(s p) n -> p s n", s=2), reads=[C.kdst_tr], writes=[kh2])
        P.dma(vh2[:], C.vdst.ap().rearrange("(s p) n -> p s n", s=2), reads=[C.vdst_tr], writes=[vh2])
        kf_ = KTh[:].rearrange("p a b -> p (a b)"); vf_ = Vh[:].rearrange("p a b -> p (a b)")
        P.op("dve", lambda e: e.tensor_scalar(out=kf_, in0=kh2[:, 0, :], scalar1=C.flags[0:64, 0:1], scalar2=None, op0=ALU.mult), reads=[kh2, C.flags], writes=[KTh])
        P.op("dve", lambda e: e.scalar_tensor_tensor(out=kf_, in0=kh2[:, 1, :], scalar=C.flags[0:64, 1:2], in1=kf_, op0=ALU.mult, op1=ALU.add),
             reads=[kh2, C.flags, KTh], writes=[KTh])
        P.op("dve", lambda e: e.tensor_scalar(out=vf_, in0=vh2[:, 0, :], scalar1=C.flags[:, 0:1], scalar2=None, op0=ALU.mult), reads=[vh2, C.flags], writes=[Vh])
        P.op("dve", lambda e: e.scalar_tensor_tensor(out=vf_, in0=vh2[:, 1, :], scalar=C.flags[:, 1:2], in1=vf_, op0=ALU.mult, op1=ALU.add),
             reads=[vh2, C.flags, Vh], writes=[Vh])

    def stage2(i):
        for jkv in range(4):
            blocks = [b for b in (i - 1, i, i + 1) if 0 <= b < NT]
            if i == NT - 1:
                blocks.append(NT)
            for b in blocks:
                rel = b - i + 1
                halo = (b == NT)
                KTb = KTh if halo else KT[b % 4]
                bank = C.sbank[rel]
                for hl in range(4):
                    hq = 4 * jkv + hl
                    P.op("pe", lambda e, bank=bank, hl=hl, b=b, hq=hq: e.matmul(
                        bank[:, hl * 128:(hl + 1) * 128], lhsT=KTb[:, jkv, :],
                        rhs=QT[i % 3][:, hq, :], start=True, stop=True),
                        reads=[KTb, QT[i % 3]], writes=[bank], accum=(hl > 0))
                s_t = sring[rel]
                bidx = 4 * jkv + (3 if halo else rel)
                P.op("dve", lambda e, bank=bank, s_t=s_t, rel=rel: e.scalar_tensor_tensor(
                    out=s_t[:], in0=bank[:], scalar=0.125, in1=biasT[:, bidx, :], op0=ALU.mult, op1=ALU.add),
                    reads=[bank, biasT], writes=[s_t])
                if halo:
                    P.op("dve", lambda e, s_t=s_t: e.tensor_scalar(out=s_t[:], in0=s_t[:], scalar1=C.flags[:, 2:3], scalar2=None,
                                                                   op0=ALU.add), reads=[s_t, C.flags], writes=[s_t])
                pt = pr[(jkv % 2) * 3 + rel]
                P.op("act", lambda e, pt=pt, s_t=s_t: e.activation(out=pt[:], in_=s_t[:], func=AF.Exp), reads=[s_t], writes=[pt])
            pvb = C.pbank[jkv % 2]
            for hl in range(4):
                for bi, b in enumerate(blocks):
                    rel = b - i + 1
                    pt = pr[(jkv % 2) * 3 + rel]
                    Vb_ = Vh if b == NT else V[b % 4]
                    P.op("pe", lambda e, pt=pt, hl=hl, b=b, bi=bi: e.matmul(
                        pvb[:, hl * 65:(hl + 1) * 65], lhsT=pt[:, hl * 128:(hl + 1) * 128], rhs=Vb_[:, jkv, 0:65],
                        start=(bi == 0), stop=(bi == len(blocks) - 1)),
                        reads=[pt, Vb_], writes=[pvb], accum=not (hl == 0 and bi == 0))
            pv3 = pvb[:, 0:260].rearrange("p (h d) -> p h d", h=4)
            P.op("dve", lambda e, pv3=pv3: e.tensor_tensor(out=den[:], in0=pv3[:, :, 64], in1=esink[:, 4 * jkv:4 * jkv + 4], op=ALU.add),
                 reads=[pvb, esink], writes=[den])
            P.op("dve", lambda e: e.reciprocal(out=rden[:], in_=den[:]), reads=[den], writes=[rden])
            P.op("dve", lambda e, pv3=pv3: e.tensor_tensor(
                out=o[:, jkv * 256:(jkv + 1) * 256].rearrange("p (h d) -> p h d", h=4), in0=pv3[:, :, 0:64],
                in1=rden[:].unsqueeze(2).to_broadcast([128, 4, 64]), op=ALU.mult),
                reads=[pvb, rden], writes=[o], accum=(jkv > 0))
        P.op("pool", lambda e: e.tensor_tensor(out=og[:], in0=o[:], in1=sz[i % 3][:], op=ALU.mult), reads=[o, sz[i % 3]], writes=[og])
        transpose_to(P, C, og, 8, ogT)
        xot = xo[i % 2]
        for g in range(2):
            bank = C.gbank()
            matmul_group(P, C, bank, 512, ogT, wo, g * 512)
            P.op("dve", lambda e, bank=bank, g=g: e.tensor_tensor(out=xot[:, g * 512:(g + 1) * 512], in0=bank[:],
                                                                  in1=xring[i % 3][:, g * 512:(g + 1) * 512], op=ALU.add),
                 reads=[bank, xring[i % 3]], writes=[xot], accum=(g > 0))
        P.dma(x_dst[i * 128:(i + 1) * 128, :], xot[:], reads=[xot], writes=[xd_tr[i]])

    import os
    dbg = int(os.environ.get("KDBG", "9"))
    for t in range(NT + 1):
        if t < NT and dbg >= 1:
            stage1(t)
            if t == NT - 1:
                halo_exchange()
        if t >= 1 and dbg >= 2:
            stage2(t - 1)
    P.barrier()
    st.close()
    P.stack = P.gstack


INPUT_SHAPES = {
    "od_w_in": [2, 1024, 2560], "od_w_out": [2, 1024, 1024], "od_norm_rep": [2, 128, 1024],
    "qg_rep": [2, 128, 64], "kg_rep": [2, 128, 64], "sink_rep": [2, 128, 16],
    "alibi": [4, 4, 128, 512], "ident": [128, 128], "flags": [128, 4],
    "ev_w_in": [2, 1024, 3200], "ev_norm_rep": [2, 128, 1024], "ev_w_out": [2, 1024, 1024],
    "s5_ar_row": [2, 2, 128, 2048], "s5_ai_row": [2, 2, 128, 2048], "s5_dt_row": [2, 2, 128, 2048],
    "s5_ar_col": [2, 2, 128, 16], "s5_ai_col": [2, 2, 128, 16], "s5_dt_col": [2, 2, 128, 16],
    "s5_b_col": [2, 2, 2, 128, 16, 16], "s5_c_col": [2, 2, 2, 128, 16, 16],
    "s5_d_col": [2, 128, 4], "glu_b_col": [2, 128, 4], "s5_glu_w": [2, 512, 512],
    "iota_col": [128, 2], "iota_row": [2, 128, 128], "tri": [2, 128, 128], "triE": [2, 128, 128],
    "rw_mu_rep": [2, 128, 1664], "rw_w0_rep": [2, 2, 128, 512], "rw_a0_rep": [2, 128, 512], "rw_k_k_rep": [2, 128, 512],
    "rw_k_a_rep": [2, 128, 512], "rw_r_k_rep": [2, 128, 512], "rw_ln_g_rep": [2, 128, 512], "rw_ln_b_rep": [2, 128, 512],
    "rw_w_up": [2, 2, 64, 512], "rw_a_up": [2, 64, 512],
}


def alibi_tables():
    slopes = np.exp2(-8.0 * np.arange(1, 17, dtype=np.float32) / 16).astype(np.float32)
    s = np.arange(128)[:, None]
    t = np.arange(128)[None, :]
    out = np.zeros((4, 4, 128, 4, 128), np.float32)
    for rel in range(4):
        sg = (s + (rel - 1) * 128) if rel < 3 else (255 - s)
        d = np.abs(t - sg).astype(np.float32)
        for j in range(4):
            for hl in range(4):
                out[j, rel, :, hl, :] = np.where(d <= 128, -slopes[4 * j + hl] * d, NEG)
    return out.reshape(4, 4, 128, 512)


def host_layout(inputs, layers):
    f = lambda a: np.ascontiguousarray(np.asarray(a, np.float32))
    rep = lambda a: f(np.broadcast_to(np.asarray(a)[:, None, :], (a.shape[0], 128, a.shape[1])))
    m = {}
    m["od_w_in"] = f(inputs["od_w_in"]); m["od_w_out"] = f(inputs["od_w_out"])
    m["od_norm_rep"] = rep(inputs["od_norm"]); m["qg_rep"] = rep(inputs["at_q_norm"]); m["kg_rep"] = rep(inputs["at_k_norm"])
    m["sink_rep"] = rep(inputs["at_sink"])
    m["alibi"] = alibi_tables(); m["ident"] = np.eye(128, dtype=np.float32)
    m["ev_w_in"] = f(inputs["ev_w_in"]); m["ev_w_out"] = f(inputs["ev_w_out"]); m["ev_norm_rep"] = rep(inputs["ev_norm"])
    NE = 2
    rowrep = lambda a: f(np.broadcast_to(a.reshape(NE, 2, 1, 2048), (NE, 2, 128, 2048)))
    m["s5_ar_row"] = rowrep(np.asarray(inputs["s5_a_re"])); m["s5_ai_row"] = rowrep(np.asarray(inputs["s5_a_im"]))
    m["s5_dt_row"] = rowrep(np.repeat(np.asarray(inputs["s5_log_dt"])[..., None], 64, axis=-1))
    col = lambda a: f(a.reshape(NE, 2, 16, 128).transpose(0, 1, 3, 2))
    m["s5_ar_col"] = col(np.asarray(inputs["s5_a_re"])); m["s5_ai_col"] = col(np.asarray(inputs["s5_a_im"]))
    m["s5_dt_col"] = col(np.repeat(np.asarray(inputs["s5_log_dt"])[..., None], 64, axis=-1))
    bcol = lambda a: np.asarray(a).reshape(NE, 2, 16, 2, 64, 16).transpose(0, 1, 3, 4, 2, 5).reshape(NE, 2, 128, 16, 16)
    m["s5_b_col"] = f(np.stack([bcol(inputs["s5_b_re"]), bcol(inputs["s5_b_im"])], axis=2))
    ccol = lambda a: np.asarray(a).reshape(NE, 2, 16, 2, 16, 64).transpose(0, 1, 3, 5, 2, 4).reshape(NE, 2, 128, 16, 16)
    m["s5_c_col"] = f(np.stack([ccol(inputs["s5_c_re"]), ccol(inputs["s5_c_im"])], axis=2))
    c4 = lambda a: f(np.asarray(a).reshape(NE, 4, 128).transpose(0, 2, 1))
    m["s5_d_col"] = c4(inputs["s5_d"]); m["glu_b_col"] = c4(inputs["s5_glu_b"]); m["s5_glu_w"] = f(inputs["s5_glu_w"])
    ar = np.arange(128, dtype=np.float32)
    m["iota_col"] = f(np.stack([ar + 1, 128 - ar], axis=1))
    m["iota_row"] = f(np.stack([np.broadcast_to(ar + 1, (128, 128)), np.broadcast_to(128 - ar, (128, 128))]))
    s_, t_ = np.arange(128)[:, None], np.arange(128)[None, :]
    m["tri"] = f(np.stack([(s_ <= t_), (s_ >= t_)]).astype(np.float32))
    m["triE"] = f(np.stack([(s_ < t_), (s_ > t_)]).astype(np.float32))
    for k in ("rw_mu", "rw_a0", "rw_k_k", "rw_k_a", "rw_ln_g", "rw_ln_b"):
        m[k + "_rep"] = rep(np.asarray(inputs[k]))
    m["rw_r_k_rep"] = rep(np.asarray(inputs["rw_r_k"]).reshape(NE, 512))
    w0 = np.asarray(inputs["rw_w0"])
    m["rw_w0_rep"] = f(np.broadcast_to(w0[:, :, None, :], (NE, 2, 128, 512)))
    m["rw_w_up"] = f(inputs["rw_w_up"]); m["rw_a_up"] = f(inputs["rw_a_up"])
    return m


def build_program(NT, layers, debug=False):
    nc = bass.Bass("TRN2", target_bir_lowering=False)
    NTOK = NT * 128
    gst = ExitStack()
    P = Prog(nc, gst)
    C = Ctx()
    C.NT = NT
    C.rr = {}
    C.inp = {k: nc.dram_tensor(k, shp, F32, kind="ExternalInput") for k, shp in INPUT_SHAPES.items()}
    xin = nc.dram_tensor("xin", [NTOK, D], F32, kind="ExternalInput")
    xout = nc.dram_tensor("xout", [NTOK, D], F32, kind="ExternalOutput")
    xa = P.dram("xa", [NTOK, D]); xb = P.dram("xb", [NTOK, D])
    banks = [P.ps([128, 512]) for _ in range(8)]
    C.gb = banks[0:3]; C.sbank = banks[3:6]; C.pbank = banks[6:8]
    C.gi = 0

    def gbank():
        C.gi += 1
        return C.gb[C.gi % len(C.gb)]
    C.gbank = gbank
    C.banks = banks
    C.ident = P.sb([128, 128]); C.flags = P.sb([128, 4])
    C.iota_col = P.sb([128, 2]); C.iota_row = [P.sb([128, 128]) for _ in range(2)]; C.tri = [P.sb([128, 128]) for _ in range(2)]
    C.zero_col = P.sb([128, 2]); C.ones_col = P.sb([128, 2]); C.triE = [P.sb([128, 128]) for _ in range(2)]
    P.op("pool", lambda e: e.memset(C.zero_col[:], 0.0), writes=[C.zero_col])
    P.op("pool", lambda e: e.memset(C.ones_col[:], 1.0), writes=[C.ones_col])
    for d in range(2):
        P.dma(C.triE[d][:], C.inp["triE"][d], writes=[C.triE[d]])
    P.dma(C.iota_col[:], C.inp["iota_col"][:, :], writes=[C.iota_col])
    for d in range(2):
        P.dma(C.iota_row[d][:], C.inp["iota_row"][d], writes=[C.iota_row[d]])
        P.dma(C.tri[d][:], C.inp["tri"][d], writes=[C.tri[d]])
    dbgk = "ExternalOutput" if debug else "Internal"
    C.proj = nc.dram_tensor("proj", [NTOK, 3200], F32, kind=dbgk)
    C.ys5T = nc.dram_tensor("ys5T", [512, NTOK], F32, kind="Internal")
    C.mixT = nc.dram_tensor("mixT", [1024, NTOK], BF16, kind=dbgk)
    C.hrw = C.proj
    C.hrow = P.sb([1, 1664])
    for nm, shp, dt in (("ksrc", [64, 512], BF16), ("kdst", [128, 512], BF16), ("vsrc", [128, 288], BF16), ("vdst", [256, 288], BF16),
                        ("s5src", [128, 32], F32), ("s5dst", [256, 32], F32), ("zsrc", [64, 512], F32), ("zdst", [128, 512], F32),
                        ("hdst", [2, 1664], F32)):
        setattr(C, nm, nc.dram_tensor(nm, shp, dt, kind="Internal"))
        setattr(C, nm + "_tr", T(None))
    C.yrw = nc.dram_tensor("yrw", [NTOK, 512], F32, kind="Internal")
    C.yrw_tr = [T(None) for _ in range(NT)]
    C.proj_tr = [T(None) for _ in range(NT)]; C.ys_tr = [T(None) for _ in range(NT)]; C.mix_tr = [T(None) for _ in range(NT)]
    P.dma(C.ident[:], C.inp["ident"][:, :], writes=[C.ident])
    P.dma(C.flags[:], C.inp["flags"][:, :], writes=[C.flags])
    bufs = [xin] + [(xa, xb)[i % 2] for i in range(len(layers) - 1)] + [xout]
    trs = [[T(None) for _ in range(NT)] for _ in range(len(layers) + 1)]
    for li, (kind, l) in enumerate(layers):
        if kind == "odd":
            odd_layer(P, C, l, bufs[li], bufs[li + 1], trs[li], trs[li + 1])
        else:
            even_layer(P, C, l, bufs[li], bufs[li + 1], trs[li], trs[li + 1])
    P.emit()
    return nc, gst


MAGIC = 12582912.0
TWO_PI = 2.0 * np.pi


def round_frac(P, eng, out, in_, tmp):
    (o_ap, o_t), (i_ap, i_t), (t_ap, t_t) = out, in_, tmp
    P.op(eng, lambda e: e.tensor_scalar(out=t_ap, in0=i_ap, scalar1=MAGIC, scalar2=MAGIC, op0=ALU.add, op1=ALU.subtract),
         reads=[i_t], writes=[t_t])
    P.op(eng, lambda e: e.tensor_tensor(out=o_ap, in0=i_ap, in1=t_ap, op=ALU.subtract), reads=[i_t, t_t], writes=[o_t])


def even_phaseA(P, C, l, x_src, xs_tr):
    NT = C.NT
    st = ExitStack(); P.stack = st
    I = C.inp
    w = P.sb([128, 8, 3200], BF16)
    stage = [P.sb([128, 3200]), P.sb([128, 3200])]
    load_weight_bf16(P, C, w, lambda c: I["ev_w_in"][l, c * 128:(c + 1) * 128, :], 8, 3200, stage)
    gn = P.sb([128, 1024])
    P.dma(gn[:], I["ev_norm_rep"][l], writes=[gn])
    S = Ctx()
    S.junk = P.sb([128, 1024]); S.ss = P.sb([128, 1]); S.ss2 = P.sb([128, 1]); S.rs = P.sb([128, 1]); S.h = P.sb([128, 1024])
    hT = P.sb([128, 8, 128], BF16)
    xring = [P.sb([128, 1024]) for _ in range(2)]
    for j in range(NT):
        xt = xring[j % 2]
        P.dma(xt[:], x_src[j * 128:(j + 1) * 128, :], reads=[xs_tr[j]], writes=[xt])
        rmsnorm_T(P, C, xt, gn, hT, S)
        pst = stage[j % 2]
        for g in range(7):
            ncol = 512 if g < 6 else 128
            bank = C.gbank()
            matmul_group(P, C, bank, ncol, hT, w, g * 512)
            copy_op(P, rr(P, C, "pev", ("act", "dve")), pst[:, g * 512:g * 512 + ncol], bank[:, 0:ncol], [bank], [pst], accum=(g > 0))
        P.dma(C.proj[j * 128:(j + 1) * 128, :], pst[:], reads=[pst], writes=[C.proj_tr[j]])
    P.barrier()
    st.close(); P.stack = P.gstack


def s5_tables(P, C, l, d, K):
    I = C.inp
    W = K.work
    a_r, a_i, dtr, t0, t1, t2 = W[0], W[1], W[2], W[3], W[4], W[5]

    def build(shape_is_row, ar_src, ai_src, dt_src, steps_fn, sign, out_re, out_im):
        P.dma(a_r[:], ar_src, writes=[a_r]); P.dma(a_i[:], ai_src, writes=[a_i]); P.dma(dtr[:], dt_src, writes=[dtr])
        P.op("act", lambda e: e.activation(out=dtr[:], in_=dtr[:], func=AF.Exp), reads=[dtr], writes=[dtr])
        P.op("dve", lambda e: e.tensor_tensor(out=a_r[:], in0=a_r[:], in1=dtr[:], op=ALU.mult), reads=[a_r, dtr], writes=[a_r])
        P.op("dve", lambda e: e.scalar_tensor_tensor(out=a_i[:], in0=a_i[:], scalar=1.0 / TWO_PI, in1=dtr[:], op0=ALU.mult, op1=ALU.mult),
             reads=[a_i, dtr], writes=[a_i])
        round_frac(P, "dve", (a_i[:], a_i), (a_i[:], a_i), (t0[:], t0))
        steps_fn(a_r, a_i)
        P.op("act", lambda e: e.activation(out=t1[:], in_=a_r[:], func=AF.Exp, scale=float(sign)), reads=[a_r], writes=[t1])
        round_frac(P, "dve", (t0[:], t0), (a_i[:], a_i), (t2[:], t2))
        P.op("act", lambda e: e.activation(out=t0[:], in_=t0[:], func=AF.Sin, scale=TWO_PI), reads=[t0], writes=[t0])
        P.op("dve", lambda e: e.tensor_scalar(out=a_i[:], in0=a_i[:], scalar1=0.25, scalar2=None, op0=ALU.add), reads=[a_i], writes=[a_i])
        round_frac(P, "dve", (a_i[:], a_i), (a_i[:], a_i), (t2[:], t2))
        P.op("act", lambda e: e.activation(out=a_i[:], in_=a_i[:], func=AF.Sin, scale=TWO_PI), reads=[a_i], writes=[a_i])
        P.op("dve", lambda e: e.tensor_tensor(out=out_re[:].rearrange("p a b -> p (a b)"), in0=t1[:], in1=a_i[:], op=ALU.mult),
             reads=[t1, a_i], writes=[out_re])
        P.op("dve", lambda e: e.scalar_tensor_tensor(out=out_im[:].rearrange("p a b -> p (a b)"), in0=t1[:], scalar=float(sign), in1=t0[:],
                                                     op0=ALU.mult, op1=ALU.mult), reads=[t1, t0], writes=[out_im])

    def steps_row(a_r, a_i):
        for t in (a_r, a_i):
            P.op("dve", lambda e, t=t: e.tensor_scalar(out=t[:], in0=t[:], scalar1=C.iota_col[:, d:d + 1], scalar2=None, op0=ALU.mult),
                 reads=[t, C.iota_col], writes=[t])
    build(True, I["s5_ar_row"][l, d], I["s5_ai_row"][l, d], I["s5_dt_row"][l, d], steps_row, -1, K.Tin_re, K.Tin_im)

    def steps_col(a_r, a_i):
        for t in (a_r, a_i):
            P.op("dve", lambda e, t=t: e.tensor_tensor(out=t[:].rearrange("p (a b) -> p a b", a=16),
                                                       in0=t[:, 0:16].unsqueeze(2).to_broadcast([128, 16, 128]),
                                                       in1=C.iota_row[d][:].unsqueeze(1).to_broadcast([128, 16, 128]), op=ALU.mult),
                 reads=[t, C.iota_row[d]], writes=[t])
    ca, ci, cd = K.col_a, K.col_i, K.col_d

    def build_col():
        P.dma(ca[:], I["s5_ar_col"][l, d], writes=[ca]); P.dma(ci[:], I["s5_ai_col"][l, d], writes=[ci]); P.dma(cd[:], I["s5_dt_col"][l, d], writes=[cd])
        P.op("act", lambda e: e.activation(out=cd[:], in_=cd[:], func=AF.Exp), reads=[cd], writes=[cd])
        P.op("dve", lambda e: e.tensor_tensor(out=K.c_ardt[:], in0=ca[:], in1=cd[:], op=ALU.mult), reads=[ca, cd], writes=[K.c_ardt])
        P.op("dve", lambda e: e.scalar_tensor_tensor(out=K.c_frac[:], in0=ci[:], scalar=1.0 / TWO_PI, in1=cd[:], op0=ALU.mult, op1=ALU.mult),
             reads=[ci, cd], writes=[K.c_frac])
        round_frac(P, "dve", (K.c_frac[:], K.c_frac), (K.c_frac[:], K.c_frac), (K.c_tmp[:], K.c_tmp))
        P.op("dve", lambda e: e.tensor_tensor(out=a_r[:].rearrange("p (a b) -> p a b", a=16),
                                              in0=K.c_ardt[:].unsqueeze(2).to_broadcast([128, 16, 128]),
                                              in1=C.iota_row[d][:].unsqueeze(1).to_broadcast([128, 16, 128]), op=ALU.mult),
             reads=[K.c_ardt, C.iota_row[d]], writes=[a_r])
        P.op("dve", lambda e: e.tensor_tensor(out=a_i[:].rearrange("p (a b) -> p a b", a=16),
                                              in0=K.c_frac[:].unsqueeze(2).to_broadcast([128, 16, 128]),
                                              in1=C.iota_row[d][:].unsqueeze(1).to_broadcast([128, 16, 128]), op=ALU.mult),
             reads=[K.c_frac, C.iota_row[d]], writes=[a_i])
        sign = 1
        P.op("act", lambda e: e.activation(out=t1[:], in_=a_r[:], func=AF.Exp, scale=float(sign)), reads=[a_r], writes=[t1])
        round_frac(P, "dve", (t0[:], t0), (a_i[:], a_i), (t2[:], t2))
        P.op("act", lambda e: e.activation(out=t0[:], in_=t0[:], func=AF.Sin, scale=TWO_PI), reads=[t0], writes=[t0])
        P.op("dve", lambda e: e.tensor_scalar(out=a_i[:], in0=a_i[:], scalar1=0.25, scalar2=None, op0=ALU.add), reads=[a_i], writes=[a_i])
        round_frac(P, "dve", (a_i[:], a_i), (a_i[:], a_i), (t2[:], t2))
        P.op("act", lambda e: e.activation(out=a_i[:], in_=a_i[:], func=AF.Sin, scale=TWO_PI), reads=[a_i], writes=[a_i])
        P.op("dve", lambda e: e.tensor_tensor(out=K.Tout_re[:].rearrange("p a b -> p (a b)"), in0=t1[:], in1=a_i[:], op=ALU.mult),
             reads=[t1, a_i], writes=[K.Tout_re])
        P.op("dve", lambda e: e.tensor_tensor(out=K.Tout_im[:].rearrange("p a b -> p (a b)"), in0=t1[:], in1=t0[:], op=ALU.mult),
             reads=[t1, t0], writes=[K.Tout_im])
    build_col()

    s1, c1, m1, nr, dn, q_r, q_i, u0, u1 = [K.small[i] for i in range(9)]
    P.op("act", lambda e: e.activation(out=m1[:], in_=K.c_ardt[:], func=AF.Exp), reads=[K.c_ardt], writes=[m1])
    P.op("act", lambda e: e.activation(out=s1[:], in_=K.c_frac[:], func=AF.Sin, scale=TWO_PI), reads=[K.c_frac], writes=[s1])
    P.op("dve", lambda e: e.tensor_scalar(out=u0[:], in0=K.c_frac[:], scalar1=0.25, scalar2=None, op0=ALU.add), reads=[K.c_frac], writes=[u0])
    round_frac(P, "dve", (u0[:], u0), (u0[:], u0), (u1[:], u1))
    P.op("act", lambda e: e.activation(out=c1[:], in_=u0[:], func=AF.Sin, scale=TWO_PI), reads=[u0], writes=[c1])
    tt = lambda o, a, b, op, eng="dve": P.op(eng, lambda e: e.tensor_tensor(out=o[:], in0=a[:], in1=b[:], op=op), reads=[a, b], writes=[o])
    tt(c1, c1, m1, ALU.mult)
    tt(s1, s1, m1, ALU.mult)
    P.op("dve", lambda e: e.tensor_scalar(out=nr[:], in0=c1[:], scalar1=-1.0, scalar2=None, op0=ALU.add), reads=[c1], writes=[nr])
    tt(dn, ca, ca, ALU.mult); tt(u0, ci, ci, ALU.mult); tt(dn, dn, u0, ALU.add)
    P.op("dve", lambda e: e.reciprocal(out=dn[:], in_=dn[:]), reads=[dn], writes=[dn])
    tt(u0, nr, ca, ALU.mult); tt(u1, s1, ci, ALU.mult); tt(u0, u0, u1, ALU.add); tt(q_r, u0, dn, ALU.mult)
    tt(u0, s1, ca, ALU.mult); tt(u1, nr, ci, ALU.mult); tt(u0, u0, u1, ALU.subtract); tt(q_i, u0, dn, ALU.mult)
    bre, bim, bbr, bbi, tb = K.bre, K.bim, K.bbr, K.bbi, K.tb
    for (dst, ri) in ((bre, 0), (bim, 1)):
        P.op("pool", lambda e, dst=dst: e.memset(dst[:], 0.0), writes=[dst])
        P.dma(dst[0:64, :, 0:16], I["s5_b_col"][l, d, ri, 0:64], writes=[dst])
        P.dma(dst[64:128, :, 16:32], I["s5_b_col"][l, d, ri, 64:128], writes=[dst])
    bc = lambda q: q[:].unsqueeze(2).to_broadcast([128, 16, 32])
    P.op("dve", lambda e: e.tensor_tensor(out=bbr[:], in0=bre[:], in1=bc(q_r), op=ALU.mult), reads=[bre, q_r], writes=[bbr])
    P.op("dve", lambda e: e.tensor_tensor(out=tb[:], in0=bim[:], in1=bc(q_i), op=ALU.mult), reads=[bim, q_i], writes=[tb])
    tt(bbr, bbr, tb, ALU.subtract)
    P.op("dve", lambda e: e.tensor_tensor(out=bbi[:], in0=bim[:], in1=bc(q_r), op=ALU.mult), reads=[bim, q_r], writes=[bbi])
    P.op("dve", lambda e: e.tensor_tensor(out=tb[:], in0=bre[:], in1=bc(q_i), op=ALU.mult), reads=[bre, q_i], writes=[tb])
    tt(bbi, bbi, tb, ALU.add)
    zp = K.zp
    for z in zp:
        P.op("pool", lambda e, z=z: e.memset(z[:], 0.0), writes=[z])
    for (src, dst) in ((bbr, K.BT_re), (bbi, K.BT_im)):
        for ch in range(4):
            bank = C.gbank()
            for pl in range(4):
                copy_op(P, "dve", zp[pl][:, 32 * pl:32 * pl + 32], src[:, 4 * ch + pl, :], [src], [zp[pl]])
                P.op("pe", lambda e, bank=bank, pl=pl: e.transpose(out=bank[:, pl * 128:(pl + 1) * 128], in_=zp[pl][:], identity=C.ident[:]),
                     reads=[zp[pl], C.ident], writes=[bank], accum=(pl > 0))
            copy_op(P, "act", dst[:, ch, :], bank[:], [bank], [dst], accum=(ch > 0))
    for (dst, ri) in ((K.Cre, 0), (K.Cimn, 1)):
        P.op("pool", lambda e, dst=dst: e.memset(dst[:], 0.0), writes=[dst])
        P.dma(dst[0:64, :, 32:48], I["s5_c_col"][l, d, ri, 0:64], writes=[dst])
        P.dma(dst[64:128, :, 48:64], I["s5_c_col"][l, d, ri, 64:128], writes=[dst])
    P.op("dve", lambda e: e.tensor_scalar(out=K.Cimn[:], in0=K.Cimn[:], scalar1=-1.0, scalar2=None, op0=ALU.mult), reads=[K.Cimn], writes=[K.Cimn])


def s5_pass(P, C, l, d):
    half = -999
    NT = C.NT
    st = ExitStack(); P.stack = st
    I = C.inp
    K = Ctx()
    K.Tin_re = P.sb([128, 4, 512]); K.Tin_im = P.sb([128, 4, 512])
    K.Tout_re = P.sb([128, 16, 128]); K.Tout_im = P.sb([128, 16, 128])
    K.BT_re = P.sb([128, 4, 512], BF16); K.BT_im = P.sb([128, 4, 512], BF16)
    K.Cre = P.sb([128, 16, 64]); K.Cimn = P.sb([128, 16, 64])
    st2 = ExitStack(); P.stack = st2
    K.work = [P.sb([128, 2048]) for _ in range(6)]
    K.col_a = P.sb([128, 16]); K.col_i = P.sb([128, 16]); K.col_d = P.sb([128, 16])
    K.c_ardt = P.sb([128, 16]); K.c_frac = P.sb([128, 16]); K.c_tmp = P.sb([128, 16])
    K.small = [P.sb([128, 16]) for _ in range(9)]
    K.bre = P.sb([128, 16, 32]); K.bim = P.sb([128, 16, 32]); K.bbr = P.sb([128, 16, 32]); K.bbi = P.sb([128, 16, 32]); K.tb = P.sb([128, 16, 32])
    K.zp = [P.sb([128, 128]) for _ in range(4)]
    s5_tables(P, C, l, d, K)
    P.barrier()
    st2.close(); P.stack = st
    tri = C.tri[d]
    zero = C.zero_col
    uring = [P.sb([128, 512]) for _ in range(2)]
    uT = [P.sb([128, 4, 128], BF16) for _ in range(2)]
    g_re = [P.sb([128, 512], BF16) for _ in range(2)]; g_im = [P.sb([128, 512], BF16) for _ in range(2)]
    ta = [P.sb([128, 512]) for _ in range(2)]; tb = [P.sb([128, 512]) for _ in range(2)]
    ta2 = [P.sb([128, 512]) for _ in range(2)]; tb2 = [P.sb([128, 512]) for _ in range(2)]
    hre = [[P.sb([128, 128]) for _ in range(16)] for _ in range(2)]
    him = [[P.sb([128, 128]) for _ in range(16)] for _ in range(2)]
    r1 = [P.sb([128, 128]) for _ in range(2)]; r2 = [P.sb([128, 128]) for _ in range(2)]
    r3 = [P.sb([128, 128]) for _ in range(2)]; r4 = [P.sb([128, 128]) for _ in range(2)]
    cc = [[P.sb([128, 2]) for _ in range(16)] for _ in range(1)][0]
    ysb = [P.sb([128, 4, 128]) for _ in range(2)]
    ysT = C.ys5T.ap().rearrange("(k p) t -> p k t", p=128)
    bk = C.banks
    if d == 1:
        dcol = P.sb([128, 4]); gbcol = P.sb([128, 4])
        P.dma(dcol[:], I["s5_d_col"][l], writes=[dcol]); P.dma(gbcol[:], I["glu_b_col"][l], writes=[gbcol])
        wg = P.sb([128, 4, 512], BF16)
        wgs = [P.sb([128, 512]), P.sb([128, 512])]
        load_weight_bf16(P, C, wg, lambda c: I["s5_glu_w"][l, c * 128:(c + 1) * 128, :], 4, 512, wgs)
        yf = [P.sb([128, 4, 128]) for _ in range(2)]
        zt = [P.sb([128, 512]) for _ in range(2)]
        szT = P.sb([128, 4, 128])
        yv = P.sb([128, 4, 128]); x2 = P.sb([128, 4, 128]); sg = P.sb([128, 4, 128]); yg = P.sb([128, 4, 128]); ygb = P.sb([128, 4, 128], BF16)
        gs = P.sb([128, 4, 128]); ya = [P.sb([128, 4, 128], BF16) for _ in range(2)]
        mixT = C.mixT.ap().rearrange("(k p) t -> p k t", p=128)

    cin = P.sb([128, 32]); cbuf = P.sb([128, 32]); cin2 = P.sb([128, 2, 32])
    if d == 1:
        P.dma(cin2[:], C.s5dst.ap().rearrange("(s p) n -> p s n", s=2), reads=[C.s5dst_tr], writes=[cin2])
        P.op("dve", lambda e: e.tensor_scalar(out=cin[:], in0=cin2[:, 0, :], scalar1=C.flags[:, 0:1], scalar2=None, op0=ALU.mult), reads=[cin2, C.flags], writes=[cin])
        P.op("dve", lambda e: e.scalar_tensor_tensor(out=cin[:], in0=cin2[:, 1, :], scalar=C.flags[:, 1:2], in1=cin[:], op0=ALU.mult, op1=ALU.add),
             reads=[cin2, C.flags, cin], writes=[cin])
    order = list(range(NT)) if d == 0 else list(range(NT - 1, -1, -1))
    last = 127 if d == 0 else 0
    trib = P.sb([128, 128], BF16)
    copy_op(P, "dve", trib[:], tri[:], [tri], [trib])

    def prologue(it):
        i = order[it]; par = it % 2
        ut = uring[par]
        P.dma(ut[:], C.proj[i * 128:(i + 1) * 128, 0:512], reads=[C.proj_tr[i]], writes=[ut])
        if d == 1:
            P.dma(yf[par][:], ysT[:, :, i * 128:(i + 1) * 128], reads=[C.ys_tr[i]], writes=[yf[par]])
            P.dma(zt[par][:], C.proj[i * 128:(i + 1) * 128, 512:1024], reads=[C.proj_tr[i]], writes=[zt[par]])
        uTt = uT[par]
        for c in range(4):
            P.op("pe", lambda e, c=c: e.transpose(out=bk[0][:, c * 128:(c + 1) * 128], in_=ut[:, c * 128:(c + 1) * 128], identity=C.ident[:]),
                 reads=[ut, C.ident], writes=[bk[0]], accum=(c > 0))
        copy_op(P, "act", uTt[:].rearrange("p a b -> p (a b)"), bk[0][:], [bk[0]], [uTt])

    def front(it, ch):
        par = it % 2; cp = ch % 2
        uTt = uT[par]
        P.op("pe", lambda e: e.matmul(bk[1][:], lhsT=uTt[:, ch, :], rhs=K.BT_re[:, ch, :], start=True, stop=True),
             reads=[uTt, K.BT_re], writes=[bk[1]])
        P.op("pe", lambda e: e.matmul(bk[2][:], lhsT=uTt[:, ch, :], rhs=K.BT_im[:, ch, :], start=True, stop=True),
             reads=[uTt, K.BT_im], writes=[bk[2]])
        Tr = K.Tin_re[:, ch, :]; Ti = K.Tin_im[:, ch, :]
        gr, gi, a_, b_, a2_, b2_ = g_re[cp], g_im[cp], ta[cp], tb[cp], ta2[cp], tb2[cp]
        P.op("dve", lambda e: e.tensor_tensor(out=a_[:], in0=bk[1][:], in1=Tr, op=ALU.mult), reads=[bk[1], K.Tin_re], writes=[a_])
        P.op("dve", lambda e: e.tensor_tensor(out=b_[:], in0=bk[2][:], in1=Ti, op=ALU.mult), reads=[bk[2], K.Tin_im], writes=[b_])
        P.op("pool", lambda e: e.tensor_tensor(out=gr[:], in0=a_[:], in1=b_[:], op=ALU.subtract), reads=[a_, b_], writes=[gr])
        P.op("dve", lambda e: e.tensor_tensor(out=a2_[:], in0=bk[1][:], in1=Ti, op=ALU.mult), reads=[bk[1], K.Tin_im], writes=[a2_])
        P.op("dve", lambda e: e.tensor_tensor(out=b2_[:], in0=bk[2][:], in1=Tr, op=ALU.mult), reads=[bk[2], K.Tin_re], writes=[b2_])
        P.op("pool", lambda e: e.tensor_tensor(out=gi[:], in0=a2_[:], in1=b2_[:], op=ALU.add), reads=[a2_, b2_], writes=[gi])

    def back(it, ch):
        par = it % 2; cp = ch % 2
        gr, gi = g_re[cp], g_im[cp]
        for pl in range(4):
            P.op("pe", lambda e, pl=pl: e.matmul(bk[3][:, pl * 128:(pl + 1) * 128], lhsT=gr[:, pl * 128:(pl + 1) * 128], rhs=trib[:],
                                                 start=True, stop=True), reads=[gr, trib], writes=[bk[3]], accum=(pl > 0))
        for pl in range(4):
            P.op("pe", lambda e, pl=pl: e.matmul(bk[4][:, pl * 128:(pl + 1) * 128], lhsT=gi[:, pl * 128:(pl + 1) * 128], rhs=trib[:],
                                                 start=True, stop=True), reads=[gi, trib], writes=[bk[4]], accum=(pl > 0))
        for pl in (0, 1, 3, 2):
            pp = 4 * ch + pl
            hr_prev, hi_prev = hre[1 - par][pp], him[1 - par][pp]
            hr, hi = hre[par][pp], him[par][pp]
            if it == 0 and d == 0:
                cr, ci_, crt = zero[:, 0:1], zero[:, 0:1], [zero]
            elif it == 0:
                cr, ci_, crt = cin[:, pp:pp + 1], cin[:, 16 + pp:17 + pp], [cin]
            else:
                cr, ci_, crt = hr_prev[:, last:last + 1], hi_prev[:, last:last + 1], [hr_prev, hi_prev]
            Gr = bk[3][:, pl * 128:(pl + 1) * 128]; Gi = bk[4][:, pl * 128:(pl + 1) * 128]
            Tor = K.Tout_re[:, pp, :]; Toi = K.Tout_im[:, pp, :]
            q1, q2, q3, q4 = r1[pl % 2], r2[pl % 2], r3[pl % 2], r4[pl % 2]
            P.op("dve", lambda e: e.scalar_tensor_tensor(out=q1[:], in0=Gr, scalar=cr, in1=Tor, op0=ALU.add, op1=ALU.mult),
                 reads=[bk[3], K.Tout_re] + crt, writes=[q1])
            P.op("dve", lambda e: e.scalar_tensor_tensor(out=q2[:], in0=Gi, scalar=ci_, in1=Toi, op0=ALU.add, op1=ALU.mult),
                 reads=[bk[4], K.Tout_im] + crt, writes=[q2])
            P.op("pool", lambda e: e.tensor_tensor(out=hr[:], in0=q1[:], in1=q2[:], op=ALU.subtract), reads=[q1, q2], writes=[hr])
            P.op("dve", lambda e: e.scalar_tensor_tensor(out=q3[:], in0=Gr, scalar=cr, in1=Toi, op0=ALU.add, op1=ALU.mult),
                 reads=[bk[3], K.Tout_im] + crt, writes=[q3])
            P.op("dve", lambda e: e.scalar_tensor_tensor(out=q4[:], in0=Gi, scalar=ci_, in1=Tor, op0=ALU.add, op1=ALU.mult),
                 reads=[bk[4], K.Tout_re] + crt, writes=[q4])
            P.op("pool", lambda e: e.tensor_tensor(out=hi[:], in0=q3[:], in1=q4[:], op=ALU.add), reads=[q3, q4], writes=[hi])
            if pl == 3:
                osl, csl, st0 = slice(64, 128), slice(0, 64), True
            elif pl == 2:
                osl, csl, st0 = slice(64, 96), slice(32, 64), False
            else:
                osl, csl, st0 = slice(32 * pl, 32 * pl + 32), slice(32, 64), True
            sgc = pl >= 2
            P.op("pe", lambda e: e.matmul(bk[5][osl, ch * 128:(ch + 1) * 128], lhsT=K.Cre[:, pp, csl], rhs=hr[:],
                                          start=st0, stop=False, skip_group_check=sgc), reads=[K.Cre, hr], writes=[bk[5]], accum=not (ch == 0 and pl == 0))
            P.op("pe", lambda e: e.matmul(bk[5][osl, ch * 128:(ch + 1) * 128], lhsT=K.Cimn[:, pp, csl], rhs=hi[:],
                                          start=False, stop=True, skip_group_check=sgc), reads=[K.Cimn, hi], writes=[bk[5]], accum=True)

    def epilogue(it):
        i = order[it]; par = it % 2
        uTt = uT[par]
        if d == 0:
            yt = ysb[par]
            copy_op(P, "act", yt[:].rearrange("p a b -> p (a b)"), bk[5][:], [bk[5]], [yt])
            P.dma(ysT[:, :, i * 128:(i + 1) * 128], yt[:], reads=[yt], writes=[C.ys_tr[i]])
        else:
            f = lambda t: t[:].rearrange("p a b -> p (a b)")
            P.op("dve", lambda e: e.tensor_tensor(out=f(yv), in0=bk[5][:], in1=f(yf[par]), op=ALU.add), reads=[bk[5], yf[par]], writes=[yv])
            for ch in range(4):
                P.op("dve", lambda e, ch=ch: e.scalar_tensor_tensor(out=yv[:, ch, :], in0=uTt[:, ch, :], scalar=dcol[:, ch:ch + 1], in1=yv[:, ch, :],
                                                                    op0=ALU.mult, op1=ALU.add), reads=[uTt, dcol, yv], writes=[yv], accum=True)
            P.op("act", lambda e: e.activation(out=f(yg), in_=f(yv), func=AF.Gelu_apprx_tanh), reads=[yv], writes=[yg])
            copy_op(P, "pool", f(ygb), f(yg), [yg], [ygb])
            for co in range(4):
                for kc in range(4):
                    P.op("pe", lambda e, co=co, kc=kc: e.matmul(bk[6][:, co * 128:(co + 1) * 128], lhsT=wg[:, kc, co * 128:(co + 1) * 128],
                                                                rhs=ygb[:, kc, :], start=(kc == 0), stop=(kc == 3)),
                         reads=[wg, ygb], writes=[bk[6]], accum=not (co == 0 and kc == 0))
            for co in range(4):
                P.op("act", lambda e, co=co: e.activation(out=sg[:, co, :], in_=bk[6][:, co * 128:(co + 1) * 128], func=AF.Sigmoid,
                                                          bias=gbcol[:, co:co + 1]), reads=[bk[6], gbcol], writes=[sg], accum=(co > 0))
            for c in range(4):
                P.op("pe", lambda e, c=c: e.transpose(out=bk[7][:, c * 128:(c + 1) * 128], in_=zt[par][:, c * 128:(c + 1) * 128], identity=C.ident[:]),
                     reads=[zt[par], C.ident], writes=[bk[7]], accum=(c > 0))
            P.op("act", lambda e: e.activation(out=f(szT), in_=bk[7][:], func=AF.Silu), reads=[bk[7]], writes=[szT])
            P.op("dve", lambda e: e.tensor_tensor(out=f(gs), in0=f(yg), in1=f(sg), op=ALU.mult), reads=[yg, sg], writes=[gs])
            P.op("pool", lambda e: e.tensor_tensor(out=f(ya[par]), in0=f(gs), in1=f(szT), op=ALU.mult), reads=[gs, szT], writes=[ya[par]])
            P.dma(mixT[:, 0:4, i * 128:(i + 1) * 128], ya[par][:], reads=[ya[par]], writes=[C.mix_tr[i]])

    units = [(it, ch) for it in range(NT) for ch in range(4)]
    prologue(0); front(0, 0)
    for u, (it, ch) in enumerate(units):
        if u + 1 < len(units):
            it2, ch2 = units[u + 1]
            if ch2 == 0:
                prologue(it2)
            front(it2, ch2)
        back(it, ch)
        if ch == 3:
            epilogue(it)
    if d == 0:
        parl = (NT - 1) % 2
        for pp in range(16):
            copy_op(P, ("dve", "pool")[pp % 2], cbuf[:, pp:pp + 1], hre[parl][pp][:, 127:128], [hre[parl][pp]], [cbuf], accum=(pp > 0))
            copy_op(P, ("pool", "dve")[pp % 2], cbuf[:, 16 + pp:17 + pp], him[parl][pp][:, 127:128], [him[parl][pp]], [cbuf], accum=True)
        P.dma(C.s5src[:, :], cbuf[:], reads=[cbuf], writes=[C.s5src_tr])
        P.collective(C.s5src[:, :], C.s5dst[:, :], reads=[C.s5src_tr], writes=[C.s5dst_tr])
    P.barrier()
    st.close(); P.stack = P.gstack


def even_layer(P, C, l, x_src, x_dst, xs_tr, xd_tr):
    import os
    dbg = int(os.environ.get("EDBG", "9"))
    even_phaseA(P, C, l, x_src, xs_tr)
    if dbg >= 1:
        s5_pass(P, C, l, 0)
        s5_pass(P, C, l, 1)
    if dbg >= 2:
        rwkv_pass(P, C, l, 0, x_src, x_dst, xs_tr, xd_tr)
        rwkv_pass(P, C, l, 1, x_src, x_dst, xs_tr, xd_tr)


def rwkv_pass(P, C, l, d, x_src, x_dst, xs_tr, xd_tr):
    NT = C.NT
    st = ExitStack(); P.stack = st
    I = C.inp
    bk = C.banks
    f32 = lambda shape: P.sb(shape)
    b16 = lambda shape: P.sb(shape, BF16)
    tt = lambda eng, o, a, b, op, rd, wr, accum=False: P.op(eng, lambda e: e.tensor_tensor(out=o, in0=a, in1=b, op=op), reads=rd, writes=wr, accum=accum)

    mu = f32([128, 1664]); P.dma(mu[:], I["rw_mu_rep"][l], writes=[mu])
    w0 = f32([128, 512]); P.dma(w0[:], I["rw_w0_rep"][l, d], writes=[w0])
    a0 = f32([128, 512]); P.dma(a0[:], I["rw_a0_rep"][l], writes=[a0])
    kkp = f32([128, 512]); P.dma(kkp[:], I["rw_k_k_rep"][l], writes=[kkp])
    kap = f32([128, 512]); P.dma(kap[:], I["rw_k_a_rep"][l], writes=[kap])
    ups = f32([128, 2, 512])
    P.dma(ups[0:64, 0, :], I["rw_w_up"][l, d], writes=[ups])
    P.dma(ups[64:128, 1, :], I["rw_a_up"][l], writes=[ups])
    triI = C.tri[d]; triE = C.triE[d]; triET = C.triE[1 - d]
    eye_b = C.ident
    if d == 1:
        rkp = f32([128, 512]); P.dma(rkp[:], I["rw_r_k_rep"][l], writes=[rkp])
        lng = f32([128, 512]); P.dma(lng[:], I["rw_ln_g_rep"][l], writes=[lng])
        lnb = f32([128, 512]); P.dma(lnb[:], I["rw_ln_b_rep"][l], writes=[lnb])
        wo = b16([128, 8, 1024])
        st2 = ExitStack(); P.stack = st2
        wst = [f32([128, 1024]), f32([128, 1024])]
        load_weight_bf16(P, C, wo, lambda c: I["ev_w_out"][l, c * 128:(c + 1) * 128, :], 8, 1024, wst)
        P.barrier()
        st2.close(); P.stack = st

    cur = [f32([128, 1664])] * 2; prv = [f32([128, 1664])] * 2; nxt = [f32([128, 1664])] * 2
    hs = f32([128, 1664]); tsum = f32([128, 1664])
    twla = f32([128, 128]); twlaT = f32([128, 128])
    a_t = f32([128, 512]); e2 = f32([128, 512]); tmp = f32([128, 512]); tmp2 = f32([128, 512])
    kkn = f32([128, 512]); pss = f32([128, 8]); prn = f32([128, 8])
    p_t = f32([128, 512]); q_t = f32([128, 512]); kp = f32([128, 512])
    GI = f32([128, 512]); GIinv = f32([128, 512]); GE = f32([128, 512])
    Pd = f32([128, 512]); Qd = f32([128, 512]); Kd = f32([128, 512]); Rd = f32([128, 512])
    Pdb = b16([128, 512]); Qdb = b16([128, 512]); Kdb = b16([128, 512]); Vb = b16([128, 512])
    PR = b16([64, 8, 2, 128]); QTt = b16([64, 8, 128]); KTt = b16([64, 8, 128])
    Bm = [b16([128, 8, 128]) for _ in range(2)]; Am = [b16([128, 8, 128]) for _ in range(2)]; Pm = b16([128, 8, 128])
    MqT = b16([128, 8, 128]); LkT = b16([128, 8, 128]); MkT = b16([128, 8, 128])
    LkV = f32([128, 512]); KV = f32([64, 8, 64])
    Z = f32([64, 8, 64]); Zb = b16([64, 8, 64]); ZK = f32([64, 8, 64]); ZKg = f32([64, 8, 64]); Ztmp = f32([64, 8, 64])
    gcol = f32([64, 8]); onescol = C.ones_col
    rhs_sb = b16([128, 512]); U_sb = b16([128, 512])
    ysb = [f32([128, 512]) for _ in range(2)]
    if d == 0:
        P.op("pool", lambda e: e.memset(Z[:], 0.0), writes=[Z])
        P.op("pool", lambda e: e.memset(Zb[:], 0.0), writes=[Zb])
        hb = P.sb([1, 2, 1664])
        P.collective(C.proj[NT * 128 - 1:NT * 128, 1024:2688], C.hdst[:, :], reads=[C.proj_tr[NT - 1]], writes=[C.hdst_tr])
        P.dma(hb[:], C.hdst.ap().rearrange("(o s) n -> o s n", o=1), reads=[C.hdst_tr], writes=[hb])
        P.op("dve", lambda e: e.tensor_scalar(out=C.hrow[:], in0=hb[:, 0, :], scalar1=C.flags[0:1, 0:1], scalar2=None, op0=ALU.mult), reads=[hb, C.flags], writes=[C.hrow])
        P.op("dve", lambda e: e.scalar_tensor_tensor(out=C.hrow[:], in0=hb[:, 1, :], scalar=C.flags[0:1, 1:2], in1=C.hrow[:], op0=ALU.mult, op1=ALU.add),
             reads=[hb, C.flags, C.hrow], writes=[C.hrow])
    else:
        z2 = P.sb([64, 2, 512])
        P.dma(z2[:], C.zdst.ap().rearrange("(s p) n -> p s n", s=2), reads=[C.zdst_tr], writes=[z2])
        zf_ = Z[:].rearrange("p a b -> p (a b)")
        P.op("dve", lambda e: e.tensor_scalar(out=zf_, in0=z2[:, 0, :], scalar1=C.flags[0:64, 0:1], scalar2=None, op0=ALU.mult), reads=[z2, C.flags], writes=[Z])
        P.op("dve", lambda e: e.scalar_tensor_tensor(out=zf_, in0=z2[:, 1, :], scalar=C.flags[0:64, 1:2], in1=zf_, op0=ALU.mult, op1=ALU.add),
             reads=[z2, C.flags, Z], writes=[Z])
        copy_op(P, "dve", Zb[:], Z[:], [Z], [Zb])
    if d == 1:
        yfw = [f32([128, 512]) for _ in range(2)]
        zrw = [f32([128, 512]) for _ in range(2)]
        xres = [f32([128, 1024]) for _ in range(2)]
        mean = f32([128, 8]); var = f32([128, 8]); cent = f32([128, 512]); rkk = f32([128, 512]); bon = f32([128, 8])
        yb = f32([128, 512]); szr = f32([128, 512])
        mixA = [b16([128, 4, 128]) for _ in range(2)]; ybT = b16([128, 4, 128])
        xo = [f32([128, 1024]) for _ in range(2)]
        mixT = C.mixT.ap().rearrange("(k p) t -> p k t", p=128)

    v3 = lambda t, n=8: t[:].rearrange("p (h d) -> p h d", h=n)
    order = list(range(NT)) if d == 0 else list(range(NT - 1, -1, -1))
    last = 127 if d == 0 else 0
    HW = C.hrw
    for it, i in enumerate(order):
        par = it % 2
        r0 = i * 128
        c_, p_, n_ = cur[par], prv[par], nxt[par]
        P.dma(c_[:], HW[r0:r0 + 128, 1024:2688], reads=[C.proj_tr[i]], writes=[c_])
        if i == 0:
            P.op("pool", lambda e: e.memset(p_[:], 0.0), writes=[p_])
            P.dma(p_[1:128, :], HW[r0:r0 + 127, 1024:2688], reads=[C.proj_tr[i]], writes=[p_])
        else:
            P.dma(p_[:], HW[r0 - 1:r0 + 127, 1024:2688], reads=[C.proj_tr[i], C.proj_tr[i - 1]], writes=[p_])
        if i == NT - 1:
            P.dma(n_[0:127, :], HW[r0 + 1:r0 + 128, 1024:2688], reads=[C.proj_tr[i]], writes=[n_])
            P.dma(n_[127:128, :], C.hrow[:], reads=[C.hrow], writes=[n_], accum=True)
        else:
            P.dma(n_[:], HW[r0 + 1:r0 + 129, 1024:2688], reads=[C.proj_tr[i], C.proj_tr[i + 1]], writes=[n_])
        if d == 1:
            P.dma(yfw[par][:], C.yrw[r0:r0 + 128, :], reads=[C.yrw_tr[i]], writes=[yfw[par]])
            P.dma(zrw[par][:], HW[r0:r0 + 128, 2688:3200], reads=[C.proj_tr[i]], writes=[zrw[par]])
            P.dma(xres[par][:], x_src[r0:r0 + 128, :], reads=[xs_tr[i]], writes=[xres[par]])
            P.dma(mixA[par][:], mixT[:, 0:4, r0:r0 + 128], reads=[C.mix_tr[i]], writes=[mixA[par]])
        tt("pool", tsum[:], p_[:], n_[:], ALU.add, [p_, n_], [tsum])
        P.op("dve", lambda e: e.scalar_tensor_tensor(out=tsum[:], in0=tsum[:], scalar=0.5, in1=c_[:], op0=ALU.mult, op1=ALU.subtract),
             reads=[tsum, c_], writes=[tsum])
        tt("pool", tsum[:], tsum[:], mu[:], ALU.mult, [tsum, mu], [tsum])
        tt("dve", hs[:], tsum[:], c_[:], ALU.add, [tsum, c_], [hs])
        r_ap, k_ap, v_ap = hs[:, 0:512], hs[:, 512:1024], hs[:, 1024:1536]
        P.op("act", lambda e: e.activation(out=twla[:, 0:64], in_=hs[:, 1536:1600], func=AF.Tanh), reads=[hs], writes=[twla])
        copy_op(P, "dve", twla[:, 64:128], hs[:, 1600:1664], [hs], [twla], accum=True)
        g0 = C.gbank()
        P.op("pe", lambda e: e.transpose(out=g0[:, 0:128], in_=twla[:], identity=C.ident[:]), reads=[twla, C.ident], writes=[g0])
        copy_op(P, "dve", twlaT[:], g0[:, 0:128], [g0], [twlaT])
        g1 = C.gbank()
        P.op("pe", lambda e: e.matmul(g1[:], lhsT=twlaT[64:128, :], rhs=ups[64:128, 1, :], start=True, stop=True), reads=[twlaT, ups], writes=[g1])
        tt("dve", a_t[:], g1[:], a0[:], ALU.add, [g1, a0], [a_t])
        P.op("act", lambda e: e.activation(out=a_t[:], in_=a_t[:], func=AF.Sigmoid), reads=[a_t], writes=[a_t])
        g2 = C.gbank()
        P.op("pe", lambda e: e.matmul(g2[:], lhsT=twlaT[0:64, :], rhs=ups[0:64, 0, :], start=True, stop=True), reads=[twlaT, ups], writes=[g2])
        tt("dve", e2[:], g2[:], w0[:], ALU.add, [g2, w0], [e2])
        P.op("act", lambda e: e.activation(out=e2[:], in_=e2[:], func=AF.Exp, scale=-1.0), reads=[e2], writes=[e2])
        P.op("act", lambda e: e.activation(out=e2[:], in_=e2[:], func=AF.Ln, bias=1.0), reads=[e2], writes=[e2])
        P.op("act", lambda e: e.activation(out=e2[:], in_=e2[:], func=AF.Exp, scale=-1.0, bias=-0.5), reads=[e2], writes=[e2])
        tt("pool", kkn[:], k_ap, kkp[:], ALU.mult, [hs, kkp], [kkn])
        tt("pool", tmp[:], kkn[:], kkn[:], ALU.mult, [kkn], [tmp])
        P.op("dve", lambda e: e.tensor_reduce(out=pss[:], in_=v3(tmp), axis=AX.X, op=ALU.add), reads=[tmp], writes=[pss])
        P.op("act", lambda e: e.activation(out=pss[:], in_=pss[:], func=AF.Sqrt), reads=[pss], writes=[pss])
        P.op("dve", lambda e: e.tensor_scalar(out=pss[:], in0=pss[:], scalar1=1e-12, scalar2=None, op0=ALU.max), reads=[pss], writes=[pss])
        P.op("dve", lambda e: e.reciprocal(out=prn[:], in_=pss[:]), reads=[pss], writes=[prn])
        tt("dve", v3(p_t), v3(kkn), prn[:].unsqueeze(2).to_broadcast([128, 8, 64]), ALU.mult, [kkn, prn], [p_t])
        tt("pool", q_t[:], p_t[:], a_t[:], ALU.mult, [p_t, a_t], [q_t])
        P.op("dve", lambda e: e.scalar_tensor_tensor(out=tmp[:], in0=a_t[:], scalar=-1.0, in1=kap[:], op0=ALU.add, op1=ALU.mult), reads=[a_t, kap], writes=[tmp])
        P.op("dve", lambda e: e.scalar_tensor_tensor(out=kp[:], in0=tmp[:], scalar=1.0, in1=k_ap, op0=ALU.add, op1=ALU.mult), reads=[tmp, hs], writes=[kp])
        copy_op(P, "pool", Vb[:], v_ap, [hs], [Vb])
        gI = C.gbank()
        P.op("pe", lambda e: e.matmul(gI[:], lhsT=triI[:], rhs=e2[:], start=True, stop=True), reads=[triI, e2], writes=[gI])
        P.op("act", lambda e: e.activation(out=GI[:], in_=gI[:], func=AF.Exp, scale=-1.0), reads=[gI], writes=[GI])
        P.op("act", lambda e: e.activation(out=GIinv[:], in_=gI[:], func=AF.Exp), reads=[gI], writes=[GIinv])
        gE = C.gbank()
        P.op("pe", lambda e: e.matmul(gE[:], lhsT=triE[:], rhs=e2[:], start=True, stop=True), reads=[triE, e2], writes=[gE])
        P.op("act", lambda e: e.activation(out=GE[:], in_=gE[:], func=AF.Exp, scale=-1.0), reads=[gE], writes=[GE])
        gT = C.gbank()
        for h in range(8):
            P.op("pe", lambda e, h=h: e.matmul(gT[0:64, h:h + 1], lhsT=e2[:, h * 64:(h + 1) * 64], rhs=onescol[:, 0:1], start=True, stop=True),
                 reads=[e2, onescol], writes=[gT], accum=(h > 0))
        P.op("act", lambda e: e.activation(out=gcol[:], in_=gT[0:64, 0:8], func=AF.Exp, scale=-1.0), reads=[gT], writes=[gcol])
        tt("dve", Pd[:], p_t[:], GE[:], ALU.mult, [p_t, GE], [Pd])
        tt("pool", Qd[:], q_t[:], GIinv[:], ALU.mult, [q_t, GIinv], [Qd])
        tt("dve", Kd[:], kp[:], GIinv[:], ALU.mult, [kp, GIinv], [Kd])
        tt("pool", Rd[:], r_ap, GI[:], ALU.mult, [hs, GI], [Rd])
        copy_op(P, "pool", Pdb[:], Pd[:], [Pd], [Pdb]); copy_op(P, "dve", Qdb[:], Qd[:], [Qd], [Qdb]); copy_op(P, "pool", Kdb[:], Kd[:], [Kd], [Kdb])
        for (src, dstfn, dstt) in ((Pd, lambda h: PR[:, h, 0, :], PR), (Rd, lambda h: PR[:, h, 1, :], PR), (Qd, lambda h: QTt[:, h, :], QTt), (Kd, lambda h: KTt[:, h, :], KTt)):
            for hb in range(2):
                g = C.gbank()
                for hl in range(4):
                    h = 4 * hb + hl
                    P.op("pe", lambda e, hl=hl, h=h, g=g, src=src: e.transpose(out=g[0:64, hl * 128:(hl + 1) * 128], in_=src[:, h * 64:(h + 1) * 64], identity=C.ident[:]),
                         reads=[src, C.ident], writes=[g], accum=(hl > 0))
                if dstt is PR:
                    which = 0 if src is Pd else 1
                    copy_op(P, ("dve", "act")[hb], PR[:, 4 * hb:4 * hb + 4, which, :], g[0:64, :].rearrange("p (a b) -> p a b", a=4), [g], [PR], accum=True)
                else:
                    copy_op(P, ("act", "dve")[hb], dstt[:, 4 * hb:4 * hb + 4, :], g[0:64, :].rearrange("p (a b) -> p a b", a=4), [g], [dstt], accum=(hb > 0))
        Bc, Ac = Bm[0], Am[0]
        for hg in range(4):
            gq = C.gbank(); gk = C.gbank()
            for hh in range(2):
                h = 2 * hg + hh
                P.op("pe", lambda e: e.matmul(gq[:, hh * 256:(hh + 1) * 256], lhsT=QTt[:, h, :],
                                              rhs=PR[:, h, :, :].rearrange("p a b -> p (a b)"), start=True, stop=True),
                     reads=[QTt, PR], writes=[gq], accum=(hh > 0))
                P.op("pe", lambda e: e.matmul(gk[:, hh * 256:(hh + 1) * 256], lhsT=KTt[:, h, :],
                                              rhs=PR[:, h, :, :].rearrange("p a b -> p (a b)"), start=True, stop=True),
                     reads=[KTt, PR], writes=[gk], accum=(hh > 0))
            gq4 = gq[:].rearrange("p (h a b) -> p h a b", h=2, a=2)
            gk4 = gk[:].rearrange("p (h a b) -> p h a b", h=2, a=2)
            hs2 = slice(2 * hg, 2 * hg + 2)
            mE = triE[:].unsqueeze(1).to_broadcast([128, 2, 128]); mI = triI[:].unsqueeze(1).to_broadcast([128, 2, 128])
            tt("dve", Bc[:, hs2, :], gq4[:, :, 0, :], mE, ALU.mult, [gq, triE], [Bc], accum=(hg > 0))
            tt("dve", MqT[:, hs2, :], gq4[:, :, 1, :], mI, ALU.mult, [gq, triI], [MqT], accum=(hg > 0))
            tt("dve", LkT[:, hs2, :], gk4[:, :, 0, :], mE, ALU.mult, [gk, triE], [LkT], accum=(hg > 0))
            tt("dve", MkT[:, hs2, :], gk4[:, :, 1, :], mI, ALU.mult, [gk, triI], [MkT], accum=(hg > 0))
        for hb in range(2):
            g = C.gbank()
            for hl in range(4):
                h = 4 * hb + hl
                P.op("pe", lambda e: e.matmul(g[:, hl * 128:(hl + 1) * 128], lhsT=PR[:, h, 0, :], rhs=QTt[:, h, :],
                                              start=True, stop=True), reads=[PR, QTt], writes=[g], accum=(hl > 0))
            tt("dve", Ac[:, 4 * hb:4 * hb + 4, :], g[:].rearrange("p (h b) -> p h b", h=4), triET[:].unsqueeze(1).to_broadcast([128, 4, 128]), ALU.mult,
               [g, triET], [Ac], accum=(hb > 0))
        P.op("dve", lambda e: e.scalar_tensor_tensor(out=Pm[:], in0=Bc[:], scalar=-1.0, in1=eye_b[:].unsqueeze(1).to_broadcast([128, 8, 128]),
                                                     op0=ALU.mult, op1=ALU.add), reads=[Bc, eye_b], writes=[Pm])
        for lev in range(6):
            Bn, An = Bm[(lev + 1) % 2], Am[(lev + 1) % 2]
            for hb in range(2):
                gA = C.gbank()
                for hl in range(4):
                    h = 4 * hb + hl
                    P.op("pe", lambda e: e.matmul(gA[:, hl * 128:(hl + 1) * 128], lhsT=Bc[:, h, :], rhs=Ac[:, h, :], start=True, stop=True),
                         reads=[Bc, Ac], writes=[gA], accum=(hl > 0))
                copy_op(P, ("act", "dve")[hb], An[:, 4 * hb:4 * hb + 4, :].rearrange("p a b -> p (a b)"), gA[:], [gA], [An], accum=(hb > 0))
                if lev < 5:
                    gB = C.gbank()
                    for hl in range(4):
                        h = 4 * hb + hl
                        P.op("pe", lambda e: e.matmul(gB[:, hl * 128:(hl + 1) * 128], lhsT=Ac[:, h, :], rhs=Bc[:, h, :], start=True, stop=True),
                             reads=[Bc, Ac], writes=[gB], accum=(hl > 0))
                    copy_op(P, ("dve", "act")[hb], Bn[:, 4 * hb:4 * hb + 4, :].rearrange("p a b -> p (a b)"), gB[:], [gB], [Bn], accum=(hb > 0))
            for hb in range(2):
                gP = C.gbank()
                for hl in range(4):
                    h = 4 * hb + hl
                    P.op("pe", lambda e: e.matmul(gP[:, hl * 128:(hl + 1) * 128], lhsT=An[:, h, :], rhs=Pm[:, h, :], start=True, stop=True),
                         reads=[An, Pm], writes=[gP], accum=(hl > 0))
                tt("dve", Pm[:, 4 * hb:4 * hb + 4, :].rearrange("p a b -> p (a b)"), gP[:], Pm[:, 4 * hb:4 * hb + 4, :].rearrange("p a b -> p (a b)"),
                   ALU.add, [gP, Pm], [Pm], accum=True)
            Bc, Ac = Bn, An
        g = C.gbank()
        for h in range(8):
            P.op("pe", lambda e: e.matmul(g[:, h * 64:(h + 1) * 64], lhsT=LkT[:, h, :], rhs=Vb[:, h * 64:(h + 1) * 64], start=True, stop=True),
                 reads=[LkT, Vb], writes=[g], accum=(h > 0))
        copy_op(P, "act", LkV[:], g[:], [g], [LkV])
        g = C.gbank()
        for h in range(8):
            P.op("pe", lambda e: e.matmul(g[0:64, h * 64:(h + 1) * 64], lhsT=Kdb[:, h * 64:(h + 1) * 64], rhs=Vb[:, h * 64:(h + 1) * 64],
                                          start=True, stop=True), reads=[Kdb, Vb], writes=[g], accum=(h > 0))
        copy_op(P, "dve", KV[:].rearrange("p a b -> p (a b)"), g[0:64, :], [g], [KV])
        gcb = gcol[:].unsqueeze(2).to_broadcast([64, 8, 64])
        tt("pool", ZK[:], Z[:], KV[:], ALU.add, [Z, KV], [ZK])
        tt("pool", ZKg[:], ZK[:], gcb, ALU.mult, [ZK, gcol], [ZKg])
        gz = bk[3]
        for h in range(8):
            P.op("pe", lambda e: e.matmul(gz[:, h * 64:(h + 1) * 64], lhsT=PR[:, h, 0, :], rhs=Zb[:, h, :], start=True, stop=True),
                 reads=[PR, Zb], writes=[gz], accum=(h > 0))
        tt("dve", rhs_sb[:], gz[:], LkV[:], ALU.add, [gz, LkV], [rhs_sb])
        gu = bk[4]
        for h in range(8):
            P.op("pe", lambda e: e.matmul(gu[:, h * 64:(h + 1) * 64], lhsT=Pm[:, h, :], rhs=rhs_sb[:, h * 64:(h + 1) * 64], start=True, stop=True),
                 reads=[Pm, rhs_sb], writes=[gu], accum=(h > 0))
        P.op("act", lambda e: e.activation(out=U_sb[:], in_=gu[:], func=AF.Copy, scale=-1.0), reads=[gu], writes=[U_sb])
        gy = bk[5]
        for h in range(8):
            osl = gy[:, h * 64:(h + 1) * 64]
            P.op("pe", lambda e: e.matmul(osl, lhsT=PR[:, h, 1, :], rhs=Zb[:, h, :], start=True, stop=False),
                 reads=[PR, Zb], writes=[gy], accum=(h > 0))
            P.op("pe", lambda e: e.matmul(osl, lhsT=MqT[:, h, :], rhs=U_sb[:, h * 64:(h + 1) * 64], start=False, stop=False),
                 reads=[MqT, U_sb], writes=[gy], accum=True)
            P.op("pe", lambda e: e.matmul(osl, lhsT=MkT[:, h, :], rhs=Vb[:, h * 64:(h + 1) * 64], start=False, stop=True),
                 reads=[MkT, Vb], writes=[gy], accum=True)
        gq_ = bk[6]
        for h in range(8):
            P.op("pe", lambda e: e.matmul(gq_[0:64, h * 64:(h + 1) * 64], lhsT=Qdb[:, h * 64:(h + 1) * 64], rhs=U_sb[:, h * 64:(h + 1) * 64],
                                          start=True, stop=True), reads=[Qdb, U_sb], writes=[gq_], accum=(h > 0))
        tt("dve", Ztmp[:], gq_[0:64, :].rearrange("p (a b) -> p a b", a=8), gcb, ALU.mult, [gq_, gcol], [Ztmp])
        tt("dve", Z[:], Ztmp[:], ZKg[:], ALU.add, [Ztmp, ZKg], [Z])
        copy_op(P, "dve", Zb[:], Z[:], [Z], [Zb])
        if d == 0:
            yt = ysb[par]
            copy_op(P, "act", yt[:], gy[:], [gy], [yt])
            P.dma(C.yrw[r0:r0 + 128, :], yt[:], reads=[yt], writes=[C.yrw_tr[i]])
            if it == NT - 1:
                P.dma(C.zsrc[:, :], Z[:].rearrange("p a b -> p (a b)"), reads=[Z], writes=[C.zsrc_tr])
                P.collective(C.zsrc[:, :], C.zdst[:, :], reads=[C.zsrc_tr], writes=[C.zdst_tr])
        else:
            y = ysb[par]
            tt("dve", y[:], gy[:], yfw[par][:], ALU.add, [gy, yfw[par]], [y])
            P.op("dve", lambda e: e.tensor_reduce(out=mean[:], in_=v3(y), axis=AX.X, op=ALU.add), reads=[y], writes=[mean])
            P.op("dve", lambda e: e.tensor_scalar(out=mean[:], in0=mean[:], scalar1=1.0 / 64, scalar2=None, op0=ALU.mult), reads=[mean], writes=[mean])
            tt("dve", v3(cent), v3(y), mean[:].unsqueeze(2).to_broadcast([128, 8, 64]), ALU.subtract, [y, mean], [cent])
            tt("pool", tmp2[:], cent[:], cent[:], ALU.mult, [cent], [tmp2])
            P.op("dve", lambda e: e.tensor_reduce(out=var[:], in_=v3(tmp2), axis=AX.X, op=ALU.add), reads=[tmp2], writes=[var])
            P.op("act", lambda e: e.activation(out=var[:], in_=var[:], func=AF.Sqrt, scale=1.0 / 64, bias=64e-5), reads=[var], writes=[var])
            P.op("dve", lambda e: e.reciprocal(out=var[:], in_=var[:]), reads=[var], writes=[var])
            tt("dve", v3(cent), v3(cent), var[:].unsqueeze(2).to_broadcast([128, 8, 64]), ALU.mult, [cent, var], [cent])
            tt("pool", cent[:], cent[:], lng[:], ALU.mult, [cent, lng], [cent])
            tt("pool", cent[:], cent[:], lnb[:], ALU.add, [cent, lnb], [cent])
            tt("pool", rkk[:], r_ap, kp[:], ALU.mult, [hs, kp], [rkk])
            tt("pool", rkk[:], rkk[:], rkp[:], ALU.mult, [rkk, rkp], [rkk])
            P.op("dve", lambda e: e.tensor_reduce(out=bon[:], in_=v3(rkk), axis=AX.X, op=ALU.add), reads=[rkk], writes=[bon])
            tt("dve", v3(rkk), hs[:, 1024:1536].rearrange("p (h d) -> p h d", h=8), bon[:].unsqueeze(2).to_broadcast([128, 8, 64]), ALU.mult, [hs, bon], [rkk])
            tt("pool", cent[:], cent[:], rkk[:], ALU.add, [cent, rkk], [cent])
            P.op("act", lambda e: e.activation(out=szr[:], in_=zrw[par][:], func=AF.Silu), reads=[zrw[par]], writes=[szr])
            tt("dve", yb[:], cent[:], szr[:], ALU.mult, [cent, szr], [yb])
            transpose_to(P, C, yb, 4, ybT)
            xot = xo[par]
            for gcol_i in range(2):
                bank = C.gbank()
                for c in range(8):
                    lhs = mixA[par][:, c, :] if c < 4 else ybT[:, c - 4, :]
                    P.op("pe", lambda e, c=c, lhs=lhs, bank=bank: e.matmul(bank[:], lhsT=lhs, rhs=wo[:, c, gcol_i * 512:(gcol_i + 1) * 512],
                                                                          start=(c == 0), stop=(c == 7)),
                         reads=[mixA[par], ybT, wo], writes=[bank], accum=(c > 0))
                tt("dve", xot[:, gcol_i * 512:(gcol_i + 1) * 512], bank[:], xres[par][:, gcol_i * 512:(gcol_i + 1) * 512], ALU.add,
                   [bank, xres[par]], [xot], accum=(gcol_i > 0))
            P.dma(x_dst[r0:r0 + 128, :], xot[:], reads=[xot], writes=[xd_tr[i]])
    P.barrier()
    st.close(); P.stack = P.gstack


NT_FULL = 64
LAYERS = [("even", 0), ("odd", 0), ("even", 1), ("odd", 1)]
DIR_KEYS = ("s5_ar_row", "s5_ai_row", "s5_dt_row", "s5_ar_col", "s5_ai_col", "s5_dt_col", "s5_b_col", "s5_c_col", "rw_w0_rep", "rw_w_up")


def core_flags(w0, w1):
    fl = np.zeros((128, 4), np.float32)
    fl[:, 0] = w0
    fl[:, 1] = w1
    fl[:, 2] = 0.0 if (w0 + w1) > 0 else NEG
    return fl


def core_maps(m, streams):
    mrev = dict(m)
    for k in DIR_KEYS:
        mrev[k] = np.ascontiguousarray(m[k][:, ::-1])
    maps = []
    for (x, rev, w0, w1) in streams:
        mm = dict(mrev if rev else m)
        mm["flags"] = core_flags(w0, w1)
        mm["xin"] = np.ascontiguousarray(x[::-1] if rev else x)
        maps.append(mm)
    return maps


def kernel(**inputs):
    xp = np.asarray(inputs["x_prompt"], np.float32)
    xs = np.asarray(inputs["x_sample"], np.float32)
    m = host_layout(inputs, LAYERS)
    nc, gst = build_program(NT_FULL, LAYERS)
    streams = [(xs[0, 0:8192], False, 0.0, 1.0), (xs[0, 8192:16384], True, 1.0, 0.0)]
    for b in range(4):
        streams.append((xp[b], False, 0.0, 0.0))
    streams += [(xp[0], False, 0.0, 0.0), (xp[1], False, 0.0, 0.0)]
    maps = core_maps(m, streams)
    res = run_bass_kernel_spmd(nc, maps, core_ids=list(range(8)))
    outs = [np.asarray(res.results[c]["xout"], np.float32) for c in range(6)]
    y_sample = np.concatenate([outs[0], outs[1][::-1]], axis=0).reshape(1, 16384, D)
    y_prompt = np.stack(outs[2:6], axis=0)
    return (y_prompt, y_sample)
```

```python
import numpy as np
from contextlib import ExitStack
import concourse.bass as bass
import concourse.mybir as mybir
from concourse.bass_utils import run_bass_kernel_spmd

F32 = mybir.dt.float32
BF16 = mybir.dt.bfloat16
AF = mybir.ActivationFunctionType
ALU = mybir.AluOpType
AX = mybir.AxisListType

import os as _os
N_DMA_SLOTS = int(_os.environ.get("NSLOTS", "24"))
D = 1024
EPS = 1e-6
NEG = -30000.0


import types


def _snap(fn):
    if fn.__closure__ is None:
        return fn
    cells = tuple(types.CellType(c.cell_contents) for c in fn.__closure__)
    return types.FunctionType(fn.__code__, fn.__globals__, fn.__name__, fn.__defaults__, cells)


class T:
    __slots__ = ("t", "name", "writers", "readers", "war")

    def __init__(self, t, name=""):
        self.t = t
        self.name = name
        self.writers = []
        self.readers = []
        self.war = []

    def __getitem__(self, idx):
        return self.t[idx]


class Prog:
    ENGS = ("pe", "act", "dve", "pool", "sp")

    def __init__(self, nc, stack):
        self.nc = nc
        self.stack = stack
        self.gstack = stack
        self.ops = {e: [] for e in self.ENGS}
        self.cnt = {e: 0 for e in self.ENGS}
        self.seen = {e: {} for e in self.ENGS}
        self.sems = {e: stack.enter_context(nc.semaphore("s_" + e)) for e in self.ENGS}
        self.dma_sems = [stack.enter_context(nc.semaphore("s_dma%d" % i)) for i in range(N_DMA_SLOTS)]
        self.dma_n = 0
        self.cc_n = 0
        self.sems["cc"] = stack.enter_context(nc.semaphore("s_cc"))
        import os
        self.same_engine_sync = not os.environ.get("NOSES")
        self._uid = 0

    def sb(self, shape, dt=F32, name=None):
        self._uid += 1
        name = "sb%d" % self._uid
        return T(self.stack.enter_context(self.nc.sbuf_tensor(name, list(shape), dt)), name)

    def ps(self, shape, dt=F32):
        self._uid += 1
        name = "ps%d" % self._uid
        return T(self.stack.enter_context(self.nc.psum_tensor(name, list(shape), dt)), name)

    def dram(self, name, shape, dt=F32):
        return self.nc.dram_tensor(name, list(shape), dt, kind="Internal")

    def _need(self, eng, dep, waits):
        key, val, deng = dep
        if deng == eng and (eng == "pe" or not self.same_engine_sync):
            return
        if self.seen[eng].get(key, -1) >= val:
            return
        self.seen[eng][key] = val
        waits.append((key, val))

    def _deps(self, eng, reads, writes, accum):
        waits = []
        for t in reads:
            for w in t.writers:
                self._need(eng, w, waits)
        for t in writes:
            if not accum:
                for w in t.writers:
                    self._need(eng, w, waits)
            else:
                for w in t.war:
                    self._need(eng, w, waits)
            for r in t.readers:
                self._need(eng, r, waits)
        return waits

    def _commit(self, tok, reads, writes, accum):
        for t in reads:
            t.readers.append(tok)
        for t in writes:
            if accum:
                t.writers.append(tok)
                t.war = t.war + t.readers
            else:
                t.war = t.writers + t.readers
                t.writers = [tok]
            t.readers = []

    def _sem(self, key):
        return self.sems[key] if isinstance(key, str) else self.dma_sems[key]

    def op(self, eng, fn, reads=(), writes=(), accum=False):
        import os
        if eng == "pool" and os.environ.get("NOPOOL"):
            eng = "dve"
        kmax = int(os.environ.get("KMAX", "0"))
        if kmax and sum(self.cnt.values()) >= kmax:
            return None
        waits = self._deps(eng, reads, writes, accum)
        self.cnt[eng] += 1
        tok = (eng, self.cnt[eng], eng)
        self._commit(tok, reads, writes, accum)
        self.ops[eng].append((waits, _snap(fn), (eng, 1)))
        return tok

    def dma(self, out_ap, in_ap, reads=(), writes=(), q="sp", accum=False):
        waits = self._deps(q, reads, writes, accum)
        i = self.dma_n
        self.dma_n += 1
        slot = i % N_DMA_SLOTS
        val = 16 * (i // N_DMA_SLOTS + 1)
        if i >= N_DMA_SLOTS and self.seen[q].get(slot, -1) < val - 16:
            self.seen[q][slot] = val - 16
            waits.append((slot, val - 16))
        tok = (slot, val, "dma")
        self._commit(tok, reads, writes, accum)

        def fn(e, out_ap=out_ap, in_ap=in_ap):
            return e.dma_start(out=out_ap, in_=in_ap)
        self.ops[q].append((waits, fn, (slot, 16)))
        return tok

    def collective(self, src_ap, dst_ap, reads=(), writes=(), groups=((0, 1), (2, 3), (4, 5), (6, 7))):
        q = "pool"
        waits = self._deps(q, reads, writes, False)
        self.cc_n += 1
        tok = ("cc", self.cc_n, "cc")
        self._commit(tok, reads, writes, False)
        rg = [list(g) for g in groups]

        def fn(e):
            return e.collective_compute("AllGather", ALU.bypass, replica_groups=rg, ins=[src_ap], outs=[dst_ap])
        self.ops[q].append((waits, fn, ("cc", 1)))
        return tok

    def barrier(self):
        for e in self.ENGS:
            waits = []
            for o in self.ENGS:
                if o != e and self.cnt[o] > 0 and self.seen[e].get(o, -1) < self.cnt[o]:
                    self.seen[e][o] = self.cnt[o]
                    waits.append((o, self.cnt[o]))
            if self.cc_n > 0 and self.seen[e].get("cc", -1) < self.cc_n:
                self.seen[e]["cc"] = self.cc_n
                waits.append(("cc", self.cc_n))
            n = self.dma_n
            for slot in range(min(n, N_DMA_SLOTS)):
                last_i = ((n - 1 - slot) // N_DMA_SLOTS) * N_DMA_SLOTS + slot
                v = 16 * (last_i // N_DMA_SLOTS + 1)
                if self.seen[e].get(slot, -1) < v:
                    self.seen[e][slot] = v
                    waits.append((slot, v))
            if waits:
                self.ops[e].append((waits, None, None))

    def emit(self):
        nc = self.nc
        self.barrier()
        block = self.gstack.enter_context(nc.Block())
        prog = self

        def run(engname, e):
            for waits, fn, inc in prog.ops[engname]:
                for key, val in waits:
                    e.wait_ge(prog._sem(key), val)
                if fn is not None:
                    fn(e).then_inc(prog._sem(inc[0]), inc[1])

        @block.tensor
        def _(e):
            run("pe", e)

        @block.scalar
        def _(e):
            run("act", e)

        @block.vector
        def _(e):
            run("dve", e)

        @block.gpsimd
        def _(e):
            run("pool", e)

        @block.sync
        def _(e):
            run("sp", e)


class Ctx:
    pass


def rr(P, C, key, engs):
    C.rr[key] = C.rr.get(key, -1) + 1
    return engs[C.rr[key] % len(engs)]


def copy_op(P, eng, out_ap, in_ap, reads, writes, accum=False):
    if eng == "act":
        P.op("act", lambda e: e.activation(out=out_ap, in_=in_ap, func=AF.Copy), reads=reads, writes=writes, accum=accum)
    elif eng == "dve":
        P.op("dve", lambda e: e.tensor_copy(out=out_ap, in_=in_ap), reads=reads, writes=writes, accum=accum)
    else:
        P.op("pool", lambda e: e.tensor_copy(out=out_ap, in_=in_ap), reads=reads, writes=writes, accum=accum)


def load_weight_bf16(P, C, dst, src_ap_fn, nchunk, ncols, stage):
    for c in range(nchunk):
        s = stage[c % len(stage)]
        P.dma(s[:, 0:ncols], src_ap_fn(c), reads=[], writes=[s])
        eng = ("act", "dve", "pool")[c % 3]
        copy_op(P, eng, dst[:, c, :], s[:, 0:ncols], [s], [dst], accum=True)


def rmsnorm_T(P, C, xt, gn, hT, S):
    junk, ss, ss2, rs, h = S.junk, S.ss, S.ss2, S.rs, S.h
    P.op("act", lambda e: e.activation(out=junk[:], in_=xt[:], func=AF.Square, accum_out=ss[:]),
         reads=[xt], writes=[junk, ss])
    P.op("act", lambda e: e.activation(out=ss2[:], in_=ss[:], func=AF.Sqrt, scale=1.0 / D, bias=EPS),
         reads=[ss], writes=[ss2])
    P.op("dve", lambda e: e.reciprocal(out=rs[:], in_=ss2[:]), reads=[ss2], writes=[rs])
    P.op("dve", lambda e: e.scalar_tensor_tensor(out=h[:], in0=xt[:], scalar=rs[:, 0:1], in1=gn[:],
                                                 op0=ALU.mult, op1=ALU.mult), reads=[xt, rs, gn], writes=[h])
    transpose_to(P, C, h, 8, hT)


def transpose_to(P, C, src, nblk, dst, src_off=0):
    for g0 in range(0, nblk, 4):
        n = min(4, nblk - g0)
        bank = C.gbank()
        for c in range(n):
            P.op("pe", lambda e, c=c, bank=bank, g0=g0: e.transpose(
                out=bank[:, c * 128:(c + 1) * 128],
                in_=src[:, src_off + (g0 + c) * 128: src_off + (g0 + c + 1) * 128], identity=C.ident[:]),
                reads=[src, C.ident], writes=[bank], accum=(c > 0))
        eng = rr(P, C, "tev", ("act", "dve"))
        copy_op(P, eng, dst[:, g0:g0 + n, :], bank[:, 0:n * 128].rearrange("p (a b) -> p a b", a=n),
                [bank], [dst], accum=(g0 > 0))


def transpose_heads(P, C, src, nheads, dst, src_off=0):
    for g0 in range(0, nheads, 4):
        n = min(4, nheads - g0)
        bank = C.gbank()
        for c in range(n):
            P.op("pe", lambda e, c=c, bank=bank, g0=g0: e.transpose(
                out=bank[0:64, c * 128:(c + 1) * 128],
                in_=src[:, src_off + (g0 + c) * 64: src_off + (g0 + c + 1) * 64], identity=C.ident[:]),
                reads=[src, C.ident], writes=[bank], accum=(c > 0))
        eng = rr(P, C, "tev", ("act", "dve"))
        copy_op(P, eng, dst[:, g0:g0 + n, :], bank[0:64, 0:n * 128].rearrange("p (a b) -> p a b", a=n),
                [bank], [dst], accum=(g0 > 0))


def matmul_group(P, C, bank, ncols, hT, W, col0, nk=8, out_off=0):
    for c in range(nk):
        P.op("pe", lambda e, c=c: e.matmul(bank[:, out_off:out_off + ncols], lhsT=hT[:, c, :],
                                           rhs=W[:, c, col0:col0 + ncols], start=(c == 0), stop=(c == nk - 1)),
             reads=[hT, W], writes=[bank], accum=(c > 0))


def odd_layer(P, C, l, x_src, x_dst, xs_tr, xd_tr):
    NT = C.NT
    st = ExitStack()
    P.stack = st
    I = C.inp
    wq = P.sb([128, 8, 2560], BF16)
    wo = P.sb([128, 8, 1024], BF16)
    st2 = ExitStack(); P.stack = st2
    stage = [P.sb([128, 2560]), P.sb([128, 2560])]
    load_weight_bf16(P, C, wq, lambda c: I["od_w_in"][l, c * 128:(c + 1) * 128, :], 8, 2560, stage)
    load_weight_bf16(P, C, wo, lambda c: I["od_w_out"][l, c * 128:(c + 1) * 128, :], 8, 1024, stage)
    P.barrier()
    st2.close(); P.stack = st
    gn = P.sb([128, 1024]); gq = P.sb([128, 64]); gk = P.sb([128, 64]); esink = P.sb([128, 16])
    P.dma(gn[:], I["od_norm_rep"][l], writes=[gn])
    P.dma(gq[:], I["qg_rep"][l], writes=[gq])
    P.dma(gk[:], I["kg_rep"][l], writes=[gk])
    P.dma(esink[:], I["sink_rep"][l], writes=[esink])
    P.op("act", lambda e: e.activation(out=esink[:], in_=esink[:], func=AF.Exp), reads=[esink], writes=[esink])
    biasT = P.sb([128, 16, 512])
    for j in range(4):
        P.dma(biasT[:, 4 * j:4 * j + 4, :], I["alibi"][j].rearrange("r s q -> s r q"), writes=[biasT], accum=True)
    KTh = P.sb([64, 4, 128], BF16); Vh = P.sb([128, 4, 72], BF16)
    kh2 = P.sb([64, 2, 512], BF16); vh2 = P.sb([128, 2, 288], BF16)

    S = Ctx()
    S.junk = P.sb([128, 1024]); S.ss = P.sb([128, 1]); S.ss2 = P.sb([128, 1]); S.rs = P.sb([128, 1]); S.h = P.sb([128, 1024])
    hT = P.sb([128, 8, 128], BF16)
    xring = [P.sb([128, 1024]) for _ in range(3)]
    qf = P.sb([128, 1024]); qsq = P.sb([128, 1024]); qss = P.sb([128, 16]); qr = P.sb([128, 16]); qn = P.sb([128, 1024])
    kf = P.sb([128, 256]); ksq = P.sb([128, 256]); kss = P.sb([128, 4]); kr = P.sb([128, 4]); kn = P.sb([128, 256])
    QT = [P.sb([64, 16, 128], BF16) for _ in range(3)]
    KT = [P.sb([64, 4, 128], BF16) for _ in range(4)]
    V = [P.sb([128, 4, 72], BF16) for _ in range(4)]
    for v in V:
        P.op("pool", lambda e, v=v: e.memset(v[:], 1.0), writes=[v])
    sz = [P.sb([128, 1024]) for _ in range(3)]
    sring = [P.sb([128, 512]) for _ in range(3)]
    pr = [P.sb([128, 512], BF16) for _ in range(6)]
    den = P.sb([128, 4]); rden = P.sb([128, 4])
    o = P.sb([128, 1024]); og = P.sb([128, 1024]); ogT = P.sb([128, 8, 128], BF16)
    xo = [P.sb([128, 1024]) for _ in range(2)]

    def rms_heads(src, sq, ssum, rinv, nh, g, outs):
        P.op("pool", lambda e: e.tensor_tensor(out=sq[:], in0=src[:], in1=src[:], op=ALU.mult), reads=[src], writes=[sq])
        P.op("dve", lambda e: e.tensor_reduce(out=ssum[:], in_=sq[:].rearrange("p (h d) -> p h d", h=nh), axis=AX.X, op=ALU.add),
             reads=[sq], writes=[ssum])
        P.op("act", lambda e: e.activation(out=ssum[:], in_=ssum[:], func=AF.Sqrt, scale=1.0 / 64, bias=EPS),
             reads=[ssum], writes=[ssum])
        P.op("dve", lambda e: e.reciprocal(out=rinv[:], in_=ssum[:]), reads=[ssum], writes=[rinv])
        P.op("dve", lambda e: e.tensor_tensor(out=sq[:].rearrange("p (h d) -> p h d", h=nh),
                                              in0=src[:].rearrange("p (h d) -> p h d", h=nh),
                                              in1=rinv[:].unsqueeze(2).to_broadcast([128, nh, 64]), op=ALU.mult),
             reads=[src, rinv], writes=[sq])
        for oi, (oap, ot) in enumerate(outs):
            P.op("pool", lambda e, oap=oap: e.tensor_tensor(out=oap, in0=sq[:].rearrange("p (h d) -> p h d", h=nh),
                                                            in1=g[:].unsqueeze(1).to_broadcast([128, nh, 64]), op=ALU.mult),
                 reads=[sq, g], writes=[ot], accum=(oi > 0))

    def stage1_parts(j):
        xt = xring[j % 3]

        def pa():
            P.dma(xt[:], x_src[j * 128:(j + 1) * 128, :], reads=[xs_tr[j]], writes=[xt])
            rmsnorm_T(P, C, xt, gn, hT, S)

        def pb():
            for g in range(2):
                bank = C.gbank()
                matmul_group(P, C, bank, 512, hT, wq, g * 512)
                copy_op(P, rr(P, C, "qev", ("act", "dve")), qf[:, g * 512:(g + 1) * 512], bank[:], [bank], [qf], accum=(g > 0))
            rms_heads(qf, qsq, qss, qr, 16, gq, [(qn[:].rearrange("p (h d) -> p h d", h=16), qn)])
            transpose_heads(P, C, qn, 16, QT[j % 3])

        def pc():
            bank = C.gbank()
            matmul_group(P, C, bank, 512, hT, wq, 1024)
            copy_op(P, "dve", kf[:], bank[:, 0:256], [bank], [kf])
            Vt = V[j % 4]
            copy_op(P, "dve", Vt[:, :, 0:64], bank[:, 256:512].rearrange("p (h d) -> p h d", h=4), [bank], [Vt])
            rms_heads(kf, ksq, kss, kr, 4, gk, [(kn[:].rearrange("p (h d) -> p h d", h=4), kn)])
            transpose_heads(P, C, kn, 4, KT[j % 4])

        def pd():
            for g in range(2):
                bank = C.gbank()
                matmul_group(P, C, bank, 512, hT, wq, 1536 + g * 512)
                szt = sz[j % 3]
                P.op("act", lambda e, bank=bank, g=g, szt=szt: e.activation(out=szt[:, g * 512:(g + 1) * 512], in_=bank[:], func=AF.Silu),
                     reads=[bank], writes=[szt], accum=(g > 0))

        return [pa, pb, pc, pd]

    def halo_exchange():
        jl = NT - 1
        ktl, vl = KT[jl % 4], V[jl % 4]
        P.dma(C.ksrc[:, :], ktl[:].rearrange("p a b -> p (a b)"), reads=[ktl], writes=[C.ksrc_tr])
        P.dma(C.vsrc[:, :], vl[:].rearrange("p a b -> p (a b)"), reads=[vl], writes=[C.vsrc_tr])
        P.collective(C.ksrc[:, :], C.kdst[:, :], reads=[C.ksrc_tr], writes=[C.kdst_tr])
        P.collective(C.vsrc[:, :], C.vdst[:, :], reads=[C.vsrc_tr], writes=[C.vdst_tr])
        P.dma(kh2[:], C.kdst.ap().rearrange("(s p) n -> p s n", s=2), reads=[C.kdst_tr], writes=[kh2])
        P.dma(vh2[:], C.vdst.ap().rearrange("(s p) n -> p s n", s=2), reads=[C.vdst_tr], writes=[vh2])
        kf_ = KTh[:].rearrange("p a b -> p (a b)"); vf_ = Vh[:].rearrange("p a b -> p (a b)")
        P.op("dve", lambda e: e.tensor_scalar(out=kf_, in0=kh2[:, 0, :], scalar1=C.flags[0:64, 0:1], scalar2=None, op0=ALU.mult), reads=[kh2, C.flags], writes=[KTh])
        P.op("dve", lambda e: e.scalar_tensor_tensor(out=kf_, in0=kh2[:, 1, :], scalar=C.flags[0:64, 1:2], in1=kf_, op0=ALU.mult, op1=ALU.add),
             reads=[kh2, C.flags, KTh], writes=[KTh])
        P.op("dve", lambda e: e.tensor_scalar(out=vf_, in0=vh2[:, 0, :], scalar1=C.flags[:, 0:1], scalar2=None, op0=ALU.mult), reads=[vh2, C.flags], writes=[Vh])
        P.op("dve", lambda e: e.scalar_tensor_tensor(out=vf_, in0=vh2[:, 1, :], scalar=C.flags[:, 1:2], in1=vf_, op0=ALU.mult, op1=ALU.add),
             reads=[vh2, C.flags, Vh], writes=[Vh])

    def stage2_parts(i):
        def pj(jkv):
            blocks = [b for b in (i - 1, i, i + 1) if 0 <= b < NT]
            if i == NT - 1:
                blocks.append(NT)
            for b in blocks:
                rel = b - i + 1
                halo = (b == NT)
                KTb = KTh if halo else KT[b % 4]
                bank = C.sbank[rel]
                for hl in range(4):
                    hq = 4 * jkv + hl
                    P.op("pe", lambda e, bank=bank, hl=hl, b=b, hq=hq: e.matmul(
                        bank[:, hl * 128:(hl + 1) * 128], lhsT=KTb[:, jkv, :],
                        rhs=QT[i % 3][:, hq, :], start=True, stop=True),
                        reads=[KTb, QT[i % 3]], writes=[bank], accum=(hl > 0))
                s_t = sring[rel]
                bidx = 4 * jkv + (3 if halo else rel)
                P.op("dve", lambda e, bank=bank, s_t=s_t, rel=rel: e.scalar_tensor_tensor(
                    out=s_t[:], in0=bank[:], scalar=0.125, in1=biasT[:, bidx, :], op0=ALU.mult, op1=ALU.add),
                    reads=[bank, biasT], writes=[s_t])
                if halo:
                    P.op("dve", lambda e, s_t=s_t: e.tensor_scalar(out=s_t[:], in0=s_t[:], scalar1=C.flags[:, 2:3], scalar2=None,
                                                                   op0=ALU.add), reads=[s_t, C.flags], writes=[s_t])
                pt = pr[(jkv % 2) * 3 + rel]
                P.op("act", lambda e, pt=pt, s_t=s_t: e.activation(out=pt[:], in_=s_t[:], func=AF.Exp), reads=[s_t], writes=[pt])
            pvb = C.pbank[jkv % 2]
            for hl in range(4):
                for bi, b in enumerate(blocks):
                    rel = b - i + 1
                    pt = pr[(jkv % 2) * 3 + rel]
                    Vb_ = Vh if b == NT else V[b % 4]
                    P.op("pe", lambda e, pt=pt, hl=hl, b=b, bi=bi: e.matmul(
                        pvb[:, hl * 65:(hl + 1) * 65], lhsT=pt[:, hl * 128:(hl + 1) * 128], rhs=Vb_[:, jkv, 0:65],
                        start=(bi == 0), stop=(bi == len(blocks) - 1)),
                        reads=[pt, Vb_], writes=[pvb], accum=not (hl == 0 and bi == 0))
            pv3 = pvb[:, 0:260].rearrange("p (h d) -> p h d", h=4)
            P.op("dve", lambda e, pv3=pv3: e.tensor_tensor(out=den[:], in0=pv3[:, :, 64], in1=esink[:, 4 * jkv:4 * jkv + 4], op=ALU.add),
                 reads=[pvb, esink], writes=[den])
            P.op("dve", lambda e: e.reciprocal(out=rden[:], in_=den[:]), reads=[den], writes=[rden])
            P.op("dve", lambda e, pv3=pv3: e.tensor_tensor(
                out=o[:, jkv * 256:(jkv + 1) * 256].rearrange("p (h d) -> p h d", h=4), in0=pv3[:, :, 0:64],
                in1=rden[:].unsqueeze(2).to_broadcast([128, 4, 64]), op=ALU.mult),
                reads=[pvb, rden], writes=[o], accum=(jkv > 0))

        def ptail():
            P.op("pool", lambda e: e.tensor_tensor(out=og[:], in0=o[:], in1=sz[i % 3][:], op=ALU.mult), reads=[o, sz[i % 3]], writes=[og])
            transpose_to(P, C, og, 8, ogT)
            xot = xo[i % 2]
            for g in range(2):
                bank = C.gbank()
                matmul_group(P, C, bank, 512, ogT, wo, g * 512)
                P.op("dve", lambda e, bank=bank, g=g: e.tensor_tensor(out=xot[:, g * 512:(g + 1) * 512], in0=bank[:],
                                                                      in1=xring[i % 3][:, g * 512:(g + 1) * 512], op=ALU.add),
                     reads=[bank, xring[i % 3]], writes=[xot], accum=(g > 0))
            P.dma(x_dst[i * 128:(i + 1) * 128, :], xot[:], reads=[xot], writes=[xd_tr[i]], q="act")

        return [lambda: pj(0), lambda: pj(1), lambda: pj(2), lambda: pj(3), ptail]

    import os
    dbg = int(os.environ.get("KDBG", "9"))
    for t in range(NT + 2):
        p1 = stage1_parts(t) if t < NT else []
        p2 = stage2_parts(t - 2) if t >= 2 else []
        for k in range(max(len(p1), len(p2))):
            if k < len(p2):
                p2[k]()
            if k < len(p1):
                p1[k]()
        if t == NT - 1:
            halo_exchange()
    P.barrier()
    st.close()
    P.stack = P.gstack


INPUT_SHAPES = {
    "od_w_in": [2, 1024, 2560], "od_w_out": [2, 1024, 1024], "od_norm_rep": [2, 128, 1024],
    "qg_rep": [2, 128, 64], "kg_rep": [2, 128, 64], "sink_rep": [2, 128, 16],
    "alibi": [4, 4, 128, 512], "ident": [128, 128], "flags": [128, 4],
    "ev_w_in": [2, 1024, 3200], "ev_norm_rep": [2, 128, 1024], "ev_w_out": [2, 1024, 1024],
    "s5_ar_row": [2, 2, 128, 2048], "s5_ai_row": [2, 2, 128, 2048], "s5_dt_row": [2, 2, 128, 2048],
    "s5_ar_col": [2, 2, 128, 16], "s5_ai_col": [2, 2, 128, 16], "s5_dt_col": [2, 2, 128, 16],
    "s5_b_col": [2, 2, 2, 128, 16, 16], "s5_c_col": [2, 2, 2, 128, 16, 16],
    "s5_d_col": [2, 128, 4], "glu_b_col": [2, 128, 4], "s5_glu_w": [2, 512, 512],
    "iota_col": [128, 2], "iota_row": [2, 128, 128], "tri": [2, 128, 128], "triE": [2, 128, 128],
    "rw_mu_rep": [2, 128, 1664], "rw_w0_rep": [2, 2, 128, 512], "rw_a0_rep": [2, 128, 512], "rw_k_k_rep": [2, 128, 512],
    "rw_k_a_rep": [2, 128, 512], "rw_r_k_rep": [2, 128, 512], "rw_ln_g_rep": [2, 128, 512], "rw_ln_b_rep": [2, 128, 512],
    "rw_w_up": [2, 2, 64, 512], "rw_a_up": [2, 64, 512],
}


def alibi_tables():
    slopes = np.exp2(-8.0 * np.arange(1, 17, dtype=np.float32) / 16).astype(np.float32)
    s = np.arange(128)[:, None]
    t = np.arange(128)[None, :]
    out = np.zeros((4, 4, 128, 4, 128), np.float32)
    for rel in range(4):
        sg = (s + (rel - 1) * 128) if rel < 3 else (255 - s)
        d = np.abs(t - sg).astype(np.float32)
        for j in range(4):
            for hl in range(4):
                out[j, rel, :, hl, :] = np.where(d <= 128, -slopes[4 * j + hl] * d, NEG)
    return out.reshape(4, 4, 128, 512)


def host_layout(inputs, layers):
    f = lambda a: np.ascontiguousarray(np.asarray(a, np.float32))
    rep = lambda a: f(np.broadcast_to(np.asarray(a)[:, None, :], (a.shape[0], 128, a.shape[1])))
    m = {}
    m["od_w_in"] = f(inputs["od_w_in"]); m["od_w_out"] = f(inputs["od_w_out"])
    m["od_norm_rep"] = rep(inputs["od_norm"]); m["qg_rep"] = rep(inputs["at_q_norm"]); m["kg_rep"] = rep(inputs["at_k_norm"])
    m["sink_rep"] = rep(inputs["at_sink"])
    m["alibi"] = alibi_tables(); m["ident"] = np.eye(128, dtype=np.float32)
    m["ev_w_in"] = f(inputs["ev_w_in"]); m["ev_w_out"] = f(inputs["ev_w_out"]); m["ev_norm_rep"] = rep(inputs["ev_norm"])
    NE = 2
    rowrep = lambda a: f(np.broadcast_to(a.reshape(NE, 2, 1, 2048), (NE, 2, 128, 2048)))
    m["s5_ar_row"] = rowrep(np.asarray(inputs["s5_a_re"])); m["s5_ai_row"] = rowrep(np.asarray(inputs["s5_a_im"]))
    m["s5_dt_row"] = rowrep(np.repeat(np.asarray(inputs["s5_log_dt"])[..., None], 64, axis=-1))
    col = lambda a: f(a.reshape(NE, 2, 16, 128).transpose(0, 1, 3, 2))
    m["s5_ar_col"] = col(np.asarray(inputs["s5_a_re"])); m["s5_ai_col"] = col(np.asarray(inputs["s5_a_im"]))
    m["s5_dt_col"] = col(np.repeat(np.asarray(inputs["s5_log_dt"])[..., None], 64, axis=-1))
    bcol = lambda a: np.asarray(a).reshape(NE, 2, 16, 2, 64, 16).transpose(0, 1, 3, 4, 2, 5).reshape(NE, 2, 128, 16, 16)
    m["s5_b_col"] = f(np.stack([bcol(inputs["s5_b_re"]), bcol(inputs["s5_b_im"])], axis=2))
    ccol = lambda a: np.asarray(a).reshape(NE, 2, 16, 2, 16, 64).transpose(0, 1, 3, 5, 2, 4).reshape(NE, 2, 128, 16, 16)
    m["s5_c_col"] = f(np.stack([ccol(inputs["s5_c_re"]), ccol(inputs["s5_c_im"])], axis=2))
    c4 = lambda a: f(np.asarray(a).reshape(NE, 4, 128).transpose(0, 2, 1))
    m["s5_d_col"] = c4(inputs["s5_d"]); m["glu_b_col"] = c4(inputs["s5_glu_b"]); m["s5_glu_w"] = f(inputs["s5_glu_w"])
    ar = np.arange(128, dtype=np.float32)
    m["iota_col"] = f(np.stack([ar + 1, 128 - ar], axis=1))
    m["iota_row"] = f(np.stack([np.broadcast_to(ar + 1, (128, 128)), np.broadcast_to(128 - ar, (128, 128))]))
    s_, t_ = np.arange(128)[:, None], np.arange(128)[None, :]
    m["tri"] = f(np.stack([(s_ <= t_), (s_ >= t_)]).astype(np.float32))
    m["triE"] = f(np.stack([(s_ < t_), (s_ > t_)]).astype(np.float32))
    for k in ("rw_mu", "rw_a0", "rw_k_k", "rw_k_a", "rw_ln_g", "rw_ln_b"):
        m[k + "_rep"] = rep(np.asarray(inputs[k]))
    m["rw_r_k_rep"] = rep(np.asarray(inputs["rw_r_k"]).reshape(NE, 512))
    w0 = np.asarray(inputs["rw_w0"])
    m["rw_w0_rep"] = f(np.broadcast_to(w0[:, :, None, :], (NE, 2, 128, 512)))
    m["rw_w_up"] = f(inputs["rw_w_up"]); m["rw_a_up"] = f(inputs["rw_a_up"])
    return m


def build_program(NT, layers, debug=False):
    nc = bass.Bass("TRN2", target_bir_lowering=False)
    NTOK = NT * 128
    gst = ExitStack()
    P = Prog(nc, gst)
    C = Ctx()
    C.NT = NT
    C.rr = {}
    C.inp = {k: nc.dram_tensor(k, shp, F32, kind="ExternalInput") for k, shp in INPUT_SHAPES.items()}
    xin = nc.dram_tensor("xin", [NTOK, D], F32, kind="ExternalInput")
    xout = nc.dram_tensor("xout", [NTOK, D], F32, kind="ExternalOutput")
    xa = P.dram("xa", [NTOK, D]); xb = P.dram("xb", [NTOK, D])
    banks = [P.ps([128, 512]) for _ in range(8)]
    C.gb = banks[0:3]; C.sbank = banks[3:6]; C.pbank = banks[6:8]
    C.gi = 0

    def gbank():
        C.gi += 1
        return C.gb[C.gi % len(C.gb)]
    C.gbank = gbank
    C.banks = banks
    C.ident = P.sb([128, 128]); C.flags = P.sb([128, 4])
    C.iota_col = P.sb([128, 2]); C.iota_row = [P.sb([128, 128]) for _ in range(2)]; C.tri = [P.sb([128, 128]) for _ in range(2)]
    C.zero_col = P.sb([128, 2]); C.ones_col = P.sb([128, 2]); C.triE = [P.sb([128, 128]) for _ in range(2)]
    P.op("pool", lambda e: e.memset(C.zero_col[:], 0.0), writes=[C.zero_col])
    P.op("pool", lambda e: e.memset(C.ones_col[:], 1.0), writes=[C.ones_col])
    for d in range(2):
        P.dma(C.triE[d][:], C.inp["triE"][d], writes=[C.triE[d]])
    P.dma(C.iota_col[:], C.inp["iota_col"][:, :], writes=[C.iota_col])
    for d in range(2):
        P.dma(C.iota_row[d][:], C.inp["iota_row"][d], writes=[C.iota_row[d]])
        P.dma(C.tri[d][:], C.inp["tri"][d], writes=[C.tri[d]])
    dbgk = "ExternalOutput" if debug else "Internal"
    C.proj = nc.dram_tensor("proj", [NTOK, 3200], F32, kind=dbgk)
    C.ys5T = nc.dram_tensor("ys5T", [512, NTOK], F32, kind="Internal")
    C.mixT = nc.dram_tensor("mixT", [1024, NTOK], BF16, kind=dbgk)
    C.hrw = C.proj
    C.hrow = P.sb([1, 1664])
    for nm, shp, dt in (("ksrc", [64, 512], BF16), ("kdst", [128, 512], BF16), ("vsrc", [128, 288], BF16), ("vdst", [256, 288], BF16),
                        ("s5src", [128, 32], F32), ("s5dst", [256, 32], F32), ("zsrc", [64, 512], F32), ("zdst", [128, 512], F32),
                        ("hdst", [2, 1664], F32)):
        setattr(C, nm, nc.dram_tensor(nm, shp, dt, kind="Internal"))
        setattr(C, nm + "_tr", T(None))
    C.yrw = nc.dram_tensor("yrw", [NTOK, 512], F32, kind="Internal")
    C.yrw_tr = [T(None) for _ in range(NT)]
    C.rwc = nc.dram_tensor("rwc", [NTOK, 2560], F32, kind="Internal")
    C.rwt = nc.dram_tensor("rwt", [NT, 64, 128], F32, kind="Internal")
    C.rwc_tr = [T(None) for _ in range(NT)]
    C.proj_tr = [T(None) for _ in range(NT)]; C.ys_tr = [T(None) for _ in range(NT)]; C.mix_tr = [T(None) for _ in range(NT)]
    P.dma(C.ident[:], C.inp["ident"][:, :], writes=[C.ident])
    P.dma(C.flags[:], C.inp["flags"][:, :], writes=[C.flags])
    bufs = [xin] + [(xa, xb)[i % 2] for i in range(len(layers) - 1)] + [xout]
    trs = [[T(None) for _ in range(NT)] for _ in range(len(layers) + 1)]
    for li, (kind, l) in enumerate(layers):
        if kind == "odd":
            odd_layer(P, C, l, bufs[li], bufs[li + 1], trs[li], trs[li + 1])
        else:
            even_layer(P, C, l, bufs[li], bufs[li + 1], trs[li], trs[li + 1])
    P.emit()
    return nc, gst


MAGIC = 12582912.0
TWO_PI = 2.0 * np.pi


def round_frac(P, eng, out, in_, tmp):
    (o_ap, o_t), (i_ap, i_t), (t_ap, t_t) = out, in_, tmp
    P.op(eng, lambda e: e.tensor_scalar(out=t_ap, in0=i_ap, scalar1=MAGIC, scalar2=MAGIC, op0=ALU.add, op1=ALU.subtract),
         reads=[i_t], writes=[t_t])
    P.op(eng, lambda e: e.tensor_tensor(out=o_ap, in0=i_ap, in1=t_ap, op=ALU.subtract), reads=[i_t, t_t], writes=[o_t])


def even_phaseA(P, C, l, x_src, xs_tr):
    NT = C.NT
    st = ExitStack(); P.stack = st
    I = C.inp
    w = P.sb([128, 8, 3200], BF16)
    stage = [P.sb([128, 3200]), P.sb([128, 3200])]
    load_weight_bf16(P, C, w, lambda c: I["ev_w_in"][l, c * 128:(c + 1) * 128, :], 8, 3200, stage)
    gn = P.sb([128, 1024])
    P.dma(gn[:], I["ev_norm_rep"][l], writes=[gn])
    S = Ctx()
    S.junk = P.sb([128, 1024]); S.ss = P.sb([128, 1]); S.ss2 = P.sb([128, 1]); S.rs = P.sb([128, 1]); S.h = P.sb([128, 1024])
    hT = P.sb([128, 8, 128], BF16)
    xring = [P.sb([128, 1024]) for _ in range(2)]
    for j in range(NT):
        xt = xring[j % 2]
        P.dma(xt[:], x_src[j * 128:(j + 1) * 128, :], reads=[xs_tr[j]], writes=[xt])
        rmsnorm_T(P, C, xt, gn, hT, S)
        pst = stage[j % 2]
        for g in range(7):
            ncol = 512 if g < 6 else 128
            bank = C.gbank()
            matmul_group(P, C, bank, ncol, hT, w, g * 512)
            copy_op(P, rr(P, C, "pev", ("act", "dve")), pst[:, g * 512:g * 512 + ncol], bank[:, 0:ncol], [bank], [pst], accum=(g > 0))
        P.dma(C.proj[j * 128:(j + 1) * 128, :], pst[:], reads=[pst], writes=[C.proj_tr[j]], q="act")
    P.barrier()
    st.close(); P.stack = P.gstack


def s5_tables(P, C, l, d, K):
    I = C.inp
    W = K.work
    a_r, a_i, dtr, t0, t1, t2 = W[0], W[1], W[2], W[3], W[4], W[5]

    def build(shape_is_row, ar_src, ai_src, dt_src, steps_fn, sign, out_re, out_im):
        P.dma(a_r[:], ar_src, writes=[a_r]); P.dma(a_i[:], ai_src, writes=[a_i]); P.dma(dtr[:], dt_src, writes=[dtr])
        P.op("act", lambda e: e.activation(out=dtr[:], in_=dtr[:], func=AF.Exp), reads=[dtr], writes=[dtr])
        P.op("dve", lambda e: e.tensor_tensor(out=a_r[:], in0=a_r[:], in1=dtr[:], op=ALU.mult), reads=[a_r, dtr], writes=[a_r])
        P.op("dve", lambda e: e.scalar_tensor_tensor(out=a_i[:], in0=a_i[:], scalar=1.0 / TWO_PI, in1=dtr[:], op0=ALU.mult, op1=ALU.mult),
             reads=[a_i, dtr], writes=[a_i])
        round_frac(P, "dve", (a_i[:], a_i), (a_i[:], a_i), (t0[:], t0))
        steps_fn(a_r, a_i)
        P.op("act", lambda e: e.activation(out=t1[:], in_=a_r[:], func=AF.Exp, scale=float(sign)), reads=[a_r], writes=[t1])
        round_frac(P, "dve", (t0[:], t0), (a_i[:], a_i), (t2[:], t2))
        P.op("act", lambda e: e.activation(out=t0[:], in_=t0[:], func=AF.Sin, scale=TWO_PI), reads=[t0], writes=[t0])
        P.op("dve", lambda e: e.tensor_scalar(out=a_i[:], in0=a_i[:], scalar1=0.25, scalar2=None, op0=ALU.add), reads=[a_i], writes=[a_i])
        round_frac(P, "dve", (a_i[:], a_i), (a_i[:], a_i), (t2[:], t2))
        P.op("act", lambda e: e.activation(out=a_i[:], in_=a_i[:], func=AF.Sin, scale=TWO_PI), reads=[a_i], writes=[a_i])
        P.op("dve", lambda e: e.tensor_tensor(out=out_re[:].rearrange("p a b -> p (a b)"), in0=t1[:], in1=a_i[:], op=ALU.mult),
             reads=[t1, a_i], writes=[out_re])
        P.op("dve", lambda e: e.scalar_tensor_tensor(out=out_im[:].rearrange("p a b -> p (a b)"), in0=t1[:], scalar=float(sign), in1=t0[:],
                                                     op0=ALU.mult, op1=ALU.mult), reads=[t1, t0], writes=[out_im])

    def steps_row(a_r, a_i):
        for t in (a_r, a_i):
            P.op("dve", lambda e, t=t: e.tensor_scalar(out=t[:], in0=t[:], scalar1=C.iota_col[:, d:d + 1], scalar2=None, op0=ALU.mult),
                 reads=[t, C.iota_col], writes=[t])
    build(True, I["s5_ar_row"][l, d], I["s5_ai_row"][l, d], I["s5_dt_row"][l, d], steps_row, -1, K.Tin_re, K.Tin_im)

    def steps_col(a_r, a_i):
        for t in (a_r, a_i):
            P.op("dve", lambda e, t=t: e.tensor_tensor(out=t[:].rearrange("p (a b) -> p a b", a=16),
                                                       in0=t[:, 0:16].unsqueeze(2).to_broadcast([128, 16, 128]),
                                                       in1=C.iota_row[d][:].unsqueeze(1).to_broadcast([128, 16, 128]), op=ALU.mult),
                 reads=[t, C.iota_row[d]], writes=[t])
    ca, ci, cd = K.col_a, K.col_i, K.col_d

    def build_col():
        P.dma(ca[:], I["s5_ar_col"][l, d], writes=[ca]); P.dma(ci[:], I["s5_ai_col"][l, d], writes=[ci]); P.dma(cd[:], I["s5_dt_col"][l, d], writes=[cd])
        P.op("act", lambda e: e.activation(out=cd[:], in_=cd[:], func=AF.Exp), reads=[cd], writes=[cd])
        P.op("dve", lambda e: e.tensor_tensor(out=K.c_ardt[:], in0=ca[:], in1=cd[:], op=ALU.mult), reads=[ca, cd], writes=[K.c_ardt])
        P.op("dve", lambda e: e.scalar_tensor_tensor(out=K.c_frac[:], in0=ci[:], scalar=1.0 / TWO_PI, in1=cd[:], op0=ALU.mult, op1=ALU.mult),
             reads=[ci, cd], writes=[K.c_frac])
        round_frac(P, "dve", (K.c_frac[:], K.c_frac), (K.c_frac[:], K.c_frac), (K.c_tmp[:], K.c_tmp))
        P.op("dve", lambda e: e.tensor_tensor(out=a_r[:].rearrange("p (a b) -> p a b", a=16),
                                              in0=K.c_ardt[:].unsqueeze(2).to_broadcast([128, 16, 128]),
                                              in1=C.iota_row[d][:].unsqueeze(1).to_broadcast([128, 16, 128]), op=ALU.mult),
             reads=[K.c_ardt, C.iota_row[d]], writes=[a_r])
        P.op("dve", lambda e: e.tensor_tensor(out=a_i[:].rearrange("p (a b) -> p a b", a=16),
                                              in0=K.c_frac[:].unsqueeze(2).to_broadcast([128, 16, 128]),
                                              in1=C.iota_row[d][:].unsqueeze(1).to_broadcast([128, 16, 128]), op=ALU.mult),
             reads=[K.c_frac, C.iota_row[d]], writes=[a_i])
        sign = 1
        P.op("act", lambda e: e.activation(out=t1[:], in_=a_r[:], func=AF.Exp, scale=float(sign)), reads=[a_r], writes=[t1])
        round_frac(P, "dve", (t0[:], t0), (a_i[:], a_i), (t2[:], t2))
        P.op("act", lambda e: e.activation(out=t0[:], in_=t0[:], func=AF.Sin, scale=TWO_PI), reads=[t0], writes=[t0])
        P.op("dve", lambda e: e.tensor_scalar(out=a_i[:], in0=a_i[:], scalar1=0.25, scalar2=None, op0=ALU.add), reads=[a_i], writes=[a_i])
        round_frac(P, "dve", (a_i[:], a_i), (a_i[:], a_i), (t2[:], t2))
        P.op("act", lambda e: e.activation(out=a_i[:], in_=a_i[:], func=AF.Sin, scale=TWO_PI), reads=[a_i], writes=[a_i])
        P.op("dve", lambda e: e.tensor_tensor(out=K.Tout_re[:].rearrange("p a b -> p (a b)"), in0=t1[:], in1=a_i[:], op=ALU.mult),
             reads=[t1, a_i], writes=[K.Tout_re])
        P.op("dve", lambda e: e.tensor_tensor(out=K.Tout_im[:].rearrange("p a b -> p (a b)"), in0=t1[:], in1=t0[:], op=ALU.mult),
             reads=[t1, t0], writes=[K.Tout_im])
    build_col()

    s1, c1, m1, nr, dn, q_r, q_i, u0, u1 = [K.small[i] for i in range(9)]
    P.op("act", lambda e: e.activation(out=m1[:], in_=K.c_ardt[:], func=AF.Exp), reads=[K.c_ardt], writes=[m1])
    P.op("act", lambda e: e.activation(out=s1[:], in_=K.c_frac[:], func=AF.Sin, scale=TWO_PI), reads=[K.c_frac], writes=[s1])
    P.op("dve", lambda e: e.tensor_scalar(out=u0[:], in0=K.c_frac[:], scalar1=0.25, scalar2=None, op0=ALU.add), reads=[K.c_frac], writes=[u0])
    round_frac(P, "dve", (u0[:], u0), (u0[:], u0), (u1[:], u1))
    P.op("act", lambda e: e.activation(out=c1[:], in_=u0[:], func=AF.Sin, scale=TWO_PI), reads=[u0], writes=[c1])
    tt = lambda o, a, b, op, eng="dve": P.op(eng, lambda e: e.tensor_tensor(out=o[:], in0=a[:], in1=b[:], op=op), reads=[a, b], writes=[o])
    tt(c1, c1, m1, ALU.mult)
    tt(s1, s1, m1, ALU.mult)
    P.op("dve", lambda e: e.tensor_scalar(out=nr[:], in0=c1[:], scalar1=-1.0, scalar2=None, op0=ALU.add), reads=[c1], writes=[nr])
    tt(dn, ca, ca, ALU.mult); tt(u0, ci, ci, ALU.mult); tt(dn, dn, u0, ALU.add)
    P.op("dve", lambda e: e.reciprocal(out=dn[:], in_=dn[:]), reads=[dn], writes=[dn])
    tt(u0, nr, ca, ALU.mult); tt(u1, s1, ci, ALU.mult); tt(u0, u0, u1, ALU.add); tt(q_r, u0, dn, ALU.mult)
    tt(u0, s1, ca, ALU.mult); tt(u1, nr, ci, ALU.mult); tt(u0, u0, u1, ALU.subtract); tt(q_i, u0, dn, ALU.mult)
    bre, bim, bbr, bbi, tb = K.bre, K.bim, K.bbr, K.bbi, K.tb
    for (dst, ri) in ((bre, 0), (bim, 1)):
        P.op("pool", lambda e, dst=dst: e.memset(dst[:], 0.0), writes=[dst])
        P.dma(dst[0:64, :, 0:16], I["s5_b_col"][l, d, ri, 0:64], writes=[dst])
        P.dma(dst[64:128, :, 16:32], I["s5_b_col"][l, d, ri, 64:128], writes=[dst])
    bc = lambda q: q[:].unsqueeze(2).to_broadcast([128, 16, 32])
    P.op("dve", lambda e: e.tensor_tensor(out=bbr[:], in0=bre[:], in1=bc(q_r), op=ALU.mult), reads=[bre, q_r], writes=[bbr])
    P.op("dve", lambda e: e.tensor_tensor(out=tb[:], in0=bim[:], in1=bc(q_i), op=ALU.mult), reads=[bim, q_i], writes=[tb])
    tt(bbr, bbr, tb, ALU.subtract)
    P.op("dve", lambda e: e.tensor_tensor(out=bbi[:], in0=bim[:], in1=bc(q_r), op=ALU.mult), reads=[bim, q_r], writes=[bbi])
    P.op("dve", lambda e: e.tensor_tensor(out=tb[:], in0=bre[:], in1=bc(q_i), op=ALU.mult), reads=[bre, q_i], writes=[tb])
    tt(bbi, bbi, tb, ALU.add)
    zp = K.zp
    for z in zp:
        P.op("pool", lambda e, z=z: e.memset(z[:], 0.0), writes=[z])
    for (src, dst) in ((bbr, K.BT_re), (bbi, K.BT_im)):
        for ch in range(4):
            bank = C.gbank()
            for pl in range(4):
                copy_op(P, "dve", zp[pl][:, 32 * pl:32 * pl + 32], src[:, 4 * ch + pl, :], [src], [zp[pl]])
                P.op("pe", lambda e, bank=bank, pl=pl: e.transpose(out=bank[:, pl * 128:(pl + 1) * 128], in_=zp[pl][:], identity=C.ident[:]),
                     reads=[zp[pl], C.ident], writes=[bank], accum=(pl > 0))
            copy_op(P, "act", dst[:, ch, :], bank[:], [bank], [dst], accum=(ch > 0))
    for (dst, ri) in ((K.Cre, 0), (K.Cimn, 1)):
        P.op("pool", lambda e, dst=dst: e.memset(dst[:], 0.0), writes=[dst])
        P.dma(dst[0:64, :, 32:48], I["s5_c_col"][l, d, ri, 0:64], writes=[dst])
        P.dma(dst[64:128, :, 48:64], I["s5_c_col"][l, d, ri, 64:128], writes=[dst])
    P.op("dve", lambda e: e.tensor_scalar(out=K.Cimn[:], in0=K.Cimn[:], scalar1=-1.0, scalar2=None, op0=ALU.mult), reads=[K.Cimn], writes=[K.Cimn])


def s5_pass(P, C, l, d):
    half = -999
    NT = C.NT
    st = ExitStack(); P.stack = st
    I = C.inp
    K = Ctx()
    K.Tin_re = P.sb([128, 4, 512]); K.Tin_im = P.sb([128, 4, 512])
    K.Tout_re = P.sb([128, 16, 128]); K.Tout_im = P.sb([128, 16, 128])
    K.BT_re = P.sb([128, 4, 512], BF16); K.BT_im = P.sb([128, 4, 512], BF16)
    K.Cre = P.sb([128, 16, 64]); K.Cimn = P.sb([128, 16, 64])
    st2 = ExitStack(); P.stack = st2
    K.work = [P.sb([128, 2048]) for _ in range(6)]
    K.col_a = P.sb([128, 16]); K.col_i = P.sb([128, 16]); K.col_d = P.sb([128, 16])
    K.c_ardt = P.sb([128, 16]); K.c_frac = P.sb([128, 16]); K.c_tmp = P.sb([128, 16])
    K.small = [P.sb([128, 16]) for _ in range(9)]
    K.bre = P.sb([128, 16, 32]); K.bim = P.sb([128, 16, 32]); K.bbr = P.sb([128, 16, 32]); K.bbi = P.sb([128, 16, 32]); K.tb = P.sb([128, 16, 32])
    K.zp = [P.sb([128, 128]) for _ in range(4)]
    s5_tables(P, C, l, d, K)
    P.barrier()
    st2.close(); P.stack = st
    tri = C.tri[d]
    zero = C.zero_col
    uring = [P.sb([128, 512]) for _ in range(2)]
    uT = [P.sb([128, 4, 128], BF16) for _ in range(2)]
    g_re = [P.sb([128, 512], BF16) for _ in range(2)]; g_im = [P.sb([128, 512], BF16) for _ in range(2)]
    ta = [P.sb([128, 512]) for _ in range(2)]; tb = [P.sb([128, 512]) for _ in range(2)]
    ta2 = [P.sb([128, 512]) for _ in range(2)]; tb2 = [P.sb([128, 512]) for _ in range(2)]
    hre = [[P.sb([128, 128]) for _ in range(16)] for _ in range(2)]
    him = [[P.sb([128, 128]) for _ in range(16)] for _ in range(2)]
    r1 = [P.sb([128, 128]) for _ in range(2)]; r2 = [P.sb([128, 128]) for _ in range(2)]
    r3 = [P.sb([128, 128]) for _ in range(2)]; r4 = [P.sb([128, 128]) for _ in range(2)]
    cc = [[P.sb([128, 2]) for _ in range(16)] for _ in range(1)][0]
    ysb = [P.sb([128, 4, 128]) for _ in range(2)]
    ysT = C.ys5T.ap().rearrange("(k p) t -> p k t", p=128)
    bk = C.banks
    if d == 1:
        dcol = P.sb([128, 4]); gbcol = P.sb([128, 4])
        P.dma(dcol[:], I["s5_d_col"][l], writes=[dcol]); P.dma(gbcol[:], I["glu_b_col"][l], writes=[gbcol])
        wg = P.sb([128, 4, 512], BF16)
        wgs = [P.sb([128, 512]), P.sb([128, 512])]
        load_weight_bf16(P, C, wg, lambda c: I["s5_glu_w"][l, c * 128:(c + 1) * 128, :], 4, 512, wgs)
        yf = [P.sb([128, 4, 128]) for _ in range(2)]
        zt = [P.sb([128, 512]) for _ in range(2)]
        szT = P.sb([128, 4, 128])
        yv = P.sb([128, 4, 128]); x2 = P.sb([128, 4, 128]); sg = P.sb([128, 4, 128]); yg = P.sb([128, 4, 128]); ygb = P.sb([128, 4, 128], BF16)
        gs = P.sb([128, 4, 128]); ya = [P.sb([128, 4, 128], BF16) for _ in range(2)]
        mixT = C.mixT.ap().rearrange("(k p) t -> p k t", p=128)

    cin = P.sb([128, 32]); cbuf = P.sb([128, 32]); cin2 = P.sb([128, 2, 32])
    if d == 1:
        P.dma(cin2[:], C.s5dst.ap().rearrange("(s p) n -> p s n", s=2), reads=[C.s5dst_tr], writes=[cin2])
        P.op("dve", lambda e: e.tensor_scalar(out=cin[:], in0=cin2[:, 0, :], scalar1=C.flags[:, 0:1], scalar2=None, op0=ALU.mult), reads=[cin2, C.flags], writes=[cin])
        P.op("dve", lambda e: e.scalar_tensor_tensor(out=cin[:], in0=cin2[:, 1, :], scalar=C.flags[:, 1:2], in1=cin[:], op0=ALU.mult, op1=ALU.add),
             reads=[cin2, C.flags, cin], writes=[cin])
    order = list(range(NT)) if d == 0 else list(range(NT - 1, -1, -1))
    last = 127 if d == 0 else 0
    trib = P.sb([128, 128], BF16)
    copy_op(P, "dve", trib[:], tri[:], [tri], [trib])

    def prologue(it):
        i = order[it]; par = it % 2
        ut = uring[par]
        P.dma(ut[:], C.proj[i * 128:(i + 1) * 128, 0:512], reads=[C.proj_tr[i]], writes=[ut])
        if d == 1:
            P.dma(yf[par][:], ysT[:, :, i * 128:(i + 1) * 128], reads=[C.ys_tr[i]], writes=[yf[par]])
            P.dma(zt[par][:], C.proj[i * 128:(i + 1) * 128, 512:1024], reads=[C.proj_tr[i]], writes=[zt[par]])
        uTt = uT[par]
        for c in range(4):
            P.op("pe", lambda e, c=c: e.transpose(out=bk[0][:, c * 128:(c + 1) * 128], in_=ut[:, c * 128:(c + 1) * 128], identity=C.ident[:]),
                 reads=[ut, C.ident], writes=[bk[0]], accum=(c > 0))
        copy_op(P, "act", uTt[:].rearrange("p a b -> p (a b)"), bk[0][:], [bk[0]], [uTt])

    def front(it, ch):
        par = it % 2; cp = ch % 2
        uTt = uT[par]
        P.op("pe", lambda e: e.matmul(bk[1][:], lhsT=uTt[:, ch, :], rhs=K.BT_re[:, ch, :], start=True, stop=True),
             reads=[uTt, K.BT_re], writes=[bk[1]])
        P.op("pe", lambda e: e.matmul(bk[2][:], lhsT=uTt[:, ch, :], rhs=K.BT_im[:, ch, :], start=True, stop=True),
             reads=[uTt, K.BT_im], writes=[bk[2]])
        Tr = K.Tin_re[:, ch, :]; Ti = K.Tin_im[:, ch, :]
        gr, gi, a_, b_, a2_, b2_ = g_re[cp], g_im[cp], ta[cp], tb[cp], ta2[cp], tb2[cp]
        P.op("dve", lambda e: e.tensor_tensor(out=a_[:], in0=bk[1][:], in1=Tr, op=ALU.mult), reads=[bk[1], K.Tin_re], writes=[a_])
        P.op("dve", lambda e: e.tensor_tensor(out=b_[:], in0=bk[2][:], in1=Ti, op=ALU.mult), reads=[bk[2], K.Tin_im], writes=[b_])
        P.op("pool", lambda e: e.tensor_tensor(out=gr[:], in0=a_[:], in1=b_[:], op=ALU.subtract), reads=[a_, b_], writes=[gr])
        P.op("dve", lambda e: e.tensor_tensor(out=a2_[:], in0=bk[1][:], in1=Ti, op=ALU.mult), reads=[bk[1], K.Tin_im], writes=[a2_])
        P.op("dve", lambda e: e.tensor_tensor(out=b2_[:], in0=bk[2][:], in1=Tr, op=ALU.mult), reads=[bk[2], K.Tin_re], writes=[b2_])
        P.op("pool", lambda e: e.tensor_tensor(out=gi[:], in0=a2_[:], in1=b2_[:], op=ALU.add), reads=[a2_, b2_], writes=[gi])

    def back(it, ch):
        par = it % 2; cp = ch % 2
        gr, gi = g_re[cp], g_im[cp]
        for pl in range(4):
            P.op("pe", lambda e, pl=pl: e.matmul(bk[3][:, pl * 128:(pl + 1) * 128], lhsT=gr[:, pl * 128:(pl + 1) * 128], rhs=trib[:],
                                                 start=True, stop=True), reads=[gr, trib], writes=[bk[3]], accum=(pl > 0))
        for pl in range(4):
            P.op("pe", lambda e, pl=pl: e.matmul(bk[4][:, pl * 128:(pl + 1) * 128], lhsT=gi[:, pl * 128:(pl + 1) * 128], rhs=trib[:],
                                                 start=True, stop=True), reads=[gi, trib], writes=[bk[4]], accum=(pl > 0))
        for pl in (0, 1, 3, 2):
            pp = 4 * ch + pl
            hr_prev, hi_prev = hre[1 - par][pp], him[1 - par][pp]
            hr, hi = hre[par][pp], him[par][pp]
            if it == 0 and d == 0:
                cr, ci_, crt = zero[:, 0:1], zero[:, 0:1], [zero]
            elif it == 0:
                cr, ci_, crt = cin[:, pp:pp + 1], cin[:, 16 + pp:17 + pp], [cin]
            else:
                cr, ci_, crt = hr_prev[:, last:last + 1], hi_prev[:, last:last + 1], [hr_prev, hi_prev]
            Gr = bk[3][:, pl * 128:(pl + 1) * 128]; Gi = bk[4][:, pl * 128:(pl + 1) * 128]
            Tor = K.Tout_re[:, pp, :]; Toi = K.Tout_im[:, pp, :]
            q1, q2, q3, q4 = r1[pl % 2], r2[pl % 2], r3[pl % 2], r4[pl % 2]
            P.op("dve", lambda e: e.scalar_tensor_tensor(out=q1[:], in0=Gr, scalar=cr, in1=Tor, op0=ALU.add, op1=ALU.mult),
                 reads=[bk[3], K.Tout_re] + crt, writes=[q1])
            P.op("dve", lambda e: e.scalar_tensor_tensor(out=q2[:], in0=Gi, scalar=ci_, in1=Toi, op0=ALU.add, op1=ALU.mult),
                 reads=[bk[4], K.Tout_im] + crt, writes=[q2])
            P.op("pool", lambda e: e.tensor_tensor(out=hr[:], in0=q1[:], in1=q2[:], op=ALU.subtract), reads=[q1, q2], writes=[hr])
            P.op("dve", lambda e: e.scalar_tensor_tensor(out=q3[:], in0=Gr, scalar=cr, in1=Toi, op0=ALU.add, op1=ALU.mult),
                 reads=[bk[3], K.Tout_im] + crt, writes=[q3])
            P.op("dve", lambda e: e.scalar_tensor_tensor(out=q4[:], in0=Gi, scalar=ci_, in1=Tor, op0=ALU.add, op1=ALU.mult),
                 reads=[bk[4], K.Tout_re] + crt, writes=[q4])
            P.op("pool", lambda e: e.tensor_tensor(out=hi[:], in0=q3[:], in1=q4[:], op=ALU.add), reads=[q3, q4], writes=[hi])
            if pl == 3:
                osl, csl, st0 = slice(64, 128), slice(0, 64), True
            elif pl == 2:
                osl, csl, st0 = slice(64, 96), slice(32, 64), False
            else:
                osl, csl, st0 = slice(32 * pl, 32 * pl + 32), slice(32, 64), True
            sgc = pl >= 2
            P.op("pe", lambda e: e.matmul(bk[5][osl, ch * 128:(ch + 1) * 128], lhsT=K.Cre[:, pp, csl], rhs=hr[:],
                                          start=st0, stop=False, skip_group_check=sgc), reads=[K.Cre, hr], writes=[bk[5]], accum=not (ch == 0 and pl == 0))
            P.op("pe", lambda e: e.matmul(bk[5][osl, ch * 128:(ch + 1) * 128], lhsT=K.Cimn[:, pp, csl], rhs=hi[:],
                                          start=False, stop=True, skip_group_check=sgc), reads=[K.Cimn, hi], writes=[bk[5]], accum=True)

    def epilogue(it):
        i = order[it]; par = it % 2
        uTt = uT[par]
        if d == 0:
            yt = ysb[par]
            copy_op(P, "act", yt[:].rearrange("p a b -> p (a b)"), bk[5][:], [bk[5]], [yt])
            P.dma(ysT[:, :, i * 128:(i + 1) * 128], yt[:], reads=[yt], writes=[C.ys_tr[i]], q="act")
        else:
            f = lambda t: t[:].rearrange("p a b -> p (a b)")
            P.op("dve", lambda e: e.tensor_tensor(out=f(yv), in0=bk[5][:], in1=f(yf[par]), op=ALU.add), reads=[bk[5], yf[par]], writes=[yv])
            for ch in range(4):
                P.op("dve", lambda e, ch=ch: e.scalar_tensor_tensor(out=yv[:, ch, :], in0=uTt[:, ch, :], scalar=dcol[:, ch:ch + 1], in1=yv[:, ch, :],
                                                                    op0=ALU.mult, op1=ALU.add), reads=[uTt, dcol, yv], writes=[yv], accum=True)
            P.op("act", lambda e: e.activation(out=f(yg), in_=f(yv), func=AF.Gelu_apprx_tanh), reads=[yv], writes=[yg])
            copy_op(P, "pool", f(ygb), f(yg), [yg], [ygb])
            for co in range(4):
                for kc in range(4):
                    P.op("pe", lambda e, co=co, kc=kc: e.matmul(bk[6][:, co * 128:(co + 1) * 128], lhsT=wg[:, kc, co * 128:(co + 1) * 128],
                                                                rhs=ygb[:, kc, :], start=(kc == 0), stop=(kc == 3)),
                         reads=[wg, ygb], writes=[bk[6]], accum=not (co == 0 and kc == 0))
            for co in range(4):
                P.op("act", lambda e, co=co: e.activation(out=sg[:, co, :], in_=bk[6][:, co * 128:(co + 1) * 128], func=AF.Sigmoid,
                                                          bias=gbcol[:, co:co + 1]), reads=[bk[6], gbcol], writes=[sg], accum=(co > 0))
            for c in range(4):
                P.op("pe", lambda e, c=c: e.transpose(out=bk[7][:, c * 128:(c + 1) * 128], in_=zt[par][:, c * 128:(c + 1) * 128], identity=C.ident[:]),
                     reads=[zt[par], C.ident], writes=[bk[7]], accum=(c > 0))
            P.op("act", lambda e: e.activation(out=f(szT), in_=bk[7][:], func=AF.Silu), reads=[bk[7]], writes=[szT])
            P.op("dve", lambda e: e.tensor_tensor(out=f(gs), in0=f(yg), in1=f(sg), op=ALU.mult), reads=[yg, sg], writes=[gs])
            P.op("pool", lambda e: e.tensor_tensor(out=f(ya[par]), in0=f(gs), in1=f(szT), op=ALU.mult), reads=[gs, szT], writes=[ya[par]])
            P.dma(mixT[:, 0:4, i * 128:(i + 1) * 128], ya[par][:], reads=[ya[par]], writes=[C.mix_tr[i]], q="pool")

    units = [(it, ch) for it in range(NT) for ch in range(4)]
    prologue(0); front(0, 0)
    for u, (it, ch) in enumerate(units):
        if u + 1 < len(units):
            it2, ch2 = units[u + 1]
            if ch2 == 0:
                prologue(it2)
            front(it2, ch2)
        back(it, ch)
        if ch == 3:
            epilogue(it)
    if d == 0:
        parl = (NT - 1) % 2
        for pp in range(16):
            copy_op(P, ("dve", "pool")[pp % 2], cbuf[:, pp:pp + 1], hre[parl][pp][:, 127:128], [hre[parl][pp]], [cbuf], accum=(pp > 0))
            copy_op(P, ("pool", "dve")[pp % 2], cbuf[:, 16 + pp:17 + pp], him[parl][pp][:, 127:128], [him[parl][pp]], [cbuf], accum=True)
        P.dma(C.s5src[:, :], cbuf[:], reads=[cbuf], writes=[C.s5src_tr])
        P.collective(C.s5src[:, :], C.s5dst[:, :], reads=[C.s5src_tr], writes=[C.s5dst_tr])
    P.barrier()
    st.close(); P.stack = P.gstack


def even_layer(P, C, l, x_src, x_dst, xs_tr, xd_tr):
    import os
    dbg = int(os.environ.get("EDBG", "9"))
    even_phaseA(P, C, l, x_src, xs_tr)
    if dbg >= 1:
        s5_pass(P, C, l, 0)
        s5_pass(P, C, l, 1)
    if dbg >= 2:
        rwkv_pass(P, C, l, 0, x_src, x_dst, xs_tr, xd_tr)
        rwkv_pass(P, C, l, 1, x_src, x_dst, xs_tr, xd_tr)


def rwkv_pass(P, C, l, d, x_src, x_dst, xs_tr, xd_tr):
    NT = C.NT
    st = ExitStack(); P.stack = st
    I = C.inp
    bk = C.banks
    f32 = lambda shape: P.sb(shape)
    b16 = lambda shape: P.sb(shape, BF16)
    tt = lambda eng, o, a, b, op, rd, wr, accum=False: P.op(eng, lambda e: e.tensor_tensor(out=o, in0=a, in1=b, op=op), reads=rd, writes=wr, accum=accum)

    mu = f32([128, 1664]); P.dma(mu[:], I["rw_mu_rep"][l], writes=[mu])
    w0 = f32([128, 512]); P.dma(w0[:], I["rw_w0_rep"][l, d], writes=[w0])
    a0 = f32([128, 512]); P.dma(a0[:], I["rw_a0_rep"][l], writes=[a0])
    kkp = f32([128, 512]); P.dma(kkp[:], I["rw_k_k_rep"][l], writes=[kkp])
    kap = f32([128, 512]); P.dma(kap[:], I["rw_k_a_rep"][l], writes=[kap])
    ups = f32([128, 2, 512])
    P.dma(ups[0:64, 0, :], I["rw_w_up"][l, d], writes=[ups])
    P.dma(ups[64:128, 1, :], I["rw_a_up"][l], writes=[ups])
    triI = C.tri[d]; triE = C.triE[d]; triET = C.triE[1 - d]
    eye_b = C.ident
    if d == 1:
        rkp = f32([128, 512]); P.dma(rkp[:], I["rw_r_k_rep"][l], writes=[rkp])
        lng = f32([128, 512]); P.dma(lng[:], I["rw_ln_g_rep"][l], writes=[lng])
        lnb = f32([128, 512]); P.dma(lnb[:], I["rw_ln_b_rep"][l], writes=[lnb])
        wo = b16([128, 8, 1024])
        st2 = ExitStack(); P.stack = st2
        wst = [f32([128, 1024]), f32([128, 1024])]
        load_weight_bf16(P, C, wo, lambda c: I["ev_w_out"][l, c * 128:(c + 1) * 128, :], 8, 1024, wst)
        P.barrier()
        st2.close(); P.stack = st

    cur = [f32([128, 1664])] * 2; prv = [f32([128, 1664])] * 2; nxt = [f32([128, 1664])] * 2
    hs = f32([128, 1664]); tsum = f32([128, 1664])
    twla = f32([128, 128]); twlaT = f32([128, 128])
    a_t = f32([128, 512]); e2 = f32([128, 512]); tmp = f32([128, 512]); tmp2 = f32([128, 512])
    kkn = f32([128, 512]); pss = f32([128, 8]); prn = f32([128, 8])
    p_t = f32([128, 512]); q_t = f32([128, 512]); kp = f32([128, 512])
    GI = f32([128, 512]); GIinv = f32([128, 512]); GE = f32([128, 512])
    Pd = f32([128, 512]); Qd = f32([128, 512]); Kd = f32([128, 512]); Rd = f32([128, 512])
    Pdb = b16([128, 512]); Qdb = b16([128, 512]); Kdb = b16([128, 512]); Vb = b16([128, 512])
    PR = b16([64, 8, 2, 128]); QTt = b16([64, 8, 128]); KTt = b16([64, 8, 128])
    Bm = [b16([128, 8, 128]) for _ in range(2)]; Am = [b16([128, 8, 128]) for _ in range(2)]; Pm = b16([128, 8, 128])
    MqT = b16([128, 8, 128]); LkT = b16([128, 8, 128]); MkT = b16([128, 8, 128])
    LkV = f32([128, 512]); KV = f32([64, 8, 64])
    Z = f32([64, 8, 64]); Zb = b16([64, 8, 64]); ZK = f32([64, 8, 64]); ZKg = f32([64, 8, 64]); Ztmp = f32([64, 8, 64])
    gcol = f32([64, 8]); onescol = C.ones_col
    rhs_sb = b16([128, 512]); U_sb = b16([128, 512])
    ysb = [f32([128, 512]) for _ in range(2)]
    if d == 0:
        P.op("pool", lambda e: e.memset(Z[:], 0.0), writes=[Z])
        P.op("pool", lambda e: e.memset(Zb[:], 0.0), writes=[Zb])
        hb = P.sb([1, 2, 1664])
        P.collective(C.proj[NT * 128 - 1:NT * 128, 1024:2688], C.hdst[:, :], reads=[C.proj_tr[NT - 1]], writes=[C.hdst_tr])
        P.dma(hb[:], C.hdst.ap().rearrange("(o s) n -> o s n", o=1), reads=[C.hdst_tr], writes=[hb])
        P.op("dve", lambda e: e.tensor_scalar(out=C.hrow[:], in0=hb[:, 0, :], scalar1=C.flags[0:1, 0:1], scalar2=None, op0=ALU.mult), reads=[hb, C.flags], writes=[C.hrow])
        P.op("dve", lambda e: e.scalar_tensor_tensor(out=C.hrow[:], in0=hb[:, 1, :], scalar=C.flags[0:1, 1:2], in1=C.hrow[:], op0=ALU.mult, op1=ALU.add),
             reads=[hb, C.flags, C.hrow], writes=[C.hrow])
    else:
        z2 = P.sb([64, 2, 512])
        P.dma(z2[:], C.zdst.ap().rearrange("(s p) n -> p s n", s=2), reads=[C.zdst_tr], writes=[z2])
        zf_ = Z[:].rearrange("p a b -> p (a b)")
        P.op("dve", lambda e: e.tensor_scalar(out=zf_, in0=z2[:, 0, :], scalar1=C.flags[0:64, 0:1], scalar2=None, op0=ALU.mult), reads=[z2, C.flags], writes=[Z])
        P.op("dve", lambda e: e.scalar_tensor_tensor(out=zf_, in0=z2[:, 1, :], scalar=C.flags[0:64, 1:2], in1=zf_, op0=ALU.mult, op1=ALU.add),
             reads=[z2, C.flags, Z], writes=[Z])
        copy_op(P, "dve", Zb[:], Z[:], [Z], [Zb])
    if d == 1:
        yfw = [f32([128, 512]) for _ in range(2)]
        zrw = [f32([128, 512]) for _ in range(2)]
        xres = [f32([128, 1024]) for _ in range(2)]
        mean = f32([128, 8]); var = f32([128, 8]); cent = f32([128, 512]); rkk = f32([128, 512]); bon = f32([128, 8])
        yb = f32([128, 512]); szr = f32([128, 512])
        mixA = [b16([128, 4, 128]) for _ in range(2)]; ybT = b16([128, 4, 128])
        xo = [f32([128, 1024]) for _ in range(2)]
        mixT = C.mixT.ap().rearrange("(k p) t -> p k t", p=128)

    v3 = lambda t, n=8: t[:].rearrange("p (h d) -> p h d", h=n)
    order = list(range(NT)) if d == 0 else list(range(NT - 1, -1, -1))
    last = 127 if d == 0 else 0
    HW = C.hrw
    for it, i in enumerate(order):
        par = it % 2
        r0 = i * 128
        c_, p_, n_ = cur[par], prv[par], nxt[par]
        r_ap, k_ap, v_ap = hs[:, 0:512], hs[:, 512:1024], hs[:, 1024:1536]
        if d == 0:
            P.dma(c_[:], HW[r0:r0 + 128, 1024:2688], reads=[C.proj_tr[i]], writes=[c_])
            if i == 0:
                P.op("pool", lambda e: e.memset(p_[:], 0.0), writes=[p_])
                P.dma(p_[1:128, :], HW[r0:r0 + 127, 1024:2688], reads=[C.proj_tr[i]], writes=[p_])
            else:
                P.dma(p_[:], HW[r0 - 1:r0 + 127, 1024:2688], reads=[C.proj_tr[i], C.proj_tr[i - 1]], writes=[p_])
            if i == NT - 1:
                P.dma(n_[0:127, :], HW[r0 + 1:r0 + 128, 1024:2688], reads=[C.proj_tr[i]], writes=[n_])
                P.dma(n_[127:128, :], C.hrow[:], reads=[C.hrow], writes=[n_], accum=True)
            else:
                P.dma(n_[:], HW[r0 + 1:r0 + 129, 1024:2688], reads=[C.proj_tr[i], C.proj_tr[i + 1]], writes=[n_])
        if d == 1:
            P.dma(yfw[par][:], C.yrw[r0:r0 + 128, :], reads=[C.yrw_tr[i]], writes=[yfw[par]])
            P.dma(zrw[par][:], HW[r0:r0 + 128, 2688:3200], reads=[C.proj_tr[i]], writes=[zrw[par]])
            P.dma(xres[par][:], x_src[r0:r0 + 128, :], reads=[xs_tr[i]], writes=[xres[par]])
            P.dma(mixA[par][:], mixT[:, 0:4, r0:r0 + 128], reads=[C.mix_tr[i]], writes=[mixA[par]])
        if d == 0:
            tt("pool", tsum[:], p_[:], n_[:], ALU.add, [p_, n_], [tsum])
            P.op("dve", lambda e: e.scalar_tensor_tensor(out=tsum[:], in0=tsum[:], scalar=0.5, in1=c_[:], op0=ALU.mult, op1=ALU.subtract),
                 reads=[tsum, c_], writes=[tsum])
            tt("pool", tsum[:], tsum[:], mu[:], ALU.mult, [tsum, mu], [tsum])
            tt("dve", hs[:], tsum[:], c_[:], ALU.add, [tsum, c_], [hs])
            P.op("act", lambda e: e.activation(out=twla[:, 0:64], in_=hs[:, 1536:1600], func=AF.Tanh), reads=[hs], writes=[twla])
            copy_op(P, "dve", twla[:, 64:128], hs[:, 1600:1664], [hs], [twla], accum=True)
            g0 = C.gbank()
            P.op("pe", lambda e: e.transpose(out=g0[:, 0:128], in_=twla[:], identity=C.ident[:]), reads=[twla, C.ident], writes=[g0])
            copy_op(P, "dve", twlaT[:], g0[:, 0:128], [g0], [twlaT])
            g1 = C.gbank()
            P.op("pe", lambda e: e.matmul(g1[:], lhsT=twlaT[64:128, :], rhs=ups[64:128, 1, :], start=True, stop=True), reads=[twlaT, ups], writes=[g1])
            tt("dve", a_t[:], g1[:], a0[:], ALU.add, [g1, a0], [a_t])
            P.op("act", lambda e: e.activation(out=a_t[:], in_=a_t[:], func=AF.Sigmoid), reads=[a_t], writes=[a_t])
            tt("pool", kkn[:], k_ap, kkp[:], ALU.mult, [hs, kkp], [kkn])
            tt("pool", tmp[:], kkn[:], kkn[:], ALU.mult, [kkn], [tmp])
            P.op("dve", lambda e: e.tensor_reduce(out=pss[:], in_=v3(tmp), axis=AX.X, op=ALU.add), reads=[tmp], writes=[pss])
            P.op("act", lambda e: e.activation(out=pss[:], in_=pss[:], func=AF.Sqrt), reads=[pss], writes=[pss])
            P.op("dve", lambda e: e.tensor_scalar(out=pss[:], in0=pss[:], scalar1=1e-12, scalar2=None, op0=ALU.max), reads=[pss], writes=[pss])
            P.op("dve", lambda e: e.reciprocal(out=prn[:], in_=pss[:]), reads=[pss], writes=[prn])
            tt("dve", v3(p_t), v3(kkn), prn[:].unsqueeze(2).to_broadcast([128, 8, 64]), ALU.mult, [kkn, prn], [p_t])
            tt("pool", q_t[:], p_t[:], a_t[:], ALU.mult, [p_t, a_t], [q_t])
            P.op("dve", lambda e: e.scalar_tensor_tensor(out=tmp[:], in0=a_t[:], scalar=-1.0, in1=kap[:], op0=ALU.add, op1=ALU.mult), reads=[a_t, kap], writes=[tmp])
            P.op("dve", lambda e: e.scalar_tensor_tensor(out=kp[:], in0=tmp[:], scalar=1.0, in1=k_ap, op0=ALU.add, op1=ALU.mult), reads=[tmp, hs], writes=[kp])
            P.dma(C.rwc[r0:r0 + 128, 0:512], hs[:, 0:512], reads=[hs], writes=[C.rwc_tr[i]])
            P.dma(C.rwc[r0:r0 + 128, 512:1024], hs[:, 1024:1536], reads=[hs], writes=[C.rwc_tr[i]], accum=True)
            P.dma(C.rwc[r0:r0 + 128, 1024:1536], p_t[:], reads=[p_t], writes=[C.rwc_tr[i]], accum=True)
            P.dma(C.rwc[r0:r0 + 128, 1536:2048], q_t[:], reads=[q_t], writes=[C.rwc_tr[i]], accum=True)
            P.dma(C.rwc[r0:r0 + 128, 2048:2560], kp[:], reads=[kp], writes=[C.rwc_tr[i]], accum=True)
            P.dma(C.rwt[i], twlaT[0:64, :], reads=[twlaT], writes=[C.rwc_tr[i]], accum=True)
        else:
            P.dma(hs[:, 0:512], C.rwc[r0:r0 + 128, 0:512], reads=[C.rwc_tr[i]], writes=[hs])
            P.dma(hs[:, 1024:1536], C.rwc[r0:r0 + 128, 512:1024], reads=[C.rwc_tr[i]], writes=[hs], accum=True)
            P.dma(p_t[:], C.rwc[r0:r0 + 128, 1024:1536], reads=[C.rwc_tr[i]], writes=[p_t])
            P.dma(q_t[:], C.rwc[r0:r0 + 128, 1536:2048], reads=[C.rwc_tr[i]], writes=[q_t])
            P.dma(kp[:], C.rwc[r0:r0 + 128, 2048:2560], reads=[C.rwc_tr[i]], writes=[kp])
            P.dma(twlaT[0:64, :], C.rwt[i], reads=[C.rwc_tr[i]], writes=[twlaT])
        g2 = C.gbank()
        P.op("pe", lambda e: e.matmul(g2[:], lhsT=twlaT[0:64, :], rhs=ups[0:64, 0, :], start=True, stop=True), reads=[twlaT, ups], writes=[g2])
        tt("dve", e2[:], g2[:], w0[:], ALU.add, [g2, w0], [e2])
        P.op("act", lambda e: e.activation(out=e2[:], in_=e2[:], func=AF.Exp, scale=-1.0), reads=[e2], writes=[e2])
        P.op("act", lambda e: e.activation(out=e2[:], in_=e2[:], func=AF.Ln, bias=1.0), reads=[e2], writes=[e2])
        P.op("act", lambda e: e.activation(out=e2[:], in_=e2[:], func=AF.Exp, scale=-1.0, bias=-0.5), reads=[e2], writes=[e2])
        copy_op(P, "pool", Vb[:], v_ap, [hs], [Vb])
        gI = C.gbank()
        P.op("pe", lambda e: e.matmul(gI[:], lhsT=triI[:], rhs=e2[:], start=True, stop=True), reads=[triI, e2], writes=[gI])
        P.op("act", lambda e: e.activation(out=GI[:], in_=gI[:], func=AF.Exp, scale=-1.0), reads=[gI], writes=[GI])
        P.op("act", lambda e: e.activation(out=GIinv[:], in_=gI[:], func=AF.Exp), reads=[gI], writes=[GIinv])
        gE = C.gbank()
        P.op("pe", lambda e: e.matmul(gE[:], lhsT=triE[:], rhs=e2[:], start=True, stop=True), reads=[triE, e2], writes=[gE])
        P.op("act", lambda e: e.activation(out=GE[:], in_=gE[:], func=AF.Exp, scale=-1.0), reads=[gE], writes=[GE])
        gT = C.gbank()
        for h in range(8):
            P.op("pe", lambda e, h=h: e.matmul(gT[0:64, h:h + 1], lhsT=e2[:, h * 64:(h + 1) * 64], rhs=onescol[:, 0:1], start=True, stop=True),
                 reads=[e2, onescol], writes=[gT], accum=(h > 0))
        P.op("act", lambda e: e.activation(out=gcol[:], in_=gT[0:64, 0:8], func=AF.Exp, scale=-1.0), reads=[gT], writes=[gcol])
        tt("dve", Pd[:], p_t[:], GE[:], ALU.mult, [p_t, GE], [Pd])
        tt("pool", Qd[:], q_t[:], GIinv[:], ALU.mult, [q_t, GIinv], [Qd])
        tt("dve", Kd[:], kp[:], GIinv[:], ALU.mult, [kp, GIinv], [Kd])
        tt("pool", Rd[:], r_ap, GI[:], ALU.mult, [hs, GI], [Rd])
        copy_op(P, "pool", Pdb[:], Pd[:], [Pd], [Pdb]); copy_op(P, "dve", Qdb[:], Qd[:], [Qd], [Qdb]); copy_op(P, "pool", Kdb[:], Kd[:], [Kd], [Kdb])
        for (src, dstfn, dstt) in ((Pd, lambda h: PR[:, h, 0, :], PR), (Rd, lambda h: PR[:, h, 1, :], PR), (Qd, lambda h: QTt[:, h, :], QTt), (Kd, lambda h: KTt[:, h, :], KTt)):
            for hb in range(2):
                g = C.gbank()
                for hl in range(4):
                    h = 4 * hb + hl
                    P.op("pe", lambda e, hl=hl, h=h, g=g, src=src: e.transpose(out=g[0:64, hl * 128:(hl + 1) * 128], in_=src[:, h * 64:(h + 1) * 64], identity=C.ident[:]),
                         reads=[src, C.ident], writes=[g], accum=(hl > 0))
                if dstt is PR:
                    which = 0 if src is Pd else 1
                    copy_op(P, ("dve", "act")[hb], PR[:, 4 * hb:4 * hb + 4, which, :], g[0:64, :].rearrange("p (a b) -> p a b", a=4), [g], [PR], accum=True)
                else:
                    copy_op(P, ("act", "dve")[hb], dstt[:, 4 * hb:4 * hb + 4, :], g[0:64, :].rearrange("p (a b) -> p a b", a=4), [g], [dstt], accum=(hb > 0))
        Bc, Ac = Bm[0], Am[0]
        for hg in range(4):
            gq = C.gbank(); gk = C.gbank()
            for hh in range(2):
                h = 2 * hg + hh
                P.op("pe", lambda e: e.matmul(gq[:, hh * 256:(hh + 1) * 256], lhsT=QTt[:, h, :],
                                              rhs=PR[:, h, :, :].rearrange("p a b -> p (a b)"), start=True, stop=True),
                     reads=[QTt, PR], writes=[gq], accum=(hh > 0))
                P.op("pe", lambda e: e.matmul(gk[:, hh * 256:(hh + 1) * 256], lhsT=KTt[:, h, :],
                                              rhs=PR[:, h, :, :].rearrange("p a b -> p (a b)"), start=True, stop=True),
                     reads=[KTt, PR], writes=[gk], accum=(hh > 0))
            gq4 = gq[:].rearrange("p (h a b) -> p h a b", h=2, a=2)
            gk4 = gk[:].rearrange("p (h a b) -> p h a b", h=2, a=2)
            hs2 = slice(2 * hg, 2 * hg + 2)
            mE = triE[:].unsqueeze(1).to_broadcast([128, 2, 128]); mI = triI[:].unsqueeze(1).to_broadcast([128, 2, 128])
            tt("dve", Bc[:, hs2, :], gq4[:, :, 0, :], mE, ALU.mult, [gq, triE], [Bc], accum=(hg > 0))
            tt("dve", MqT[:, hs2, :], gq4[:, :, 1, :], mI, ALU.mult, [gq, triI], [MqT], accum=(hg > 0))
            tt("dve", LkT[:, hs2, :], gk4[:, :, 0, :], mE, ALU.mult, [gk, triE], [LkT], accum=(hg > 0))
            tt("dve", MkT[:, hs2, :], gk4[:, :, 1, :], mI, ALU.mult, [gk, triI], [MkT], accum=(hg > 0))
        for hb in range(2):
            g = C.gbank()
            for hl in range(4):
                h = 4 * hb + hl
                P.op("pe", lambda e: e.matmul(g[:, hl * 128:(hl + 1) * 128], lhsT=PR[:, h, 0, :], rhs=QTt[:, h, :],
                                              start=True, stop=True), reads=[PR, QTt], writes=[g], accum=(hl > 0))
            tt("dve", Ac[:, 4 * hb:4 * hb + 4, :], g[:].rearrange("p (h b) -> p h b", h=4), triET[:].unsqueeze(1).to_broadcast([128, 4, 128]), ALU.mult,
               [g, triET], [Ac], accum=(hb > 0))
        P.op("dve", lambda e: e.scalar_tensor_tensor(out=Pm[:], in0=Bc[:], scalar=-1.0, in1=eye_b[:].unsqueeze(1).to_broadcast([128, 8, 128]),
                                                     op0=ALU.mult, op1=ALU.add), reads=[Bc, eye_b], writes=[Pm])
        for lev in range(6):
            Bn, An = Bm[(lev + 1) % 2], Am[(lev + 1) % 2]
            for hb in range(2):
                gA = C.gbank()
                for hl in range(4):
                    h = 4 * hb + hl
                    P.op("pe", lambda e: e.matmul(gA[:, hl * 128:(hl + 1) * 128], lhsT=Bc[:, h, :], rhs=Ac[:, h, :], start=True, stop=True),
                         reads=[Bc, Ac], writes=[gA], accum=(hl > 0))
                copy_op(P, ("act", "dve")[hb], An[:, 4 * hb:4 * hb + 4, :].rearrange("p a b -> p (a b)"), gA[:], [gA], [An], accum=(hb > 0))
                if lev < 5:
                    gB = C.gbank()
                    for hl in range(4):
                        h = 4 * hb + hl
                        P.op("pe", lambda e: e.matmul(gB[:, hl * 128:(hl + 1) * 128], lhsT=Ac[:, h, :], rhs=Bc[:, h, :], start=True, stop=True),
                             reads=[Bc, Ac], writes=[gB], accum=(hl > 0))
                    copy_op(P, ("dve", "act")[hb], Bn[:, 4 * hb:4 * hb + 4, :].rearrange("p a b -> p (a b)"), gB[:], [gB], [Bn], accum=(hb > 0))
            for hb in range(2):
                gP = C.gbank()
                for hl in range(4):
                    h = 4 * hb + hl
                    P.op("pe", lambda e: e.matmul(gP[:, hl * 128:(hl + 1) * 128], lhsT=An[:, h, :], rhs=Pm[:, h, :], start=True, stop=True),
                         reads=[An, Pm], writes=[gP], accum=(hl > 0))
                tt("dve", Pm[:, 4 * hb:4 * hb + 4, :].rearrange("p a b -> p (a b)"), gP[:], Pm[:, 4 * hb:4 * hb + 4, :].rearrange("p a b -> p (a b)"),
                   ALU.add, [gP, Pm], [Pm], accum=True)
            Bc, Ac = Bn, An
        g = C.gbank()
        for h in range(8):
            P.op("pe", lambda e: e.matmul(g[:, h * 64:(h + 1) * 64], lhsT=LkT[:, h, :], rhs=Vb[:, h * 64:(h + 1) * 64], start=True, stop=True),
                 reads=[LkT, Vb], writes=[g], accum=(h > 0))
        copy_op(P, "act", LkV[:], g[:], [g], [LkV])
        g = C.gbank()
        for h in range(8):
            P.op("pe", lambda e: e.matmul(g[0:64, h * 64:(h + 1) * 64], lhsT=Kdb[:, h * 64:(h + 1) * 64], rhs=Vb[:, h * 64:(h + 1) * 64],
                                          start=True, stop=True), reads=[Kdb, Vb], writes=[g], accum=(h > 0))
        copy_op(P, "dve", KV[:].rearrange("p a b -> p (a b)"), g[0:64, :], [g], [KV])
        gcb = gcol[:].unsqueeze(2).to_broadcast([64, 8, 64])
        tt("pool", ZK[:], Z[:], KV[:], ALU.add, [Z, KV], [ZK])
        tt("pool", ZKg[:], ZK[:], gcb, ALU.mult, [ZK, gcol], [ZKg])
        gz = bk[3]
        for h in range(8):
            P.op("pe", lambda e: e.matmul(gz[:, h * 64:(h + 1) * 64], lhsT=PR[:, h, 0, :], rhs=Zb[:, h, :], start=True, stop=True),
                 reads=[PR, Zb], writes=[gz], accum=(h > 0))
        tt("dve", rhs_sb[:], gz[:], LkV[:], ALU.add, [gz, LkV], [rhs_sb])
        gu = bk[4]
        for h in range(8):
            P.op("pe", lambda e: e.matmul(gu[:, h * 64:(h + 1) * 64], lhsT=Pm[:, h, :], rhs=rhs_sb[:, h * 64:(h + 1) * 64], start=True, stop=True),
                 reads=[Pm, rhs_sb], writes=[gu], accum=(h > 0))
        P.op("act", lambda e: e.activation(out=U_sb[:], in_=gu[:], func=AF.Copy, scale=-1.0), reads=[gu], writes=[U_sb])
        gy = bk[5]
        for h in range(8):
            osl = gy[:, h * 64:(h + 1) * 64]
            P.op("pe", lambda e: e.matmul(osl, lhsT=PR[:, h, 1, :], rhs=Zb[:, h, :], start=True, stop=False),
                 reads=[PR, Zb], writes=[gy], accum=(h > 0))
            P.op("pe", lambda e: e.matmul(osl, lhsT=MqT[:, h, :], rhs=U_sb[:, h * 64:(h + 1) * 64], start=False, stop=False),
                 reads=[MqT, U_sb], writes=[gy], accum=True)
            P.op("pe", lambda e: e.matmul(osl, lhsT=MkT[:, h, :], rhs=Vb[:, h * 64:(h + 1) * 64], start=False, stop=True),
                 reads=[MkT, Vb], writes=[gy], accum=True)
        gq_ = bk[6]
        for h in range(8):
            P.op("pe", lambda e: e.matmul(gq_[0:64, h * 64:(h + 1) * 64], lhsT=Qdb[:, h * 64:(h + 1) * 64], rhs=U_sb[:, h * 64:(h + 1) * 64],
                                          start=True, stop=True), reads=[Qdb, U_sb], writes=[gq_], accum=(h > 0))
        tt("dve", Ztmp[:], gq_[0:64, :].rearrange("p (a b) -> p a b", a=8), gcb, ALU.mult, [gq_, gcol], [Ztmp])
        tt("dve", Z[:], Ztmp[:], ZKg[:], ALU.add, [Ztmp, ZKg], [Z])
        copy_op(P, "dve", Zb[:], Z[:], [Z], [Zb])
        if d == 0:
            yt = ysb[par]
            copy_op(P, "act", yt[:], gy[:], [gy], [yt])
            P.dma(C.yrw[r0:r0 + 128, :], yt[:], reads=[yt], writes=[C.yrw_tr[i]], q="act")
            if it == NT - 1:
                P.dma(C.zsrc[:, :], Z[:].rearrange("p a b -> p (a b)"), reads=[Z], writes=[C.zsrc_tr])
                P.collective(C.zsrc[:, :], C.zdst[:, :], reads=[C.zsrc_tr], writes=[C.zdst_tr])
        else:
            y = ysb[par]
            tt("dve", y[:], gy[:], yfw[par][:], ALU.add, [gy, yfw[par]], [y])
            P.op("dve", lambda e: e.tensor_reduce(out=mean[:], in_=v3(y), axis=AX.X, op=ALU.add), reads=[y], writes=[mean])
            P.op("dve", lambda e: e.tensor_scalar(out=mean[:], in0=mean[:], scalar1=1.0 / 64, scalar2=None, op0=ALU.mult), reads=[mean], writes=[mean])
            tt("dve", v3(cent), v3(y), mean[:].unsqueeze(2).to_broadcast([128, 8, 64]), ALU.subtract, [y, mean], [cent])
            tt("pool", tmp2[:], cent[:], cent[:], ALU.mult, [cent], [tmp2])
            P.op("dve", lambda e: e.tensor_reduce(out=var[:], in_=v3(tmp2), axis=AX.X, op=ALU.add), reads=[tmp2], writes=[var])
            P.op("act", lambda e: e.activation(out=var[:], in_=var[:], func=AF.Sqrt, scale=1.0 / 64, bias=64e-5), reads=[var], writes=[var])
            P.op("dve", lambda e: e.reciprocal(out=var[:], in_=var[:]), reads=[var], writes=[var])
            tt("dve", v3(cent), v3(cent), var[:].unsqueeze(2).to_broadcast([128, 8, 64]), ALU.mult, [cent, var], [cent])
            tt("pool", cent[:], cent[:], lng[:], ALU.mult, [cent, lng], [cent])
            tt("pool", cent[:], cent[:], lnb[:], ALU.add, [cent, lnb], [cent])
            tt("pool", rkk[:], r_ap, kp[:], ALU.mult, [hs, kp], [rkk])
            tt("pool", rkk[:], rkk[:], rkp[:], ALU.mult, [rkk, rkp], [rkk])
            P.op("dve", lambda e: e.tensor_reduce(out=bon[:], in_=v3(rkk), axis=AX.X, op=ALU.add), reads=[rkk], writes=[bon])
            tt("dve", v3(rkk), hs[:, 1024:1536].rearrange("p (h d) -> p h d", h=8), bon[:].unsqueeze(2).to_broadcast([128, 8, 64]), ALU.mult, [hs, bon], [rkk])
            tt("pool", cent[:], cent[:], rkk[:], ALU.add, [cent, rkk], [cent])
            P.op("act", lambda e: e.activation(out=szr[:], in_=zrw[par][:], func=AF.Silu), reads=[zrw[par]], writes=[szr])
            tt("dve", yb[:], cent[:], szr[:], ALU.mult, [cent, szr], [yb])
            transpose_to(P, C, yb, 4, ybT)
            xot = xo[par]
            for gcol_i in range(2):
                bank = C.gbank()
                for c in range(8):
                    lhs = mixA[par][:, c, :] if c < 4 else ybT[:, c - 4, :]
                    P.op("pe", lambda e, c=c, lhs=lhs, bank=bank: e.matmul(bank[:], lhsT=lhs, rhs=wo[:, c, gcol_i * 512:(gcol_i + 1) * 512],
                                                                          start=(c == 0), stop=(c == 7)),
                         reads=[mixA[par], ybT, wo], writes=[bank], accum=(c > 0))
                tt("dve", xot[:, gcol_i * 512:(gcol_i + 1) * 512], bank[:], xres[par][:, gcol_i * 512:(gcol_i + 1) * 512], ALU.add,
                   [bank, xres[par]], [xot], accum=(gcol_i > 0))
            P.dma(x_dst[r0:r0 + 128, :], xot[:], reads=[xot], writes=[xd_tr[i]], q="act")
    P.barrier()
    st.close(); P.stack = P.gstack


NT_FULL = 64
LAYERS = [("even", 0), ("odd", 0), ("even", 1), ("odd", 1)]
DIR_KEYS = ("s5_ar_row", "s5_ai_row", "s5_dt_row", "s5_ar_col", "s5_ai_col", "s5_dt_col", "s5_b_col", "s5_c_col", "rw_w0_rep", "rw_w_up")


def core_flags(w0, w1):
    fl = np.zeros((128, 4), np.float32)
    fl[:, 0] = w0
    fl[:, 1] = w1
    fl[:, 2] = 0.0 if (w0 + w1) > 0 else NEG
    return fl


def core_maps(m, streams):
    mrev = dict(m)
    for k in DIR_KEYS:
        mrev[k] = np.ascontiguousarray(m[k][:, ::-1])
    maps = []
    for (x, rev, w0, w1) in streams:
        mm = dict(mrev if rev else m)
        mm["flags"] = core_flags(w0, w1)
        mm["xin"] = np.ascontiguousarray(x[::-1] if rev else x)
        maps.append(mm)
    return maps


def kernel(**inputs):
    xp = np.asarray(inputs["x_prompt"], np.float32)
    xs = np.asarray(inputs["x_sample"], np.float32)
    m = host_layout(inputs, LAYERS)
    nc, gst = build_program(NT_FULL, LAYERS)
    streams = [(xs[0, 0:8192], False, 0.0, 1.0), (xs[0, 8192:16384], True, 1.0, 0.0)]
    for b in range(4):
        streams.append((xp[b], False, 0.0, 0.0))
    streams += [(xp[0], False, 0.0, 0.0), (xp[1], False, 0.0, 0.0)]
    maps = core_maps(m, streams)
    res = run_bass_kernel_spmd(nc, maps, core_ids=list(range(8)))
    outs = [np.asarray(res.results[c]["xout"], np.float32) for c in range(6)]
    y_sample = np.concatenate([outs[0], outs[1][::-1]], axis=0).reshape(1, 16384, D)
    y_prompt = np.stack(outs[2:6], axis=0)
    return (y_prompt, y_sample)
```

```python
import numpy as np
from contextlib import ExitStack
import concourse.bass as bass
import concourse.mybir as mybir
from concourse.bass_utils import run_bass_kernel_spmd

F32 = mybir.dt.float32
BF16 = mybir.dt.bfloat16
AF = mybir.ActivationFunctionType
ALU = mybir.AluOpType
AX = mybir.AxisListType

import os as _os
N_DMA_SLOTS = int(_os.environ.get("NSLOTS", "24"))
D = 1024
EPS = 1e-6
NEG = -30000.0


import types


def _snap(fn):
    if fn.__closure__ is None:
        return fn
    cells = tuple(types.CellType(c.cell_contents) for c in fn.__closure__)
    return types.FunctionType(fn.__code__, fn.__globals__, fn.__name__, fn.__defaults__, cells)


class T:
    __slots__ = ("t", "name", "writers", "readers", "war")

    def __init__(self, t, name=""):
        self.t = t
        self.name = name
        self.writers = []
        self.readers = []
        self.war = []

    def __getitem__(self, idx):
        return self.t[idx]


class Prog:
    ENGS = ("pe", "act", "dve", "pool", "sp")

    def __init__(self, nc, stack):
        self.nc = nc
        self.stack = stack
        self.gstack = stack
        self.ops = {e: [] for e in self.ENGS}
        self.cnt = {e: 0 for e in self.ENGS}
        self.seen = {e: {} for e in self.ENGS}
        self.sems = {e: stack.enter_context(nc.semaphore("s_" + e)) for e in self.ENGS}
        self.dma_sems = [stack.enter_context(nc.semaphore("s_dma%d" % i)) for i in range(N_DMA_SLOTS)]
        self.dma_n = 0
        self.cc_n = 0
        self.sems["cc"] = stack.enter_context(nc.semaphore("s_cc"))
        import os
        self.same_engine_sync = not os.environ.get("NOSES")
        self._uid = 0

    def sb(self, shape, dt=F32, name=None):
        self._uid += 1
        name = "sb%d" % self._uid
        return T(self.stack.enter_context(self.nc.sbuf_tensor(name, list(shape), dt)), name)

    def ps(self, shape, dt=F32):
        self._uid += 1
        name = "ps%d" % self._uid
        return T(self.stack.enter_context(self.nc.psum_tensor(name, list(shape), dt)), name)

    def dram(self, name, shape, dt=F32):
        return self.nc.dram_tensor(name, list(shape), dt, kind="Internal")

    def _need(self, eng, dep, waits):
        key, val, deng = dep
        if deng == eng and (eng == "pe" or not self.same_engine_sync):
            return
        if self.seen[eng].get(key, -1) >= val:
            return
        self.seen[eng][key] = val
        waits.append((key, val))

    def _deps(self, eng, reads, writes, accum):
        waits = []
        for t in reads:
            for w in t.writers:
                self._need(eng, w, waits)
        for t in writes:
            if not accum:
                for w in t.writers:
                    self._need(eng, w, waits)
            else:
                for w in t.war:
                    self._need(eng, w, waits)
            for r in t.readers:
                self._need(eng, r, waits)
        return waits

    def _commit(self, tok, reads, writes, accum):
        for t in reads:
            t.readers.append(tok)
        for t in writes:
            if accum:
                t.writers.append(tok)
                t.war = t.war + t.readers
            else:
                t.war = t.writers + t.readers
                t.writers = [tok]
            t.readers = []

    def _sem(self, key):
        return self.sems[key] if isinstance(key, str) else self.dma_sems[key]

    def op(self, eng, fn, reads=(), writes=(), accum=False):
        import os
        if eng == "pool" and os.environ.get("NOPOOL"):
            eng = "dve"
        kmax = int(os.environ.get("KMAX", "0"))
        if kmax and sum(self.cnt.values()) >= kmax:
            return None
        waits = self._deps(eng, reads, writes, accum)
        self.cnt[eng] += 1
        tok = (eng, self.cnt[eng], eng)
        self._commit(tok, reads, writes, accum)
        self.ops[eng].append((waits, _snap(fn), (eng, 1)))
        return tok

    def dma(self, out_ap, in_ap, reads=(), writes=(), q="sp", accum=False):
        waits = self._deps(q, reads, writes, accum)
        i = self.dma_n
        self.dma_n += 1
        slot = i % N_DMA_SLOTS
        val = 16 * (i // N_DMA_SLOTS + 1)
        if i >= N_DMA_SLOTS and self.seen[q].get(slot, -1) < val - 16:
            self.seen[q][slot] = val - 16
            waits.append((slot, val - 16))
        tok = (slot, val, "dma")
        self._commit(tok, reads, writes, accum)

        def fn(e, out_ap=out_ap, in_ap=in_ap):
            return e.dma_start(out=out_ap, in_=in_ap)
        self.ops[q].append((waits, fn, (slot, 16)))
        return tok

    def collective(self, src_ap, dst_ap, reads=(), writes=(), groups=((0, 1), (2, 3), (4, 5), (6, 7))):
        q = "pool"
        waits = self._deps(q, reads, writes, False)
        self.cc_n += 1
        tok = ("cc", self.cc_n, "cc")
        self._commit(tok, reads, writes, False)
        rg = [list(g) for g in groups]

        def fn(e):
            return e.collective_compute("AllGather", ALU.bypass, replica_groups=rg, ins=[src_ap], outs=[dst_ap])
        self.ops[q].append((waits, fn, ("cc", 1)))
        return tok

    def barrier(self):
        for e in self.ENGS:
            waits = []
            for o in self.ENGS:
                if o != e and self.cnt[o] > 0 and self.seen[e].get(o, -1) < self.cnt[o]:
                    self.seen[e][o] = self.cnt[o]
                    waits.append((o, self.cnt[o]))
            if self.cc_n > 0 and self.seen[e].get("cc", -1) < self.cc_n:
                self.seen[e]["cc"] = self.cc_n
                waits.append(("cc", self.cc_n))
            n = self.dma_n
            for slot in range(min(n, N_DMA_SLOTS)):
                last_i = ((n - 1 - slot) // N_DMA_SLOTS) * N_DMA_SLOTS + slot
                v = 16 * (last_i // N_DMA_SLOTS + 1)
                if self.seen[e].get(slot, -1) < v:
                    self.seen[e][slot] = v
                    waits.append((slot, v))
            if waits:
                self.ops[e].append((waits, None, None))

    def emit(self):
        nc = self.nc
        self.barrier()
        block = self.gstack.enter_context(nc.Block())
        prog = self

        def run(engname, e):
            for waits, fn, inc in prog.ops[engname]:
                for key, val in waits:
                    e.wait_ge(prog._sem(key), val)
                if fn is not None:
                    fn(e).then_inc(prog._sem(inc[0]), inc[1])

        @block.tensor
        def _(e):
            run("pe", e)

        @block.scalar
        def _(e):
            run("act", e)

        @block.vector
        def _(e):
            run("dve", e)

        @block.gpsimd
        def _(e):
            run("pool", e)

        @block.sync
        def _(e):
            run("sp", e)


class Ctx:
    pass


def rr(P, C, key, engs):
    C.rr[key] = C.rr.get(key, -1) + 1
    return engs[C.rr[key] % len(engs)]


def copy_op(P, eng, out_ap, in_ap, reads, writes, accum=False):
    if eng == "act":
        P.op("act", lambda e: e.activation(out=out_ap, in_=in_ap, func=AF.Copy), reads=reads, writes=writes, accum=accum)
    elif eng == "dve":
        P.op("dve", lambda e: e.tensor_copy(out=out_ap, in_=in_ap), reads=reads, writes=writes, accum=accum)
    else:
        P.op("pool", lambda e: e.tensor_copy(out=out_ap, in_=in_ap), reads=reads, writes=writes, accum=accum)


def load_weight_bf16(P, C, dst, src_ap_fn, nchunk, ncols, stage):
    for c in range(nchunk):
        s = stage[c % len(stage)]
        P.dma(s[:, 0:ncols], src_ap_fn(c), reads=[], writes=[s])
        eng = ("act", "dve", "pool")[c % 3]
        copy_op(P, eng, dst[:, c, :], s[:, 0:ncols], [s], [dst], accum=True)


def rmsnorm_T(P, C, xt, gn, hT, S):
    junk, ss, ss2, rs, h = S.junk, S.ss, S.ss2, S.rs, S.h
    P.op("act", lambda e: e.activation(out=junk[:], in_=xt[:], func=AF.Square, accum_out=ss[:]),
         reads=[xt], writes=[junk, ss])
    P.op("act", lambda e: e.activation(out=ss2[:], in_=ss[:], func=AF.Sqrt, scale=1.0 / D, bias=EPS),
         reads=[ss], writes=[ss2])
    P.op("dve", lambda e: e.reciprocal(out=rs[:], in_=ss2[:]), reads=[ss2], writes=[rs])
    P.op("dve", lambda e: e.scalar_tensor_tensor(out=h[:], in0=xt[:], scalar=rs[:, 0:1], in1=gn[:],
                                                 op0=ALU.mult, op1=ALU.mult), reads=[xt, rs, gn], writes=[h])
    transpose_to(P, C, h, 8, hT)


def transpose_to(P, C, src, nblk, dst, src_off=0):
    for g0 in range(0, nblk, 4):
        n = min(4, nblk - g0)
        bank = C.gbank()
        for c in range(n):
            P.op("pe", lambda e, c=c, bank=bank, g0=g0: e.transpose(
                out=bank[:, c * 128:(c + 1) * 128],
                in_=src[:, src_off + (g0 + c) * 128: src_off + (g0 + c + 1) * 128], identity=C.ident[:]),
                reads=[src, C.ident], writes=[bank], accum=(c > 0))
        eng = rr(P, C, "tev", ("act", "dve"))
        copy_op(P, eng, dst[:, g0:g0 + n, :], bank[:, 0:n * 128].rearrange("p (a b) -> p a b", a=n),
                [bank], [dst], accum=(g0 > 0))


def transpose_heads(P, C, src, nheads, dst, src_off=0):
    for g0 in range(0, nheads, 4):
        n = min(4, nheads - g0)
        bank = C.gbank()
        for c in range(n):
            P.op("pe", lambda e, c=c, bank=bank, g0=g0: e.transpose(
                out=bank[0:64, c * 128:(c + 1) * 128],
                in_=src[:, src_off + (g0 + c) * 64: src_off + (g0 + c + 1) * 64], identity=C.ident[:]),
                reads=[src, C.ident], writes=[bank], accum=(c > 0))
        eng = rr(P, C, "tev", ("act", "dve"))
        copy_op(P, eng, dst[:, g0:g0 + n, :], bank[0:64, 0:n * 128].rearrange("p (a b) -> p a b", a=n),
                [bank], [dst], accum=(g0 > 0))


def matmul_group(P, C, bank, ncols, hT, W, col0, nk=8, out_off=0):
    for c in range(nk):
        P.op("pe", lambda e, c=c: e.matmul(bank[:, out_off:out_off + ncols], lhsT=hT[:, c, :],
                                           rhs=W[:, c, col0:col0 + ncols], start=(c == 0), stop=(c == nk - 1)),
             reads=[hT, W], writes=[bank], accum=(c > 0))


def odd_layer(P, C, l, x_src, x_dst, xs_tr, xd_tr):
    NT = C.NT
    st = ExitStack()
    P.stack = st
    I = C.inp
    wq = P.sb([128, 8, 2560], BF16)
    wo = P.sb([128, 8, 1024], BF16)
    st2 = ExitStack(); P.stack = st2
    stage = [P.sb([128, 2560]), P.sb([128, 2560])]
    load_weight_bf16(P, C, wq, lambda c: I["od_w_in"][l, c * 128:(c + 1) * 128, :], 8, 2560, stage)
    load_weight_bf16(P, C, wo, lambda c: I["od_w_out"][l, c * 128:(c + 1) * 128, :], 8, 1024, stage)
    P.barrier()
    st2.close(); P.stack = st
    gn = P.sb([128, 1024]); gq = P.sb([128, 64]); gk = P.sb([128, 64]); esink = P.sb([128, 16])
    P.dma(gn[:], I["od_norm_rep"][l], writes=[gn])
    P.dma(gq[:], I["qg_rep"][l], writes=[gq])
    P.dma(gk[:], I["kg_rep"][l], writes=[gk])
    P.dma(esink[:], I["sink_rep"][l], writes=[esink])
    P.op("act", lambda e: e.activation(out=esink[:], in_=esink[:], func=AF.Exp), reads=[esink], writes=[esink])
    biasT = P.sb([128, 16, 512])
    for j in range(4):
        P.dma(biasT[:, 4 * j:4 * j + 4, :], I["alibi"][j].rearrange("r s q -> s r q"), writes=[biasT], accum=True)
    KTh = P.sb([64, 4, 128], BF16); Vh = P.sb([128, 4, 72], BF16)
    kh2 = P.sb([64, 2, 512], BF16); vh2 = P.sb([128, 2, 288], BF16)

    S = Ctx()
    S.junk = P.sb([128, 1024]); S.ss = P.sb([128, 1]); S.ss2 = P.sb([128, 1]); S.rs = P.sb([128, 1]); S.h = P.sb([128, 1024])
    hT = P.sb([128, 8, 128], BF16)
    xring = [P.sb([128, 1024]) for _ in range(3)]
    qf = P.sb([128, 1024]); qsq = P.sb([128, 1024]); qss = P.sb([128, 16]); qr = P.sb([128, 16]); qn = P.sb([128, 1024])
    kf = P.sb([128, 256]); ksq = P.sb([128, 256]); kss = P.sb([128, 4]); kr = P.sb([128, 4]); kn = P.sb([128, 256])
    QT = [P.sb([64, 16, 128], BF16) for _ in range(3)]
    KT = [P.sb([64, 4, 128], BF16) for _ in range(4)]
    V = [P.sb([128, 4, 72], BF16) for _ in range(4)]
    for v in V:
        P.op("pool", lambda e, v=v: e.memset(v[:], 1.0), writes=[v])
    sz = [P.sb([128, 1024]) for _ in range(3)]
    sring = [P.sb([128, 512]) for _ in range(3)]
    pr = [P.sb([128, 512], BF16) for _ in range(6)]
    den = P.sb([128, 4]); rden = P.sb([128, 4])
    o = P.sb([128, 1024]); og = P.sb([128, 1024]); ogT = P.sb([128, 8, 128], BF16)
    xo = [P.sb([128, 1024]) for _ in range(2)]

    def rms_heads(src, sq, ssum, rinv, nh, g, outs):
        P.op("pool", lambda e: e.tensor_tensor(out=sq[:], in0=src[:], in1=src[:], op=ALU.mult), reads=[src], writes=[sq])
        P.op("dve", lambda e: e.tensor_reduce(out=ssum[:], in_=sq[:].rearrange("p (h d) -> p h d", h=nh), axis=AX.X, op=ALU.add),
             reads=[sq], writes=[ssum])
        P.op("act", lambda e: e.activation(out=ssum[:], in_=ssum[:], func=AF.Sqrt, scale=1.0 / 64, bias=EPS),
             reads=[ssum], writes=[ssum])
        P.op("dve", lambda e: e.reciprocal(out=rinv[:], in_=ssum[:]), reads=[ssum], writes=[rinv])
        P.op("dve", lambda e: e.tensor_tensor(out=sq[:].rearrange("p (h d) -> p h d", h=nh),
                                              in0=src[:].rearrange("p (h d) -> p h d", h=nh),
                                              in1=rinv[:].unsqueeze(2).to_broadcast([128, nh, 64]), op=ALU.mult),
             reads=[src, rinv], writes=[sq])
        for oi, (oap, ot) in enumerate(outs):
            P.op("pool", lambda e, oap=oap: e.tensor_tensor(out=oap, in0=sq[:].rearrange("p (h d) -> p h d", h=nh),
                                                            in1=g[:].unsqueeze(1).to_broadcast([128, nh, 64]), op=ALU.mult),
                 reads=[sq, g], writes=[ot], accum=(oi > 0))

    def stage1_parts(j):
        xt = xring[j % 3]

        def pa():
            P.dma(xt[:], x_src[j * 128:(j + 1) * 128, :], reads=[xs_tr[j]], writes=[xt])
            rmsnorm_T(P, C, xt, gn, hT, S)

        def pb():
            for g in range(2):
                bank = C.gbank()
                matmul_group(P, C, bank, 512, hT, wq, g * 512)
                copy_op(P, rr(P, C, "qev", ("act", "dve")), qf[:, g * 512:(g + 1) * 512], bank[:], [bank], [qf], accum=(g > 0))
            rms_heads(qf, qsq, qss, qr, 16, gq, [(qn[:].rearrange("p (h d) -> p h d", h=16), qn)])
            transpose_heads(P, C, qn, 16, QT[j % 3])

        def pc():
            bank = C.gbank()
            matmul_group(P, C, bank, 512, hT, wq, 1024)
            copy_op(P, "dve", kf[:], bank[:, 0:256], [bank], [kf])
            Vt = V[j % 4]
            copy_op(P, "dve", Vt[:, :, 0:64], bank[:, 256:512].rearrange("p (h d) -> p h d", h=4), [bank], [Vt])
            rms_heads(kf, ksq, kss, kr, 4, gk, [(kn[:].rearrange("p (h d) -> p h d", h=4), kn)])
            transpose_heads(P, C, kn, 4, KT[j % 4])

        def pd():
            for g in range(2):
                bank = C.gbank()
                matmul_group(P, C, bank, 512, hT, wq, 1536 + g * 512)
                szt = sz[j % 3]
                P.op("act", lambda e, bank=bank, g=g, szt=szt: e.activation(out=szt[:, g * 512:(g + 1) * 512], in_=bank[:], func=AF.Silu),
                     reads=[bank], writes=[szt], accum=(g > 0))

        return [pa, pb, pc, pd]

    def halo_exchange():
        jl = NT - 1
        ktl, vl = KT[jl % 4], V[jl % 4]
        P.dma(C.ksrc[:, :], ktl[:].rearrange("p a b -> p (a b)"), reads=[ktl], writes=[C.ksrc_tr])
        P.dma(C.vsrc[:, :], vl[:].rearrange("p a b -> p (a b)"), reads=[vl], writes=[C.vsrc_tr])
        P.collective(C.ksrc[:, :], C.kdst[:, :], reads=[C.ksrc_tr], writes=[C.kdst_tr])
        P.collective(C.vsrc[:, :], C.vdst[:, :], reads=[C.vsrc_tr], writes=[C.vdst_tr])
        P.dma(kh2[:], C.kdst.ap().rearrange("(s p) n -> p s n", s=2), reads=[C.kdst_tr], writes=[kh2])
        P.dma(vh2[:], C.vdst.ap().rearrange("(s p) n -> p s n", s=2), reads=[C.vdst_tr], writes=[vh2])
        kf_ = KTh[:].rearrange("p a b -> p (a b)"); vf_ = Vh[:].rearrange("p a b -> p (a b)")
        P.op("dve", lambda e: e.tensor_scalar(out=kf_, in0=kh2[:, 0, :], scalar1=C.flags[0:64, 0:1], scalar2=None, op0=ALU.mult), reads=[kh2, C.flags], writes=[KTh])
        P.op("dve", lambda e: e.scalar_tensor_tensor(out=kf_, in0=kh2[:, 1, :], scalar=C.flags[0:64, 1:2], in1=kf_, op0=ALU.mult, op1=ALU.add),
             reads=[kh2, C.flags, KTh], writes=[KTh])
        P.op("dve", lambda e: e.tensor_scalar(out=vf_, in0=vh2[:, 0, :], scalar1=C.flags[:, 0:1], scalar2=None, op0=ALU.mult), reads=[vh2, C.flags], writes=[Vh])
        P.op("dve", lambda e: e.scalar_tensor_tensor(out=vf_, in0=vh2[:, 1, :], scalar=C.flags[:, 1:2], in1=vf_, op0=ALU.mult, op1=ALU.add),
             reads=[vh2, C.flags, Vh], writes=[Vh])

    def stage2_parts(i):
        def pj(jkv):
            blocks = [b for b in (i - 1, i, i + 1) if 0 <= b < NT]
            if i == NT - 1:
                blocks.append(NT)
            for b in blocks:
                rel = b - i + 1
                halo = (b == NT)
                KTb = KTh if halo else KT[b % 4]
                bank = C.sbank[rel]
                for hl in range(4):
                    hq = 4 * jkv + hl
                    P.op("pe", lambda e, bank=bank, hl=hl, b=b, hq=hq: e.matmul(
                        bank[:, hl * 128:(hl + 1) * 128], lhsT=KTb[:, jkv, :],
                        rhs=QT[i % 3][:, hq, :], start=True, stop=True),
                        reads=[KTb, QT[i % 3]], writes=[bank], accum=(hl > 0))
                s_t = sring[rel]
                bidx = 4 * jkv + (3 if halo else rel)
                P.op("dve", lambda e, bank=bank, s_t=s_t, rel=rel: e.scalar_tensor_tensor(
                    out=s_t[:], in0=bank[:], scalar=0.125, in1=biasT[:, bidx, :], op0=ALU.mult, op1=ALU.add),
                    reads=[bank, biasT], writes=[s_t])
                if halo:
                    P.op("dve", lambda e, s_t=s_t: e.tensor_scalar(out=s_t[:], in0=s_t[:], scalar1=C.flags[:, 2:3], scalar2=None,
                                                                   op0=ALU.add), reads=[s_t, C.flags], writes=[s_t])
                pt = pr[(jkv % 2) * 3 + rel]
                P.op("act", lambda e, pt=pt, s_t=s_t: e.activation(out=pt[:], in_=s_t[:], func=AF.Exp), reads=[s_t], writes=[pt])
            pvb = C.pbank[jkv % 2]
            for hl in range(4):
                for bi, b in enumerate(blocks):
                    rel = b - i + 1
                    pt = pr[(jkv % 2) * 3 + rel]
                    Vb_ = Vh if b == NT else V[b % 4]
                    P.op("pe", lambda e, pt=pt, hl=hl, b=b, bi=bi: e.matmul(
                        pvb[:, hl * 65:(hl + 1) * 65], lhsT=pt[:, hl * 128:(hl + 1) * 128], rhs=Vb_[:, jkv, 0:65],
                        start=(bi == 0), stop=(bi == len(blocks) - 1)),
                        reads=[pt, Vb_], writes=[pvb], accum=not (hl == 0 and bi == 0))
            pv3 = pvb[:, 0:260].rearrange("p (h d) -> p h d", h=4)
            P.op("dve", lambda e, pv3=pv3: e.tensor_tensor(out=den[:], in0=pv3[:, :, 64], in1=esink[:, 4 * jkv:4 * jkv + 4], op=ALU.add),
                 reads=[pvb, esink], writes=[den])
            P.op("dve", lambda e: e.reciprocal(out=rden[:], in_=den[:]), reads=[den], writes=[rden])
            P.op("dve", lambda e, pv3=pv3: e.tensor_tensor(
                out=o[:, jkv * 256:(jkv + 1) * 256].rearrange("p (h d) -> p h d", h=4), in0=pv3[:, :, 0:64],
                in1=rden[:].unsqueeze(2).to_broadcast([128, 4, 64]), op=ALU.mult),
                reads=[pvb, rden], writes=[o], accum=(jkv > 0))

        def ptail():
            P.op("pool", lambda e: e.tensor_tensor(out=og[:], in0=o[:], in1=sz[i % 3][:], op=ALU.mult), reads=[o, sz[i % 3]], writes=[og])
            transpose_to(P, C, og, 8, ogT)
            xot = xo[i % 2]
            for g in range(2):
                bank = C.gbank()
                matmul_group(P, C, bank, 512, ogT, wo, g * 512)
                P.op("dve", lambda e, bank=bank, g=g: e.tensor_tensor(out=xot[:, g * 512:(g + 1) * 512], in0=bank[:],
                                                                      in1=xring[i % 3][:, g * 512:(g + 1) * 512], op=ALU.add),
                     reads=[bank, xring[i % 3]], writes=[xot], accum=(g > 0))
            P.dma(x_dst[i * 128:(i + 1) * 128, :], xot[:], reads=[xot], writes=[xd_tr[i]], q="act")

        return [lambda: pj(0), lambda: pj(1), lambda: pj(2), lambda: pj(3), ptail]

    import os
    dbg = int(os.environ.get("KDBG", "9"))
    for t in range(NT + 2):
        p1 = stage1_parts(t) if t < NT else []
        p2 = stage2_parts(t - 2) if t >= 2 else []
        for k in range(max(len(p1), len(p2))):
            if k < len(p2):
                p2[k]()
            if k < len(p1):
                p1[k]()
        if t == NT - 1:
            halo_exchange()
    P.barrier()
    st.close()
    P.stack = P.gstack


INPUT_SHAPES = {
    "od_w_in": [2, 1024, 2560], "od_w_out": [2, 1024, 1024], "od_norm_rep": [2, 128, 1024],
    "qg_rep": [2, 128, 64], "kg_rep": [2, 128, 64], "sink_rep": [2, 128, 16],
    "alibi": [4, 4, 128, 512], "ident": [128, 128], "flags": [128, 4],
    "ev_w_in": [2, 1024, 3200], "ev_norm_rep": [2, 128, 1024], "ev_w_out": [2, 1024, 1024],
    "s5_ar_row": [2, 2, 128, 2048], "s5_ai_row": [2, 2, 128, 2048], "s5_dt_row": [2, 2, 128, 2048],
    "s5_ar_col": [2, 2, 128, 16], "s5_ai_col": [2, 2, 128, 16], "s5_dt_col": [2, 2, 128, 16],
    "s5_b_col": [2, 2, 2, 128, 16, 16], "s5_c_col": [2, 2, 2, 128, 16, 16],
    "s5_d_col": [2, 128, 4], "glu_b_col": [2, 128, 4], "s5_glu_w": [2, 512, 512],
    "iota_col": [128, 2], "iota_row": [2, 128, 128], "tri": [2, 128, 128], "triE": [2, 128, 128],
    "rw_mu_rep": [2, 128, 1664], "rw_w0_rep": [2, 2, 128, 512], "rw_a0_rep": [2, 128, 512], "rw_k_k_rep": [2, 128, 512],
    "rw_k_a_rep": [2, 128, 512], "rw_r_k_rep": [2, 128, 512], "rw_ln_g_rep": [2, 128, 512], "rw_ln_b_rep": [2, 128, 512],
    "rw_w_up": [2, 2, 64, 512], "rw_a_up": [2, 64, 512],
}


def alibi_tables():
    slopes = np.exp2(-8.0 * np.arange(1, 17, dtype=np.float32) / 16).astype(np.float32)
    s = np.arange(128)[:, None]
    t = np.arange(128)[None, :]
    out = np.zeros((4, 4, 128, 4, 128), np.float32)
    for rel in range(4):
        sg = (s + (rel - 1) * 128) if rel < 3 else (255 - s)
        d = np.abs(t - sg).astype(np.float32)
        for j in range(4):
            for hl in range(4):
                out[j, rel, :, hl, :] = np.where(d <= 128, -slopes[4 * j + hl] * d, NEG)
    return out.reshape(4, 4, 128, 512)


def host_layout(inputs, layers):
    f = lambda a: np.ascontiguousarray(np.asarray(a, np.float32))
    rep = lambda a: f(np.broadcast_to(np.asarray(a)[:, None, :], (a.shape[0], 128, a.shape[1])))
    m = {}
    m["od_w_in"] = f(inputs["od_w_in"]); m["od_w_out"] = f(inputs["od_w_out"])
    m["od_norm_rep"] = rep(inputs["od_norm"]); m["qg_rep"] = rep(inputs["at_q_norm"]); m["kg_rep"] = rep(inputs["at_k_norm"])
    m["sink_rep"] = rep(inputs["at_sink"])
    m["alibi"] = alibi_tables(); m["ident"] = np.eye(128, dtype=np.float32)
    m["ev_w_in"] = f(inputs["ev_w_in"]); m["ev_w_out"] = f(inputs["ev_w_out"]); m["ev_norm_rep"] = rep(inputs["ev_norm"])
    NE = 2
    rowrep = lambda a: f(np.broadcast_to(a.reshape(NE, 2, 1, 2048), (NE, 2, 128, 2048)))
    m["s5_ar_row"] = rowrep(np.asarray(inputs["s5_a_re"])); m["s5_ai_row"] = rowrep(np.asarray(inputs["s5_a_im"]))
    m["s5_dt_row"] = rowrep(np.repeat(np.asarray(inputs["s5_log_dt"])[..., None], 64, axis=-1))
    col = lambda a: f(a.reshape(NE, 2, 16, 128).transpose(0, 1, 3, 2))
    m["s5_ar_col"] = col(np.asarray(inputs["s5_a_re"])); m["s5_ai_col"] = col(np.asarray(inputs["s5_a_im"]))
    m["s5_dt_col"] = col(np.repeat(np.asarray(inputs["s5_log_dt"])[..., None], 64, axis=-1))
    bcol = lambda a: np.asarray(a).reshape(NE, 2, 16, 2, 64, 16).transpose(0, 1, 3, 4, 2, 5).reshape(NE, 2, 128, 16, 16)
    m["s5_b_col"] = f(np.stack([bcol(inputs["s5_b_re"]), bcol(inputs["s5_b_im"])], axis=2))
    ccol = lambda a: np.asarray(a).reshape(NE, 2, 16, 2, 16, 64).transpose(0, 1, 3, 5, 2, 4).reshape(NE, 2, 128, 16, 16)
    m["s5_c_col"] = f(np.stack([ccol(inputs["s5_c_re"]), ccol(inputs["s5_c_im"])], axis=2))
    c4 = lambda a: f(np.asarray(a).reshape(NE, 4, 128).transpose(0, 2, 1))
    m["s5_d_col"] = c4(inputs["s5_d"]); m["glu_b_col"] = c4(inputs["s5_glu_b"]); m["s5_glu_w"] = f(inputs["s5_glu_w"])
    ar = np.arange(128, dtype=np.float32)
    m["iota_col"] = f(np.stack([ar + 1, 128 - ar], axis=1))
    m["iota_row"] = f(np.stack([np.broadcast_to(ar + 1, (128, 128)), np.broadcast_to(128 - ar, (128, 128))]))
    s_, t_ = np.arange(128)[:, None], np.arange(128)[None, :]
    m["tri"] = f(np.stack([(s_ <= t_), (s_ >= t_)]).astype(np.float32))
    m["triE"] = f(np.stack([(s_ < t_), (s_ > t_)]).astype(np.float32))
    for k in ("rw_mu", "rw_a0", "rw_k_k", "rw_k_a", "rw_ln_g", "rw_ln_b"):
        m[k + "_rep"] = rep(np.asarray(inputs[k]))
    m["rw_r_k_rep"] = rep(np.asarray(inputs["rw_r_k"]).reshape(NE, 512))
    w0 = np.asarray(inputs["rw_w0"])
    m["rw_w0_rep"] = f(np.broadcast_to(w0[:, :, None, :], (NE, 2, 128, 512)))
    m["rw_w_up"] = f(inputs["rw_w_up"]); m["rw_a_up"] = f(inputs["rw_a_up"])
    return m


def build_program(NT, layers, debug=False):
    nc = bass.Bass("TRN2", target_bir_lowering=False)
    NTOK = NT * 128
    gst = ExitStack()
    P = Prog(nc, gst)
    C = Ctx()
    C.NT = NT
    C.rr = {}
    C.inp = {k: nc.dram_tensor(k, shp, F32, kind="ExternalInput") for k, shp in INPUT_SHAPES.items()}
    xin = nc.dram_tensor("xin", [NTOK, D], F32, kind="ExternalInput")
    xout = nc.dram_tensor("xout", [NTOK, D], F32, kind="ExternalOutput")
    xa = P.dram("xa", [NTOK, D]); xb = P.dram("xb", [NTOK, D])
    banks = [P.ps([128, 512]) for _ in range(8)]
    C.gb = banks[0:3]; C.sbank = banks[3:6]; C.pbank = banks[6:8]
    C.gi = 0

    def gbank():
        C.gi += 1
        return C.gb[C.gi % len(C.gb)]
    C.gbank = gbank
    C.banks = banks
    C.ident = P.sb([128, 128]); C.flags = P.sb([128, 4])
    C.iota_col = P.sb([128, 2]); C.iota_row = [P.sb([128, 128]) for _ in range(2)]; C.tri = [P.sb([128, 128]) for _ in range(2)]
    C.zero_col = P.sb([128, 2]); C.ones_col = P.sb([128, 2]); C.triE = [P.sb([128, 128]) for _ in range(2)]
    P.op("pool", lambda e: e.memset(C.zero_col[:], 0.0), writes=[C.zero_col])
    P.op("pool", lambda e: e.memset(C.ones_col[:], 1.0), writes=[C.ones_col])
    for d in range(2):
        P.dma(C.triE[d][:], C.inp["triE"][d], writes=[C.triE[d]])
    P.dma(C.iota_col[:], C.inp["iota_col"][:, :], writes=[C.iota_col])
    for d in range(2):
        P.dma(C.iota_row[d][:], C.inp["iota_row"][d], writes=[C.iota_row[d]])
        P.dma(C.tri[d][:], C.inp["tri"][d], writes=[C.tri[d]])
    dbgk = "ExternalOutput" if debug else "Internal"
    C.proj = nc.dram_tensor("proj", [NTOK, 3200], F32, kind=dbgk)
    C.ys5T = nc.dram_tensor("ys5T", [512, NTOK], F32, kind="Internal")
    C.mixT = nc.dram_tensor("mixT", [1024, NTOK], BF16, kind=dbgk)
    C.hrw = C.proj
    C.hrow = P.sb([1, 1664])
    for nm, shp, dt in (("ksrc", [64, 512], BF16), ("kdst", [128, 512], BF16), ("vsrc", [128, 288], BF16), ("vdst", [256, 288], BF16),
                        ("s5src", [128, 32], F32), ("s5dst", [256, 32], F32), ("zsrc", [64, 512], F32), ("zdst", [128, 512], F32),
                        ("hdst", [2, 1664], F32)):
        setattr(C, nm, nc.dram_tensor(nm, shp, dt, kind="Internal"))
        setattr(C, nm + "_tr", T(None))
    C.yrw = nc.dram_tensor("yrw", [NTOK, 512], F32, kind="Internal")
    C.yrw_tr = [T(None) for _ in range(NT)]
    C.rwc = nc.dram_tensor("rwc", [NTOK, 2560], F32, kind="Internal")
    C.rwt = nc.dram_tensor("rwt", [NT, 64, 128], F32, kind="Internal")
    C.rwc_tr = [T(None) for _ in range(NT)]
    C.proj_tr = [T(None) for _ in range(NT)]; C.ys_tr = [T(None) for _ in range(NT)]; C.mix_tr = [T(None) for _ in range(NT)]
    P.dma(C.ident[:], C.inp["ident"][:, :], writes=[C.ident])
    P.dma(C.flags[:], C.inp["flags"][:, :], writes=[C.flags])
    bufs = [xin] + [(xa, xb)[i % 2] for i in range(len(layers) - 1)] + [xout]
    trs = [[T(None) for _ in range(NT)] for _ in range(len(layers) + 1)]
    for li, (kind, l) in enumerate(layers):
        if kind == "odd":
            odd_layer(P, C, l, bufs[li], bufs[li + 1], trs[li], trs[li + 1])
        else:
            even_layer(P, C, l, bufs[li], bufs[li + 1], trs[li], trs[li + 1])
    P.emit()
    return nc, gst


MAGIC = 12582912.0
TWO_PI = 2.0 * np.pi


def round_frac(P, eng, out, in_, tmp):
    (o_ap, o_t), (i_ap, i_t), (t_ap, t_t) = out, in_, tmp
    P.op(eng, lambda e: e.tensor_scalar(out=t_ap, in0=i_ap, scalar1=MAGIC, scalar2=MAGIC, op0=ALU.add, op1=ALU.subtract),
         reads=[i_t], writes=[t_t])
    P.op(eng, lambda e: e.tensor_tensor(out=o_ap, in0=i_ap, in1=t_ap, op=ALU.subtract), reads=[i_t, t_t], writes=[o_t])


def even_phaseA(P, C, l, x_src, xs_tr):
    NT = C.NT
    st = ExitStack(); P.stack = st
    I = C.inp
    w = P.sb([128, 8, 3200], BF16)
    stage = [P.sb([128, 3200]), P.sb([128, 3200])]
    load_weight_bf16(P, C, w, lambda c: I["ev_w_in"][l, c * 128:(c + 1) * 128, :], 8, 3200, stage)
    gn = P.sb([128, 1024])
    P.dma(gn[:], I["ev_norm_rep"][l], writes=[gn])
    S = Ctx()
    S.junk = P.sb([128, 1024]); S.ss = P.sb([128, 1]); S.ss2 = P.sb([128, 1]); S.rs = P.sb([128, 1]); S.h = P.sb([128, 1024])
    hT = P.sb([128, 8, 128], BF16)
    xring = [P.sb([128, 1024]) for _ in range(2)]
    for j in range(NT):
        xt = xring[j % 2]
        P.dma(xt[:], x_src[j * 128:(j + 1) * 128, :], reads=[xs_tr[j]], writes=[xt])
        rmsnorm_T(P, C, xt, gn, hT, S)
        pst = stage[j % 2]
        for g in range(7):
            ncol = 512 if g < 6 else 128
            bank = C.gbank()
            matmul_group(P, C, bank, ncol, hT, w, g * 512)
            copy_op(P, rr(P, C, "pev", ("act", "dve")), pst[:, g * 512:g * 512 + ncol], bank[:, 0:ncol], [bank], [pst], accum=(g > 0))
        P.dma(C.proj[j * 128:(j + 1) * 128, :], pst[:], reads=[pst], writes=[C.proj_tr[j]], q="act")
    P.barrier()
    st.close(); P.stack = P.gstack


def s5_tables(P, C, l, d, K):
    I = C.inp
    W = K.work
    a_r, a_i, dtr, t0, t1, t2 = W[0], W[1], W[2], W[3], W[4], W[5]

    def build(shape_is_row, ar_src, ai_src, dt_src, steps_fn, sign, out_re, out_im):
        P.dma(a_r[:], ar_src, writes=[a_r]); P.dma(a_i[:], ai_src, writes=[a_i]); P.dma(dtr[:], dt_src, writes=[dtr])
        P.op("act", lambda e: e.activation(out=dtr[:], in_=dtr[:], func=AF.Exp), reads=[dtr], writes=[dtr])
        P.op("dve", lambda e: e.tensor_tensor(out=a_r[:], in0=a_r[:], in1=dtr[:], op=ALU.mult), reads=[a_r, dtr], writes=[a_r])
        P.op("dve", lambda e: e.scalar_tensor_tensor(out=a_i[:], in0=a_i[:], scalar=1.0 / TWO_PI, in1=dtr[:], op0=ALU.mult, op1=ALU.mult),
             reads=[a_i, dtr], writes=[a_i])
        round_frac(P, "dve", (a_i[:], a_i), (a_i[:], a_i), (t0[:], t0))
        steps_fn(a_r, a_i)
        P.op("act", lambda e: e.activation(out=t1[:], in_=a_r[:], func=AF.Exp, scale=float(sign)), reads=[a_r], writes=[t1])
        round_frac(P, "dve", (t0[:], t0), (a_i[:], a_i), (t2[:], t2))
        P.op("act", lambda e: e.activation(out=t0[:], in_=t0[:], func=AF.Sin, scale=TWO_PI), reads=[t0], writes=[t0])
        P.op("dve", lambda e: e.tensor_scalar(out=a_i[:], in0=a_i[:], scalar1=0.25, scalar2=None, op0=ALU.add), reads=[a_i], writes=[a_i])
        round_frac(P, "dve", (a_i[:], a_i), (a_i[:], a_i), (t2[:], t2))
        P.op("act", lambda e: e.activation(out=a_i[:], in_=a_i[:], func=AF.Sin, scale=TWO_PI), reads=[a_i], writes=[a_i])
        P.op("dve", lambda e: e.tensor_tensor(out=out_re[:].rearrange("p a b -> p (a b)"), in0=t1[:], in1=a_i[:], op=ALU.mult),
             reads=[t1, a_i], writes=[out_re])
        P.op("dve", lambda e: e.scalar_tensor_tensor(out=out_im[:].rearrange("p a b -> p (a b)"), in0=t1[:], scalar=float(sign), in1=t0[:],
                                                     op0=ALU.mult, op1=ALU.mult), reads=[t1, t0], writes=[out_im])

    def steps_row(a_r, a_i):
        for t in (a_r, a_i):
            P.op("dve", lambda e, t=t: e.tensor_scalar(out=t[:], in0=t[:], scalar1=C.iota_col[:, d:d + 1], scalar2=None, op0=ALU.mult),
                 reads=[t, C.iota_col], writes=[t])
    build(True, I["s5_ar_row"][l, d], I["s5_ai_row"][l, d], I["s5_dt_row"][l, d], steps_row, -1, K.Tin_re, K.Tin_im)

    def steps_col(a_r, a_i):
        for t in (a_r, a_i):
            P.op("dve", lambda e, t=t: e.tensor_tensor(out=t[:].rearrange("p (a b) -> p a b", a=16),
                                                       in0=t[:, 0:16].unsqueeze(2).to_broadcast([128, 16, 128]),
                                                       in1=C.iota_row[d][:].unsqueeze(1).to_broadcast([128, 16, 128]), op=ALU.mult),
                 reads=[t, C.iota_row[d]], writes=[t])
    ca, ci, cd = K.col_a, K.col_i, K.col_d

    def build_col():
        P.dma(ca[:], I["s5_ar_col"][l, d], writes=[ca]); P.dma(ci[:], I["s5_ai_col"][l, d], writes=[ci]); P.dma(cd[:], I["s5_dt_col"][l, d], writes=[cd])
        P.op("act", lambda e: e.activation(out=cd[:], in_=cd[:], func=AF.Exp), reads=[cd], writes=[cd])
        P.op("dve", lambda e: e.tensor_tensor(out=K.c_ardt[:], in0=ca[:], in1=cd[:], op=ALU.mult), reads=[ca, cd], writes=[K.c_ardt])
        P.op("dve", lambda e: e.scalar_tensor_tensor(out=K.c_frac[:], in0=ci[:], scalar=1.0 / TWO_PI, in1=cd[:], op0=ALU.mult, op1=ALU.mult),
             reads=[ci, cd], writes=[K.c_frac])
        round_frac(P, "dve", (K.c_frac[:], K.c_frac), (K.c_frac[:], K.c_frac), (K.c_tmp[:], K.c_tmp))
        P.op("dve", lambda e: e.tensor_tensor(out=a_r[:].rearrange("p (a b) -> p a b", a=16),
                                              in0=K.c_ardt[:].unsqueeze(2).to_broadcast([128, 16, 128]),
                                              in1=C.iota_row[d][:].unsqueeze(1).to_broadcast([128, 16, 128]), op=ALU.mult),
             reads=[K.c_ardt, C.iota_row[d]], writes=[a_r])
        P.op("dve", lambda e: e.tensor_tensor(out=a_i[:].rearrange("p (a b) -> p a b", a=16),
                                              in0=K.c_frac[:].unsqueeze(2).to_broadcast([128, 16, 128]),
                                              in1=C.iota_row[d][:].unsqueeze(1).to_broadcast([128, 16, 128]), op=ALU.mult),
             reads=[K.c_frac, C.iota_row[d]], writes=[a_i])
        sign = 1
        P.op("act", lambda e: e.activation(out=t1[:], in_=a_r[:], func=AF.Exp, scale=float(sign)), reads=[a_r], writes=[t1])
        round_frac(P, "dve", (t0[:], t0), (a_i[:], a_i), (t2[:], t2))
        P.op("act", lambda e: e.activation(out=t0[:], in_=t0[:], func=AF.Sin, scale=TWO_PI), reads=[t0], writes=[t0])
        P.op("dve", lambda e: e.tensor_scalar(out=a_i[:], in0=a_i[:], scalar1=0.25, scalar2=None, op0=ALU.add), reads=[a_i], writes=[a_i])
        round_frac(P, "dve", (a_i[:], a_i), (a_i[:], a_i), (t2[:], t2))
        P.op("act", lambda e: e.activation(out=a_i[:], in_=a_i[:], func=AF.Sin, scale=TWO_PI), reads=[a_i], writes=[a_i])
        P.op("dve", lambda e: e.tensor_tensor(out=K.Tout_re[:].rearrange("p a b -> p (a b)"), in0=t1[:], in1=a_i[:], op=ALU.mult),
             reads=[t1, a_i], writes=[K.Tout_re])
        P.op("dve", lambda e: e.tensor_tensor(out=K.Tout_im[:].rearrange("p a b -> p (a b)"), in0=t1[:], in1=t0[:], op=ALU.mult),
             reads=[t1, t0], writes=[K.Tout_im])
    build_col()

    s1, c1, m1, nr, dn, q_r, q_i, u0, u1 = [K.small[i] for i in range(9)]
    P.op("act", lambda e: e.activation(out=m1[:], in_=K.c_ardt[:], func=AF.Exp), reads=[K.c_ardt], writes=[m1])
    P.op("act", lambda e: e.activation(out=s1[:], in_=K.c_frac[:], func=AF.Sin, scale=TWO_PI), reads=[K.c_frac], writes=[s1])
    P.op("dve", lambda e: e.tensor_scalar(out=u0[:], in0=K.c_frac[:], scalar1=0.25, scalar2=None, op0=ALU.add), reads=[K.c_frac], writes=[u0])
    round_frac(P, "dve", (u0[:], u0), (u0[:], u0), (u1[:], u1))
    P.op("act", lambda e: e.activation(out=c1[:], in_=u0[:], func=AF.Sin, scale=TWO_PI), reads=[u0], writes=[c1])
    tt = lambda o, a, b, op, eng="dve": P.op(eng, lambda e: e.tensor_tensor(out=o[:], in0=a[:], in1=b[:], op=op), reads=[a, b], writes=[o])
    tt(c1, c1, m1, ALU.mult)
    tt(s1, s1, m1, ALU.mult)
    P.op("dve", lambda e: e.tensor_scalar(out=nr[:], in0=c1[:], scalar1=-1.0, scalar2=None, op0=ALU.add), reads=[c1], writes=[nr])
    tt(dn, ca, ca, ALU.mult); tt(u0, ci, ci, ALU.mult); tt(dn, dn, u0, ALU.add)
    P.op("dve", lambda e: e.reciprocal(out=dn[:], in_=dn[:]), reads=[dn], writes=[dn])
    tt(u0, nr, ca, ALU.mult); tt(u1, s1, ci, ALU.mult); tt(u0, u0, u1, ALU.add); tt(q_r, u0, dn, ALU.mult)
    tt(u0, s1, ca, ALU.mult); tt(u1, nr, ci, ALU.mult); tt(u0, u0, u1, ALU.subtract); tt(q_i, u0, dn, ALU.mult)
    bre, bim, bbr, bbi, tb = K.bre, K.bim, K.bbr, K.bbi, K.tb
    for (dst, ri) in ((bre, 0), (bim, 1)):
        P.op("pool", lambda e, dst=dst: e.memset(dst[:], 0.0), writes=[dst])
        P.dma(dst[0:64, :, 0:16], I["s5_b_col"][l, d, ri, 0:64], writes=[dst])
        P.dma(dst[64:128, :, 16:32], I["s5_b_col"][l, d, ri, 64:128], writes=[dst])
    bc = lambda q: q[:].unsqueeze(2).to_broadcast([128, 16, 32])
    P.op("dve", lambda e: e.tensor_tensor(out=bbr[:], in0=bre[:], in1=bc(q_r), op=ALU.mult), reads=[bre, q_r], writes=[bbr])
    P.op("dve", lambda e: e.tensor_tensor(out=tb[:], in0=bim[:], in1=bc(q_i), op=ALU.mult), reads=[bim, q_i], writes=[tb])
    tt(bbr, bbr, tb, ALU.subtract)
    P.op("dve", lambda e: e.tensor_tensor(out=bbi[:], in0=bim[:], in1=bc(q_r), op=ALU.mult), reads=[bim, q_r], writes=[bbi])
    P.op("dve", lambda e: e.tensor_tensor(out=tb[:], in0=bre[:], in1=bc(q_i), op=ALU.mult), reads=[bre, q_i], writes=[tb])
    tt(bbi, bbi, tb, ALU.add)
    zp = K.zp
    for z in zp:
        P.op("pool", lambda e, z=z: e.memset(z[:], 0.0), writes=[z])
    for (src, dst) in ((bbr, K.BT_re), (bbi, K.BT_im)):
        for ch in range(4):
            bank = C.gbank()
            for pl in range(4):
                copy_op(P, "dve", zp[pl][:, 32 * pl:32 * pl + 32], src[:, 4 * ch + pl, :], [src], [zp[pl]])
                P.op("pe", lambda e, bank=bank, pl=pl: e.transpose(out=bank[:, pl * 128:(pl + 1) * 128], in_=zp[pl][:], identity=C.ident[:]),
                     reads=[zp[pl], C.ident], writes=[bank], accum=(pl > 0))
            copy_op(P, "act", dst[:, ch, :], bank[:], [bank], [dst], accum=(ch > 0))
    for (dst, ri) in ((K.Cre, 0), (K.Cimn, 1)):
        P.op("pool", lambda e, dst=dst: e.memset(dst[:], 0.0), writes=[dst])
        P.dma(dst[0:64, :, 32:48], I["s5_c_col"][l, d, ri, 0:64], writes=[dst])
        P.dma(dst[64:128, :, 48:64], I["s5_c_col"][l, d, ri, 64:128], writes=[dst])
    P.op("dve", lambda e: e.tensor_scalar(out=K.Cimn[:], in0=K.Cimn[:], scalar1=-1.0, scalar2=None, op0=ALU.mult), reads=[K.Cimn], writes=[K.Cimn])


def s5_pass(P, C, l, d):
    half = -999
    NT = C.NT
    st = ExitStack(); P.stack = st
    I = C.inp
    K = Ctx()
    K.Tin_re = P.sb([128, 4, 512]); K.Tin_im = P.sb([128, 4, 512])
    K.Tout_re = P.sb([128, 16, 128]); K.Tout_im = P.sb([128, 16, 128])
    K.BT_re = P.sb([128, 4, 512], BF16); K.BT_im = P.sb([128, 4, 512], BF16)
    K.Cre = P.sb([128, 16, 64]); K.Cimn = P.sb([128, 16, 64])
    st2 = ExitStack(); P.stack = st2
    K.work = [P.sb([128, 2048]) for _ in range(6)]
    K.col_a = P.sb([128, 16]); K.col_i = P.sb([128, 16]); K.col_d = P.sb([128, 16])
    K.c_ardt = P.sb([128, 16]); K.c_frac = P.sb([128, 16]); K.c_tmp = P.sb([128, 16])
    K.small = [P.sb([128, 16]) for _ in range(9)]
    K.bre = P.sb([128, 16, 32]); K.bim = P.sb([128, 16, 32]); K.bbr = P.sb([128, 16, 32]); K.bbi = P.sb([128, 16, 32]); K.tb = P.sb([128, 16, 32])
    K.zp = [P.sb([128, 128]) for _ in range(4)]
    s5_tables(P, C, l, d, K)
    P.barrier()
    st2.close(); P.stack = st
    tri = C.tri[d]
    zero = C.zero_col
    uring = [P.sb([128, 512]) for _ in range(2)]
    uT = [P.sb([128, 4, 128], BF16) for _ in range(2)]
    g_re = [P.sb([128, 512], BF16) for _ in range(2)]; g_im = [P.sb([128, 512], BF16) for _ in range(2)]
    ta = [P.sb([128, 512]) for _ in range(2)]; tb = [P.sb([128, 512]) for _ in range(2)]
    ta2 = [P.sb([128, 512]) for _ in range(2)]; tb2 = [P.sb([128, 512]) for _ in range(2)]
    hre = [[P.sb([128, 128]) for _ in range(16)] for _ in range(2)]
    him = [[P.sb([128, 128]) for _ in range(16)] for _ in range(2)]
    r1 = [P.sb([128, 128]) for _ in range(2)]; r2 = [P.sb([128, 128]) for _ in range(2)]
    r3 = [P.sb([128, 128]) for _ in range(2)]; r4 = [P.sb([128, 128]) for _ in range(2)]
    cc = [[P.sb([128, 2]) for _ in range(16)] for _ in range(1)][0]
    ysb = [P.sb([128, 4, 128]) for _ in range(2)]
    ysT = C.ys5T.ap().rearrange("(k p) t -> p k t", p=128)
    bk = C.banks
    if d == 1:
        dcol = P.sb([128, 4]); gbcol = P.sb([128, 4])
        P.dma(dcol[:], I["s5_d_col"][l], writes=[dcol]); P.dma(gbcol[:], I["glu_b_col"][l], writes=[gbcol])
        wg = P.sb([128, 4, 512], BF16)
        wgs = [P.sb([128, 512]), P.sb([128, 512])]
        load_weight_bf16(P, C, wg, lambda c: I["s5_glu_w"][l, c * 128:(c + 1) * 128, :], 4, 512, wgs)
        yf = [P.sb([128, 4, 128]) for _ in range(2)]
        zt = [P.sb([128, 512]) for _ in range(2)]
        szT = P.sb([128, 4, 128])
        yv = P.sb([128, 4, 128]); x2 = P.sb([128, 4, 128]); sg = P.sb([128, 4, 128]); yg = P.sb([128, 4, 128]); ygb = P.sb([128, 4, 128], BF16)
        gs = P.sb([128, 4, 128]); ya = [P.sb([128, 4, 128], BF16) for _ in range(2)]
        mixT = C.mixT.ap().rearrange("(k p) t -> p k t", p=128)

    cin = P.sb([128, 32]); cbuf = P.sb([128, 32]); cin2 = P.sb([128, 2, 32])
    if d == 1:
        P.dma(cin2[:], C.s5dst.ap().rearrange("(s p) n -> p s n", s=2), reads=[C.s5dst_tr], writes=[cin2])
        P.op("dve", lambda e: e.tensor_scalar(out=cin[:], in0=cin2[:, 0, :], scalar1=C.flags[:, 0:1], scalar2=None, op0=ALU.mult), reads=[cin2, C.flags], writes=[cin])
        P.op("dve", lambda e: e.scalar_tensor_tensor(out=cin[:], in0=cin2[:, 1, :], scalar=C.flags[:, 1:2], in1=cin[:], op0=ALU.mult, op1=ALU.add),
             reads=[cin2, C.flags, cin], writes=[cin])
    order = list(range(NT)) if d == 0 else list(range(NT - 1, -1, -1))
    last = 127 if d == 0 else 0
    trib = P.sb([128, 128], BF16)
    copy_op(P, "dve", trib[:], tri[:], [tri], [trib])

    def prologue(it):
        i = order[it]; par = it % 2
        ut = uring[par]
        P.dma(ut[:], C.proj[i * 128:(i + 1) * 128, 0:512], reads=[C.proj_tr[i]], writes=[ut])
        if d == 1:
            P.dma(yf[par][:], ysT[:, :, i * 128:(i + 1) * 128], reads=[C.ys_tr[i]], writes=[yf[par]])
            P.dma(zt[par][:], C.proj[i * 128:(i + 1) * 128, 512:1024], reads=[C.proj_tr[i]], writes=[zt[par]])
        uTt = uT[par]
        for c in range(4):
            P.op("pe", lambda e, c=c: e.transpose(out=bk[0][:, c * 128:(c + 1) * 128], in_=ut[:, c * 128:(c + 1) * 128], identity=C.ident[:]),
                 reads=[ut, C.ident], writes=[bk[0]], accum=(c > 0))
        copy_op(P, "act", uTt[:].rearrange("p a b -> p (a b)"), bk[0][:], [bk[0]], [uTt])

    def front(it, ch):
        par = it % 2; cp = ch % 2
        uTt = uT[par]
        P.op("pe", lambda e: e.matmul(bk[1][:], lhsT=uTt[:, ch, :], rhs=K.BT_re[:, ch, :], start=True, stop=True),
             reads=[uTt, K.BT_re], writes=[bk[1]])
        P.op("pe", lambda e: e.matmul(bk[2][:], lhsT=uTt[:, ch, :], rhs=K.BT_im[:, ch, :], start=True, stop=True),
             reads=[uTt, K.BT_im], writes=[bk[2]])
        Tr = K.Tin_re[:, ch, :]; Ti = K.Tin_im[:, ch, :]
        gr, gi, a_, b_, a2_, b2_ = g_re[cp], g_im[cp], ta[cp], tb[cp], ta2[cp], tb2[cp]
        P.op("dve", lambda e: e.tensor_tensor(out=a_[:], in0=bk[1][:], in1=Tr, op=ALU.mult), reads=[bk[1], K.Tin_re], writes=[a_])
        P.op("dve", lambda e: e.tensor_tensor(out=b_[:], in0=bk[2][:], in1=Ti, op=ALU.mult), reads=[bk[2], K.Tin_im], writes=[b_])
        P.op("pool", lambda e: e.tensor_tensor(out=gr[:], in0=a_[:], in1=b_[:], op=ALU.subtract), reads=[a_, b_], writes=[gr])
        P.op("dve", lambda e: e.tensor_tensor(out=a2_[:], in0=bk[1][:], in1=Ti, op=ALU.mult), reads=[bk[1], K.Tin_im], writes=[a2_])
        P.op("dve", lambda e: e.tensor_tensor(out=b2_[:], in0=bk[2][:], in1=Tr, op=ALU.mult), reads=[bk[2], K.Tin_re], writes=[b2_])
        P.op("pool", lambda e: e.tensor_tensor(out=gi[:], in0=a2_[:], in1=b2_[:], op=ALU.add), reads=[a2_, b2_], writes=[gi])

    def back(it, ch):
        par = it % 2; cp = ch % 2
        gr, gi = g_re[cp], g_im[cp]
        for pl in range(4):
            P.op("pe", lambda e, pl=pl: e.matmul(bk[3][:, pl * 128:(pl + 1) * 128], lhsT=gr[:, pl * 128:(pl + 1) * 128], rhs=trib[:],
                                                 start=True, stop=True), reads=[gr, trib], writes=[bk[3]], accum=(pl > 0))
        for pl in range(4):
            P.op("pe", lambda e, pl=pl: e.matmul(bk[4][:, pl * 128:(pl + 1) * 128], lhsT=gi[:, pl * 128:(pl + 1) * 128], rhs=trib[:],
                                                 start=True, stop=True), reads=[gi, trib], writes=[bk[4]], accum=(pl > 0))
        for pl in (0, 1, 3, 2):
            pp = 4 * ch + pl
            hr_prev, hi_prev = hre[1 - par][pp], him[1 - par][pp]
            hr, hi = hre[par][pp], him[par][pp]
            if it == 0 and d == 0:
                cr, ci_, crt = zero[:, 0:1], zero[:, 0:1], [zero]
            elif it == 0:
                cr, ci_, crt = cin[:, pp:pp + 1], cin[:, 16 + pp:17 + pp], [cin]
            else:
                cr, ci_, crt = hr_prev[:, last:last + 1], hi_prev[:, last:last + 1], [hr_prev, hi_prev]
            Gr = bk[3][:, pl * 128:(pl + 1) * 128]; Gi = bk[4][:, pl * 128:(pl + 1) * 128]
            Tor = K.Tout_re[:, pp, :]; Toi = K.Tout_im[:, pp, :]
            q1, q2, q3, q4 = r1[pl % 2], r2[pl % 2], r3[pl % 2], r4[pl % 2]
            P.op("dve", lambda e: e.scalar_tensor_tensor(out=q1[:], in0=Gr, scalar=cr, in1=Tor, op0=ALU.add, op1=ALU.mult),
                 reads=[bk[3], K.Tout_re] + crt, writes=[q1])
            P.op("dve", lambda e: e.scalar_tensor_tensor(out=q2[:], in0=Gi, scalar=ci_, in1=Toi, op0=ALU.add, op1=ALU.mult),
                 reads=[bk[4], K.Tout_im] + crt, writes=[q2])
            P.op("pool", lambda e: e.tensor_tensor(out=hr[:], in0=q1[:], in1=q2[:], op=ALU.subtract), reads=[q1, q2], writes=[hr])
            P.op("dve", lambda e: e.scalar_tensor_tensor(out=q3[:], in0=Gr, scalar=cr, in1=Toi, op0=ALU.add, op1=ALU.mult),
                 reads=[bk[3], K.Tout_im] + crt, writes=[q3])
            P.op("dve", lambda e: e.scalar_tensor_tensor(out=q4[:], in0=Gi, scalar=ci_, in1=Tor, op0=ALU.add, op1=ALU.mult),
                 reads=[bk[4], K.Tout_re] + crt, writes=[q4])
            P.op("pool", lambda e: e.tensor_tensor(out=hi[:], in0=q3[:], in1=q4[:], op=ALU.add), reads=[q3, q4], writes=[hi])
            if pl == 3:
                osl, csl, st0 = slice(64, 128), slice(0, 64), True
            elif pl == 2:
                osl, csl, st0 = slice(64, 96), slice(32, 64), False
            else:
                osl, csl, st0 = slice(32 * pl, 32 * pl + 32), slice(32, 64), True
            sgc = pl >= 2
            P.op("pe", lambda e: e.matmul(bk[5][osl, ch * 128:(ch + 1) * 128], lhsT=K.Cre[:, pp, csl], rhs=hr[:],
                                          start=st0, stop=False, skip_group_check=sgc), reads=[K.Cre, hr], writes=[bk[5]], accum=not (ch == 0 and pl == 0))
            P.op("pe", lambda e: e.matmul(bk[5][osl, ch * 128:(ch + 1) * 128], lhsT=K.Cimn[:, pp, csl], rhs=hi[:],
                                          start=False, stop=True, skip_group_check=sgc), reads=[K.Cimn, hi], writes=[bk[5]], accum=True)

    def epilogue(it):
        i = order[it]; par = it % 2
        uTt = uT[par]
        if d == 0:
            yt = ysb[par]
            copy_op(P, "act", yt[:].rearrange("p a b -> p (a b)"), bk[5][:], [bk[5]], [yt])
            P.dma(ysT[:, :, i * 128:(i + 1) * 128], yt[:], reads=[yt], writes=[C.ys_tr[i]], q="act")
        else:
            f = lambda t: t[:].rearrange("p a b -> p (a b)")
            P.op("dve", lambda e: e.tensor_tensor(out=f(yv), in0=bk[5][:], in1=f(yf[par]), op=ALU.add), reads=[bk[5], yf[par]], writes=[yv])
            for ch in range(4):
                P.op("dve", lambda e, ch=ch: e.scalar_tensor_tensor(out=yv[:, ch, :], in0=uTt[:, ch, :], scalar=dcol[:, ch:ch + 1], in1=yv[:, ch, :],
                                                                    op0=ALU.mult, op1=ALU.add), reads=[uTt, dcol, yv], writes=[yv], accum=True)
            P.op("act", lambda e: e.activation(out=f(yg), in_=f(yv), func=AF.Gelu_apprx_tanh), reads=[yv], writes=[yg])
            copy_op(P, "pool", f(ygb), f(yg), [yg], [ygb])
            for co in range(4):
                for kc in range(4):
                    P.op("pe", lambda e, co=co, kc=kc: e.matmul(bk[6][:, co * 128:(co + 1) * 128], lhsT=wg[:, kc, co * 128:(co + 1) * 128],
                                                                rhs=ygb[:, kc, :], start=(kc == 0), stop=(kc == 3)),
                         reads=[wg, ygb], writes=[bk[6]], accum=not (co == 0 and kc == 0))
            for co in range(4):
                P.op("act", lambda e, co=co: e.activation(out=sg[:, co, :], in_=bk[6][:, co * 128:(co + 1) * 128], func=AF.Sigmoid,
                                                          bias=gbcol[:, co:co + 1]), reads=[bk[6], gbcol], writes=[sg], accum=(co > 0))
            for c in range(4):
                P.op("pe", lambda e, c=c: e.transpose(out=bk[7][:, c * 128:(c + 1) * 128], in_=zt[par][:, c * 128:(c + 1) * 128], identity=C.ident[:]),
                     reads=[zt[par], C.ident], writes=[bk[7]], accum=(c > 0))
            P.op("act", lambda e: e.activation(out=f(szT), in_=bk[7][:], func=AF.Silu), reads=[bk[7]], writes=[szT])
            P.op("dve", lambda e: e.tensor_tensor(out=f(gs), in0=f(yg), in1=f(sg), op=ALU.mult), reads=[yg, sg], writes=[gs])
            P.op("pool", lambda e: e.tensor_tensor(out=f(ya[par]), in0=f(gs), in1=f(szT), op=ALU.mult), reads=[gs, szT], writes=[ya[par]])
            P.dma(mixT[:, 0:4, i * 128:(i + 1) * 128], ya[par][:], reads=[ya[par]], writes=[C.mix_tr[i]], q="pool")

    units = [(it, ch) for it in range(NT) for ch in range(4)]
    prologue(0); front(0, 0)
    for u, (it, ch) in enumerate(units):
        if u + 1 < len(units):
            it2, ch2 = units[u + 1]
            if ch2 == 0:
                prologue(it2)
            front(it2, ch2)
        back(it, ch)
        if ch == 3:
            epilogue(it)
    if d == 0:
        parl = (NT - 1) % 2
        for pp in range(16):
            copy_op(P, ("dve", "pool")[pp % 2], cbuf[:, pp:pp + 1], hre[parl][pp][:, 127:128], [hre[parl][pp]], [cbuf], accum=(pp > 0))
            copy_op(P, ("pool", "dve")[pp % 2], cbuf[:, 16 + pp:17 + pp], him[parl][pp][:, 127:128], [him[parl][pp]], [cbuf], accum=True)
        P.dma(C.s5src[:, :], cbuf[:], reads=[cbuf], writes=[C.s5src_tr])
        P.collective(C.s5src[:, :], C.s5dst[:, :], reads=[C.s5src_tr], writes=[C.s5dst_tr])
    P.barrier()
    st.close(); P.stack = P.gstack


def even_layer(P, C, l, x_src, x_dst, xs_tr, xd_tr):
    import os
    dbg = int(os.environ.get("EDBG", "9"))
    even_phaseA(P, C, l, x_src, xs_tr)
    if dbg >= 1:
        s5_pass(P, C, l, 0)
        s5_pass(P, C, l, 1)
    if dbg >= 2:
        rwkv_pass(P, C, l, 0, x_src, x_dst, xs_tr, xd_tr)
        rwkv_pass(P, C, l, 1, x_src, x_dst, xs_tr, xd_tr)


def rwkv_pass(P, C, l, d, x_src, x_dst, xs_tr, xd_tr):
    NT = C.NT
    st = ExitStack(); P.stack = st
    I = C.inp
    bk = C.banks
    f32 = lambda shape: P.sb(shape)
    b16 = lambda shape: P.sb(shape, BF16)
    tt = lambda eng, o, a, b, op, rd, wr, accum=False: P.op(eng, lambda e: e.tensor_tensor(out=o, in0=a, in1=b, op=op), reads=rd, writes=wr, accum=accum)

    mu = f32([128, 1664]); P.dma(mu[:], I["rw_mu_rep"][l], writes=[mu])
    w0 = f32([128, 512]); P.dma(w0[:], I["rw_w0_rep"][l, d], writes=[w0])
    a0 = f32([128, 512]); P.dma(a0[:], I["rw_a0_rep"][l], writes=[a0])
    kkp = f32([128, 512]); P.dma(kkp[:], I["rw_k_k_rep"][l], writes=[kkp])
    kap = f32([128, 512]); P.dma(kap[:], I["rw_k_a_rep"][l], writes=[kap])
    ups = f32([128, 2, 512])
    P.dma(ups[0:64, 0, :], I["rw_w_up"][l, d], writes=[ups])
    P.dma(ups[64:128, 1, :], I["rw_a_up"][l], writes=[ups])
    triI = C.tri[d]; triE = C.triE[d]; triET = C.triE[1 - d]
    eye_b = C.ident
    if d == 1:
        rkp = f32([128, 512]); P.dma(rkp[:], I["rw_r_k_rep"][l], writes=[rkp])
        lng = f32([128, 512]); P.dma(lng[:], I["rw_ln_g_rep"][l], writes=[lng])
        lnb = f32([128, 512]); P.dma(lnb[:], I["rw_ln_b_rep"][l], writes=[lnb])
        wo = b16([128, 8, 1024])
        st2 = ExitStack(); P.stack = st2
        wst = [f32([128, 1024]), f32([128, 1024])]
        load_weight_bf16(P, C, wo, lambda c: I["ev_w_out"][l, c * 128:(c + 1) * 128, :], 8, 1024, wst)
        P.barrier()
        st2.close(); P.stack = st

    if d == 0:
        cur = [f32([128, 1664])] * 2; prv = [f32([128, 1664])] * 2; nxt = [f32([128, 1664])] * 2
        tsum = f32([128, 1664])
    else:
        cur = prv = nxt = [None, None]; tsum = None
    hsr = [f32([128, 1664]) for _ in range(2)]
    twla = f32([128, 128]); twlaT = f32([128, 128])
    a_t = f32([128, 512]); e2 = f32([128, 512]); tmp = f32([128, 512]); tmp2 = f32([128, 512])
    kkn = f32([128, 512]); pss = f32([128, 8]); prn = f32([128, 8])
    p_t = f32([128, 512]); q_t = f32([128, 512]); kpr = [f32([128, 512]) for _ in range(2)]
    GI = f32([128, 512]); GIinv = f32([128, 512]); GE = f32([128, 512])
    Pd = f32([128, 512]); Qd = f32([128, 512]); Kd = f32([128, 512]); Rd = f32([128, 512])
    Pdb = b16([128, 512]); Qdbr = [b16([128, 512]) for _ in range(2)]; Kdb = b16([128, 512]); Vbr = [b16([128, 512]) for _ in range(2)]
    PRr = [b16([64, 8, 2, 128]) for _ in range(2)]; QTt = b16([64, 8, 128]); KTt = b16([64, 8, 128])
    Bm = [b16([128, 8, 128]) for _ in range(2)]; Am = [b16([128, 8, 128]) for _ in range(2)]; Pmr = [b16([128, 8, 128]) for _ in range(2)]
    MqTr = [b16([128, 8, 128]) for _ in range(2)]; LkT = b16([128, 8, 128]); MkTr = [b16([128, 8, 128]) for _ in range(2)]
    LkVr = [f32([128, 512]) for _ in range(2)]; KVr = [f32([64, 8, 64]) for _ in range(2)]
    Z = f32([64, 8, 64]); Zb = b16([64, 8, 64]); ZK = f32([64, 8, 64]); ZKg = f32([64, 8, 64]); Ztmp = f32([64, 8, 64])
    gcolr = [f32([64, 8]) for _ in range(2)]; onescol = C.ones_col
    rhs_sb = b16([128, 512]); U_sb = b16([128, 512])
    ysb = [f32([128, 512]) for _ in range(2)]
    if d == 0:
        P.op("pool", lambda e: e.memset(Z[:], 0.0), writes=[Z])
        P.op("pool", lambda e: e.memset(Zb[:], 0.0), writes=[Zb])
        hb = P.sb([1, 2, 1664])
        P.collective(C.proj[NT * 128 - 1:NT * 128, 1024:2688], C.hdst[:, :], reads=[C.proj_tr[NT - 1]], writes=[C.hdst_tr])
        P.dma(hb[:], C.hdst.ap().rearrange("(o s) n -> o s n", o=1), reads=[C.hdst_tr], writes=[hb])
        P.op("dve", lambda e: e.tensor_scalar(out=C.hrow[:], in0=hb[:, 0, :], scalar1=C.flags[0:1, 0:1], scalar2=None, op0=ALU.mult), reads=[hb, C.flags], writes=[C.hrow])
        P.op("dve", lambda e: e.scalar_tensor_tensor(out=C.hrow[:], in0=hb[:, 1, :], scalar=C.flags[0:1, 1:2], in1=C.hrow[:], op0=ALU.mult, op1=ALU.add),
             reads=[hb, C.flags, C.hrow], writes=[C.hrow])
    else:
        z2 = P.sb([64, 2, 512])
        P.dma(z2[:], C.zdst.ap().rearrange("(s p) n -> p s n", s=2), reads=[C.zdst_tr], writes=[z2])
        zf_ = Z[:].rearrange("p a b -> p (a b)")
        P.op("dve", lambda e: e.tensor_scalar(out=zf_, in0=z2[:, 0, :], scalar1=C.flags[0:64, 0:1], scalar2=None, op0=ALU.mult), reads=[z2, C.flags], writes=[Z])
        P.op("dve", lambda e: e.scalar_tensor_tensor(out=zf_, in0=z2[:, 1, :], scalar=C.flags[0:64, 1:2], in1=zf_, op0=ALU.mult, op1=ALU.add),
             reads=[z2, C.flags, Z], writes=[Z])
        copy_op(P, "dve", Zb[:], Z[:], [Z], [Zb])
    if d == 1:
        yfw = [f32([128, 512]) for _ in range(2)]
        zrw = [f32([128, 512]) for _ in range(2)]
        xres = [f32([128, 1024]) for _ in range(2)]
        mean = f32([128, 8]); var = f32([128, 8]); cent = f32([128, 512]); rkk = f32([128, 512]); bon = f32([128, 8])
        yb = f32([128, 512]); szr = f32([128, 512])
        mixA = [b16([128, 4, 128]) for _ in range(2)]; ybT = b16([128, 4, 128])
        xo = [f32([128, 1024]) for _ in range(2)]
        mixT = C.mixT.ap().rearrange("(k p) t -> p k t", p=128)

    v3 = lambda t, n=8: t[:].rearrange("p (h d) -> p h d", h=n)
    order = list(range(NT)) if d == 0 else list(range(NT - 1, -1, -1))
    last = 127 if d == 0 else 0
    HW = C.hrw
    def front(it):
        i = order[it]; par = it % 2; r0 = i * 128
        PR, Pm, MqT, MkT, LkV, KV, Qdb, Vb, gcol, hs, kp = PRr[par], Pmr[par], MqTr[par], MkTr[par], LkVr[par], KVr[par], Qdbr[par], Vbr[par], gcolr[par], hsr[par], kpr[par]
        r_ap, k_ap, v_ap = hs[:, 0:512], hs[:, 512:1024], hs[:, 1024:1536]
        c_, p_, n_ = cur[par], prv[par], nxt[par]
        r_ap, k_ap, v_ap = hs[:, 0:512], hs[:, 512:1024], hs[:, 1024:1536]
        if d == 0:
            P.dma(c_[:], HW[r0:r0 + 128, 1024:2688], reads=[C.proj_tr[i]], writes=[c_])
            if i == 0:
                P.op("pool", lambda e: e.memset(p_[:], 0.0), writes=[p_])
                P.dma(p_[1:128, :], HW[r0:r0 + 127, 1024:2688], reads=[C.proj_tr[i]], writes=[p_])
            else:
                P.dma(p_[:], HW[r0 - 1:r0 + 127, 1024:2688], reads=[C.proj_tr[i], C.proj_tr[i - 1]], writes=[p_])
            if i == NT - 1:
                P.dma(n_[0:127, :], HW[r0 + 1:r0 + 128, 1024:2688], reads=[C.proj_tr[i]], writes=[n_])
                P.dma(n_[127:128, :], C.hrow[:], reads=[C.hrow], writes=[n_], accum=True)
            else:
                P.dma(n_[:], HW[r0 + 1:r0 + 129, 1024:2688], reads=[C.proj_tr[i], C.proj_tr[i + 1]], writes=[n_])
        if d == 1:
            P.dma(yfw[par][:], C.yrw[r0:r0 + 128, :], reads=[C.yrw_tr[i]], writes=[yfw[par]])
            P.dma(zrw[par][:], HW[r0:r0 + 128, 2688:3200], reads=[C.proj_tr[i]], writes=[zrw[par]])
            P.dma(xres[par][:], x_src[r0:r0 + 128, :], reads=[xs_tr[i]], writes=[xres[par]])
            P.dma(mixA[par][:], mixT[:, 0:4, r0:r0 + 128], reads=[C.mix_tr[i]], writes=[mixA[par]])
        if d == 0:
            tt("pool", tsum[:], p_[:], n_[:], ALU.add, [p_, n_], [tsum])
            P.op("dve", lambda e: e.scalar_tensor_tensor(out=tsum[:], in0=tsum[:], scalar=0.5, in1=c_[:], op0=ALU.mult, op1=ALU.subtract),
                 reads=[tsum, c_], writes=[tsum])
            tt("pool", tsum[:], tsum[:], mu[:], ALU.mult, [tsum, mu], [tsum])
            tt("dve", hs[:], tsum[:], c_[:], ALU.add, [tsum, c_], [hs])
            P.op("act", lambda e: e.activation(out=twla[:, 0:64], in_=hs[:, 1536:1600], func=AF.Tanh), reads=[hs], writes=[twla])
            copy_op(P, "dve", twla[:, 64:128], hs[:, 1600:1664], [hs], [twla], accum=True)
            g0 = C.gbank()
            P.op("pe", lambda e: e.transpose(out=g0[:, 0:128], in_=twla[:], identity=C.ident[:]), reads=[twla, C.ident], writes=[g0])
            copy_op(P, "dve", twlaT[:], g0[:, 0:128], [g0], [twlaT])
            g1 = C.gbank()
            P.op("pe", lambda e: e.matmul(g1[:], lhsT=twlaT[64:128, :], rhs=ups[64:128, 1, :], start=True, stop=True), reads=[twlaT, ups], writes=[g1])
            tt("dve", a_t[:], g1[:], a0[:], ALU.add, [g1, a0], [a_t])
            P.op("act", lambda e: e.activation(out=a_t[:], in_=a_t[:], func=AF.Sigmoid), reads=[a_t], writes=[a_t])
            tt("pool", kkn[:], k_ap, kkp[:], ALU.mult, [hs, kkp], [kkn])
            tt("pool", tmp[:], kkn[:], kkn[:], ALU.mult, [kkn], [tmp])
            P.op("dve", lambda e: e.tensor_reduce(out=pss[:], in_=v3(tmp), axis=AX.X, op=ALU.add), reads=[tmp], writes=[pss])
            P.op("act", lambda e: e.activation(out=pss[:], in_=pss[:], func=AF.Sqrt), reads=[pss], writes=[pss])
            P.op("dve", lambda e: e.tensor_scalar(out=pss[:], in0=pss[:], scalar1=1e-12, scalar2=None, op0=ALU.max), reads=[pss], writes=[pss])
            P.op("dve", lambda e: e.reciprocal(out=prn[:], in_=pss[:]), reads=[pss], writes=[prn])
            tt("dve", v3(p_t), v3(kkn), prn[:].unsqueeze(2).to_broadcast([128, 8, 64]), ALU.mult, [kkn, prn], [p_t])
            tt("pool", q_t[:], p_t[:], a_t[:], ALU.mult, [p_t, a_t], [q_t])
            P.op("dve", lambda e: e.scalar_tensor_tensor(out=tmp[:], in0=a_t[:], scalar=-1.0, in1=kap[:], op0=ALU.add, op1=ALU.mult), reads=[a_t, kap], writes=[tmp])
            P.op("dve", lambda e: e.scalar_tensor_tensor(out=kp[:], in0=tmp[:], scalar=1.0, in1=k_ap, op0=ALU.add, op1=ALU.mult), reads=[tmp, hs], writes=[kp])
            P.dma(C.rwc[r0:r0 + 128, 0:512], hs[:, 0:512], reads=[hs], writes=[C.rwc_tr[i]])
            P.dma(C.rwc[r0:r0 + 128, 512:1024], hs[:, 1024:1536], reads=[hs], writes=[C.rwc_tr[i]], accum=True)
            P.dma(C.rwc[r0:r0 + 128, 1024:1536], p_t[:], reads=[p_t], writes=[C.rwc_tr[i]], accum=True)
            P.dma(C.rwc[r0:r0 + 128, 1536:2048], q_t[:], reads=[q_t], writes=[C.rwc_tr[i]], accum=True)
            P.dma(C.rwc[r0:r0 + 128, 2048:2560], kp[:], reads=[kp], writes=[C.rwc_tr[i]], accum=True)
            P.dma(C.rwt[i], twlaT[0:64, :], reads=[twlaT], writes=[C.rwc_tr[i]], accum=True)
        else:
            P.dma(hs[:, 0:512], C.rwc[r0:r0 + 128, 0:512], reads=[C.rwc_tr[i]], writes=[hs])
            P.dma(hs[:, 1024:1536], C.rwc[r0:r0 + 128, 512:1024], reads=[C.rwc_tr[i]], writes=[hs], accum=True)
            P.dma(p_t[:], C.rwc[r0:r0 + 128, 1024:1536], reads=[C.rwc_tr[i]], writes=[p_t])
            P.dma(q_t[:], C.rwc[r0:r0 + 128, 1536:2048], reads=[C.rwc_tr[i]], writes=[q_t])
            P.dma(kp[:], C.rwc[r0:r0 + 128, 2048:2560], reads=[C.rwc_tr[i]], writes=[kp])
            P.dma(twlaT[0:64, :], C.rwt[i], reads=[C.rwc_tr[i]], writes=[twlaT])
        g2 = C.gbank()
        P.op("pe", lambda e: e.matmul(g2[:], lhsT=twlaT[0:64, :], rhs=ups[0:64, 0, :], start=True, stop=True), reads=[twlaT, ups], writes=[g2])
        tt("dve", e2[:], g2[:], w0[:], ALU.add, [g2, w0], [e2])
        P.op("act", lambda e: e.activation(out=e2[:], in_=e2[:], func=AF.Exp, scale=-1.0), reads=[e2], writes=[e2])
        P.op("act", lambda e: e.activation(out=e2[:], in_=e2[:], func=AF.Ln, bias=1.0), reads=[e2], writes=[e2])
        P.op("act", lambda e: e.activation(out=e2[:], in_=e2[:], func=AF.Exp, scale=-1.0, bias=-0.5), reads=[e2], writes=[e2])
        copy_op(P, "pool", Vb[:], v_ap, [hs], [Vb])
        yield
        gI = C.gbank()
        P.op("pe", lambda e: e.matmul(gI[:], lhsT=triI[:], rhs=e2[:], start=True, stop=True), reads=[triI, e2], writes=[gI])
        P.op("act", lambda e: e.activation(out=GI[:], in_=gI[:], func=AF.Exp, scale=-1.0), reads=[gI], writes=[GI])
        P.op("act", lambda e: e.activation(out=GIinv[:], in_=gI[:], func=AF.Exp), reads=[gI], writes=[GIinv])
        gE = C.gbank()
        P.op("pe", lambda e: e.matmul(gE[:], lhsT=triE[:], rhs=e2[:], start=True, stop=True), reads=[triE, e2], writes=[gE])
        P.op("act", lambda e: e.activation(out=GE[:], in_=gE[:], func=AF.Exp, scale=-1.0), reads=[gE], writes=[GE])
        gT = C.gbank()
        for h in range(8):
            P.op("pe", lambda e, h=h: e.matmul(gT[0:64, h:h + 1], lhsT=e2[:, h * 64:(h + 1) * 64], rhs=onescol[:, 0:1], start=True, stop=True),
                 reads=[e2, onescol], writes=[gT], accum=(h > 0))
        P.op("act", lambda e: e.activation(out=gcol[:], in_=gT[0:64, 0:8], func=AF.Exp, scale=-1.0), reads=[gT], writes=[gcol])
        tt("dve", Pd[:], p_t[:], GE[:], ALU.mult, [p_t, GE], [Pd])
        tt("pool", Qd[:], q_t[:], GIinv[:], ALU.mult, [q_t, GIinv], [Qd])
        tt("dve", Kd[:], kp[:], GIinv[:], ALU.mult, [kp, GIinv], [Kd])
        tt("pool", Rd[:], r_ap, GI[:], ALU.mult, [hs, GI], [Rd])
        copy_op(P, "pool", Pdb[:], Pd[:], [Pd], [Pdb]); copy_op(P, "dve", Qdb[:], Qd[:], [Qd], [Qdb]); copy_op(P, "pool", Kdb[:], Kd[:], [Kd], [Kdb])
        yield
        for (src, dstfn, dstt) in ((Pd, lambda h: PR[:, h, 0, :], PR), (Rd, lambda h: PR[:, h, 1, :], PR), (Qd, lambda h: QTt[:, h, :], QTt), (Kd, lambda h: KTt[:, h, :], KTt)):
            for hb in range(2):
                g = C.gbank()
                for hl in range(4):
                    h = 4 * hb + hl
                    P.op("pe", lambda e, hl=hl, h=h, g=g, src=src: e.transpose(out=g[0:64, hl * 128:(hl + 1) * 128], in_=src[:, h * 64:(h + 1) * 64], identity=C.ident[:]),
                         reads=[src, C.ident], writes=[g], accum=(hl > 0))
                if dstt is PR:
                    which = 0 if src is Pd else 1
                    copy_op(P, ("dve", "act")[hb], PR[:, 4 * hb:4 * hb + 4, which, :], g[0:64, :].rearrange("p (a b) -> p a b", a=4), [g], [PR], accum=True)
                else:
                    copy_op(P, ("act", "dve")[hb], dstt[:, 4 * hb:4 * hb + 4, :], g[0:64, :].rearrange("p (a b) -> p a b", a=4), [g], [dstt], accum=(hb > 0))
        yield
        Bc, Ac = Bm[0], Am[0]
        for hg in range(4):
            gq = C.gbank(); gk = C.gbank()
            for hh in range(2):
                h = 2 * hg + hh
                P.op("pe", lambda e: e.matmul(gq[:, hh * 256:(hh + 1) * 256], lhsT=QTt[:, h, :],
                                              rhs=PR[:, h, :, :].rearrange("p a b -> p (a b)"), start=True, stop=True),
                     reads=[QTt, PR], writes=[gq], accum=(hh > 0))
                P.op("pe", lambda e: e.matmul(gk[:, hh * 256:(hh + 1) * 256], lhsT=KTt[:, h, :],
                                              rhs=PR[:, h, :, :].rearrange("p a b -> p (a b)"), start=True, stop=True),
                     reads=[KTt, PR], writes=[gk], accum=(hh > 0))
            gq4 = gq[:].rearrange("p (h a b) -> p h a b", h=2, a=2)
            gk4 = gk[:].rearrange("p (h a b) -> p h a b", h=2, a=2)
            hs2 = slice(2 * hg, 2 * hg + 2)
            mE = triE[:].unsqueeze(1).to_broadcast([128, 2, 128]); mI = triI[:].unsqueeze(1).to_broadcast([128, 2, 128])
            tt("dve", Bc[:, hs2, :], gq4[:, :, 0, :], mE, ALU.mult, [gq, triE], [Bc], accum=(hg > 0))
            tt("dve", MqT[:, hs2, :], gq4[:, :, 1, :], mI, ALU.mult, [gq, triI], [MqT], accum=(hg > 0))
            tt("dve", LkT[:, hs2, :], gk4[:, :, 0, :], mE, ALU.mult, [gk, triE], [LkT], accum=(hg > 0))
            tt("dve", MkT[:, hs2, :], gk4[:, :, 1, :], mI, ALU.mult, [gk, triI], [MkT], accum=(hg > 0))
        for hb in range(2):
            g = C.gbank()
            for hl in range(4):
                h = 4 * hb + hl
                P.op("pe", lambda e: e.matmul(g[:, hl * 128:(hl + 1) * 128], lhsT=PR[:, h, 0, :], rhs=QTt[:, h, :],
                                              start=True, stop=True), reads=[PR, QTt], writes=[g], accum=(hl > 0))
            tt("dve", Ac[:, 4 * hb:4 * hb + 4, :], g[:].rearrange("p (h b) -> p h b", h=4), triET[:].unsqueeze(1).to_broadcast([128, 4, 128]), ALU.mult,
               [g, triET], [Ac], accum=(hb > 0))
        yield
        P.op("dve", lambda e: e.scalar_tensor_tensor(out=Pm[:], in0=Bc[:], scalar=-1.0, in1=eye_b[:].unsqueeze(1).to_broadcast([128, 8, 128]),
                                                     op0=ALU.mult, op1=ALU.add), reads=[Bc, eye_b], writes=[Pm])
        for lev in range(6):
            Bn, An = Bm[(lev + 1) % 2], Am[(lev + 1) % 2]
            for hb in range(2):
                gA = C.gbank()
                for hl in range(4):
                    h = 4 * hb + hl
                    P.op("pe", lambda e: e.matmul(gA[:, hl * 128:(hl + 1) * 128], lhsT=Bc[:, h, :], rhs=Ac[:, h, :], start=True, stop=True),
                         reads=[Bc, Ac], writes=[gA], accum=(hl > 0))
                copy_op(P, ("act", "dve")[hb], An[:, 4 * hb:4 * hb + 4, :].rearrange("p a b -> p (a b)"), gA[:], [gA], [An], accum=(hb > 0))
                if lev < 5:
                    gB = C.gbank()
                    for hl in range(4):
                        h = 4 * hb + hl
                        P.op("pe", lambda e: e.matmul(gB[:, hl * 128:(hl + 1) * 128], lhsT=Ac[:, h, :], rhs=Bc[:, h, :], start=True, stop=True),
                             reads=[Bc, Ac], writes=[gB], accum=(hl > 0))
                    copy_op(P, ("dve", "act")[hb], Bn[:, 4 * hb:4 * hb + 4, :].rearrange("p a b -> p (a b)"), gB[:], [gB], [Bn], accum=(hb > 0))
            for hb in range(2):
                gP = C.gbank()
                for hl in range(4):
                    h = 4 * hb + hl
                    P.op("pe", lambda e: e.matmul(gP[:, hl * 128:(hl + 1) * 128], lhsT=An[:, h, :], rhs=Pm[:, h, :], start=True, stop=True),
                         reads=[An, Pm], writes=[gP], accum=(hl > 0))
                tt("dve", Pm[:, 4 * hb:4 * hb + 4, :].rearrange("p a b -> p (a b)"), gP[:], Pm[:, 4 * hb:4 * hb + 4, :].rearrange("p a b -> p (a b)"),
                   ALU.add, [gP, Pm], [Pm], accum=True)
            Bc, Ac = Bn, An
            yield
        yield
        g = C.gbank()
        for h in range(8):
            P.op("pe", lambda e: e.matmul(g[:, h * 64:(h + 1) * 64], lhsT=LkT[:, h, :], rhs=Vb[:, h * 64:(h + 1) * 64], start=True, stop=True),
                 reads=[LkT, Vb], writes=[g], accum=(h > 0))
        copy_op(P, "act", LkV[:], g[:], [g], [LkV])
        g = C.gbank()
        for h in range(8):
            P.op("pe", lambda e: e.matmul(g[0:64, h * 64:(h + 1) * 64], lhsT=Kdb[:, h * 64:(h + 1) * 64], rhs=Vb[:, h * 64:(h + 1) * 64],
                                          start=True, stop=True), reads=[Kdb, Vb], writes=[g], accum=(h > 0))
        copy_op(P, "dve", KV[:].rearrange("p a b -> p (a b)"), g[0:64, :], [g], [KV])

    def back(it):
        i = order[it]; par = it % 2; r0 = i * 128
        PR, Pm, MqT, MkT, LkV, KV, Qdb, Vb, gcol, hs, kp = PRr[par], Pmr[par], MqTr[par], MkTr[par], LkVr[par], KVr[par], Qdbr[par], Vbr[par], gcolr[par], hsr[par], kpr[par]
        r_ap, k_ap, v_ap = hs[:, 0:512], hs[:, 512:1024], hs[:, 1024:1536]
        gcb = gcol[:].unsqueeze(2).to_broadcast([64, 8, 64])
        tt("pool", ZK[:], Z[:], KV[:], ALU.add, [Z, KV], [ZK])
        tt("pool", ZKg[:], ZK[:], gcb, ALU.mult, [ZK, gcol], [ZKg])
        gz = bk[3]
        for h in range(8):
            P.op("pe", lambda e: e.matmul(gz[:, h * 64:(h + 1) * 64], lhsT=PR[:, h, 0, :], rhs=Zb[:, h, :], start=True, stop=True),
                 reads=[PR, Zb], writes=[gz], accum=(h > 0))
        tt("dve", rhs_sb[:], gz[:], LkV[:], ALU.add, [gz, LkV], [rhs_sb])
        yield
        gu = bk[4]
        for h in range(8):
            P.op("pe", lambda e: e.matmul(gu[:, h * 64:(h + 1) * 64], lhsT=Pm[:, h, :], rhs=rhs_sb[:, h * 64:(h + 1) * 64], start=True, stop=True),
                 reads=[Pm, rhs_sb], writes=[gu], accum=(h > 0))
        P.op("act", lambda e: e.activation(out=U_sb[:], in_=gu[:], func=AF.Copy, scale=-1.0), reads=[gu], writes=[U_sb])
        yield
        gy = bk[5]
        for h in range(8):
            osl = gy[:, h * 64:(h + 1) * 64]
            P.op("pe", lambda e: e.matmul(osl, lhsT=PR[:, h, 1, :], rhs=Zb[:, h, :], start=True, stop=False),
                 reads=[PR, Zb], writes=[gy], accum=(h > 0))
            P.op("pe", lambda e: e.matmul(osl, lhsT=MqT[:, h, :], rhs=U_sb[:, h * 64:(h + 1) * 64], start=False, stop=False),
                 reads=[MqT, U_sb], writes=[gy], accum=True)
            P.op("pe", lambda e: e.matmul(osl, lhsT=MkT[:, h, :], rhs=Vb[:, h * 64:(h + 1) * 64], start=False, stop=True),
                 reads=[MkT, Vb], writes=[gy], accum=True)
        gq_ = bk[6]
        for h in range(8):
            P.op("pe", lambda e: e.matmul(gq_[0:64, h * 64:(h + 1) * 64], lhsT=Qdb[:, h * 64:(h + 1) * 64], rhs=U_sb[:, h * 64:(h + 1) * 64],
                                          start=True, stop=True), reads=[Qdb, U_sb], writes=[gq_], accum=(h > 0))
        tt("dve", Ztmp[:], gq_[0:64, :].rearrange("p (a b) -> p a b", a=8), gcb, ALU.mult, [gq_, gcol], [Ztmp])
        tt("dve", Z[:], Ztmp[:], ZKg[:], ALU.add, [Ztmp, ZKg], [Z])
        copy_op(P, "dve", Zb[:], Z[:], [Z], [Zb])
        yield
        if d == 0:
            yt = ysb[par]
            copy_op(P, "act", yt[:], gy[:], [gy], [yt])
            P.dma(C.yrw[r0:r0 + 128, :], yt[:], reads=[yt], writes=[C.yrw_tr[i]], q="act")
            if it == NT - 1:
                P.dma(C.zsrc[:, :], Z[:].rearrange("p a b -> p (a b)"), reads=[Z], writes=[C.zsrc_tr])
                P.collective(C.zsrc[:, :], C.zdst[:, :], reads=[C.zsrc_tr], writes=[C.zdst_tr])
        else:
            y = ysb[par]
            tt("dve", y[:], gy[:], yfw[par][:], ALU.add, [gy, yfw[par]], [y])
            P.op("dve", lambda e: e.tensor_reduce(out=mean[:], in_=v3(y), axis=AX.X, op=ALU.add), reads=[y], writes=[mean])
            P.op("dve", lambda e: e.tensor_scalar(out=mean[:], in0=mean[:], scalar1=1.0 / 64, scalar2=None, op0=ALU.mult), reads=[mean], writes=[mean])
            tt("dve", v3(cent), v3(y), mean[:].unsqueeze(2).to_broadcast([128, 8, 64]), ALU.subtract, [y, mean], [cent])
            tt("pool", tmp2[:], cent[:], cent[:], ALU.mult, [cent], [tmp2])
            P.op("dve", lambda e: e.tensor_reduce(out=var[:], in_=v3(tmp2), axis=AX.X, op=ALU.add), reads=[tmp2], writes=[var])
            P.op("act", lambda e: e.activation(out=var[:], in_=var[:], func=AF.Sqrt, scale=1.0 / 64, bias=64e-5), reads=[var], writes=[var])
            P.op("dve", lambda e: e.reciprocal(out=var[:], in_=var[:]), reads=[var], writes=[var])
            tt("dve", v3(cent), v3(cent), var[:].unsqueeze(2).to_broadcast([128, 8, 64]), ALU.mult, [cent, var], [cent])
            tt("pool", cent[:], cent[:], lng[:], ALU.mult, [cent, lng], [cent])
            tt("pool", cent[:], cent[:], lnb[:], ALU.add, [cent, lnb], [cent])
            tt("pool", rkk[:], r_ap, kp[:], ALU.mult, [hs, kp], [rkk])
            tt("pool", rkk[:], rkk[:], rkp[:], ALU.mult, [rkk, rkp], [rkk])
            P.op("dve", lambda e: e.tensor_reduce(out=bon[:], in_=v3(rkk), axis=AX.X, op=ALU.add), reads=[rkk], writes=[bon])
            tt("dve", v3(rkk), hs[:, 1024:1536].rearrange("p (h d) -> p h d", h=8), bon[:].unsqueeze(2).to_broadcast([128, 8, 64]), ALU.mult, [hs, bon], [rkk])
            tt("pool", cent[:], cent[:], rkk[:], ALU.add, [cent, rkk], [cent])
            P.op("act", lambda e: e.activation(out=szr[:], in_=zrw[par][:], func=AF.Silu), reads=[zrw[par]], writes=[szr])
            tt("dve", yb[:], cent[:], szr[:], ALU.mult, [cent, szr], [yb])
            yield
            transpose_to(P, C, yb, 4, ybT)
            xot = xo[par]
            for gcol_i in range(2):
                bank = C.gbank()
                for c in range(8):
                    lhs = mixA[par][:, c, :] if c < 4 else ybT[:, c - 4, :]
                    P.op("pe", lambda e, c=c, lhs=lhs, bank=bank: e.matmul(bank[:], lhsT=lhs, rhs=wo[:, c, gcol_i * 512:(gcol_i + 1) * 512],
                                                                          start=(c == 0), stop=(c == 7)),
                         reads=[mixA[par], ybT, wo], writes=[bank], accum=(c > 0))
                tt("dve", xot[:, gcol_i * 512:(gcol_i + 1) * 512], bank[:], xres[par][:, gcol_i * 512:(gcol_i + 1) * 512], ALU.add,
                   [bank, xres[par]], [xot], accum=(gcol_i > 0))
            P.dma(x_dst[r0:r0 + 128, :], xot[:], reads=[xot], writes=[xd_tr[i]], q="act")

    import os
    if os.environ.get("NOSKEW"):
        for it in range(NT):
            for _ in front(it):
                pass
            for _ in back(it):
                pass
    else:
        for _ in front(0):
            pass
        for it in range(NT):
            gf = front(it + 1) if it + 1 < NT else iter(())
            gb = back(it)
            fdone = bdone = False
            while not (fdone and bdone):
                if not fdone:
                    try:
                        next(gf)
                    except StopIteration:
                        fdone = True
                if not bdone:
                    try:
                        next(gb)
                    except StopIteration:
                        bdone = True
    P.barrier()
    st.close(); P.stack = P.gstack


NT_FULL = 64
LAYERS = [("even", 0), ("odd", 0), ("even", 1), ("odd", 1)]
DIR_KEYS = ("s5_ar_row", "s5_ai_row", "s5_dt_row", "s5_ar_col", "s5_ai_col", "s5_dt_col", "s5_b_col", "s5_c_col", "rw_w0_rep", "rw_w_up")


def core_flags(w0, w1):
    fl = np.zeros((128, 4), np.float32)
    fl[:, 0] = w0
    fl[:, 1] = w1
    fl[:, 2] = 0.0 if (w0 + w1) > 0 else NEG
    return fl


def core_maps(m, streams):
    mrev = dict(m)
    for k in DIR_KEYS:
        mrev[k] = np.ascontiguousarray(m[k][:, ::-1])
    maps = []
    for (x, rev, w0, w1) in streams:
        mm = dict(mrev if rev else m)
        mm["flags"] = core_flags(w0, w1)
        mm["xin"] = np.ascontiguousarray(x[::-1] if rev else x)
        maps.append(mm)
    return maps


def kernel(**inputs):
    xp = np.asarray(inputs["x_prompt"], np.float32)
    xs = np.asarray(inputs["x_sample"], np.float32)
    m = host_layout(inputs, LAYERS)
    nc, gst = build_program(NT_FULL, LAYERS)
    streams = [(xs[0, 0:8192], False, 0.0, 1.0), (xs[0, 8192:16384], True, 1.0, 0.0)]
    for b in range(4):
        streams.append((xp[b], False, 0.0, 0.0))
    streams += [(xp[0], False, 0.0, 0.0), (xp[1], False, 0.0, 0.0)]
    maps = core_maps(m, streams)
    res = run_bass_kernel_spmd(nc, maps, core_ids=list(range(8)))
    outs = [np.asarray(res.results[c]["xout"], np.float32) for c in range(6)]
    y_sample = np.concatenate([outs[0], outs[1][::-1]], axis=0).reshape(1, 16384, D)
    y_prompt = np.stack(outs[2:6], axis=0)
    return (y_prompt, y_sample)
```

```python
import numpy as np
from contextlib import ExitStack
import concourse.bass as bass
import concourse.mybir as mybir
from concourse.bass_utils import run_bass_kernel_spmd

F32 = mybir.dt.float32
BF16 = mybir.dt.bfloat16
AF = mybir.ActivationFunctionType
ALU = mybir.AluOpType
AX = mybir.AxisListType

import os as _os
N_DMA_SLOTS = int(_os.environ.get("NSLOTS", "24"))
D = 1024
EPS = 1e-6
NEG = -30000.0


import types


def _snap(fn):
    if fn.__closure__ is None:
        return fn
    cells = tuple(types.CellType(c.cell_contents) for c in fn.__closure__)
    return types.FunctionType(fn.__code__, fn.__globals__, fn.__name__, fn.__defaults__, cells)


class T:
    __slots__ = ("t", "name", "writers", "readers", "war")

    def __init__(self, t, name=""):
        self.t = t
        self.name = name
        self.writers = []
        self.readers = []
        self.war = []

    def __getitem__(self, idx):
        return self.t[idx]


class Prog:
    ENGS = ("pe", "act", "dve", "pool", "sp")

    def __init__(self, nc, stack):
        self.nc = nc
        self.stack = stack
        self.gstack = stack
        self.ops = {e: [] for e in self.ENGS}
        self.cnt = {e: 0 for e in self.ENGS}
        self.seen = {e: {} for e in self.ENGS}
        self.sems = {e: stack.enter_context(nc.semaphore("s_" + e)) for e in self.ENGS}
        self.dma_sems = [stack.enter_context(nc.semaphore("s_dma%d" % i)) for i in range(N_DMA_SLOTS)]
        self.dma_n = 0
        self.cc_n = 0
        self.sems["cc"] = stack.enter_context(nc.semaphore("s_cc"))
        import os
        self.same_engine_sync = not os.environ.get("NOSES")
        self._uid = 0

    def sb(self, shape, dt=F32, name=None):
        self._uid += 1
        name = "sb%d" % self._uid
        return T(self.stack.enter_context(self.nc.sbuf_tensor(name, list(shape), dt)), name)

    def ps(self, shape, dt=F32):
        self._uid += 1
        name = "ps%d" % self._uid
        return T(self.stack.enter_context(self.nc.psum_tensor(name, list(shape), dt)), name)

    def dram(self, name, shape, dt=F32):
        return self.nc.dram_tensor(name, list(shape), dt, kind="Internal")

    def _need(self, eng, dep, waits):
        key, val, deng = dep
        if deng == eng and (eng == "pe" or not self.same_engine_sync):
            return
        if self.seen[eng].get(key, -1) >= val:
            return
        self.seen[eng][key] = val
        waits.append((key, val))

    def _deps(self, eng, reads, writes, accum):
        waits = []
        for t in reads:
            for w in t.writers:
                self._need(eng, w, waits)
        for t in writes:
            if not accum:
                for w in t.writers:
                    self._need(eng, w, waits)
            else:
                for w in t.war:
                    self._need(eng, w, waits)
            for r in t.readers:
                self._need(eng, r, waits)
        return waits

    def _commit(self, tok, reads, writes, accum):
        for t in reads:
            t.readers.append(tok)
        for t in writes:
            if accum:
                t.writers.append(tok)
                t.war = t.war + t.readers
            else:
                t.war = t.writers + t.readers
                t.writers = [tok]
            t.readers = []

    def _sem(self, key):
        return self.sems[key] if isinstance(key, str) else self.dma_sems[key]

    def op(self, eng, fn, reads=(), writes=(), accum=False):
        import os
        if eng == "pool" and os.environ.get("NOPOOL"):
            eng = "dve"
        kmax = int(os.environ.get("KMAX", "0"))
        if kmax and sum(self.cnt.values()) >= kmax:
            return None
        waits = self._deps(eng, reads, writes, accum)
        self.cnt[eng] += 1
        tok = (eng, self.cnt[eng], eng)
        self._commit(tok, reads, writes, accum)
        self.ops[eng].append((waits, _snap(fn), (eng, 1)))
        return tok

    def dma(self, out_ap, in_ap, reads=(), writes=(), q="sp", accum=False):
        waits = self._deps(q, reads, writes, accum)
        i = self.dma_n
        self.dma_n += 1
        slot = i % N_DMA_SLOTS
        val = 16 * (i // N_DMA_SLOTS + 1)
        if i >= N_DMA_SLOTS and self.seen[q].get(slot, -1) < val - 16:
            self.seen[q][slot] = val - 16
            waits.append((slot, val - 16))
        tok = (slot, val, "dma")
        self._commit(tok, reads, writes, accum)

        def fn(e, out_ap=out_ap, in_ap=in_ap):
            return e.dma_start(out=out_ap, in_=in_ap)
        self.ops[q].append((waits, fn, (slot, 16)))
        return tok

    def collective(self, src_ap, dst_ap, reads=(), writes=(), groups=((0, 1), (2, 3), (4, 5), (6, 7))):
        q = "pool"
        waits = self._deps(q, reads, writes, False)
        self.cc_n += 1
        tok = ("cc", self.cc_n, "cc")
        self._commit(tok, reads, writes, False)
        rg = [list(g) for g in groups]

        def fn(e):
            return e.collective_compute("AllGather", ALU.bypass, replica_groups=rg, ins=[src_ap], outs=[dst_ap])
        self.ops[q].append((waits, fn, ("cc", 1)))
        return tok

    def barrier(self):
        for e in self.ENGS:
            waits = []
            for o in self.ENGS:
                if o != e and self.cnt[o] > 0 and self.seen[e].get(o, -1) < self.cnt[o]:
                    self.seen[e][o] = self.cnt[o]
                    waits.append((o, self.cnt[o]))
            if self.cc_n > 0 and self.seen[e].get("cc", -1) < self.cc_n:
                self.seen[e]["cc"] = self.cc_n
                waits.append(("cc", self.cc_n))
            n = self.dma_n
            for slot in range(min(n, N_DMA_SLOTS)):
                last_i = ((n - 1 - slot) // N_DMA_SLOTS) * N_DMA_SLOTS + slot
                v = 16 * (last_i // N_DMA_SLOTS + 1)
                if self.seen[e].get(slot, -1) < v:
                    self.seen[e][slot] = v
                    waits.append((slot, v))
            if waits:
                self.ops[e].append((waits, None, None))

    def emit(self):
        nc = self.nc
        self.barrier()
        block = self.gstack.enter_context(nc.Block())
        prog = self

        def run(engname, e):
            for waits, fn, inc in prog.ops[engname]:
                for key, val in waits:
                    e.wait_ge(prog._sem(key), val)
                if fn is not None:
                    fn(e).then_inc(prog._sem(inc[0]), inc[1])

        @block.tensor
        def _(e):
            run("pe", e)

        @block.scalar
        def _(e):
            run("act", e)

        @block.vector
        def _(e):
            run("dve", e)

        @block.gpsimd
        def _(e):
            run("pool", e)

        @block.sync
        def _(e):
            run("sp", e)


class Ctx:
    pass


def rr(P, C, key, engs):
    C.rr[key] = C.rr.get(key, -1) + 1
    return engs[C.rr[key] % len(engs)]


def copy_op(P, eng, out_ap, in_ap, reads, writes, accum=False):
    if eng == "act":
        P.op("act", lambda e: e.activation(out=out_ap, in_=in_ap, func=AF.Copy), reads=reads, writes=writes, accum=accum)
    elif eng == "dve":
        P.op("dve", lambda e: e.tensor_copy(out=out_ap, in_=in_ap), reads=reads, writes=writes, accum=accum)
    else:
        P.op("pool", lambda e: e.tensor_copy(out=out_ap, in_=in_ap), reads=reads, writes=writes, accum=accum)


def load_weight_bf16(P, C, dst, src_ap_fn, nchunk, ncols, stage):
    for c in range(nchunk):
        s = stage[c % len(stage)]
        P.dma(s[:, 0:ncols], src_ap_fn(c), reads=[], writes=[s])
        eng = ("act", "dve", "pool")[c % 3]
        copy_op(P, eng, dst[:, c, :], s[:, 0:ncols], [s], [dst], accum=True)


def rmsnorm_T(P, C, xt, gn, hT, S):
    junk, ss, ss2, rs, h = S.junk, S.ss, S.ss2, S.rs, S.h
    P.op("act", lambda e: e.activation(out=junk[:], in_=xt[:], func=AF.Square, accum_out=ss[:]),
         reads=[xt], writes=[junk, ss])
    P.op("act", lambda e: e.activation(out=ss2[:], in_=ss[:], func=AF.Sqrt, scale=1.0 / D, bias=EPS),
         reads=[ss], writes=[ss2])
    P.op("dve", lambda e: e.reciprocal(out=rs[:], in_=ss2[:]), reads=[ss2], writes=[rs])
    P.op("dve", lambda e: e.scalar_tensor_tensor(out=h[:], in0=xt[:], scalar=rs[:, 0:1], in1=gn[:],
                                                 op0=ALU.mult, op1=ALU.mult), reads=[xt, rs, gn], writes=[h])
    transpose_to(P, C, h, 8, hT)


def transpose_to(P, C, src, nblk, dst, src_off=0):
    for g0 in range(0, nblk, 4):
        n = min(4, nblk - g0)
        bank = C.gbank()
        for c in range(n):
            P.op("pe", lambda e, c=c, bank=bank, g0=g0: e.transpose(
                out=bank[:, c * 128:(c + 1) * 128],
                in_=src[:, src_off + (g0 + c) * 128: src_off + (g0 + c + 1) * 128], identity=C.ident[:]),
                reads=[src, C.ident], writes=[bank], accum=(c > 0))
        eng = rr(P, C, "tev", ("act", "dve"))
        copy_op(P, eng, dst[:, g0:g0 + n, :], bank[:, 0:n * 128].rearrange("p (a b) -> p a b", a=n),
                [bank], [dst], accum=(g0 > 0))


def transpose_heads(P, C, src, nheads, dst, src_off=0):
    for g0 in range(0, nheads, 4):
        n = min(4, nheads - g0)
        bank = C.gbank()
        for c in range(n):
            P.op("pe", lambda e, c=c, bank=bank, g0=g0: e.transpose(
                out=bank[0:64, c * 128:(c + 1) * 128],
                in_=src[:, src_off + (g0 + c) * 64: src_off + (g0 + c + 1) * 64], identity=C.ident[:]),
                reads=[src, C.ident], writes=[bank], accum=(c > 0))
        eng = rr(P, C, "tev", ("act", "dve"))
        copy_op(P, eng, dst[:, g0:g0 + n, :], bank[0:64, 0:n * 128].rearrange("p (a b) -> p a b", a=n),
                [bank], [dst], accum=(g0 > 0))


def matmul_group(P, C, bank, ncols, hT, W, col0, nk=8, out_off=0):
    for c in range(nk):
        P.op("pe", lambda e, c=c: e.matmul(bank[:, out_off:out_off + ncols], lhsT=hT[:, c, :],
                                           rhs=W[:, c, col0:col0 + ncols], start=(c == 0), stop=(c == nk - 1)),
             reads=[hT, W], writes=[bank], accum=(c > 0))


def odd_layer(P, C, l, x_src, x_dst, xs_tr, xd_tr):
    NT = C.NT
    st = ExitStack()
    P.stack = st
    I = C.inp
    wq = P.sb([128, 8, 2560], BF16)
    wo = P.sb([128, 8, 1024], BF16)
    st2 = ExitStack(); P.stack = st2
    stage = [P.sb([128, 2560]), P.sb([128, 2560])]
    load_weight_bf16(P, C, wq, lambda c: I["od_w_in"][l, c * 128:(c + 1) * 128, :], 8, 2560, stage)
    load_weight_bf16(P, C, wo, lambda c: I["od_w_out"][l, c * 128:(c + 1) * 128, :], 8, 1024, stage)
    P.barrier()
    st2.close(); P.stack = st
    gn = P.sb([128, 1024]); gq = P.sb([128, 64]); gk = P.sb([128, 64]); esink = P.sb([128, 16])
    P.dma(gn[:], I["od_norm_rep"][l], writes=[gn])
    P.dma(gq[:], I["qg_rep"][l], writes=[gq])
    P.dma(gk[:], I["kg_rep"][l], writes=[gk])
    P.dma(esink[:], I["sink_rep"][l], writes=[esink])
    P.op("act", lambda e: e.activation(out=esink[:], in_=esink[:], func=AF.Exp), reads=[esink], writes=[esink])
    biasT = P.sb([128, 16, 512])
    for j in range(4):
        P.dma(biasT[:, 4 * j:4 * j + 4, :], I["alibi"][j].rearrange("r s q -> s r q"), writes=[biasT], accum=True)
    KTh = P.sb([64, 4, 128], BF16); Vh = P.sb([128, 4, 72], BF16)
    kh2 = P.sb([64, 2, 512], BF16); vh2 = P.sb([128, 2, 288], BF16)

    S = Ctx()
    S.junk = P.sb([128, 1024]); S.ss = P.sb([128, 1]); S.ss2 = P.sb([128, 1]); S.rs = P.sb([128, 1]); S.h = P.sb([128, 1024])
    hT = P.sb([128, 8, 128], BF16)
    xring = [P.sb([128, 1024]) for _ in range(3)]
    qf = P.sb([128, 1024]); qsq = P.sb([128, 1024]); qss = P.sb([128, 16]); qr = P.sb([128, 16]); qn = P.sb([128, 1024])
    kf = P.sb([128, 256]); ksq = P.sb([128, 256]); kss = P.sb([128, 4]); kr = P.sb([128, 4]); kn = P.sb([128, 256])
    QT = [P.sb([64, 16, 128], BF16) for _ in range(3)]
    KT = [P.sb([64, 4, 128], BF16) for _ in range(4)]
    V = [P.sb([128, 4, 72], BF16) for _ in range(4)]
    for v in V:
        P.op("pool", lambda e, v=v: e.memset(v[:], 1.0), writes=[v])
    sz = [P.sb([128, 1024]) for _ in range(3)]
    sring = [P.sb([128, 512]) for _ in range(3)]
    pr = [P.sb([128, 512], BF16) for _ in range(6)]
    den = P.sb([128, 4]); rden = P.sb([128, 4])
    o = P.sb([128, 1024]); og = P.sb([128, 1024]); ogT = P.sb([128, 8, 128], BF16)
    xo = [P.sb([128, 1024]) for _ in range(2)]

    def rms_heads(src, sq, ssum, rinv, nh, g, outs):
        P.op("act", lambda e: e.activation(out=sq[:], in_=src[:], func=AF.Square), reads=[src], writes=[sq])
        P.op("dve", lambda e: e.tensor_reduce(out=ssum[:], in_=sq[:].rearrange("p (h d) -> p h d", h=nh), axis=AX.X, op=ALU.add),
             reads=[sq], writes=[ssum])
        P.op("act", lambda e: e.activation(out=ssum[:], in_=ssum[:], func=AF.Sqrt, scale=1.0 / 64, bias=EPS),
             reads=[ssum], writes=[ssum])
        P.op("dve", lambda e: e.reciprocal(out=rinv[:], in_=ssum[:]), reads=[ssum], writes=[rinv])
        P.op("dve", lambda e: e.tensor_tensor(out=sq[:].rearrange("p (h d) -> p h d", h=nh),
                                              in0=src[:].rearrange("p (h d) -> p h d", h=nh),
                                              in1=rinv[:].unsqueeze(2).to_broadcast([128, nh, 64]), op=ALU.mult),
             reads=[src, rinv], writes=[sq])
        for oi, (oap, ot) in enumerate(outs):
            P.op("pool", lambda e, oap=oap: e.tensor_tensor(out=oap, in0=sq[:].rearrange("p (h d) -> p h d", h=nh),
                                                            in1=g[:].unsqueeze(1).to_broadcast([128, nh, 64]), op=ALU.mult),
                 reads=[sq, g], writes=[ot], accum=(oi > 0))

    def stage1_parts(j):
        xt = xring[j % 3]

        def pa():
            P.dma(xt[:], x_src[j * 128:(j + 1) * 128, :], reads=[xs_tr[j]], writes=[xt])
            rmsnorm_T(P, C, xt, gn, hT, S)

        def pb():
            for g in range(2):
                bank = C.gbank()
                matmul_group(P, C, bank, 512, hT, wq, g * 512)
                copy_op(P, rr(P, C, "qev", ("act", "dve")), qf[:, g * 512:(g + 1) * 512], bank[:], [bank], [qf], accum=(g > 0))
            rms_heads(qf, qsq, qss, qr, 16, gq, [(qn[:].rearrange("p (h d) -> p h d", h=16), qn)])
            transpose_heads(P, C, qn, 16, QT[j % 3])

        def pc():
            bank = C.gbank()
            matmul_group(P, C, bank, 512, hT, wq, 1024)
            copy_op(P, "dve", kf[:], bank[:, 0:256], [bank], [kf])
            Vt = V[j % 4]
            copy_op(P, "dve", Vt[:, :, 0:64], bank[:, 256:512].rearrange("p (h d) -> p h d", h=4), [bank], [Vt])
            rms_heads(kf, ksq, kss, kr, 4, gk, [(kn[:].rearrange("p (h d) -> p h d", h=4), kn)])
            transpose_heads(P, C, kn, 4, KT[j % 4])

        def pd():
            for g in range(2):
                bank = C.gbank()
                matmul_group(P, C, bank, 512, hT, wq, 1536 + g * 512)
                szt = sz[j % 3]
                P.op("act", lambda e, bank=bank, g=g, szt=szt: e.activation(out=szt[:, g * 512:(g + 1) * 512], in_=bank[:], func=AF.Silu),
                     reads=[bank], writes=[szt], accum=(g > 0))

        return [pa, pb, pc, pd]

    def halo_exchange():
        jl = NT - 1
        ktl, vl = KT[jl % 4], V[jl % 4]
        P.dma(C.ksrc[:, :], ktl[:].rearrange("p a b -> p (a b)"), reads=[ktl], writes=[C.ksrc_tr])
        P.dma(C.vsrc[:, :], vl[:].rearrange("p a b -> p (a b)"), reads=[vl], writes=[C.vsrc_tr])
        P.collective(C.ksrc[:, :], C.kdst[:, :], reads=[C.ksrc_tr], writes=[C.kdst_tr])
        P.collective(C.vsrc[:, :], C.vdst[:, :], reads=[C.vsrc_tr], writes=[C.vdst_tr])
        P.dma(kh2[:], C.kdst.ap().rearrange("(s p) n -> p s n", s=2), reads=[C.kdst_tr], writes=[kh2])
        P.dma(vh2[:], C.vdst.ap().rearrange("(s p) n -> p s n", s=2), reads=[C.vdst_tr], writes=[vh2])
        kf_ = KTh[:].rearrange("p a b -> p (a b)"); vf_ = Vh[:].rearrange("p a b -> p (a b)")
        P.op("dve", lambda e: e.tensor_scalar(out=kf_, in0=kh2[:, 0, :], scalar1=C.flags[0:64, 0:1], scalar2=None, op0=ALU.mult), reads=[kh2, C.flags], writes=[KTh])
        P.op("dve", lambda e: e.scalar_tensor_tensor(out=kf_, in0=kh2[:, 1, :], scalar=C.flags[0:64, 1:2], in1=kf_, op0=ALU.mult, op1=ALU.add),
             reads=[kh2, C.flags, KTh], writes=[KTh])
        P.op("dve", lambda e: e.tensor_scalar(out=vf_, in0=vh2[:, 0, :], scalar1=C.flags[:, 0:1], scalar2=None, op0=ALU.mult), reads=[vh2, C.flags], writes=[Vh])
        P.op("dve", lambda e: e.scalar_tensor_tensor(out=vf_, in0=vh2[:, 1, :], scalar=C.flags[:, 1:2], in1=vf_, op0=ALU.mult, op1=ALU.add),
             reads=[vh2, C.flags, Vh], writes=[Vh])

    def stage2_parts(i):
        def pj(jkv):
            blocks = [b for b in (i - 1, i, i + 1) if 0 <= b < NT]
            if i == NT - 1:
                blocks.append(NT)
            for b in blocks:
                rel = b - i + 1
                halo = (b == NT)
                KTb = KTh if halo else KT[b % 4]
                bank = C.sbank[rel]
                for hl in range(4):
                    hq = 4 * jkv + hl
                    P.op("pe", lambda e, bank=bank, hl=hl, b=b, hq=hq: e.matmul(
                        bank[:, hl * 128:(hl + 1) * 128], lhsT=KTb[:, jkv, :],
                        rhs=QT[i % 3][:, hq, :], start=True, stop=True),
                        reads=[KTb, QT[i % 3]], writes=[bank], accum=(hl > 0))
                s_t = sring[rel]
                bidx = 4 * jkv + (3 if halo else rel)
                P.op("dve", lambda e, bank=bank, s_t=s_t, rel=rel: e.scalar_tensor_tensor(
                    out=s_t[:], in0=bank[:], scalar=0.125, in1=biasT[:, bidx, :], op0=ALU.mult, op1=ALU.add),
                    reads=[bank, biasT], writes=[s_t])
                if halo:
                    P.op("dve", lambda e, s_t=s_t: e.tensor_scalar(out=s_t[:], in0=s_t[:], scalar1=C.flags[:, 2:3], scalar2=None,
                                                                   op0=ALU.add), reads=[s_t, C.flags], writes=[s_t])
                pt = pr[(jkv % 2) * 3 + rel]
                P.op("act", lambda e, pt=pt, s_t=s_t: e.activation(out=pt[:], in_=s_t[:], func=AF.Exp), reads=[s_t], writes=[pt])
            pvb = C.pbank[jkv % 2]
            for hl in range(4):
                for bi, b in enumerate(blocks):
                    rel = b - i + 1
                    pt = pr[(jkv % 2) * 3 + rel]
                    Vb_ = Vh if b == NT else V[b % 4]
                    P.op("pe", lambda e, pt=pt, hl=hl, b=b, bi=bi: e.matmul(
                        pvb[:, hl * 65:(hl + 1) * 65], lhsT=pt[:, hl * 128:(hl + 1) * 128], rhs=Vb_[:, jkv, 0:65],
                        start=(bi == 0), stop=(bi == len(blocks) - 1)),
                        reads=[pt, Vb_], writes=[pvb], accum=not (hl == 0 and bi == 0))
            pv3 = pvb[:, 0:260].rearrange("p (h d) -> p h d", h=4)
            P.op("dve", lambda e, pv3=pv3: e.tensor_tensor(out=den[:], in0=pv3[:, :, 64], in1=esink[:, 4 * jkv:4 * jkv + 4], op=ALU.add),
                 reads=[pvb, esink], writes=[den])
            P.op("dve", lambda e: e.reciprocal(out=rden[:], in_=den[:]), reads=[den], writes=[rden])
            P.op("dve", lambda e, pv3=pv3: e.tensor_tensor(
                out=o[:, jkv * 256:(jkv + 1) * 256].rearrange("p (h d) -> p h d", h=4), in0=pv3[:, :, 0:64],
                in1=rden[:].unsqueeze(2).to_broadcast([128, 4, 64]), op=ALU.mult),
                reads=[pvb, rden], writes=[o], accum=(jkv > 0))

        def ptail():
            P.op("pool", lambda e: e.tensor_tensor(out=og[:], in0=o[:], in1=sz[i % 3][:], op=ALU.mult), reads=[o, sz[i % 3]], writes=[og])
            transpose_to(P, C, og, 8, ogT)
            xot = xo[i % 2]
            for g in range(2):
                bank = C.gbank()
                matmul_group(P, C, bank, 512, ogT, wo, g * 512)
                P.op("dve", lambda e, bank=bank, g=g: e.tensor_tensor(out=xot[:, g * 512:(g + 1) * 512], in0=bank[:],
                                                                      in1=xring[i % 3][:, g * 512:(g + 1) * 512], op=ALU.add),
                     reads=[bank, xring[i % 3]], writes=[xot], accum=(g > 0))
            P.dma(x_dst[i * 128:(i + 1) * 128, :], xot[:], reads=[xot], writes=[xd_tr[i]], q="act")

        return [lambda: pj(0), lambda: pj(1), lambda: pj(2), lambda: pj(3), ptail]

    import os
    dbg = int(os.environ.get("KDBG", "9"))
    for t in range(NT + 2):
        p1 = stage1_parts(t) if t < NT else []
        p2 = stage2_parts(t - 2) if t >= 2 else []
        for k in range(max(len(p1), len(p2))):
            if k < len(p2):
                p2[k]()
            if k < len(p1):
                p1[k]()
        if t == NT - 1:
            halo_exchange()
    P.barrier()
    st.close()
    P.stack = P.gstack


INPUT_SHAPES = {
    "od_w_in": [2, 1024, 2560], "od_w_out": [2, 1024, 1024], "od_norm_rep": [2, 128, 1024],
    "qg_rep": [2, 128, 64], "kg_rep": [2, 128, 64], "sink_rep": [2, 128, 16],
    "alibi": [4, 4, 128, 512], "ident": [128, 128], "flags": [128, 4],
    "ev_w_in": [2, 1024, 3200], "ev_norm_rep": [2, 128, 1024], "ev_w_out": [2, 1024, 1024],
    "s5_ar_row": [2, 2, 128, 2048], "s5_ai_row": [2, 2, 128, 2048], "s5_dt_row": [2, 2, 128, 2048],
    "s5_ar_col": [2, 2, 128, 16], "s5_ai_col": [2, 2, 128, 16], "s5_dt_col": [2, 2, 128, 16],
    "s5_b_col": [2, 2, 2, 128, 16, 16], "s5_c_col": [2, 2, 2, 128, 16, 16],
    "s5_d_col": [2, 128, 4], "glu_b_col": [2, 128, 4], "s5_glu_w": [2, 512, 512],
    "iota_col": [128, 2], "iota_row": [2, 128, 128], "tri": [2, 128, 128], "triE": [2, 128, 128],
    "rw_mu_rep": [2, 128, 1664], "rw_w0_rep": [2, 2, 128, 512], "rw_a0_rep": [2, 128, 512], "rw_k_k_rep": [2, 128, 512],
    "rw_k_a_rep": [2, 128, 512], "rw_r_k_rep": [2, 128, 512], "rw_ln_g_rep": [2, 128, 512], "rw_ln_b_rep": [2, 128, 512],
    "rw_w_up": [2, 2, 64, 512], "rw_a_up": [2, 64, 512],
}


def alibi_tables():
    slopes = np.exp2(-8.0 * np.arange(1, 17, dtype=np.float32) / 16).astype(np.float32)
    s = np.arange(128)[:, None]
    t = np.arange(128)[None, :]
    out = np.zeros((4, 4, 128, 4, 128), np.float32)
    for rel in range(4):
        sg = (s + (rel - 1) * 128) if rel < 3 else (255 - s)
        d = np.abs(t - sg).astype(np.float32)
        for j in range(4):
            for hl in range(4):
                out[j, rel, :, hl, :] = np.where(d <= 128, -slopes[4 * j + hl] * d, NEG)
    return out.reshape(4, 4, 128, 512)


def host_layout(inputs, layers):
    f = lambda a: np.ascontiguousarray(np.asarray(a, np.float32))
    rep = lambda a: f(np.broadcast_to(np.asarray(a)[:, None, :], (a.shape[0], 128, a.shape[1])))
    m = {}
    m["od_w_in"] = f(inputs["od_w_in"]); m["od_w_out"] = f(inputs["od_w_out"])
    m["od_norm_rep"] = rep(inputs["od_norm"]); m["qg_rep"] = rep(inputs["at_q_norm"]); m["kg_rep"] = rep(inputs["at_k_norm"])
    m["sink_rep"] = rep(inputs["at_sink"])
    m["alibi"] = alibi_tables(); m["ident"] = np.eye(128, dtype=np.float32)
    m["ev_w_in"] = f(inputs["ev_w_in"]); m["ev_w_out"] = f(inputs["ev_w_out"]); m["ev_norm_rep"] = rep(inputs["ev_norm"])
    NE = 2
    rowrep = lambda a: f(np.broadcast_to(a.reshape(NE, 2, 1, 2048), (NE, 2, 128, 2048)))
    m["s5_ar_row"] = rowrep(np.asarray(inputs["s5_a_re"])); m["s5_ai_row"] = rowrep(np.asarray(inputs["s5_a_im"]))
    m["s5_dt_row"] = rowrep(np.repeat(np.asarray(inputs["s5_log_dt"])[..., None], 64, axis=-1))
    col = lambda a: f(a.reshape(NE, 2, 16, 128).transpose(0, 1, 3, 2))
    m["s5_ar_col"] = col(np.asarray(inputs["s5_a_re"])); m["s5_ai_col"] = col(np.asarray(inputs["s5_a_im"]))
    m["s5_dt_col"] = col(np.repeat(np.asarray(inputs["s5_log_dt"])[..., None], 64, axis=-1))
    bcol = lambda a: np.asarray(a).reshape(NE, 2, 16, 2, 64, 16).transpose(0, 1, 3, 4, 2, 5).reshape(NE, 2, 128, 16, 16)
    m["s5_b_col"] = f(np.stack([bcol(inputs["s5_b_re"]), bcol(inputs["s5_b_im"])], axis=2))
    ccol = lambda a: np.asarray(a).reshape(NE, 2, 16, 2, 16, 64).transpose(0, 1, 3, 5, 2, 4).reshape(NE, 2, 128, 16, 16)
    m["s5_c_col"] = f(np.stack([ccol(inputs["s5_c_re"]), ccol(inputs["s5_c_im"])], axis=2))
    c4 = lambda a: f(np.asarray(a).reshape(NE, 4, 128).transpose(0, 2, 1))
    m["s5_d_col"] = c4(inputs["s5_d"]); m["glu_b_col"] = c4(inputs["s5_glu_b"]); m["s5_glu_w"] = f(inputs["s5_glu_w"])
    ar = np.arange(128, dtype=np.float32)
    m["iota_col"] = f(np.stack([ar + 1, 128 - ar], axis=1))
    m["iota_row"] = f(np.stack([np.broadcast_to(ar + 1, (128, 128)), np.broadcast_to(128 - ar, (128, 128))]))
    s_, t_ = np.arange(128)[:, None], np.arange(128)[None, :]
    m["tri"] = f(np.stack([(s_ <= t_), (s_ >= t_)]).astype(np.float32))
    m["triE"] = f(np.stack([(s_ < t_), (s_ > t_)]).astype(np.float32))
    for k in ("rw_mu", "rw_a0", "rw_k_k", "rw_k_a", "rw_ln_g", "rw_ln_b"):
        m[k + "_rep"] = rep(np.asarray(inputs[k]))
    m["rw_r_k_rep"] = rep(np.asarray(inputs["rw_r_k"]).reshape(NE, 512))
    w0 = np.asarray(inputs["rw_w0"])
    m["rw_w0_rep"] = f(np.broadcast_to(w0[:, :, None, :], (NE, 2, 128, 512)))
    m["rw_w_up"] = f(inputs["rw_w_up"]); m["rw_a_up"] = f(inputs["rw_a_up"])
    return m


def build_program(NT, layers, debug=False):
    nc = bass.Bass("TRN2", target_bir_lowering=False)
    NTOK = NT * 128
    gst = ExitStack()
    P = Prog(nc, gst)
    C = Ctx()
    C.NT = NT
    C.rr = {}
    C.inp = {k: nc.dram_tensor(k, shp, F32, kind="ExternalInput") for k, shp in INPUT_SHAPES.items()}
    xin = nc.dram_tensor("xin", [NTOK, D], F32, kind="ExternalInput")
    xout = nc.dram_tensor("xout", [NTOK, D], F32, kind="ExternalOutput")
    xa = P.dram("xa", [NTOK, D]); xb = P.dram("xb", [NTOK, D])
    banks = [P.ps([128, 512]) for _ in range(8)]
    C.gb = banks[0:3]; C.sbank = banks[3:6]; C.pbank = banks[6:8]
    C.gi = 0

    def gbank():
        C.gi += 1
        return C.gb[C.gi % len(C.gb)]
    C.gbank = gbank
    C.banks = banks
    C.ident = P.sb([128, 128]); C.flags = P.sb([128, 4])
    C.iota_col = P.sb([128, 2]); C.iota_row = [P.sb([128, 128]) for _ in range(2)]; C.tri = [P.sb([128, 128]) for _ in range(2)]
    C.zero_col = P.sb([128, 2]); C.ones_col = P.sb([128, 2]); C.triE = [P.sb([128, 128]) for _ in range(2)]
    P.op("pool", lambda e: e.memset(C.zero_col[:], 0.0), writes=[C.zero_col])
    P.op("pool", lambda e: e.memset(C.ones_col[:], 1.0), writes=[C.ones_col])
    for d in range(2):
        P.dma(C.triE[d][:], C.inp["triE"][d], writes=[C.triE[d]])
    P.dma(C.iota_col[:], C.inp["iota_col"][:, :], writes=[C.iota_col])
    for d in range(2):
        P.dma(C.iota_row[d][:], C.inp["iota_row"][d], writes=[C.iota_row[d]])
        P.dma(C.tri[d][:], C.inp["tri"][d], writes=[C.tri[d]])
    dbgk = "ExternalOutput" if debug else "Internal"
    C.proj = nc.dram_tensor("proj", [NTOK, 3200], F32, kind=dbgk)
    C.ys5T = nc.dram_tensor("ys5T", [512, NTOK], F32, kind="Internal")
    C.mixT = nc.dram_tensor("mixT", [1024, NTOK], BF16, kind=dbgk)
    C.hrw = C.proj
    C.hrow = P.sb([1, 1664])
    for nm, shp, dt in (("ksrc", [64, 512], BF16), ("kdst", [128, 512], BF16), ("vsrc", [128, 288], BF16), ("vdst", [256, 288], BF16),
                        ("s5src", [128, 32], F32), ("s5dst", [256, 32], F32), ("zsrc", [64, 512], F32), ("zdst", [128, 512], F32),
                        ("hdst", [2, 1664], F32)):
        setattr(C, nm, nc.dram_tensor(nm, shp, dt, kind="Internal"))
        setattr(C, nm + "_tr", T(None))
    C.yrw = nc.dram_tensor("yrw", [NTOK, 512], F32, kind="Internal")
    C.yrw_tr = [T(None) for _ in range(NT)]
    C.rwc = nc.dram_tensor("rwc", [NTOK, 2560], F32, kind="Internal")
    C.rwt = nc.dram_tensor("rwt", [NT, 64, 128], F32, kind="Internal")
    C.rwc_tr = [T(None) for _ in range(NT)]
    C.proj_tr = [T(None) for _ in range(NT)]; C.ys_tr = [T(None) for _ in range(NT)]; C.mix_tr = [T(None) for _ in range(NT)]
    P.dma(C.ident[:], C.inp["ident"][:, :], writes=[C.ident])
    P.dma(C.flags[:], C.inp["flags"][:, :], writes=[C.flags])
    bufs = [xin] + [(xa, xb)[i % 2] for i in range(len(layers) - 1)] + [xout]
    trs = [[T(None) for _ in range(NT)] for _ in range(len(layers) + 1)]
    for li, (kind, l) in enumerate(layers):
        if kind == "odd":
            odd_layer(P, C, l, bufs[li], bufs[li + 1], trs[li], trs[li + 1])
        else:
            even_layer(P, C, l, bufs[li], bufs[li + 1], trs[li], trs[li + 1])
    P.emit()
    return nc, gst


MAGIC = 12582912.0
TWO_PI = 2.0 * np.pi


def round_frac(P, eng, out, in_, tmp):
    (o_ap, o_t), (i_ap, i_t), (t_ap, t_t) = out, in_, tmp
    P.op(eng, lambda e: e.tensor_scalar(out=t_ap, in0=i_ap, scalar1=MAGIC, scalar2=MAGIC, op0=ALU.add, op1=ALU.subtract),
         reads=[i_t], writes=[t_t])
    P.op(eng, lambda e: e.tensor_tensor(out=o_ap, in0=i_ap, in1=t_ap, op=ALU.subtract), reads=[i_t, t_t], writes=[o_t])


def even_phaseA(P, C, l, x_src, xs_tr):
    NT = C.NT
    st = ExitStack(); P.stack = st
    I = C.inp
    w = P.sb([128, 8, 3200], BF16)
    stage = [P.sb([128, 3200]), P.sb([128, 3200])]
    load_weight_bf16(P, C, w, lambda c: I["ev_w_in"][l, c * 128:(c + 1) * 128, :], 8, 3200, stage)
    gn = P.sb([128, 1024])
    P.dma(gn[:], I["ev_norm_rep"][l], writes=[gn])
    S = Ctx()
    S.junk = P.sb([128, 1024]); S.ss = P.sb([128, 1]); S.ss2 = P.sb([128, 1]); S.rs = P.sb([128, 1]); S.h = P.sb([128, 1024])
    hT = P.sb([128, 8, 128], BF16)
    xring = [P.sb([128, 1024]) for _ in range(2)]
    for j in range(NT):
        xt = xring[j % 2]
        P.dma(xt[:], x_src[j * 128:(j + 1) * 128, :], reads=[xs_tr[j]], writes=[xt])
        rmsnorm_T(P, C, xt, gn, hT, S)
        pst = stage[j % 2]
        for g in range(7):
            ncol = 512 if g < 6 else 128
            bank = C.gbank()
            matmul_group(P, C, bank, ncol, hT, w, g * 512)
            copy_op(P, rr(P, C, "pev", ("act", "dve")), pst[:, g * 512:g * 512 + ncol], bank[:, 0:ncol], [bank], [pst], accum=(g > 0))
        P.dma(C.proj[j * 128:(j + 1) * 128, :], pst[:], reads=[pst], writes=[C.proj_tr[j]], q="act")
    P.barrier()
    st.close(); P.stack = P.gstack


def s5_tables(P, C, l, d, K):
    I = C.inp
    W = K.work
    a_r, a_i, dtr, t0, t1, t2 = W[0], W[1], W[2], W[3], W[4], W[5]

    def build(shape_is_row, ar_src, ai_src, dt_src, steps_fn, sign, out_re, out_im):
        P.dma(a_r[:], ar_src, writes=[a_r]); P.dma(a_i[:], ai_src, writes=[a_i]); P.dma(dtr[:], dt_src, writes=[dtr])
        P.op("act", lambda e: e.activation(out=dtr[:], in_=dtr[:], func=AF.Exp), reads=[dtr], writes=[dtr])
        P.op("dve", lambda e: e.tensor_tensor(out=a_r[:], in0=a_r[:], in1=dtr[:], op=ALU.mult), reads=[a_r, dtr], writes=[a_r])
        P.op("dve", lambda e: e.scalar_tensor_tensor(out=a_i[:], in0=a_i[:], scalar=1.0 / TWO_PI, in1=dtr[:], op0=ALU.mult, op1=ALU.mult),
             reads=[a_i, dtr], writes=[a_i])
        round_frac(P, "dve", (a_i[:], a_i), (a_i[:], a_i), (t0[:], t0))
        steps_fn(a_r, a_i)
        P.op("act", lambda e: e.activation(out=t1[:], in_=a_r[:], func=AF.Exp, scale=float(sign)), reads=[a_r], writes=[t1])
        round_frac(P, "dve", (t0[:], t0), (a_i[:], a_i), (t2[:], t2))
        P.op("act", lambda e: e.activation(out=t0[:], in_=t0[:], func=AF.Sin, scale=TWO_PI), reads=[t0], writes=[t0])
        P.op("dve", lambda e: e.tensor_scalar(out=a_i[:], in0=a_i[:], scalar1=0.25, scalar2=None, op0=ALU.add), reads=[a_i], writes=[a_i])
        round_frac(P, "dve", (a_i[:], a_i), (a_i[:], a_i), (t2[:], t2))
        P.op("act", lambda e: e.activation(out=a_i[:], in_=a_i[:], func=AF.Sin, scale=TWO_PI), reads=[a_i], writes=[a_i])
        P.op("dve", lambda e: e.tensor_tensor(out=out_re[:].rearrange("p a b -> p (a b)"), in0=t1[:], in1=a_i[:], op=ALU.mult),
             reads=[t1, a_i], writes=[out_re])
        P.op("dve", lambda e: e.scalar_tensor_tensor(out=out_im[:].rearrange("p a b -> p (a b)"), in0=t1[:], scalar=float(sign), in1=t0[:],
                                                     op0=ALU.mult, op1=ALU.mult), reads=[t1, t0], writes=[out_im])

    def steps_row(a_r, a_i):
        for t in (a_r, a_i):
            P.op("dve", lambda e, t=t: e.tensor_scalar(out=t[:], in0=t[:], scalar1=C.iota_col[:, d:d + 1], scalar2=None, op0=ALU.mult),
                 reads=[t, C.iota_col], writes=[t])
    build(True, I["s5_ar_row"][l, d], I["s5_ai_row"][l, d], I["s5_dt_row"][l, d], steps_row, -1, K.Tin_re, K.Tin_im)

    def steps_col(a_r, a_i):
        for t in (a_r, a_i):
            P.op("dve", lambda e, t=t: e.tensor_tensor(out=t[:].rearrange("p (a b) -> p a b", a=16),
                                                       in0=t[:, 0:16].unsqueeze(2).to_broadcast([128, 16, 128]),
                                                       in1=C.iota_row[d][:].unsqueeze(1).to_broadcast([128, 16, 128]), op=ALU.mult),
                 reads=[t, C.iota_row[d]], writes=[t])
    ca, ci, cd = K.col_a, K.col_i, K.col_d

    def build_col():
        P.dma(ca[:], I["s5_ar_col"][l, d], writes=[ca]); P.dma(ci[:], I["s5_ai_col"][l, d], writes=[ci]); P.dma(cd[:], I["s5_dt_col"][l, d], writes=[cd])
        P.op("act", lambda e: e.activation(out=cd[:], in_=cd[:], func=AF.Exp), reads=[cd], writes=[cd])
        P.op("dve", lambda e: e.tensor_tensor(out=K.c_ardt[:], in0=ca[:], in1=cd[:], op=ALU.mult), reads=[ca, cd], writes=[K.c_ardt])
        P.op("dve", lambda e: e.scalar_tensor_tensor(out=K.c_frac[:], in0=ci[:], scalar=1.0 / TWO_PI, in1=cd[:], op0=ALU.mult, op1=ALU.mult),
             reads=[ci, cd], writes=[K.c_frac])
        round_frac(P, "dve", (K.c_frac[:], K.c_frac), (K.c_frac[:], K.c_frac), (K.c_tmp[:], K.c_tmp))
        P.op("dve", lambda e: e.tensor_tensor(out=a_r[:].rearrange("p (a b) -> p a b", a=16),
                                              in0=K.c_ardt[:].unsqueeze(2).to_broadcast([128, 16, 128]),
                                              in1=C.iota_row[d][:].unsqueeze(1).to_broadcast([128, 16, 128]), op=ALU.mult),
             reads=[K.c_ardt, C.iota_row[d]], writes=[a_r])
        P.op("dve", lambda e: e.tensor_tensor(out=a_i[:].rearrange("p (a b) -> p a b", a=16),
                                              in0=K.c_frac[:].unsqueeze(2).to_broadcast([128, 16, 128]),
                                              in1=C.iota_row[d][:].unsqueeze(1).to_broadcast([128, 16, 128]), op=ALU.mult),
             reads=[K.c_frac, C.iota_row[d]], writes=[a_i])
        sign = 1
        P.op("act", lambda e: e.activation(out=t1[:], in_=a_r[:], func=AF.Exp, scale=float(sign)), reads=[a_r], writes=[t1])
        round_frac(P, "dve", (t0[:], t0), (a_i[:], a_i), (t2[:], t2))
        P.op("act", lambda e: e.activation(out=t0[:], in_=t0[:], func=AF.Sin, scale=TWO_PI), reads=[t0], writes=[t0])
        P.op("dve", lambda e: e.tensor_scalar(out=a_i[:], in0=a_i[:], scalar1=0.25, scalar2=None, op0=ALU.add), reads=[a_i], writes=[a_i])
        round_frac(P, "dve", (a_i[:], a_i), (a_i[:], a_i), (t2[:], t2))
        P.op("act", lambda e: e.activation(out=a_i[:], in_=a_i[:], func=AF.Sin, scale=TWO_PI), reads=[a_i], writes=[a_i])
        P.op("dve", lambda e: e.tensor_tensor(out=K.Tout_re[:].rearrange("p a b -> p (a b)"), in0=t1[:], in1=a_i[:], op=ALU.mult),
             reads=[t1, a_i], writes=[K.Tout_re])
        P.op("dve", lambda e: e.tensor_tensor(out=K.Tout_im[:].rearrange("p a b -> p (a b)"), in0=t1[:], in1=t0[:], op=ALU.mult),
             reads=[t1, t0], writes=[K.Tout_im])
    build_col()

    s1, c1, m1, nr, dn, q_r, q_i, u0, u1 = [K.small[i] for i in range(9)]
    P.op("act", lambda e: e.activation(out=m1[:], in_=K.c_ardt[:], func=AF.Exp), reads=[K.c_ardt], writes=[m1])
    P.op("act", lambda e: e.activation(out=s1[:], in_=K.c_frac[:], func=AF.Sin, scale=TWO_PI), reads=[K.c_frac], writes=[s1])
    P.op("dve", lambda e: e.tensor_scalar(out=u0[:], in0=K.c_frac[:], scalar1=0.25, scalar2=None, op0=ALU.add), reads=[K.c_frac], writes=[u0])
    round_frac(P, "dve", (u0[:], u0), (u0[:], u0), (u1[:], u1))
    P.op("act", lambda e: e.activation(out=c1[:], in_=u0[:], func=AF.Sin, scale=TWO_PI), reads=[u0], writes=[c1])
    tt = lambda o, a, b, op, eng="dve": P.op(eng, lambda e: e.tensor_tensor(out=o[:], in0=a[:], in1=b[:], op=op), reads=[a, b], writes=[o])
    tt(c1, c1, m1, ALU.mult)
    tt(s1, s1, m1, ALU.mult)
    P.op("dve", lambda e: e.tensor_scalar(out=nr[:], in0=c1[:], scalar1=-1.0, scalar2=None, op0=ALU.add), reads=[c1], writes=[nr])
    tt(dn, ca, ca, ALU.mult); tt(u0, ci, ci, ALU.mult); tt(dn, dn, u0, ALU.add)
    P.op("dve", lambda e: e.reciprocal(out=dn[:], in_=dn[:]), reads=[dn], writes=[dn])
    tt(u0, nr, ca, ALU.mult); tt(u1, s1, ci, ALU.mult); tt(u0, u0, u1, ALU.add); tt(q_r, u0, dn, ALU.mult)
    tt(u0, s1, ca, ALU.mult); tt(u1, nr, ci, ALU.mult); tt(u0, u0, u1, ALU.subtract); tt(q_i, u0, dn, ALU.mult)
    bre, bim, bbr, bbi, tb = K.bre, K.bim, K.bbr, K.bbi, K.tb
    for (dst, ri) in ((bre, 0), (bim, 1)):
        P.op("pool", lambda e, dst=dst: e.memset(dst[:], 0.0), writes=[dst])
        P.dma(dst[0:64, :, 0:16], I["s5_b_col"][l, d, ri, 0:64], writes=[dst])
        P.dma(dst[64:128, :, 16:32], I["s5_b_col"][l, d, ri, 64:128], writes=[dst])
    bc = lambda q: q[:].unsqueeze(2).to_broadcast([128, 16, 32])
    P.op("dve", lambda e: e.tensor_tensor(out=bbr[:], in0=bre[:], in1=bc(q_r), op=ALU.mult), reads=[bre, q_r], writes=[bbr])
    P.op("dve", lambda e: e.tensor_tensor(out=tb[:], in0=bim[:], in1=bc(q_i), op=ALU.mult), reads=[bim, q_i], writes=[tb])
    tt(bbr, bbr, tb, ALU.subtract)
    P.op("dve", lambda e: e.tensor_tensor(out=bbi[:], in0=bim[:], in1=bc(q_r), op=ALU.mult), reads=[bim, q_r], writes=[bbi])
    P.op("dve", lambda e: e.tensor_tensor(out=tb[:], in0=bre[:], in1=bc(q_i), op=ALU.mult), reads=[bre, q_i], writes=[tb])
    tt(bbi, bbi, tb, ALU.add)
    zp = K.zp
    for z in zp:
        P.op("pool", lambda e, z=z: e.memset(z[:], 0.0), writes=[z])
    for (src, dst) in ((bbr, K.BT_re), (bbi, K.BT_im)):
        for ch in range(4):
            bank = C.gbank()
            for pl in range(4):
                copy_op(P, "dve", zp[pl][:, 32 * pl:32 * pl + 32], src[:, 4 * ch + pl, :], [src], [zp[pl]])
                P.op("pe", lambda e, bank=bank, pl=pl: e.transpose(out=bank[:, pl * 128:(pl + 1) * 128], in_=zp[pl][:], identity=C.ident[:]),
                     reads=[zp[pl], C.ident], writes=[bank], accum=(pl > 0))
            copy_op(P, "act", dst[:, ch, :], bank[:], [bank], [dst], accum=(ch > 0))
    for (dst, ri) in ((K.Cre, 0), (K.Cimn, 1)):
        P.op("pool", lambda e, dst=dst: e.memset(dst[:], 0.0), writes=[dst])
        P.dma(dst[0:64, :, 32:48], I["s5_c_col"][l, d, ri, 0:64], writes=[dst])
        P.dma(dst[64:128, :, 48:64], I["s5_c_col"][l, d, ri, 64:128], writes=[dst])
    P.op("dve", lambda e: e.tensor_scalar(out=K.Cimn[:], in0=K.Cimn[:], scalar1=-1.0, scalar2=None, op0=ALU.mult), reads=[K.Cimn], writes=[K.Cimn])


def s5_pass(P, C, l, d):
    half = -999
    NT = C.NT
    st = ExitStack(); P.stack = st
    I = C.inp
    K = Ctx()
    K.Tin_re = P.sb([128, 4, 512]); K.Tin_im = P.sb([128, 4, 512])
    K.Tout_re = P.sb([128, 16, 128]); K.Tout_im = P.sb([128, 16, 128])
    K.BT_re = P.sb([128, 4, 512], BF16); K.BT_im = P.sb([128, 4, 512], BF16)
    K.Cre = P.sb([128, 16, 64]); K.Cimn = P.sb([128, 16, 64])
    st2 = ExitStack(); P.stack = st2
    K.work = [P.sb([128, 2048]) for _ in range(6)]
    K.col_a = P.sb([128, 16]); K.col_i = P.sb([128, 16]); K.col_d = P.sb([128, 16])
    K.c_ardt = P.sb([128, 16]); K.c_frac = P.sb([128, 16]); K.c_tmp = P.sb([128, 16])
    K.small = [P.sb([128, 16]) for _ in range(9)]
    K.bre = P.sb([128, 16, 32]); K.bim = P.sb([128, 16, 32]); K.bbr = P.sb([128, 16, 32]); K.bbi = P.sb([128, 16, 32]); K.tb = P.sb([128, 16, 32])
    K.zp = [P.sb([128, 128]) for _ in range(4)]
    s5_tables(P, C, l, d, K)
    P.barrier()
    st2.close(); P.stack = st
    tri = C.tri[d]
    zero = C.zero_col
    uring = [P.sb([128, 512]) for _ in range(2)]
    uT = [P.sb([128, 4, 128], BF16) for _ in range(2)]
    g_re = [P.sb([128, 512], BF16) for _ in range(2)]; g_im = [P.sb([128, 512], BF16) for _ in range(2)]
    ta = [P.sb([128, 512]) for _ in range(2)]; tb = [P.sb([128, 512]) for _ in range(2)]
    ta2 = [P.sb([128, 512]) for _ in range(2)]; tb2 = [P.sb([128, 512]) for _ in range(2)]
    hre = [[P.sb([128, 128]) for _ in range(16)] for _ in range(2)]
    him = [[P.sb([128, 128]) for _ in range(16)] for _ in range(2)]
    r1 = [P.sb([128, 128]) for _ in range(2)]; r2 = [P.sb([128, 128]) for _ in range(2)]
    r3 = [P.sb([128, 128]) for _ in range(2)]; r4 = [P.sb([128, 128]) for _ in range(2)]
    cc = [[P.sb([128, 2]) for _ in range(16)] for _ in range(1)][0]
    ysb = [P.sb([128, 4, 128]) for _ in range(2)]
    ysT = C.ys5T.ap().rearrange("(k p) t -> p k t", p=128)
    bk = C.banks
    if d == 1:
        dcol = P.sb([128, 4]); gbcol = P.sb([128, 4])
        P.dma(dcol[:], I["s5_d_col"][l], writes=[dcol]); P.dma(gbcol[:], I["glu_b_col"][l], writes=[gbcol])
        wg = P.sb([128, 4, 512], BF16)
        wgs = [P.sb([128, 512]), P.sb([128, 512])]
        load_weight_bf16(P, C, wg, lambda c: I["s5_glu_w"][l, c * 128:(c + 1) * 128, :], 4, 512, wgs)
        yf = [P.sb([128, 4, 128]) for _ in range(2)]
        zt = [P.sb([128, 512]) for _ in range(2)]
        szT = P.sb([128, 4, 128])
        yv = P.sb([128, 4, 128]); x2 = P.sb([128, 4, 128]); sg = P.sb([128, 4, 128]); yg = P.sb([128, 4, 128]); ygb = P.sb([128, 4, 128], BF16)
        gs = P.sb([128, 4, 128]); ya = [P.sb([128, 4, 128], BF16) for _ in range(2)]
        mixT = C.mixT.ap().rearrange("(k p) t -> p k t", p=128)

    cin = P.sb([128, 32]); cbuf = P.sb([128, 32]); cin2 = P.sb([128, 2, 32])
    if d == 1:
        P.dma(cin2[:], C.s5dst.ap().rearrange("(s p) n -> p s n", s=2), reads=[C.s5dst_tr], writes=[cin2])
        P.op("dve", lambda e: e.tensor_scalar(out=cin[:], in0=cin2[:, 0, :], scalar1=C.flags[:, 0:1], scalar2=None, op0=ALU.mult), reads=[cin2, C.flags], writes=[cin])
        P.op("dve", lambda e: e.scalar_tensor_tensor(out=cin[:], in0=cin2[:, 1, :], scalar=C.flags[:, 1:2], in1=cin[:], op0=ALU.mult, op1=ALU.add),
             reads=[cin2, C.flags, cin], writes=[cin])
    order = list(range(NT)) if d == 0 else list(range(NT - 1, -1, -1))
    last = 127 if d == 0 else 0
    trib = P.sb([128, 128], BF16)
    copy_op(P, "dve", trib[:], tri[:], [tri], [trib])

    def prologue(it):
        i = order[it]; par = it % 2
        ut = uring[par]
        P.dma(ut[:], C.proj[i * 128:(i + 1) * 128, 0:512], reads=[C.proj_tr[i]], writes=[ut])
        if d == 1:
            P.dma(yf[par][:], ysT[:, :, i * 128:(i + 1) * 128], reads=[C.ys_tr[i]], writes=[yf[par]])
            P.dma(zt[par][:], C.proj[i * 128:(i + 1) * 128, 512:1024], reads=[C.proj_tr[i]], writes=[zt[par]])
        uTt = uT[par]
        for c in range(4):
            P.op("pe", lambda e, c=c: e.transpose(out=bk[0][:, c * 128:(c + 1) * 128], in_=ut[:, c * 128:(c + 1) * 128], identity=C.ident[:]),
                 reads=[ut, C.ident], writes=[bk[0]], accum=(c > 0))
        copy_op(P, "act", uTt[:].rearrange("p a b -> p (a b)"), bk[0][:], [bk[0]], [uTt])

    def front(it, ch):
        par = it % 2; cp = ch % 2
        uTt = uT[par]
        b1, b2 = (bk[6], bk[7]) if (d == 0 and ch % 2 == 1) else (bk[1], bk[2])
        P.op("pe", lambda e: e.matmul(b1[:], lhsT=uTt[:, ch, :], rhs=K.BT_re[:, ch, :], start=True, stop=True),
             reads=[uTt, K.BT_re], writes=[b1])
        P.op("pe", lambda e: e.matmul(b2[:], lhsT=uTt[:, ch, :], rhs=K.BT_im[:, ch, :], start=True, stop=True),
             reads=[uTt, K.BT_im], writes=[b2])
        Tr = K.Tin_re[:, ch, :]; Ti = K.Tin_im[:, ch, :]
        gr, gi, a_, b_, a2_, b2_ = g_re[cp], g_im[cp], ta[cp], tb[cp], ta2[cp], tb2[cp]
        P.op("dve", lambda e: e.tensor_tensor(out=a_[:], in0=b1[:], in1=Tr, op=ALU.mult), reads=[b1, K.Tin_re], writes=[a_])
        P.op("dve", lambda e: e.tensor_tensor(out=b_[:], in0=b2[:], in1=Ti, op=ALU.mult), reads=[b2, K.Tin_im], writes=[b_])
        P.op("pool", lambda e: e.tensor_tensor(out=gr[:], in0=a_[:], in1=b_[:], op=ALU.subtract), reads=[a_, b_], writes=[gr])
        P.op("dve", lambda e: e.tensor_tensor(out=a2_[:], in0=b1[:], in1=Ti, op=ALU.mult), reads=[b1, K.Tin_im], writes=[a2_])
        P.op("dve", lambda e: e.tensor_tensor(out=b2_[:], in0=b2[:], in1=Tr, op=ALU.mult), reads=[b2, K.Tin_re], writes=[b2_])
        P.op("pool", lambda e: e.tensor_tensor(out=gi[:], in0=a2_[:], in1=b2_[:], op=ALU.add), reads=[a2_, b2_], writes=[gi])

    def back(it, ch):
        par = it % 2; cp = ch % 2
        gr, gi = g_re[cp], g_im[cp]
        for pl in range(4):
            P.op("pe", lambda e, pl=pl: e.matmul(bk[3][:, pl * 128:(pl + 1) * 128], lhsT=gr[:, pl * 128:(pl + 1) * 128], rhs=trib[:],
                                                 start=True, stop=True), reads=[gr, trib], writes=[bk[3]], accum=(pl > 0))
        for pl in range(4):
            P.op("pe", lambda e, pl=pl: e.matmul(bk[4][:, pl * 128:(pl + 1) * 128], lhsT=gi[:, pl * 128:(pl + 1) * 128], rhs=trib[:],
                                                 start=True, stop=True), reads=[gi, trib], writes=[bk[4]], accum=(pl > 0))
        for pl in (0, 1, 3, 2):
            pp = 4 * ch + pl
            hr_prev, hi_prev = hre[1 - par][pp], him[1 - par][pp]
            hr, hi = hre[par][pp], him[par][pp]
            if it == 0 and d == 0:
                cr, ci_, crt = zero[:, 0:1], zero[:, 0:1], [zero]
            elif it == 0:
                cr, ci_, crt = cin[:, pp:pp + 1], cin[:, 16 + pp:17 + pp], [cin]
            else:
                cr, ci_, crt = hr_prev[:, last:last + 1], hi_prev[:, last:last + 1], [hr_prev, hi_prev]
            Gr = bk[3][:, pl * 128:(pl + 1) * 128]; Gi = bk[4][:, pl * 128:(pl + 1) * 128]
            Tor = K.Tout_re[:, pp, :]; Toi = K.Tout_im[:, pp, :]
            q1, q2, q3, q4 = r1[pl % 2], r2[pl % 2], r3[pl % 2], r4[pl % 2]
            P.op("dve", lambda e: e.scalar_tensor_tensor(out=q1[:], in0=Gr, scalar=cr, in1=Tor, op0=ALU.add, op1=ALU.mult),
                 reads=[bk[3], K.Tout_re] + crt, writes=[q1])
            P.op("dve", lambda e: e.scalar_tensor_tensor(out=q2[:], in0=Gi, scalar=ci_, in1=Toi, op0=ALU.add, op1=ALU.mult),
                 reads=[bk[4], K.Tout_im] + crt, writes=[q2])
            P.op("pool", lambda e: e.tensor_tensor(out=hr[:], in0=q1[:], in1=q2[:], op=ALU.subtract), reads=[q1, q2], writes=[hr])
            P.op("dve", lambda e: e.scalar_tensor_tensor(out=q3[:], in0=Gr, scalar=cr, in1=Toi, op0=ALU.add, op1=ALU.mult),
                 reads=[bk[3], K.Tout_im] + crt, writes=[q3])
            P.op("dve", lambda e: e.scalar_tensor_tensor(out=q4[:], in0=Gi, scalar=ci_, in1=Tor, op0=ALU.add, op1=ALU.mult),
                 reads=[bk[4], K.Tout_re] + crt, writes=[q4])
            P.op("pool", lambda e: e.tensor_tensor(out=hi[:], in0=q3[:], in1=q4[:], op=ALU.add), reads=[q3, q4], writes=[hi])
            if pl == 3:
                osl, csl, st0 = slice(64, 128), slice(0, 64), True
            elif pl == 2:
                osl, csl, st0 = slice(64, 96), slice(32, 64), False
            else:
                osl, csl, st0 = slice(32 * pl, 32 * pl + 32), slice(32, 64), True
            sgc = pl >= 2
            P.op("pe", lambda e: e.matmul(bk[5][osl, ch * 128:(ch + 1) * 128], lhsT=K.Cre[:, pp, csl], rhs=hr[:],
                                          start=st0, stop=False, skip_group_check=sgc), reads=[K.Cre, hr], writes=[bk[5]], accum=not (ch == 0 and pl == 0))
            P.op("pe", lambda e: e.matmul(bk[5][osl, ch * 128:(ch + 1) * 128], lhsT=K.Cimn[:, pp, csl], rhs=hi[:],
                                          start=False, stop=True, skip_group_check=sgc), reads=[K.Cimn, hi], writes=[bk[5]], accum=True)

    def epilogue(it):
        i = order[it]; par = it % 2
        uTt = uT[par]
        if d == 0:
            yt = ysb[par]
            copy_op(P, "act", yt[:].rearrange("p a b -> p (a b)"), bk[5][:], [bk[5]], [yt])
            P.dma(ysT[:, :, i * 128:(i + 1) * 128], yt[:], reads=[yt], writes=[C.ys_tr[i]], q="act")
        else:
            f = lambda t: t[:].rearrange("p a b -> p (a b)")
            P.op("dve", lambda e: e.tensor_tensor(out=f(yv), in0=bk[5][:], in1=f(yf[par]), op=ALU.add), reads=[bk[5], yf[par]], writes=[yv])
            for ch in range(4):
                P.op("dve", lambda e, ch=ch: e.scalar_tensor_tensor(out=yv[:, ch, :], in0=uTt[:, ch, :], scalar=dcol[:, ch:ch + 1], in1=yv[:, ch, :],
                                                                    op0=ALU.mult, op1=ALU.add), reads=[uTt, dcol, yv], writes=[yv], accum=True)
            P.op("act", lambda e: e.activation(out=f(yg), in_=f(yv), func=AF.Gelu_apprx_tanh), reads=[yv], writes=[yg])
            copy_op(P, "act", f(ygb), f(yg), [yg], [ygb])
            for co in range(4):
                for kc in range(4):
                    P.op("pe", lambda e, co=co, kc=kc: e.matmul(bk[6][:, co * 128:(co + 1) * 128], lhsT=wg[:, kc, co * 128:(co + 1) * 128],
                                                                rhs=ygb[:, kc, :], start=(kc == 0), stop=(kc == 3)),
                         reads=[wg, ygb], writes=[bk[6]], accum=not (co == 0 and kc == 0))
            for co in range(4):
                P.op("act", lambda e, co=co: e.activation(out=sg[:, co, :], in_=bk[6][:, co * 128:(co + 1) * 128], func=AF.Sigmoid,
                                                          bias=gbcol[:, co:co + 1]), reads=[bk[6], gbcol], writes=[sg], accum=(co > 0))
            for c in range(4):
                P.op("pe", lambda e, c=c: e.transpose(out=bk[7][:, c * 128:(c + 1) * 128], in_=zt[par][:, c * 128:(c + 1) * 128], identity=C.ident[:]),
                     reads=[zt[par], C.ident], writes=[bk[7]], accum=(c > 0))
            P.op("act", lambda e: e.activation(out=f(szT), in_=bk[7][:], func=AF.Silu), reads=[bk[7]], writes=[szT])
            P.op("dve", lambda e: e.tensor_tensor(out=f(gs), in0=f(yg), in1=f(sg), op=ALU.mult), reads=[yg, sg], writes=[gs])
            P.op("pool", lambda e: e.tensor_tensor(out=f(ya[par]), in0=f(gs), in1=f(szT), op=ALU.mult), reads=[gs, szT], writes=[ya[par]])
            P.dma(mixT[:, 0:4, i * 128:(i + 1) * 128], ya[par][:], reads=[ya[par]], writes=[C.mix_tr[i]], q="pool")

    units = [(it, ch) for it in range(NT) for ch in range(4)]
    prologue(0); front(0, 0)
    for u, (it, ch) in enumerate(units):
        if u + 1 < len(units):
            it2, ch2 = units[u + 1]
            if ch2 == 0:
                prologue(it2)
            front(it2, ch2)
        back(it, ch)
        if ch == 3:
            epilogue(it)
    if d == 0:
        parl = (NT - 1) % 2
        for pp in range(16):
            copy_op(P, ("dve", "pool")[pp % 2], cbuf[:, pp:pp + 1], hre[parl][pp][:, 127:128], [hre[parl][pp]], [cbuf], accum=(pp > 0))
            copy_op(P, ("pool", "dve")[pp % 2], cbuf[:, 16 + pp:17 + pp], him[parl][pp][:, 127:128], [him[parl][pp]], [cbuf], accum=True)
        P.dma(C.s5src[:, :], cbuf[:], reads=[cbuf], writes=[C.s5src_tr])
        P.collective(C.s5src[:, :], C.s5dst[:, :], reads=[C.s5src_tr], writes=[C.s5dst_tr])
    P.barrier()
    st.close(); P.stack = P.gstack


def even_layer(P, C, l, x_src, x_dst, xs_tr, xd_tr):
    import os
    dbg = int(os.environ.get("EDBG", "9"))
    even_phaseA(P, C, l, x_src, xs_tr)
    if dbg >= 1:
        s5_pass(P, C, l, 0)
        s5_pass(P, C, l, 1)
    if dbg >= 2:
        rwkv_pass(P, C, l, 0, x_src, x_dst, xs_tr, xd_tr)
        rwkv_pass(P, C, l, 1, x_src, x_dst, xs_tr, xd_tr)


def rwkv_pass(P, C, l, d, x_src, x_dst, xs_tr, xd_tr):
    NT = C.NT
    st = ExitStack(); P.stack = st
    I = C.inp
    bk = C.banks
    f32 = lambda shape: P.sb(shape)
    b16 = lambda shape: P.sb(shape, BF16)
    tt = lambda eng, o, a, b, op, rd, wr, accum=False: P.op(eng, lambda e: e.tensor_tensor(out=o, in0=a, in1=b, op=op), reads=rd, writes=wr, accum=accum)

    mu = f32([128, 1664]); P.dma(mu[:], I["rw_mu_rep"][l], writes=[mu])
    w0 = f32([128, 512]); P.dma(w0[:], I["rw_w0_rep"][l, d], writes=[w0])
    a0 = f32([128, 512]); P.dma(a0[:], I["rw_a0_rep"][l], writes=[a0])
    kkp = f32([128, 512]); P.dma(kkp[:], I["rw_k_k_rep"][l], writes=[kkp])
    kap = f32([128, 512]); P.dma(kap[:], I["rw_k_a_rep"][l], writes=[kap])
    ups = f32([128, 2, 512])
    P.dma(ups[0:64, 0, :], I["rw_w_up"][l, d], writes=[ups])
    P.dma(ups[64:128, 1, :], I["rw_a_up"][l], writes=[ups])
    triI = C.tri[d]; triE = C.triE[d]; triET = C.triE[1 - d]
    eye_b = C.ident
    if d == 1:
        rkp = f32([128, 512]); P.dma(rkp[:], I["rw_r_k_rep"][l], writes=[rkp])
        lng = f32([128, 512]); P.dma(lng[:], I["rw_ln_g_rep"][l], writes=[lng])
        lnb = f32([128, 512]); P.dma(lnb[:], I["rw_ln_b_rep"][l], writes=[lnb])
        wo = b16([128, 8, 1024])
        st2 = ExitStack(); P.stack = st2
        wst = [f32([128, 1024]), f32([128, 1024])]
        load_weight_bf16(P, C, wo, lambda c: I["ev_w_out"][l, c * 128:(c + 1) * 128, :], 8, 1024, wst)
        P.barrier()
        st2.close(); P.stack = st

    if d == 0:
        cur = [f32([128, 1664])] * 2; prv = [f32([128, 1664])] * 2; nxt = [f32([128, 1664])] * 2
        tsum = f32([128, 1664])
    else:
        cur = prv = nxt = [None, None]; tsum = None
    hsr = [f32([128, 1664]) for _ in range(2)]
    twla = f32([128, 128]); twlaT = f32([128, 128])
    a_t = f32([128, 512]); e2 = f32([128, 512]); tmp = f32([128, 512]); tmp2 = f32([128, 512])
    kkn = f32([128, 512]); pss = f32([128, 8]); prn = f32([128, 8])
    p_t = f32([128, 512]); q_t = f32([128, 512]); kpr = [f32([128, 512]) for _ in range(2)]
    GI = f32([128, 512]); GIinv = f32([128, 512]); GE = f32([128, 512])
    Pd = f32([128, 512]); Qd = f32([128, 512]); Kd = f32([128, 512]); Rd = f32([128, 512])
    Pdb = b16([128, 512]); Qdbr = [b16([128, 512]) for _ in range(2)]; Kdb = b16([128, 512]); Vbr = [b16([128, 512]) for _ in range(2)]
    PRr = [b16([64, 8, 2, 128]) for _ in range(2)]; QTt = b16([64, 8, 128]); KTt = b16([64, 8, 128])
    Bm = [b16([128, 8, 128]) for _ in range(2)]; Am = [b16([128, 8, 128]) for _ in range(2)]; Pmr = [b16([128, 8, 128]) for _ in range(2)]
    MqTr = [b16([128, 8, 128]) for _ in range(2)]; LkT = b16([128, 8, 128]); MkTr = [b16([128, 8, 128]) for _ in range(2)]
    LkVr = [f32([128, 512]) for _ in range(2)]; KVr = [f32([64, 8, 64]) for _ in range(2)]
    Z = f32([64, 8, 64]); Zb = b16([64, 8, 64]); ZK = f32([64, 8, 64]); ZKg = f32([64, 8, 64]); Ztmp = f32([64, 8, 64])
    gcolr = [f32([64, 8]) for _ in range(2)]; onescol = C.ones_col
    rhs_sb = b16([128, 512]); U_sb = b16([128, 512])
    ysb = [f32([128, 512]) for _ in range(2)]
    if d == 0:
        P.op("pool", lambda e: e.memset(Z[:], 0.0), writes=[Z])
        P.op("pool", lambda e: e.memset(Zb[:], 0.0), writes=[Zb])
        hb = P.sb([1, 2, 1664])
        P.collective(C.proj[NT * 128 - 1:NT * 128, 1024:2688], C.hdst[:, :], reads=[C.proj_tr[NT - 1]], writes=[C.hdst_tr])
        P.dma(hb[:], C.hdst.ap().rearrange("(o s) n -> o s n", o=1), reads=[C.hdst_tr], writes=[hb])
        P.op("dve", lambda e: e.tensor_scalar(out=C.hrow[:], in0=hb[:, 0, :], scalar1=C.flags[0:1, 0:1], scalar2=None, op0=ALU.mult), reads=[hb, C.flags], writes=[C.hrow])
        P.op("dve", lambda e: e.scalar_tensor_tensor(out=C.hrow[:], in0=hb[:, 1, :], scalar=C.flags[0:1, 1:2], in1=C.hrow[:], op0=ALU.mult, op1=ALU.add),
             reads=[hb, C.flags, C.hrow], writes=[C.hrow])
    else:
        z2 = P.sb([64, 2, 512])
        P.dma(z2[:], C.zdst.ap().rearrange("(s p) n -> p s n", s=2), reads=[C.zdst_tr], writes=[z2])
        zf_ = Z[:].rearrange("p a b -> p (a b)")
        P.op("dve", lambda e: e.tensor_scalar(out=zf_, in0=z2[:, 0, :], scalar1=C.flags[0:64, 0:1], scalar2=None, op0=ALU.mult), reads=[z2, C.flags], writes=[Z])
        P.op("dve", lambda e: e.scalar_tensor_tensor(out=zf_, in0=z2[:, 1, :], scalar=C.flags[0:64, 1:2], in1=zf_, op0=ALU.mult, op1=ALU.add),
             reads=[z2, C.flags, Z], writes=[Z])
        copy_op(P, "dve", Zb[:], Z[:], [Z], [Zb])
    if d == 1:
        yfw = [f32([128, 512]) for _ in range(2)]
        zrw = [f32([128, 512]) for _ in range(2)]
        xres = [f32([128, 1024]) for _ in range(2)]
        mean = f32([128, 8]); var = f32([128, 8]); cent = f32([128, 512]); rkk = f32([128, 512]); bon = f32([128, 8])
        yb = f32([128, 512]); szr = f32([128, 512])
        mixA = [b16([128, 4, 128]) for _ in range(2)]; ybT = b16([128, 4, 128])
        xo = [f32([128, 1024]) for _ in range(2)]
        mixT = C.mixT.ap().rearrange("(k p) t -> p k t", p=128)

    v3 = lambda t, n=8: t[:].rearrange("p (h d) -> p h d", h=n)
    order = list(range(NT)) if d == 0 else list(range(NT - 1, -1, -1))
    last = 127 if d == 0 else 0
    HW = C.hrw
    def front(it):
        i = order[it]; par = it % 2; r0 = i * 128
        PR, Pm, MqT, MkT, LkV, KV, Qdb, Vb, gcol, hs, kp = PRr[par], Pmr[par], MqTr[par], MkTr[par], LkVr[par], KVr[par], Qdbr[par], Vbr[par], gcolr[par], hsr[par], kpr[par]
        r_ap, k_ap, v_ap = hs[:, 0:512], hs[:, 512:1024], hs[:, 1024:1536]
        c_, p_, n_ = cur[par], prv[par], nxt[par]
        r_ap, k_ap, v_ap = hs[:, 0:512], hs[:, 512:1024], hs[:, 1024:1536]
        if d == 0:
            P.dma(c_[:], HW[r0:r0 + 128, 1024:2688], reads=[C.proj_tr[i]], writes=[c_])
            if i == 0:
                P.op("pool", lambda e: e.memset(p_[:], 0.0), writes=[p_])
                P.dma(p_[1:128, :], HW[r0:r0 + 127, 1024:2688], reads=[C.proj_tr[i]], writes=[p_])
            else:
                P.dma(p_[:], HW[r0 - 1:r0 + 127, 1024:2688], reads=[C.proj_tr[i], C.proj_tr[i - 1]], writes=[p_])
            if i == NT - 1:
                P.dma(n_[0:127, :], HW[r0 + 1:r0 + 128, 1024:2688], reads=[C.proj_tr[i]], writes=[n_])
                P.dma(n_[127:128, :], C.hrow[:], reads=[C.hrow], writes=[n_], accum=True)
            else:
                P.dma(n_[:], HW[r0 + 1:r0 + 129, 1024:2688], reads=[C.proj_tr[i], C.proj_tr[i + 1]], writes=[n_])
        if d == 1:
            P.dma(yfw[par][:], C.yrw[r0:r0 + 128, :], reads=[C.yrw_tr[i]], writes=[yfw[par]])
            P.dma(zrw[par][:], HW[r0:r0 + 128, 2688:3200], reads=[C.proj_tr[i]], writes=[zrw[par]])
            P.dma(xres[par][:], x_src[r0:r0 + 128, :], reads=[xs_tr[i]], writes=[xres[par]])
            P.dma(mixA[par][:], mixT[:, 0:4, r0:r0 + 128], reads=[C.mix_tr[i]], writes=[mixA[par]])
        if d == 0:
            tt("pool", tsum[:], p_[:], n_[:], ALU.add, [p_, n_], [tsum])
            P.op("dve", lambda e: e.scalar_tensor_tensor(out=tsum[:], in0=tsum[:], scalar=0.5, in1=c_[:], op0=ALU.mult, op1=ALU.subtract),
                 reads=[tsum, c_], writes=[tsum])
            tt("pool", tsum[:], tsum[:], mu[:], ALU.mult, [tsum, mu], [tsum])
            tt("dve", hs[:], tsum[:], c_[:], ALU.add, [tsum, c_], [hs])
            P.op("act", lambda e: e.activation(out=twla[:, 0:64], in_=hs[:, 1536:1600], func=AF.Tanh), reads=[hs], writes=[twla])
            copy_op(P, "dve", twla[:, 64:128], hs[:, 1600:1664], [hs], [twla], accum=True)
            g0 = C.gbank()
            P.op("pe", lambda e: e.transpose(out=g0[:, 0:128], in_=twla[:], identity=C.ident[:]), reads=[twla, C.ident], writes=[g0])
            copy_op(P, "dve", twlaT[:], g0[:, 0:128], [g0], [twlaT])
            g1 = C.gbank()
            P.op("pe", lambda e: e.matmul(g1[:], lhsT=twlaT[64:128, :], rhs=ups[64:128, 1, :], start=True, stop=True), reads=[twlaT, ups], writes=[g1])
            tt("dve", a_t[:], g1[:], a0[:], ALU.add, [g1, a0], [a_t])
            P.op("act", lambda e: e.activation(out=a_t[:], in_=a_t[:], func=AF.Sigmoid), reads=[a_t], writes=[a_t])
            tt("pool", kkn[:], k_ap, kkp[:], ALU.mult, [hs, kkp], [kkn])
            tt("pool", tmp[:], kkn[:], kkn[:], ALU.mult, [kkn], [tmp])
            P.op("dve", lambda e: e.tensor_reduce(out=pss[:], in_=v3(tmp), axis=AX.X, op=ALU.add), reads=[tmp], writes=[pss])
            P.op("act", lambda e: e.activation(out=pss[:], in_=pss[:], func=AF.Sqrt), reads=[pss], writes=[pss])
            P.op("dve", lambda e: e.tensor_scalar(out=pss[:], in0=pss[:], scalar1=1e-12, scalar2=None, op0=ALU.max), reads=[pss], writes=[pss])
            P.op("dve", lambda e: e.reciprocal(out=prn[:], in_=pss[:]), reads=[pss], writes=[prn])
            tt("dve", v3(p_t), v3(kkn), prn[:].unsqueeze(2).to_broadcast([128, 8, 64]), ALU.mult, [kkn, prn], [p_t])
            tt("pool", q_t[:], p_t[:], a_t[:], ALU.mult, [p_t, a_t], [q_t])
            P.op("dve", lambda e: e.scalar_tensor_tensor(out=tmp[:], in0=a_t[:], scalar=-1.0, in1=kap[:], op0=ALU.add, op1=ALU.mult), reads=[a_t, kap], writes=[tmp])
            P.op("dve", lambda e: e.scalar_tensor_tensor(out=kp[:], in0=tmp[:], scalar=1.0, in1=k_ap, op0=ALU.add, op1=ALU.mult), reads=[tmp, hs], writes=[kp])
            P.dma(C.rwc[r0:r0 + 128, 0:512], hs[:, 0:512], reads=[hs], writes=[C.rwc_tr[i]])
            P.dma(C.rwc[r0:r0 + 128, 512:1024], hs[:, 1024:1536], reads=[hs], writes=[C.rwc_tr[i]], accum=True)
            P.dma(C.rwc[r0:r0 + 128, 1024:1536], p_t[:], reads=[p_t], writes=[C.rwc_tr[i]], accum=True)
            P.dma(C.rwc[r0:r0 + 128, 1536:2048], q_t[:], reads=[q_t], writes=[C.rwc_tr[i]], accum=True)
            P.dma(C.rwc[r0:r0 + 128, 2048:2560], kp[:], reads=[kp], writes=[C.rwc_tr[i]], accum=True)
            P.dma(C.rwt[i], twlaT[0:64, :], reads=[twlaT], writes=[C.rwc_tr[i]], accum=True)
        else:
            P.dma(hs[:, 0:512], C.rwc[r0:r0 + 128, 0:512], reads=[C.rwc_tr[i]], writes=[hs])
            P.dma(hs[:, 1024:1536], C.rwc[r0:r0 + 128, 512:1024], reads=[C.rwc_tr[i]], writes=[hs], accum=True)
            P.dma(p_t[:], C.rwc[r0:r0 + 128, 1024:1536], reads=[C.rwc_tr[i]], writes=[p_t])
            P.dma(q_t[:], C.rwc[r0:r0 + 128, 1536:2048], reads=[C.rwc_tr[i]], writes=[q_t])
            P.dma(kp[:], C.rwc[r0:r0 + 128, 2048:2560], reads=[C.rwc_tr[i]], writes=[kp])
            P.dma(twlaT[0:64, :], C.rwt[i], reads=[C.rwc_tr[i]], writes=[twlaT])
        g2 = C.gbank()
        P.op("pe", lambda e: e.matmul(g2[:], lhsT=twlaT[0:64, :], rhs=ups[0:64, 0, :], start=True, stop=True), reads=[twlaT, ups], writes=[g2])
        tt("dve", e2[:], g2[:], w0[:], ALU.add, [g2, w0], [e2])
        P.op("act", lambda e: e.activation(out=e2[:], in_=e2[:], func=AF.Exp, scale=-1.0), reads=[e2], writes=[e2])
        P.op("act", lambda e: e.activation(out=e2[:], in_=e2[:], func=AF.Ln, bias=1.0), reads=[e2], writes=[e2])
        P.op("act", lambda e: e.activation(out=e2[:], in_=e2[:], func=AF.Exp, scale=-1.0, bias=-0.5), reads=[e2], writes=[e2])
        copy_op(P, "act", Vb[:], v_ap, [hs], [Vb])
        yield
        gI = C.gbank()
        P.op("pe", lambda e: e.matmul(gI[:], lhsT=triI[:], rhs=e2[:], start=True, stop=True), reads=[triI, e2], writes=[gI])
        P.op("act", lambda e: e.activation(out=GI[:], in_=gI[:], func=AF.Exp, scale=-1.0), reads=[gI], writes=[GI])
        P.op("act", lambda e: e.activation(out=GIinv[:], in_=gI[:], func=AF.Exp), reads=[gI], writes=[GIinv])
        gE = C.gbank()
        P.op("pe", lambda e: e.matmul(gE[:], lhsT=triE[:], rhs=e2[:], start=True, stop=True), reads=[triE, e2], writes=[gE])
        P.op("act", lambda e: e.activation(out=GE[:], in_=gE[:], func=AF.Exp, scale=-1.0), reads=[gE], writes=[GE])
        gT = C.gbank()
        for h in range(8):
            P.op("pe", lambda e, h=h: e.matmul(gT[0:64, h:h + 1], lhsT=e2[:, h * 64:(h + 1) * 64], rhs=onescol[:, 0:1], start=True, stop=True),
                 reads=[e2, onescol], writes=[gT], accum=(h > 0))
        P.op("act", lambda e: e.activation(out=gcol[:], in_=gT[0:64, 0:8], func=AF.Exp, scale=-1.0), reads=[gT], writes=[gcol])
        tt("dve", Pd[:], p_t[:], GE[:], ALU.mult, [p_t, GE], [Pd])
        tt("pool", Qd[:], q_t[:], GIinv[:], ALU.mult, [q_t, GIinv], [Qd])
        tt("dve", Kd[:], kp[:], GIinv[:], ALU.mult, [kp, GIinv], [Kd])
        tt("pool", Rd[:], r_ap, GI[:], ALU.mult, [hs, GI], [Rd])
        copy_op(P, "act", Pdb[:], Pd[:], [Pd], [Pdb]); copy_op(P, "dve", Qdb[:], Qd[:], [Qd], [Qdb]); copy_op(P, "act", Kdb[:], Kd[:], [Kd], [Kdb])
        yield
        for (src, dstfn, dstt) in ((Pd, lambda h: PR[:, h, 0, :], PR), (Rd, lambda h: PR[:, h, 1, :], PR), (Qd, lambda h: QTt[:, h, :], QTt), (Kd, lambda h: KTt[:, h, :], KTt)):
            for hb in range(2):
                g = C.gbank()
                for hl in range(4):
                    h = 4 * hb + hl
                    P.op("pe", lambda e, hl=hl, h=h, g=g, src=src: e.transpose(out=g[0:64, hl * 128:(hl + 1) * 128], in_=src[:, h * 64:(h + 1) * 64], identity=C.ident[:]),
                         reads=[src, C.ident], writes=[g], accum=(hl > 0))
                if dstt is PR:
                    which = 0 if src is Pd else 1
                    copy_op(P, ("dve", "act")[hb], PR[:, 4 * hb:4 * hb + 4, which, :], g[0:64, :].rearrange("p (a b) -> p a b", a=4), [g], [PR], accum=True)
                else:
                    copy_op(P, ("act", "dve")[hb], dstt[:, 4 * hb:4 * hb + 4, :], g[0:64, :].rearrange("p (a b) -> p a b", a=4), [g], [dstt], accum=(hb > 0))
        yield
        Bc, Ac = Bm[0], Am[0]
        for hg in range(4):
            gq = C.gbank(); gk = C.gbank()
            for hh in range(2):
                h = 2 * hg + hh
                P.op("pe", lambda e: e.matmul(gq[:, hh * 256:(hh + 1) * 256], lhsT=QTt[:, h, :],
                                              rhs=PR[:, h, :, :].rearrange("p a b -> p (a b)"), start=True, stop=True),
                     reads=[QTt, PR], writes=[gq], accum=(hh > 0))
                P.op("pe", lambda e: e.matmul(gk[:, hh * 256:(hh + 1) * 256], lhsT=KTt[:, h, :],
                                              rhs=PR[:, h, :, :].rearrange("p a b -> p (a b)"), start=True, stop=True),
                     reads=[KTt, PR], writes=[gk], accum=(hh > 0))
            gq4 = gq[:].rearrange("p (h a b) -> p h a b", h=2, a=2)
            gk4 = gk[:].rearrange("p (h a b) -> p h a b", h=2, a=2)
            hs2 = slice(2 * hg, 2 * hg + 2)
            mE = triE[:].unsqueeze(1).to_broadcast([128, 2, 128]); mI = triI[:].unsqueeze(1).to_broadcast([128, 2, 128])
            tt("dve", Bc[:, hs2, :], gq4[:, :, 0, :], mE, ALU.mult, [gq, triE], [Bc], accum=(hg > 0))
            tt("dve", MqT[:, hs2, :], gq4[:, :, 1, :], mI, ALU.mult, [gq, triI], [MqT], accum=(hg > 0))
            tt("dve", LkT[:, hs2, :], gk4[:, :, 0, :], mE, ALU.mult, [gk, triE], [LkT], accum=(hg > 0))
            tt("dve", MkT[:, hs2, :], gk4[:, :, 1, :], mI, ALU.mult, [gk, triI], [MkT], accum=(hg > 0))
        for hb in range(2):
            g = C.gbank()
            for hl in range(4):
                h = 4 * hb + hl
                P.op("pe", lambda e: e.matmul(g[:, hl * 128:(hl + 1) * 128], lhsT=PR[:, h, 0, :], rhs=QTt[:, h, :],
                                              start=True, stop=True), reads=[PR, QTt], writes=[g], accum=(hl > 0))
            tt("dve", Ac[:, 4 * hb:4 * hb + 4, :], g[:].rearrange("p (h b) -> p h b", h=4), triET[:].unsqueeze(1).to_broadcast([128, 4, 128]), ALU.mult,
               [g, triET], [Ac], accum=(hb > 0))
        yield
        P.op("dve", lambda e: e.scalar_tensor_tensor(out=Pm[:], in0=Bc[:], scalar=-1.0, in1=eye_b[:].unsqueeze(1).to_broadcast([128, 8, 128]),
                                                     op0=ALU.mult, op1=ALU.add), reads=[Bc, eye_b], writes=[Pm])
        for lev in range(6):
            Bn, An = Bm[(lev + 1) % 2], Am[(lev + 1) % 2]
            for hb in range(2):
                gA = C.gbank()
                for hl in range(4):
                    h = 4 * hb + hl
                    P.op("pe", lambda e: e.matmul(gA[:, hl * 128:(hl + 1) * 128], lhsT=Bc[:, h, :], rhs=Ac[:, h, :], start=True, stop=True),
                         reads=[Bc, Ac], writes=[gA], accum=(hl > 0))
                copy_op(P, ("act", "dve")[hb], An[:, 4 * hb:4 * hb + 4, :].rearrange("p a b -> p (a b)"), gA[:], [gA], [An], accum=(hb > 0))
                if lev < 5:
                    gB = C.gbank()
                    for hl in range(4):
                        h = 4 * hb + hl
                        P.op("pe", lambda e: e.matmul(gB[:, hl * 128:(hl + 1) * 128], lhsT=Ac[:, h, :], rhs=Bc[:, h, :], start=True, stop=True),
                             reads=[Bc, Ac], writes=[gB], accum=(hl > 0))
                    copy_op(P, ("dve", "act")[hb], Bn[:, 4 * hb:4 * hb + 4, :].rearrange("p a b -> p (a b)"), gB[:], [gB], [Bn], accum=(hb > 0))
            for hb in range(2):
                gP = C.gbank()
                for hl in range(4):
                    h = 4 * hb + hl
                    P.op("pe", lambda e: e.matmul(gP[:, hl * 128:(hl + 1) * 128], lhsT=An[:, h, :], rhs=Pm[:, h, :], start=True, stop=True),
                         reads=[An, Pm], writes=[gP], accum=(hl > 0))
                tt("dve", Pm[:, 4 * hb:4 * hb + 4, :].rearrange("p a b -> p (a b)"), gP[:], Pm[:, 4 * hb:4 * hb + 4, :].rearrange("p a b -> p (a b)"),
                   ALU.add, [gP, Pm], [Pm], accum=True)
            Bc, Ac = Bn, An
            yield
        yield
        g = C.gbank()
        for h in range(8):
            P.op("pe", lambda e: e.matmul(g[:, h * 64:(h + 1) * 64], lhsT=LkT[:, h, :], rhs=Vb[:, h * 64:(h + 1) * 64], start=True, stop=True),
                 reads=[LkT, Vb], writes=[g], accum=(h > 0))
        copy_op(P, "act", LkV[:], g[:], [g], [LkV])
        g = C.gbank()
        for h in range(8):
            P.op("pe", lambda e: e.matmul(g[0:64, h * 64:(h + 1) * 64], lhsT=Kdb[:, h * 64:(h + 1) * 64], rhs=Vb[:, h * 64:(h + 1) * 64],
                                          start=True, stop=True), reads=[Kdb, Vb], writes=[g], accum=(h > 0))
        copy_op(P, "dve", KV[:].rearrange("p a b -> p (a b)"), g[0:64, :], [g], [KV])

    def back(it):
        i = order[it]; par = it % 2; r0 = i * 128
        PR, Pm, MqT, MkT, LkV, KV, Qdb, Vb, gcol, hs, kp = PRr[par], Pmr[par], MqTr[par], MkTr[par], LkVr[par], KVr[par], Qdbr[par], Vbr[par], gcolr[par], hsr[par], kpr[par]
        r_ap, k_ap, v_ap = hs[:, 0:512], hs[:, 512:1024], hs[:, 1024:1536]
        gcb = gcol[:].unsqueeze(2).to_broadcast([64, 8, 64])
        tt("pool", ZK[:], Z[:], KV[:], ALU.add, [Z, KV], [ZK])
        tt("pool", ZKg[:], ZK[:], gcb, ALU.mult, [ZK, gcol], [ZKg])
        gz = bk[3]
        for h in range(8):
            P.op("pe", lambda e: e.matmul(gz[:, h * 64:(h + 1) * 64], lhsT=PR[:, h, 0, :], rhs=Zb[:, h, :], start=True, stop=True),
                 reads=[PR, Zb], writes=[gz], accum=(h > 0))
        tt("dve", rhs_sb[:], gz[:], LkV[:], ALU.add, [gz, LkV], [rhs_sb])
        yield
        gu = bk[4]
        for h in range(8):
            P.op("pe", lambda e: e.matmul(gu[:, h * 64:(h + 1) * 64], lhsT=Pm[:, h, :], rhs=rhs_sb[:, h * 64:(h + 1) * 64], start=True, stop=True),
                 reads=[Pm, rhs_sb], writes=[gu], accum=(h > 0))
        P.op("act", lambda e: e.activation(out=U_sb[:], in_=gu[:], func=AF.Copy, scale=-1.0), reads=[gu], writes=[U_sb])
        yield
        gy = bk[5]
        for h in range(8):
            osl = gy[:, h * 64:(h + 1) * 64]
            P.op("pe", lambda e: e.matmul(osl, lhsT=PR[:, h, 1, :], rhs=Zb[:, h, :], start=True, stop=False),
                 reads=[PR, Zb], writes=[gy], accum=(h > 0))
            P.op("pe", lambda e: e.matmul(osl, lhsT=MqT[:, h, :], rhs=U_sb[:, h * 64:(h + 1) * 64], start=False, stop=False),
                 reads=[MqT, U_sb], writes=[gy], accum=True)
            P.op("pe", lambda e: e.matmul(osl, lhsT=MkT[:, h, :], rhs=Vb[:, h * 64:(h + 1) * 64], start=False, stop=True),
                 reads=[MkT, Vb], writes=[gy], accum=True)
        gq_ = bk[6]
        for h in range(8):
            P.op("pe", lambda e: e.matmul(gq_[0:64, h * 64:(h + 1) * 64], lhsT=Qdb[:, h * 64:(h + 1) * 64], rhs=U_sb[:, h * 64:(h + 1) * 64],
                                          start=True, stop=True), reads=[Qdb, U_sb], writes=[gq_], accum=(h > 0))
        tt("dve", Ztmp[:], gq_[0:64, :].rearrange("p (a b) -> p a b", a=8), gcb, ALU.mult, [gq_, gcol], [Ztmp])
        tt("dve", Z[:], Ztmp[:], ZKg[:], ALU.add, [Ztmp, ZKg], [Z])
        copy_op(P, "dve", Zb[:], Z[:], [Z], [Zb])
        yield
        if d == 0:
            yt = ysb[par]
            copy_op(P, "act", yt[:], gy[:], [gy], [yt])
            P.dma(C.yrw[r0:r0 + 128, :], yt[:], reads=[yt], writes=[C.yrw_tr[i]], q="act")
            if it == NT - 1:
                P.dma(C.zsrc[:, :], Z[:].rearrange("p a b -> p (a b)"), reads=[Z], writes=[C.zsrc_tr])
                P.collective(C.zsrc[:, :], C.zdst[:, :], reads=[C.zsrc_tr], writes=[C.zdst_tr])
        else:
            y = ysb[par]
            tt("dve", y[:], gy[:], yfw[par][:], ALU.add, [gy, yfw[par]], [y])
            P.op("dve", lambda e: e.tensor_reduce(out=mean[:], in_=v3(y), axis=AX.X, op=ALU.add), reads=[y], writes=[mean])
            P.op("dve", lambda e: e.tensor_scalar(out=mean[:], in0=mean[:], scalar1=1.0 / 64, scalar2=None, op0=ALU.mult), reads=[mean], writes=[mean])
            tt("dve", v3(cent), v3(y), mean[:].unsqueeze(2).to_broadcast([128, 8, 64]), ALU.subtract, [y, mean], [cent])
            tt("pool", tmp2[:], cent[:], cent[:], ALU.mult, [cent], [tmp2])
            P.op("dve", lambda e: e.tensor_reduce(out=var[:], in_=v3(tmp2), axis=AX.X, op=ALU.add), reads=[tmp2], writes=[var])
            P.op("act", lambda e: e.activation(out=var[:], in_=var[:], func=AF.Sqrt, scale=1.0 / 64, bias=64e-5), reads=[var], writes=[var])
            P.op("dve", lambda e: e.reciprocal(out=var[:], in_=var[:]), reads=[var], writes=[var])
            tt("dve", v3(cent), v3(cent), var[:].unsqueeze(2).to_broadcast([128, 8, 64]), ALU.mult, [cent, var], [cent])
            tt("pool", cent[:], cent[:], lng[:], ALU.mult, [cent, lng], [cent])
            tt("pool", cent[:], cent[:], lnb[:], ALU.add, [cent, lnb], [cent])
            tt("pool", rkk[:], r_ap, kp[:], ALU.mult, [hs, kp], [rkk])
            tt("pool", rkk[:], rkk[:], rkp[:], ALU.mult, [rkk, rkp], [rkk])
            P.op("dve", lambda e: e.tensor_reduce(out=bon[:], in_=v3(rkk), axis=AX.X, op=ALU.add), reads=[rkk], writes=[bon])
            tt("dve", v3(rkk), hs[:, 1024:1536].rearrange("p (h d) -> p h d", h=8), bon[:].unsqueeze(2).to_broadcast([128, 8, 64]), ALU.mult, [hs, bon], [rkk])
            tt("pool", cent[:], cent[:], rkk[:], ALU.add, [cent, rkk], [cent])
            P.op("act", lambda e: e.activation(out=szr[:], in_=zrw[par][:], func=AF.Silu), reads=[zrw[par]], writes=[szr])
            tt("dve", yb[:], cent[:], szr[:], ALU.mult, [cent, szr], [yb])
            yield
            transpose_to(P, C, yb, 4, ybT)
            xot = xo[par]
            for gcol_i in range(2):
                bank = C.gbank()
                for c in range(8):
                    lhs = mixA[par][:, c, :] if c < 4 else ybT[:, c - 4, :]
                    P.op("pe", lambda e, c=c, lhs=lhs, bank=bank: e.matmul(bank[:], lhsT=lhs, rhs=wo[:, c, gcol_i * 512:(gcol_i + 1) * 512],
                                                                          start=(c == 0), stop=(c == 7)),
                         reads=[mixA[par], ybT, wo], writes=[bank], accum=(c > 0))
                tt("dve", xot[:, gcol_i * 512:(gcol_i + 1) * 512], bank[:], xres[par][:, gcol_i * 512:(gcol_i + 1) * 512], ALU.add,
                   [bank, xres[par]], [xot], accum=(gcol_i > 0))
            P.dma(x_dst[r0:r0 + 128, :], xot[:], reads=[xot], writes=[xd_tr[i]], q="act")

    import os
    if os.environ.get("NOSKEW"):
        for it in range(NT):
            for _ in front(it):
                pass
            for _ in back(it):
                pass
    else:
        for _ in front(0):
            pass
        for it in range(NT):
            gf = front(it + 1) if it + 1 < NT else iter(())
            gb = back(it)
            fdone = bdone = False
            while not (fdone and bdone):
                if not fdone:
                    try:
                        next(gf)
                    except StopIteration:
                        fdone = True
                if not bdone:
                    try:
                        next(gb)
                    except StopIteration:
                        bdone = True
    P.barrier()
    st.close(); P.stack = P.gstack


NT_FULL = 64
LAYERS = [("even", 0), ("odd", 0), ("even", 1), ("odd", 1)]
DIR_KEYS = ("s5_ar_row", "s5_ai_row", "s5_dt_row", "s5_ar_col", "s5_ai_col", "s5_dt_col", "s5_b_col", "s5_c_col", "rw_w0_rep", "rw_w_up")


def core_flags(w0, w1):
    fl = np.zeros((128, 4), np.float32)
    fl[:, 0] = w0
    fl[:, 1] = w1
    fl[:, 2] = 0.0 if (w0 + w1) > 0 else NEG
    return fl


def core_maps(m, streams):
    mrev = dict(m)
    for k in DIR_KEYS:
        mrev[k] = np.ascontiguousarray(m[k][:, ::-1])
    maps = []
    for (x, rev, w0, w1) in streams:
        mm = dict(mrev if rev else m)
        mm["flags"] = core_flags(w0, w1)
        mm["xin"] = np.ascontiguousarray(x[::-1] if rev else x)
        maps.append(mm)
    return maps


def kernel(**inputs):
    xp = np.asarray(inputs["x_prompt"], np.float32)
    xs = np.asarray(inputs["x_sample"], np.float32)
    m = host_layout(inputs, LAYERS)
    nc, gst = build_program(NT_FULL, LAYERS)
    streams = [(xs[0, 0:8192], False, 0.0, 1.0), (xs[0, 8192:16384], True, 1.0, 0.0)]
    for b in range(4):
        streams.append((xp[b], False, 0.0, 0.0))
    streams += [(xp[0], False, 0.0, 0.0), (xp[1], False, 0.0, 0.0)]
    maps = core_maps(m, streams)
    res = run_bass_kernel_spmd(nc, maps, core_ids=list(range(8)))
    outs = [np.asarray(res.results[c]["xout"], np.float32) for c in range(6)]
    y_sample = np.concatenate([outs[0], outs[1][::-1]], axis=0).reshape(1, 16384, D)
    y_prompt = np.stack(outs[2:6], axis=0)
    return (y_prompt, y_sample)
```

```python
import numpy as np
from contextlib import ExitStack
import concourse.bass as bass
import concourse.mybir as mybir
from concourse.bass_utils import run_bass_kernel_spmd

F32 = mybir.dt.float32
BF16 = mybir.dt.bfloat16
AF = mybir.ActivationFunctionType
ALU = mybir.AluOpType
AX = mybir.AxisListType

import os as _os
N_DMA_SLOTS = int(_os.environ.get("NSLOTS", "24"))
D = 1024
EPS = 1e-6
NEG = -30000.0


import types


def _snap(fn):
    if fn.__closure__ is None:
        return fn
    cells = tuple(types.CellType(c.cell_contents) for c in fn.__closure__)
    return types.FunctionType(fn.__code__, fn.__globals__, fn.__name__, fn.__defaults__, cells)


class T:
    __slots__ = ("t", "name", "writers", "readers", "war")

    def __init__(self, t, name=""):
        self.t = t
        self.name = name
        self.writers = []
        self.readers = []
        self.war = []

    def __getitem__(self, idx):
        return self.t[idx]


class Prog:
    ENGS = ("pe", "act", "dve", "pool", "sp")

    def __init__(self, nc, stack):
        self.nc = nc
        self.stack = stack
        self.gstack = stack
        self.ops = {e: [] for e in self.ENGS}
        self.cnt = {e: 0 for e in self.ENGS}
        self.seen = {e: {} for e in self.ENGS}
        self.sems = {e: stack.enter_context(nc.semaphore("s_" + e)) for e in self.ENGS}
        self.dma_sems = [stack.enter_context(nc.semaphore("s_dma%d" % i)) for i in range(N_DMA_SLOTS)]
        self.dma_n = 0
        self.cc_n = 0
        self.sems["cc"] = stack.enter_context(nc.semaphore("s_cc"))
        import os
        self.same_engine_sync = not os.environ.get("NOSES")
        self._uid = 0

    def sb(self, shape, dt=F32, name=None):
        self._uid += 1
        name = "sb%d" % self._uid
        return T(self.stack.enter_context(self.nc.sbuf_tensor(name, list(shape), dt)), name)

    def ps(self, shape, dt=F32):
        self._uid += 1
        name = "ps%d" % self._uid
        return T(self.stack.enter_context(self.nc.psum_tensor(name, list(shape), dt)), name)

    def dram(self, name, shape, dt=F32):
        return self.nc.dram_tensor(name, list(shape), dt, kind="Internal")

    def _need(self, eng, dep, waits):
        key, val, deng = dep
        if deng == eng and (eng == "pe" or not self.same_engine_sync):
            return
        if self.seen[eng].get(key, -1) >= val:
            return
        self.seen[eng][key] = val
        waits.append((key, val))

    def _deps(self, eng, reads, writes, accum):
        waits = []
        for t in reads:
            for w in t.writers:
                self._need(eng, w, waits)
        for t in writes:
            if not accum:
                for w in t.writers:
                    self._need(eng, w, waits)
            else:
                for w in t.war:
                    self._need(eng, w, waits)
            for r in t.readers:
                self._need(eng, r, waits)
        return waits

    def _commit(self, tok, reads, writes, accum):
        for t in reads:
            t.readers.append(tok)
        for t in writes:
            if accum:
                t.writers.append(tok)
                t.war = t.war + t.readers
            else:
                t.war = t.writers + t.readers
                t.writers = [tok]
            t.readers = []

    def _sem(self, key):
        return self.sems[key] if isinstance(key, str) else self.dma_sems[key]

    def op(self, eng, fn, reads=(), writes=(), accum=False):
        import os
        if eng == "pool" and os.environ.get("NOPOOL"):
            eng = "dve"
        kmax = int(os.environ.get("KMAX", "0"))
        if kmax and sum(self.cnt.values()) >= kmax:
            return None
        waits = self._deps(eng, reads, writes, accum)
        self.cnt[eng] += 1
        tok = (eng, self.cnt[eng], eng)
        self._commit(tok, reads, writes, accum)
        self.ops[eng].append((waits, _snap(fn), (eng, 1)))
        return tok

    def dma(self, out_ap, in_ap, reads=(), writes=(), q="sp", accum=False):
        waits = self._deps(q, reads, writes, accum)
        i = self.dma_n
        self.dma_n += 1
        slot = i % N_DMA_SLOTS
        val = 16 * (i // N_DMA_SLOTS + 1)
        if i >= N_DMA_SLOTS and self.seen[q].get(slot, -1) < val - 16:
            self.seen[q][slot] = val - 16
            waits.append((slot, val - 16))
        tok = (slot, val, "dma")
        self._commit(tok, reads, writes, accum)

        def fn(e, out_ap=out_ap, in_ap=in_ap):
            return e.dma_start(out=out_ap, in_=in_ap)
        self.ops[q].append((waits, fn, (slot, 16)))
        return tok

    def collective(self, src_ap, dst_ap, reads=(), writes=(), groups=((0, 1), (2, 3), (4, 5), (6, 7))):
        q = "pool"
        waits = self._deps(q, reads, writes, False)
        self.cc_n += 1
        tok = ("cc", self.cc_n, "cc")
        self._commit(tok, reads, writes, False)
        rg = [list(g) for g in groups]

        def fn(e):
            return e.collective_compute("AllGather", ALU.bypass, replica_groups=rg, ins=[src_ap], outs=[dst_ap])
        self.ops[q].append((waits, fn, ("cc", 1)))
        return tok

    def barrier(self):
        for e in self.ENGS:
            waits = []
            for o in self.ENGS:
                if o != e and self.cnt[o] > 0 and self.seen[e].get(o, -1) < self.cnt[o]:
                    self.seen[e][o] = self.cnt[o]
                    waits.append((o, self.cnt[o]))
            if self.cc_n > 0 and self.seen[e].get("cc", -1) < self.cc_n:
                self.seen[e]["cc"] = self.cc_n
                waits.append(("cc", self.cc_n))
            n = self.dma_n
            for slot in range(min(n, N_DMA_SLOTS)):
                last_i = ((n - 1 - slot) // N_DMA_SLOTS) * N_DMA_SLOTS + slot
                v = 16 * (last_i // N_DMA_SLOTS + 1)
                if self.seen[e].get(slot, -1) < v:
                    self.seen[e][slot] = v
                    waits.append((slot, v))
            if waits:
                self.ops[e].append((waits, None, None))

    def emit(self):
        nc = self.nc
        self.barrier()
        block = self.gstack.enter_context(nc.Block())
        prog = self

        def run(engname, e):
            for waits, fn, inc in prog.ops[engname]:
                for key, val in waits:
                    e.wait_ge(prog._sem(key), val)
                if fn is not None:
                    fn(e).then_inc(prog._sem(inc[0]), inc[1])

        @block.tensor
        def _(e):
            run("pe", e)

        @block.scalar
        def _(e):
            run("act", e)

        @block.vector
        def _(e):
            run("dve", e)

        @block.gpsimd
        def _(e):
            run("pool", e)

        @block.sync
        def _(e):
            run("sp", e)


class Ctx:
    pass


def rr(P, C, key, engs):
    C.rr[key] = C.rr.get(key, -1) + 1
    return engs[C.rr[key] % len(engs)]


def copy_op(P, eng, out_ap, in_ap, reads, writes, accum=False):
    if eng == "act":
        P.op("act", lambda e: e.activation(out=out_ap, in_=in_ap, func=AF.Copy), reads=reads, writes=writes, accum=accum)
    elif eng == "dve":
        P.op("dve", lambda e: e.tensor_copy(out=out_ap, in_=in_ap), reads=reads, writes=writes, accum=accum)
    else:
        P.op("pool", lambda e: e.tensor_copy(out=out_ap, in_=in_ap), reads=reads, writes=writes, accum=accum)


def load_weight_bf16(P, C, dst, src_ap_fn, nchunk, ncols, stage):
    for c in range(nchunk):
        s = stage[c % len(stage)]
        P.dma(s[:, 0:ncols], src_ap_fn(c), reads=[], writes=[s])
        eng = ("act", "dve", "pool")[c % 3]
        copy_op(P, eng, dst[:, c, :], s[:, 0:ncols], [s], [dst], accum=True)


def rmsnorm_T(P, C, xt, gn, hT, S):
    junk, ss, ss2, rs, h = S.junk, S.ss, S.ss2, S.rs, S.h
    P.op("act", lambda e: e.activation(out=junk[:], in_=xt[:], func=AF.Square, accum_out=ss[:]),
         reads=[xt], writes=[junk, ss])
    P.op("act", lambda e: e.activation(out=ss2[:], in_=ss[:], func=AF.Sqrt, scale=1.0 / D, bias=EPS),
         reads=[ss], writes=[ss2])
    P.op("dve", lambda e: e.reciprocal(out=rs[:], in_=ss2[:]), reads=[ss2], writes=[rs])
    P.op("dve", lambda e: e.scalar_tensor_tensor(out=h[:], in0=xt[:], scalar=rs[:, 0:1], in1=gn[:],
                                                 op0=ALU.mult, op1=ALU.mult), reads=[xt, rs, gn], writes=[h])
    transpose_to(P, C, h, 8, hT)


def transpose_to(P, C, src, nblk, dst, src_off=0):
    for g0 in range(0, nblk, 4):
        n = min(4, nblk - g0)
        bank = C.gbank()
        for c in range(n):
            P.op("pe", lambda e, c=c, bank=bank, g0=g0: e.transpose(
                out=bank[:, c * 128:(c + 1) * 128],
                in_=src[:, src_off + (g0 + c) * 128: src_off + (g0 + c + 1) * 128], identity=C.ident[:]),
                reads=[src, C.ident], writes=[bank], accum=(c > 0))
        eng = rr(P, C, "tev", ("act", "dve"))
        copy_op(P, eng, dst[:, g0:g0 + n, :], bank[:, 0:n * 128].rearrange("p (a b) -> p a b", a=n),
                [bank], [dst], accum=(g0 > 0))


def transpose_heads(P, C, src, nheads, dst, src_off=0):
    for g0 in range(0, nheads, 4):
        n = min(4, nheads - g0)
        bank = C.gbank()
        for c in range(n):
            P.op("pe", lambda e, c=c, bank=bank, g0=g0: e.transpose(
                out=bank[0:64, c * 128:(c + 1) * 128],
                in_=src[:, src_off + (g0 + c) * 64: src_off + (g0 + c + 1) * 64], identity=C.ident[:]),
                reads=[src, C.ident], writes=[bank], accum=(c > 0))
        eng = rr(P, C, "tev", ("act", "dve"))
        copy_op(P, eng, dst[:, g0:g0 + n, :], bank[0:64, 0:n * 128].rearrange("p (a b) -> p a b", a=n),
                [bank], [dst], accum=(g0 > 0))


def matmul_group(P, C, bank, ncols, hT, W, col0, nk=8, out_off=0):
    for c in range(nk):
        P.op("pe", lambda e, c=c: e.matmul(bank[:, out_off:out_off + ncols], lhsT=hT[:, c, :],
                                           rhs=W[:, c, col0:col0 + ncols], start=(c == 0), stop=(c == nk - 1)),
             reads=[hT, W], writes=[bank], accum=(c > 0))


def odd_layer(P, C, l, x_src, x_dst, xs_tr, xd_tr):
    NT = C.NT
    st = ExitStack()
    P.stack = st
    I = C.inp
    wq = P.sb([128, 8, 2560], BF16)
    wo = P.sb([128, 8, 1024], BF16)
    st2 = ExitStack(); P.stack = st2
    stage = [P.sb([128, 2560]), P.sb([128, 2560])]
    load_weight_bf16(P, C, wq, lambda c: I["od_w_in"][l, c * 128:(c + 1) * 128, :], 8, 2560, stage)
    load_weight_bf16(P, C, wo, lambda c: I["od_w_out"][l, c * 128:(c + 1) * 128, :], 8, 1024, stage)
    P.barrier()
    st2.close(); P.stack = st
    gn = P.sb([128, 1024]); gq = P.sb([128, 64]); gk = P.sb([128, 64]); esink = P.sb([128, 16])
    P.dma(gn[:], I["od_norm_rep"][l], writes=[gn])
    P.dma(gq[:], I["qg_rep"][l], writes=[gq])
    P.dma(gk[:], I["kg_rep"][l], writes=[gk])
    P.dma(esink[:], I["sink_rep"][l], writes=[esink])
    P.op("act", lambda e: e.activation(out=esink[:], in_=esink[:], func=AF.Exp), reads=[esink], writes=[esink])
    biasT = P.sb([128, 16, 512])
    for j in range(4):
        P.dma(biasT[:, 4 * j:4 * j + 4, :], I["alibi"][j].rearrange("r s q -> s r q"), writes=[biasT], accum=True)
    KTh = P.sb([64, 4, 128], BF16); Vh = P.sb([128, 4, 72], BF16)
    kh2 = P.sb([64, 2, 512], BF16); vh2 = P.sb([128, 2, 288], BF16)

    S = Ctx()
    S.junk = P.sb([128, 1024]); S.ss = P.sb([128, 1]); S.ss2 = P.sb([128, 1]); S.rs = P.sb([128, 1]); S.h = P.sb([128, 1024])
    hT = P.sb([128, 8, 128], BF16)
    xring = [P.sb([128, 1024]) for _ in range(3)]
    qf = P.sb([128, 1024]); qsq = P.sb([128, 1024]); qss = P.sb([128, 16]); qr = P.sb([128, 16]); qn = P.sb([128, 1024])
    kf = P.sb([128, 256]); ksq = P.sb([128, 256]); kss = P.sb([128, 4]); kr = P.sb([128, 4]); kn = P.sb([128, 256])
    QT = [P.sb([64, 16, 128], BF16) for _ in range(3)]
    KT = [P.sb([64, 4, 128], BF16) for _ in range(4)]
    V = [P.sb([128, 4, 72], BF16) for _ in range(4)]
    for v in V:
        P.op("pool", lambda e, v=v: e.memset(v[:], 1.0), writes=[v])
    sz = [P.sb([128, 1024]) for _ in range(3)]
    sring = [P.sb([128, 512]) for _ in range(3)]
    pr = [P.sb([128, 512], BF16) for _ in range(6)]
    den = P.sb([128, 4]); rden = P.sb([128, 4])
    o = P.sb([128, 1024]); og = P.sb([128, 1024]); ogT = P.sb([128, 8, 128], BF16)
    xo = [P.sb([128, 1024]) for _ in range(2)]

    def rms_heads(src, sq, ssum, rinv, nh, g, outs):
        P.op("act", lambda e: e.activation(out=sq[:], in_=src[:], func=AF.Square), reads=[src], writes=[sq])
        P.op("dve", lambda e: e.tensor_reduce(out=ssum[:], in_=sq[:].rearrange("p (h d) -> p h d", h=nh), axis=AX.X, op=ALU.add),
             reads=[sq], writes=[ssum])
        P.op("act", lambda e: e.activation(out=ssum[:], in_=ssum[:], func=AF.Sqrt, scale=1.0 / 64, bias=EPS),
             reads=[ssum], writes=[ssum])
        P.op("dve", lambda e: e.reciprocal(out=rinv[:], in_=ssum[:]), reads=[ssum], writes=[rinv])
        P.op("dve", lambda e: e.tensor_tensor(out=sq[:].rearrange("p (h d) -> p h d", h=nh),
                                              in0=src[:].rearrange("p (h d) -> p h d", h=nh),
                                              in1=rinv[:].unsqueeze(2).to_broadcast([128, nh, 64]), op=ALU.mult),
             reads=[src, rinv], writes=[sq])
        for oi, (oap, ot) in enumerate(outs):
            P.op("dve", lambda e, oap=oap: e.tensor_tensor(out=oap, in0=sq[:].rearrange("p (h d) -> p h d", h=nh),
                                                            in1=g[:].unsqueeze(1).to_broadcast([128, nh, 64]), op=ALU.mult),
                 reads=[sq, g], writes=[ot], accum=(oi > 0))

    def stage1_parts(j):
        xt = xring[j % 3]

        def pa():
            P.dma(xt[:], x_src[j * 128:(j + 1) * 128, :], reads=[xs_tr[j]], writes=[xt])
            rmsnorm_T(P, C, xt, gn, hT, S)

        def pb():
            for g in range(2):
                bank = C.gbank()
                matmul_group(P, C, bank, 512, hT, wq, g * 512)
                copy_op(P, rr(P, C, "qev", ("act", "dve")), qf[:, g * 512:(g + 1) * 512], bank[:], [bank], [qf], accum=(g > 0))
            rms_heads(qf, qsq, qss, qr, 16, gq, [(qn[:].rearrange("p (h d) -> p h d", h=16), qn)])
            transpose_heads(P, C, qn, 16, QT[j % 3])

        def pc():
            bank = C.gbank()
            matmul_group(P, C, bank, 512, hT, wq, 1024)
            copy_op(P, "dve", kf[:], bank[:, 0:256], [bank], [kf])
            Vt = V[j % 4]
            copy_op(P, "dve", Vt[:, :, 0:64], bank[:, 256:512].rearrange("p (h d) -> p h d", h=4), [bank], [Vt])
            rms_heads(kf, ksq, kss, kr, 4, gk, [(kn[:].rearrange("p (h d) -> p h d", h=4), kn)])
            transpose_heads(P, C, kn, 4, KT[j % 4])

        def pd():
            for g in range(2):
                bank = C.gbank()
                matmul_group(P, C, bank, 512, hT, wq, 1536 + g * 512)
                szt = sz[j % 3]
                P.op("act", lambda e, bank=bank, g=g, szt=szt: e.activation(out=szt[:, g * 512:(g + 1) * 512], in_=bank[:], func=AF.Silu),
                     reads=[bank], writes=[szt], accum=(g > 0))

        return [pa, pb, pc, pd]

    def halo_exchange():
        jl = NT - 1
        ktl, vl = KT[jl % 4], V[jl % 4]
        P.dma(C.ksrc[:, :], ktl[:].rearrange("p a b -> p (a b)"), reads=[ktl], writes=[C.ksrc_tr])
        P.dma(C.vsrc[:, :], vl[:].rearrange("p a b -> p (a b)"), reads=[vl], writes=[C.vsrc_tr])
        P.collective(C.ksrc[:, :], C.kdst[:, :], reads=[C.ksrc_tr], writes=[C.kdst_tr])
        P.collective(C.vsrc[:, :], C.vdst[:, :], reads=[C.vsrc_tr], writes=[C.vdst_tr])
        P.dma(kh2[:], C.kdst.ap().rearrange("(s p) n -> p s n", s=2), reads=[C.kdst_tr], writes=[kh2])
        P.dma(vh2[:], C.vdst.ap().rearrange("(s p) n -> p s n", s=2), reads=[C.vdst_tr], writes=[vh2])
        kf_ = KTh[:].rearrange("p a b -> p (a b)"); vf_ = Vh[:].rearrange("p a b -> p (a b)")
        P.op("dve", lambda e: e.tensor_scalar(out=kf_, in0=kh2[:, 0, :], scalar1=C.flags[0:64, 0:1], scalar2=None, op0=ALU.mult), reads=[kh2, C.flags], writes=[KTh])
        P.op("dve", lambda e: e.scalar_tensor_tensor(out=kf_, in0=kh2[:, 1, :], scalar=C.flags[0:64, 1:2], in1=kf_, op0=ALU.mult, op1=ALU.add),
             reads=[kh2, C.flags, KTh], writes=[KTh])
        P.op("dve", lambda e: e.tensor_scalar(out=vf_, in0=vh2[:, 0, :], scalar1=C.flags[:, 0:1], scalar2=None, op0=ALU.mult), reads=[vh2, C.flags], writes=[Vh])
        P.op("dve", lambda e: e.scalar_tensor_tensor(out=vf_, in0=vh2[:, 1, :], scalar=C.flags[:, 1:2], in1=vf_, op0=ALU.mult, op1=ALU.add),
             reads=[vh2, C.flags, Vh], writes=[Vh])

    def stage2_parts(i):
        def pj(jkv):
            blocks = [b for b in (i - 1, i, i + 1) if 0 <= b < NT]
            if i == NT - 1:
                blocks.append(NT)
            for b in blocks:
                rel = b - i + 1
                halo = (b == NT)
                KTb = KTh if halo else KT[b % 4]
                bank = C.sbank[rel]
                for hl in range(4):
                    hq = 4 * jkv + hl
                    P.op("pe", lambda e, bank=bank, hl=hl, b=b, hq=hq: e.matmul(
                        bank[:, hl * 128:(hl + 1) * 128], lhsT=KTb[:, jkv, :],
                        rhs=QT[i % 3][:, hq, :], start=True, stop=True),
                        reads=[KTb, QT[i % 3]], writes=[bank], accum=(hl > 0))
                s_t = sring[rel]
                bidx = 4 * jkv + (3 if halo else rel)
                P.op("dve", lambda e, bank=bank, s_t=s_t, rel=rel: e.scalar_tensor_tensor(
                    out=s_t[:], in0=bank[:], scalar=0.125, in1=biasT[:, bidx, :], op0=ALU.mult, op1=ALU.add),
                    reads=[bank, biasT], writes=[s_t])
                if halo:
                    P.op("dve", lambda e, s_t=s_t: e.tensor_scalar(out=s_t[:], in0=s_t[:], scalar1=C.flags[:, 2:3], scalar2=None,
                                                                   op0=ALU.add), reads=[s_t, C.flags], writes=[s_t])
                pt = pr[(jkv % 2) * 3 + rel]
                P.op("act", lambda e, pt=pt, s_t=s_t: e.activation(out=pt[:], in_=s_t[:], func=AF.Exp), reads=[s_t], writes=[pt])
            pvb = C.pbank[jkv % 2]
            for hl in range(4):
                for bi, b in enumerate(blocks):
                    rel = b - i + 1
                    pt = pr[(jkv % 2) * 3 + rel]
                    Vb_ = Vh if b == NT else V[b % 4]
                    P.op("pe", lambda e, pt=pt, hl=hl, b=b, bi=bi: e.matmul(
                        pvb[:, hl * 65:(hl + 1) * 65], lhsT=pt[:, hl * 128:(hl + 1) * 128], rhs=Vb_[:, jkv, 0:65],
                        start=(bi == 0), stop=(bi == len(blocks) - 1)),
                        reads=[pt, Vb_], writes=[pvb], accum=not (hl == 0 and bi == 0))
            pv3 = pvb[:, 0:260].rearrange("p (h d) -> p h d", h=4)
            P.op("dve", lambda e, pv3=pv3: e.tensor_tensor(out=den[:], in0=pv3[:, :, 64], in1=esink[:, 4 * jkv:4 * jkv + 4], op=ALU.add),
                 reads=[pvb, esink], writes=[den])
            P.op("dve", lambda e: e.reciprocal(out=rden[:], in_=den[:]), reads=[den], writes=[rden])
            P.op("dve", lambda e, pv3=pv3: e.tensor_tensor(
                out=o[:, jkv * 256:(jkv + 1) * 256].rearrange("p (h d) -> p h d", h=4), in0=pv3[:, :, 0:64],
                in1=rden[:].unsqueeze(2).to_broadcast([128, 4, 64]), op=ALU.mult),
                reads=[pvb, rden], writes=[o], accum=(jkv > 0))

        def ptail():
            P.op("dve", lambda e: e.tensor_tensor(out=og[:], in0=o[:], in1=sz[i % 3][:], op=ALU.mult), reads=[o, sz[i % 3]], writes=[og])
            transpose_to(P, C, og, 8, ogT)
            xot = xo[i % 2]
            for g in range(2):
                bank = C.gbank()
                matmul_group(P, C, bank, 512, ogT, wo, g * 512)
                P.op("dve", lambda e, bank=bank, g=g: e.tensor_tensor(out=xot[:, g * 512:(g + 1) * 512], in0=bank[:],
                                                                      in1=xring[i % 3][:, g * 512:(g + 1) * 512], op=ALU.add),
                     reads=[bank, xring[i % 3]], writes=[xot], accum=(g > 0))
            P.dma(x_dst[i * 128:(i + 1) * 128, :], xot[:], reads=[xot], writes=[xd_tr[i]], q="act")

        return [lambda: pj(0), lambda: pj(1), lambda: pj(2), lambda: pj(3), ptail]

    import os
    dbg = int(os.environ.get("KDBG", "9"))
    for t in range(NT + 2):
        p1 = stage1_parts(t) if t < NT else []
        p2 = stage2_parts(t - 2) if t >= 2 else []
        for k in range(max(len(p1), len(p2))):
            if k < len(p2):
                p2[k]()
            if k < len(p1):
                p1[k]()
        if t == NT - 1:
            halo_exchange()
    P.barrier()
    st.close()
    P.stack = P.gstack


INPUT_SHAPES = {
    "od_w_in": [2, 1024, 2560], "od_w_out": [2, 1024, 1024], "od_norm_rep": [2, 128, 1024],
    "qg_rep": [2, 128, 64], "kg_rep": [2, 128, 64], "sink_rep": [2, 128, 16],
    "alibi": [4, 4, 128, 512], "ident": [128, 128], "flags": [128, 4],
    "ev_w_in": [2, 1024, 3200], "ev_norm_rep": [2, 128, 1024], "ev_w_out": [2, 1024, 1024],
    "s5_ar_row": [2, 2, 128, 2048], "s5_ai_row": [2, 2, 128, 2048], "s5_dt_row": [2, 2, 128, 2048],
    "s5_ar_col": [2, 2, 128, 16], "s5_ai_col": [2, 2, 128, 16], "s5_dt_col": [2, 2, 128, 16],
    "s5_b_col": [2, 2, 2, 128, 16, 16], "s5_c_col": [2, 2, 2, 128, 16, 16],
    "s5_d_col": [2, 128, 4], "glu_b_col": [2, 128, 4], "s5_glu_w": [2, 512, 512],
    "iota_col": [128, 2], "iota_row": [2, 128, 128], "tri": [2, 128, 128], "triE": [2, 128, 128],
    "rw_mu_rep": [2, 128, 1664], "rw_w0_rep": [2, 2, 128, 512], "rw_a0_rep": [2, 128, 512], "rw_k_k_rep": [2, 128, 512],
    "rw_k_a_rep": [2, 128, 512], "rw_r_k_rep": [2, 128, 512], "rw_ln_g_rep": [2, 128, 512], "rw_ln_b_rep": [2, 128, 512],
    "rw_w_up": [2, 2, 64, 512], "rw_a_up": [2, 64, 512],
}


def alibi_tables():
    slopes = np.exp2(-8.0 * np.arange(1, 17, dtype=np.float32) / 16).astype(np.float32)
    s = np.arange(128)[:, None]
    t = np.arange(128)[None, :]
    out = np.zeros((4, 4, 128, 4, 128), np.float32)
    for rel in range(4):
        sg = (s + (rel - 1) * 128) if rel < 3 else (255 - s)
        d = np.abs(t - sg).astype(np.float32)
        for j in range(4):
            for hl in range(4):
                out[j, rel, :, hl, :] = np.where(d <= 128, -slopes[4 * j + hl] * d, NEG)
    return out.reshape(4, 4, 128, 512)


def host_layout(inputs, layers):
    f = lambda a: np.ascontiguousarray(np.asarray(a, np.float32))
    rep = lambda a: f(np.broadcast_to(np.asarray(a)[:, None, :], (a.shape[0], 128, a.shape[1])))
    m = {}
    m["od_w_in"] = f(inputs["od_w_in"]); m["od_w_out"] = f(inputs["od_w_out"])
    m["od_norm_rep"] = rep(inputs["od_norm"]); m["qg_rep"] = rep(inputs["at_q_norm"]); m["kg_rep"] = rep(inputs["at_k_norm"])
    m["sink_rep"] = rep(inputs["at_sink"])
    m["alibi"] = alibi_tables(); m["ident"] = np.eye(128, dtype=np.float32)
    m["ev_w_in"] = f(inputs["ev_w_in"]); m["ev_w_out"] = f(inputs["ev_w_out"]); m["ev_norm_rep"] = rep(inputs["ev_norm"])
    NE = 2
    rowrep = lambda a: f(np.broadcast_to(a.reshape(NE, 2, 1, 2048), (NE, 2, 128, 2048)))
    m["s5_ar_row"] = rowrep(np.asarray(inputs["s5_a_re"])); m["s5_ai_row"] = rowrep(np.asarray(inputs["s5_a_im"]))
    m["s5_dt_row"] = rowrep(np.repeat(np.asarray(inputs["s5_log_dt"])[..., None], 64, axis=-1))
    col = lambda a: f(a.reshape(NE, 2, 16, 128).transpose(0, 1, 3, 2))
    m["s5_ar_col"] = col(np.asarray(inputs["s5_a_re"])); m["s5_ai_col"] = col(np.asarray(inputs["s5_a_im"]))
    m["s5_dt_col"] = col(np.repeat(np.asarray(inputs["s5_log_dt"])[..., None], 64, axis=-1))
    bcol = lambda a: np.asarray(a).reshape(NE, 2, 16, 2, 64, 16).transpose(0, 1, 3, 4, 2, 5).reshape(NE, 2, 128, 16, 16)
    m["s5_b_col"] = f(np.stack([bcol(inputs["s5_b_re"]), bcol(inputs["s5_b_im"])], axis=2))
    ccol = lambda a: np.asarray(a).reshape(NE, 2, 16, 2, 16, 64).transpose(0, 1, 3, 5, 2, 4).reshape(NE, 2, 128, 16, 16)
    m["s5_c_col"] = f(np.stack([ccol(inputs["s5_c_re"]), ccol(inputs["s5_c_im"])], axis=2))
    c4 = lambda a: f(np.asarray(a).reshape(NE, 4, 128).transpose(0, 2, 1))
    m["s5_d_col"] = c4(inputs["s5_d"]); m["glu_b_col"] = c4(inputs["s5_glu_b"]); m["s5_glu_w"] = f(inputs["s5_glu_w"])
    ar = np.arange(128, dtype=np.float32)
    m["iota_col"] = f(np.stack([ar + 1, 128 - ar], axis=1))
    m["iota_row"] = f(np.stack([np.broadcast_to(ar + 1, (128, 128)), np.broadcast_to(128 - ar, (128, 128))]))
    s_, t_ = np.arange(128)[:, None], np.arange(128)[None, :]
    m["tri"] = f(np.stack([(s_ <= t_), (s_ >= t_)]).astype(np.float32))
    m["triE"] = f(np.stack([(s_ < t_), (s_ > t_)]).astype(np.float32))
    for k in ("rw_mu", "rw_a0", "rw_k_k", "rw_k_a", "rw_ln_g", "rw_ln_b"):
        m[k + "_rep"] = rep(np.asarray(inputs[k]))
    m["rw_r_k_rep"] = rep(np.asarray(inputs["rw_r_k"]).reshape(NE, 512))
    w0 = np.asarray(inputs["rw_w0"])
    m["rw_w0_rep"] = f(np.broadcast_to(w0[:, :, None, :], (NE, 2, 128, 512)))
    m["rw_w_up"] = f(inputs["rw_w_up"]); m["rw_a_up"] = f(inputs["rw_a_up"])
    return m


def build_program(NT, layers, debug=False):
    nc = bass.Bass("TRN2", target_bir_lowering=False)
    NTOK = NT * 128
    gst = ExitStack()
    P = Prog(nc, gst)
    C = Ctx()
    C.NT = NT
    C.rr = {}
    C.inp = {k: nc.dram_tensor(k, shp, F32, kind="ExternalInput") for k, shp in INPUT_SHAPES.items()}
    xin = nc.dram_tensor("xin", [NTOK, D], F32, kind="ExternalInput")
    xout = nc.dram_tensor("xout", [NTOK, D], F32, kind="ExternalOutput")
    xa = P.dram("xa", [NTOK, D]); xb = P.dram("xb", [NTOK, D])
    banks = [P.ps([128, 512]) for _ in range(8)]
    C.gb = banks[0:3]; C.sbank = banks[3:6]; C.pbank = banks[6:8]
    C.gi = 0

    def gbank():
        C.gi += 1
        return C.gb[C.gi % len(C.gb)]
    C.gbank = gbank
    C.banks = banks
    C.ident = P.sb([128, 128]); C.flags = P.sb([128, 4])
    C.iota_col = P.sb([128, 2]); C.iota_row = [P.sb([128, 128]) for _ in range(2)]; C.tri = [P.sb([128, 128]) for _ in range(2)]
    C.zero_col = P.sb([128, 2]); C.ones_col = P.sb([128, 2]); C.triE = [P.sb([128, 128]) for _ in range(2)]
    P.op("pool", lambda e: e.memset(C.zero_col[:], 0.0), writes=[C.zero_col])
    P.op("pool", lambda e: e.memset(C.ones_col[:], 1.0), writes=[C.ones_col])
    for d in range(2):
        P.dma(C.triE[d][:], C.inp["triE"][d], writes=[C.triE[d]])
    P.dma(C.iota_col[:], C.inp["iota_col"][:, :], writes=[C.iota_col])
    for d in range(2):
        P.dma(C.iota_row[d][:], C.inp["iota_row"][d], writes=[C.iota_row[d]])
        P.dma(C.tri[d][:], C.inp["tri"][d], writes=[C.tri[d]])
    dbgk = "ExternalOutput" if debug else "Internal"
    C.proj = nc.dram_tensor("proj", [NTOK, 3200], F32, kind=dbgk)
    C.ys5T = nc.dram_tensor("ys5T", [512, NTOK], F32, kind="Internal")
    C.mixT = nc.dram_tensor("mixT", [1024, NTOK], BF16, kind=dbgk)
    C.hrw = C.proj
    C.hrow = P.sb([1, 1664])
    for nm, shp, dt in (("ksrc", [64, 512], BF16), ("kdst", [128, 512], BF16), ("vsrc", [128, 288], BF16), ("vdst", [256, 288], BF16),
                        ("s5src", [128, 32], F32), ("s5dst", [256, 32], F32), ("zsrc", [64, 512], F32), ("zdst", [128, 512], F32),
                        ("hdst", [2, 1664], F32)):
        setattr(C, nm, nc.dram_tensor(nm, shp, dt, kind="Internal"))
        setattr(C, nm + "_tr", T(None))
    C.yrw = nc.dram_tensor("yrw", [NTOK, 512], F32, kind="Internal")
    C.yrw_tr = [T(None) for _ in range(NT)]
    C.rwc = nc.dram_tensor("rwc", [NTOK, 2560], F32, kind="Internal")
    C.rwt = nc.dram_tensor("rwt", [NT, 64, 128], F32, kind="Internal")
    C.rwc_tr = [T(None) for _ in range(NT)]
    C.proj_tr = [T(None) for _ in range(NT)]; C.ys_tr = [T(None) for _ in range(NT)]; C.mix_tr = [T(None) for _ in range(NT)]
    P.dma(C.ident[:], C.inp["ident"][:, :], writes=[C.ident])
    P.dma(C.flags[:], C.inp["flags"][:, :], writes=[C.flags])
    bufs = [xin] + [(xa, xb)[i % 2] for i in range(len(layers) - 1)] + [xout]
    trs = [[T(None) for _ in range(NT)] for _ in range(len(layers) + 1)]
    for li, (kind, l) in enumerate(layers):
        if kind == "odd":
            odd_layer(P, C, l, bufs[li], bufs[li + 1], trs[li], trs[li + 1])
        else:
            even_layer(P, C, l, bufs[li], bufs[li + 1], trs[li], trs[li + 1])
    P.emit()
    return nc, gst


MAGIC = 12582912.0
TWO_PI = 2.0 * np.pi


def round_frac(P, eng, out, in_, tmp):
    (o_ap, o_t), (i_ap, i_t), (t_ap, t_t) = out, in_, tmp
    P.op(eng, lambda e: e.tensor_scalar(out=t_ap, in0=i_ap, scalar1=MAGIC, scalar2=MAGIC, op0=ALU.add, op1=ALU.subtract),
         reads=[i_t], writes=[t_t])
    P.op(eng, lambda e: e.tensor_tensor(out=o_ap, in0=i_ap, in1=t_ap, op=ALU.subtract), reads=[i_t, t_t], writes=[o_t])


def even_phaseA(P, C, l, x_src, xs_tr):
    NT = C.NT
    st = ExitStack(); P.stack = st
    I = C.inp
    w = P.sb([128, 8, 3200], BF16)
    stage = [P.sb([128, 3200]), P.sb([128, 3200])]
    load_weight_bf16(P, C, w, lambda c: I["ev_w_in"][l, c * 128:(c + 1) * 128, :], 8, 3200, stage)
    gn = P.sb([128, 1024])
    P.dma(gn[:], I["ev_norm_rep"][l], writes=[gn])
    S = Ctx()
    S.junk = P.sb([128, 1024]); S.ss = P.sb([128, 1]); S.ss2 = P.sb([128, 1]); S.rs = P.sb([128, 1]); S.h = P.sb([128, 1024])
    hT = P.sb([128, 8, 128], BF16)
    xring = [P.sb([128, 1024]) for _ in range(2)]
    for j in range(NT):
        xt = xring[j % 2]
        P.dma(xt[:], x_src[j * 128:(j + 1) * 128, :], reads=[xs_tr[j]], writes=[xt])
        rmsnorm_T(P, C, xt, gn, hT, S)
        pst = stage[j % 2]
        for g in range(7):
            ncol = 512 if g < 6 else 128
            bank = C.gbank()
            matmul_group(P, C, bank, ncol, hT, w, g * 512)
            copy_op(P, rr(P, C, "pev", ("act", "dve")), pst[:, g * 512:g * 512 + ncol], bank[:, 0:ncol], [bank], [pst], accum=(g > 0))
        P.dma(C.proj[j * 128:(j + 1) * 128, :], pst[:], reads=[pst], writes=[C.proj_tr[j]], q="act")
    P.barrier()
    st.close(); P.stack = P.gstack


def s5_tables(P, C, l, d, K):
    I = C.inp
    W = K.work
    a_r, a_i, dtr, t0, t1, t2 = W[0], W[1], W[2], W[3], W[4], W[5]

    def build(shape_is_row, ar_src, ai_src, dt_src, steps_fn, sign, out_re, out_im):
        P.dma(a_r[:], ar_src, writes=[a_r]); P.dma(a_i[:], ai_src, writes=[a_i]); P.dma(dtr[:], dt_src, writes=[dtr])
        P.op("act", lambda e: e.activation(out=dtr[:], in_=dtr[:], func=AF.Exp), reads=[dtr], writes=[dtr])
        P.op("dve", lambda e: e.tensor_tensor(out=a_r[:], in0=a_r[:], in1=dtr[:], op=ALU.mult), reads=[a_r, dtr], writes=[a_r])
        P.op("dve", lambda e: e.scalar_tensor_tensor(out=a_i[:], in0=a_i[:], scalar=1.0 / TWO_PI, in1=dtr[:], op0=ALU.mult, op1=ALU.mult),
             reads=[a_i, dtr], writes=[a_i])
        round_frac(P, "dve", (a_i[:], a_i), (a_i[:], a_i), (t0[:], t0))
        steps_fn(a_r, a_i)
        P.op("act", lambda e: e.activation(out=t1[:], in_=a_r[:], func=AF.Exp, scale=float(sign)), reads=[a_r], writes=[t1])
        round_frac(P, "dve", (t0[:], t0), (a_i[:], a_i), (t2[:], t2))
        P.op("act", lambda e: e.activation(out=t0[:], in_=t0[:], func=AF.Sin, scale=TWO_PI), reads=[t0], writes=[t0])
        P.op("dve", lambda e: e.tensor_scalar(out=a_i[:], in0=a_i[:], scalar1=0.25, scalar2=None, op0=ALU.add), reads=[a_i], writes=[a_i])
        round_frac(P, "dve", (a_i[:], a_i), (a_i[:], a_i), (t2[:], t2))
        P.op("act", lambda e: e.activation(out=a_i[:], in_=a_i[:], func=AF.Sin, scale=TWO_PI), reads=[a_i], writes=[a_i])
        P.op("dve", lambda e: e.tensor_tensor(out=out_re[:].rearrange("p a b -> p (a b)"), in0=t1[:], in1=a_i[:], op=ALU.mult),
             reads=[t1, a_i], writes=[out_re])
        P.op("dve", lambda e: e.scalar_tensor_tensor(out=out_im[:].rearrange("p a b -> p (a b)"), in0=t1[:], scalar=float(sign), in1=t0[:],
                                                     op0=ALU.mult, op1=ALU.mult), reads=[t1, t0], writes=[out_im])

    def steps_row(a_r, a_i):
        for t in (a_r, a_i):
            P.op("dve", lambda e, t=t: e.tensor_scalar(out=t[:], in0=t[:], scalar1=C.iota_col[:, d:d + 1], scalar2=None, op0=ALU.mult),
                 reads=[t, C.iota_col], writes=[t])
    build(True, I["s5_ar_row"][l, d], I["s5_ai_row"][l, d], I["s5_dt_row"][l, d], steps_row, -1, K.Tin_re, K.Tin_im)

    def steps_col(a_r, a_i):
        for t in (a_r, a_i):
            P.op("dve", lambda e, t=t: e.tensor_tensor(out=t[:].rearrange("p (a b) -> p a b", a=16),
                                                       in0=t[:, 0:16].unsqueeze(2).to_broadcast([128, 16, 128]),
                                                       in1=C.iota_row[d][:].unsqueeze(1).to_broadcast([128, 16, 128]), op=ALU.mult),
                 reads=[t, C.iota_row[d]], writes=[t])
    ca, ci, cd = K.col_a, K.col_i, K.col_d

    def build_col():
        P.dma(ca[:], I["s5_ar_col"][l, d], writes=[ca]); P.dma(ci[:], I["s5_ai_col"][l, d], writes=[ci]); P.dma(cd[:], I["s5_dt_col"][l, d], writes=[cd])
        P.op("act", lambda e: e.activation(out=cd[:], in_=cd[:], func=AF.Exp), reads=[cd], writes=[cd])
        P.op("dve", lambda e: e.tensor_tensor(out=K.c_ardt[:], in0=ca[:], in1=cd[:], op=ALU.mult), reads=[ca, cd], writes=[K.c_ardt])
        P.op("dve", lambda e: e.scalar_tensor_tensor(out=K.c_frac[:], in0=ci[:], scalar=1.0 / TWO_PI, in1=cd[:], op0=ALU.mult, op1=ALU.mult),
             reads=[ci, cd], writes=[K.c_frac])
        round_frac(P, "dve", (K.c_frac[:], K.c_frac), (K.c_frac[:], K.c_frac), (K.c_tmp[:], K.c_tmp))
        P.op("dve", lambda e: e.tensor_tensor(out=a_r[:].rearrange("p (a b) -> p a b", a=16),
                                              in0=K.c_ardt[:].unsqueeze(2).to_broadcast([128, 16, 128]),
                                              in1=C.iota_row[d][:].unsqueeze(1).to_broadcast([128, 16, 128]), op=ALU.mult),
             reads=[K.c_ardt, C.iota_row[d]], writes=[a_r])
        P.op("dve", lambda e: e.tensor_tensor(out=a_i[:].rearrange("p (a b) -> p a b", a=16),
                                              in0=K.c_frac[:].unsqueeze(2).to_broadcast([128, 16, 128]),
                                              in1=C.iota_row[d][:].unsqueeze(1).to_broadcast([128, 16, 128]), op=ALU.mult),
             reads=[K.c_frac, C.iota_row[d]], writes=[a_i])
        sign = 1
        P.op("act", lambda e: e.activation(out=t1[:], in_=a_r[:], func=AF.Exp, scale=float(sign)), reads=[a_r], writes=[t1])
        round_frac(P, "dve", (t0[:], t0), (a_i[:], a_i), (t2[:], t2))
        P.op("act", lambda e: e.activation(out=t0[:], in_=t0[:], func=AF.Sin, scale=TWO_PI), reads=[t0], writes=[t0])
        P.op("dve", lambda e: e.tensor_scalar(out=a_i[:], in0=a_i[:], scalar1=0.25, scalar2=None, op0=ALU.add), reads=[a_i], writes=[a_i])
        round_frac(P, "dve", (a_i[:], a_i), (a_i[:], a_i), (t2[:], t2))
        P.op("act", lambda e: e.activation(out=a_i[:], in_=a_i[:], func=AF.Sin, scale=TWO_PI), reads=[a_i], writes=[a_i])
        P.op("dve", lambda e: e.tensor_tensor(out=K.Tout_re[:].rearrange("p a b -> p (a b)"), in0=t1[:], in1=a_i[:], op=ALU.mult),
             reads=[t1, a_i], writes=[K.Tout_re])
        P.op("dve", lambda e: e.tensor_tensor(out=K.Tout_im[:].rearrange("p a b -> p (a b)"), in0=t1[:], in1=t0[:], op=ALU.mult),
             reads=[t1, t0], writes=[K.Tout_im])
    build_col()

    s1, c1, m1, nr, dn, q_r, q_i, u0, u1 = [K.small[i] for i in range(9)]
    P.op("act", lambda e: e.activation(out=m1[:], in_=K.c_ardt[:], func=AF.Exp), reads=[K.c_ardt], writes=[m1])
    P.op("act", lambda e: e.activation(out=s1[:], in_=K.c_frac[:], func=AF.Sin, scale=TWO_PI), reads=[K.c_frac], writes=[s1])
    P.op("dve", lambda e: e.tensor_scalar(out=u0[:], in0=K.c_frac[:], scalar1=0.25, scalar2=None, op0=ALU.add), reads=[K.c_frac], writes=[u0])
    round_frac(P, "dve", (u0[:], u0), (u0[:], u0), (u1[:], u1))
    P.op("act", lambda e: e.activation(out=c1[:], in_=u0[:], func=AF.Sin, scale=TWO_PI), reads=[u0], writes=[c1])
    tt = lambda o, a, b, op, eng="dve": P.op(eng, lambda e: e.tensor_tensor(out=o[:], in0=a[:], in1=b[:], op=op), reads=[a, b], writes=[o])
    tt(c1, c1, m1, ALU.mult)
    tt(s1, s1, m1, ALU.mult)
    P.op("dve", lambda e: e.tensor_scalar(out=nr[:], in0=c1[:], scalar1=-1.0, scalar2=None, op0=ALU.add), reads=[c1], writes=[nr])
    tt(dn, ca, ca, ALU.mult); tt(u0, ci, ci, ALU.mult); tt(dn, dn, u0, ALU.add)
    P.op("dve", lambda e: e.reciprocal(out=dn[:], in_=dn[:]), reads=[dn], writes=[dn])
    tt(u0, nr, ca, ALU.mult); tt(u1, s1, ci, ALU.mult); tt(u0, u0, u1, ALU.add); tt(q_r, u0, dn, ALU.mult)
    tt(u0, s1, ca, ALU.mult); tt(u1, nr, ci, ALU.mult); tt(u0, u0, u1, ALU.subtract); tt(q_i, u0, dn, ALU.mult)
    bre, bim, bbr, bbi, tb = K.bre, K.bim, K.bbr, K.bbi, K.tb
    for (dst, ri) in ((bre, 0), (bim, 1)):
        P.op("pool", lambda e, dst=dst: e.memset(dst[:], 0.0), writes=[dst])
        P.dma(dst[0:64, :, 0:16], I["s5_b_col"][l, d, ri, 0:64], writes=[dst])
        P.dma(dst[64:128, :, 16:32], I["s5_b_col"][l, d, ri, 64:128], writes=[dst])
    bc = lambda q: q[:].unsqueeze(2).to_broadcast([128, 16, 32])
    P.op("dve", lambda e: e.tensor_tensor(out=bbr[:], in0=bre[:], in1=bc(q_r), op=ALU.mult), reads=[bre, q_r], writes=[bbr])
    P.op("dve", lambda e: e.tensor_tensor(out=tb[:], in0=bim[:], in1=bc(q_i), op=ALU.mult), reads=[bim, q_i], writes=[tb])
    tt(bbr, bbr, tb, ALU.subtract)
    P.op("dve", lambda e: e.tensor_tensor(out=bbi[:], in0=bim[:], in1=bc(q_r), op=ALU.mult), reads=[bim, q_r], writes=[bbi])
    P.op("dve", lambda e: e.tensor_tensor(out=tb[:], in0=bre[:], in1=bc(q_i), op=ALU.mult), reads=[bre, q_i], writes=[tb])
    tt(bbi, bbi, tb, ALU.add)
    zp = K.zp
    for z in zp:
        P.op("pool", lambda e, z=z: e.memset(z[:], 0.0), writes=[z])
    for (src, dst) in ((bbr, K.BT_re), (bbi, K.BT_im)):
        for ch in range(4):
            bank = C.gbank()
            for pl in range(4):
                copy_op(P, "dve", zp[pl][:, 32 * pl:32 * pl + 32], src[:, 4 * ch + pl, :], [src], [zp[pl]])
                P.op("pe", lambda e, bank=bank, pl=pl: e.transpose(out=bank[:, pl * 128:(pl + 1) * 128], in_=zp[pl][:], identity=C.ident[:]),
                     reads=[zp[pl], C.ident], writes=[bank], accum=(pl > 0))
            copy_op(P, "act", dst[:, ch, :], bank[:], [bank], [dst], accum=(ch > 0))
    for (dst, ri) in ((K.Cre, 0), (K.Cimn, 1)):
        P.op("pool", lambda e, dst=dst: e.memset(dst[:], 0.0), writes=[dst])
        P.dma(dst[0:64, :, 32:48], I["s5_c_col"][l, d, ri, 0:64], writes=[dst])
        P.dma(dst[64:128, :, 48:64], I["s5_c_col"][l, d, ri, 64:128], writes=[dst])
    P.op("dve", lambda e: e.tensor_scalar(out=K.Cimn[:], in0=K.Cimn[:], scalar1=-1.0, scalar2=None, op0=ALU.mult), reads=[K.Cimn], writes=[K.Cimn])


def s5_pass(P, C, l, d):
    half = -999
    NT = C.NT
    st = ExitStack(); P.stack = st
    I = C.inp
    K = Ctx()
    K.Tin_re = P.sb([128, 4, 512]); K.Tin_im = P.sb([128, 4, 512])
    K.Tout_re = P.sb([128, 16, 128]); K.Tout_im = P.sb([128, 16, 128])
    K.BT_re = P.sb([128, 4, 512], BF16); K.BT_im = P.sb([128, 4, 512], BF16)
    K.Cre = P.sb([128, 16, 64]); K.Cimn = P.sb([128, 16, 64])
    st2 = ExitStack(); P.stack = st2
    K.work = [P.sb([128, 2048]) for _ in range(6)]
    K.col_a = P.sb([128, 16]); K.col_i = P.sb([128, 16]); K.col_d = P.sb([128, 16])
    K.c_ardt = P.sb([128, 16]); K.c_frac = P.sb([128, 16]); K.c_tmp = P.sb([128, 16])
    K.small = [P.sb([128, 16]) for _ in range(9)]
    K.bre = P.sb([128, 16, 32]); K.bim = P.sb([128, 16, 32]); K.bbr = P.sb([128, 16, 32]); K.bbi = P.sb([128, 16, 32]); K.tb = P.sb([128, 16, 32])
    K.zp = [P.sb([128, 128]) for _ in range(4)]
    s5_tables(P, C, l, d, K)
    P.barrier()
    st2.close(); P.stack = st
    tri = C.tri[d]
    zero = C.zero_col
    uring = [P.sb([128, 512]) for _ in range(2)]
    uT = [P.sb([128, 4, 128], BF16) for _ in range(2)]
    g_re = [P.sb([128, 512], BF16) for _ in range(2)]; g_im = [P.sb([128, 512], BF16) for _ in range(2)]
    ta = [P.sb([128, 512]) for _ in range(2)]; tb = [P.sb([128, 512]) for _ in range(2)]
    ta2 = [P.sb([128, 512]) for _ in range(2)]; tb2 = [P.sb([128, 512]) for _ in range(2)]
    hre = [[P.sb([128, 128]) for _ in range(16)] for _ in range(2)]
    him = [[P.sb([128, 128]) for _ in range(16)] for _ in range(2)]
    r1 = [P.sb([128, 128]) for _ in range(2)]; r2 = [P.sb([128, 128]) for _ in range(2)]
    r3 = [P.sb([128, 128]) for _ in range(2)]; r4 = [P.sb([128, 128]) for _ in range(2)]
    cc = [[P.sb([128, 2]) for _ in range(16)] for _ in range(1)][0]
    ysb = [P.sb([128, 4, 128]) for _ in range(2)]
    ysT = C.ys5T.ap().rearrange("(k p) t -> p k t", p=128)
    bk = C.banks
    if d == 1:
        dcol = P.sb([128, 4]); gbcol = P.sb([128, 4])
        P.dma(dcol[:], I["s5_d_col"][l], writes=[dcol]); P.dma(gbcol[:], I["glu_b_col"][l], writes=[gbcol])
        wg = P.sb([128, 4, 512], BF16)
        wgs = [P.sb([128, 512]), P.sb([128, 512])]
        load_weight_bf16(P, C, wg, lambda c: I["s5_glu_w"][l, c * 128:(c + 1) * 128, :], 4, 512, wgs)
        yf = [P.sb([128, 4, 128]) for _ in range(2)]
        zt = [P.sb([128, 512]) for _ in range(2)]
        szT = P.sb([128, 4, 128])
        yv = P.sb([128, 4, 128]); x2 = P.sb([128, 4, 128]); sg = P.sb([128, 4, 128]); yg = P.sb([128, 4, 128]); ygb = P.sb([128, 4, 128], BF16)
        gs = P.sb([128, 4, 128]); ya = [P.sb([128, 4, 128], BF16) for _ in range(2)]
        mixT = C.mixT.ap().rearrange("(k p) t -> p k t", p=128)

    cin = P.sb([128, 32]); cbuf = P.sb([128, 32]); cin2 = P.sb([128, 2, 32])
    if d == 1:
        P.dma(cin2[:], C.s5dst.ap().rearrange("(s p) n -> p s n", s=2), reads=[C.s5dst_tr], writes=[cin2])
        P.op("dve", lambda e: e.tensor_scalar(out=cin[:], in0=cin2[:, 0, :], scalar1=C.flags[:, 0:1], scalar2=None, op0=ALU.mult), reads=[cin2, C.flags], writes=[cin])
        P.op("dve", lambda e: e.scalar_tensor_tensor(out=cin[:], in0=cin2[:, 1, :], scalar=C.flags[:, 1:2], in1=cin[:], op0=ALU.mult, op1=ALU.add),
             reads=[cin2, C.flags, cin], writes=[cin])
    order = list(range(NT)) if d == 0 else list(range(NT - 1, -1, -1))
    last = 127 if d == 0 else 0
    trib = P.sb([128, 128], BF16)
    copy_op(P, "dve", trib[:], tri[:], [tri], [trib])

    def prologue(it):
        i = order[it]; par = it % 2
        ut = uring[par]
        P.dma(ut[:], C.proj[i * 128:(i + 1) * 128, 0:512], reads=[C.proj_tr[i]], writes=[ut])
        if d == 1:
            P.dma(yf[par][:], ysT[:, :, i * 128:(i + 1) * 128], reads=[C.ys_tr[i]], writes=[yf[par]])
            P.dma(zt[par][:], C.proj[i * 128:(i + 1) * 128, 512:1024], reads=[C.proj_tr[i]], writes=[zt[par]])
        uTt = uT[par]
        for c in range(4):
            P.op("pe", lambda e, c=c: e.transpose(out=bk[0][:, c * 128:(c + 1) * 128], in_=ut[:, c * 128:(c + 1) * 128], identity=C.ident[:]),
                 reads=[ut, C.ident], writes=[bk[0]], accum=(c > 0))
        copy_op(P, "act", uTt[:].rearrange("p a b -> p (a b)"), bk[0][:], [bk[0]], [uTt])

    def front(it, ch):
        par = it % 2; cp = ch % 2
        uTt = uT[par]
        b1, b2 = (bk[6], bk[7]) if (d == 0 and ch % 2 == 1) else (bk[1], bk[2])
        P.op("pe", lambda e: e.matmul(b1[:], lhsT=uTt[:, ch, :], rhs=K.BT_re[:, ch, :], start=True, stop=True),
             reads=[uTt, K.BT_re], writes=[b1])
        P.op("pe", lambda e: e.matmul(b2[:], lhsT=uTt[:, ch, :], rhs=K.BT_im[:, ch, :], start=True, stop=True),
             reads=[uTt, K.BT_im], writes=[b2])
        Tr = K.Tin_re[:, ch, :]; Ti = K.Tin_im[:, ch, :]
        gr, gi, a_, b_, a2_, b2_ = g_re[cp], g_im[cp], ta[cp], tb[cp], ta2[cp], tb2[cp]
        P.op("dve", lambda e: e.tensor_tensor(out=a_[:], in0=b1[:], in1=Tr, op=ALU.mult), reads=[b1, K.Tin_re], writes=[a_])
        P.op("dve", lambda e: e.tensor_tensor(out=b_[:], in0=b2[:], in1=Ti, op=ALU.mult), reads=[b2, K.Tin_im], writes=[b_])
        P.op("pool", lambda e: e.tensor_tensor(out=gr[:], in0=a_[:], in1=b_[:], op=ALU.subtract), reads=[a_, b_], writes=[gr])
        P.op("dve", lambda e: e.tensor_tensor(out=a2_[:], in0=b1[:], in1=Ti, op=ALU.mult), reads=[b1, K.Tin_im], writes=[a2_])
        P.op("dve", lambda e: e.tensor_tensor(out=b2_[:], in0=b2[:], in1=Tr, op=ALU.mult), reads=[b2, K.Tin_re], writes=[b2_])
        P.op("pool", lambda e: e.tensor_tensor(out=gi[:], in0=a2_[:], in1=b2_[:], op=ALU.add), reads=[a2_, b2_], writes=[gi])

    def back(it, ch):
        par = it % 2; cp = ch % 2
        gr, gi = g_re[cp], g_im[cp]
        for pl in range(4):
            P.op("pe", lambda e, pl=pl: e.matmul(bk[3][:, pl * 128:(pl + 1) * 128], lhsT=gr[:, pl * 128:(pl + 1) * 128], rhs=trib[:],
                                                 start=True, stop=True), reads=[gr, trib], writes=[bk[3]], accum=(pl > 0))
        for pl in range(4):
            P.op("pe", lambda e, pl=pl: e.matmul(bk[4][:, pl * 128:(pl + 1) * 128], lhsT=gi[:, pl * 128:(pl + 1) * 128], rhs=trib[:],
                                                 start=True, stop=True), reads=[gi, trib], writes=[bk[4]], accum=(pl > 0))
        for pl in (0, 1, 3, 2):
            pp = 4 * ch + pl
            hr_prev, hi_prev = hre[1 - par][pp], him[1 - par][pp]
            hr, hi = hre[par][pp], him[par][pp]
            if it == 0 and d == 0:
                cr, ci_, crt = zero[:, 0:1], zero[:, 0:1], [zero]
            elif it == 0:
                cr, ci_, crt = cin[:, pp:pp + 1], cin[:, 16 + pp:17 + pp], [cin]
            else:
                cr, ci_, crt = hr_prev[:, last:last + 1], hi_prev[:, last:last + 1], [hr_prev, hi_prev]
            Gr = bk[3][:, pl * 128:(pl + 1) * 128]; Gi = bk[4][:, pl * 128:(pl + 1) * 128]
            Tor = K.Tout_re[:, pp, :]; Toi = K.Tout_im[:, pp, :]
            q1, q2, q3, q4 = r1[pl % 2], r2[pl % 2], r3[pl % 2], r4[pl % 2]
            P.op("dve", lambda e: e.scalar_tensor_tensor(out=q1[:], in0=Gr, scalar=cr, in1=Tor, op0=ALU.add, op1=ALU.mult),
                 reads=[bk[3], K.Tout_re] + crt, writes=[q1])
            P.op("dve", lambda e: e.scalar_tensor_tensor(out=q2[:], in0=Gi, scalar=ci_, in1=Toi, op0=ALU.add, op1=ALU.mult),
                 reads=[bk[4], K.Tout_im] + crt, writes=[q2])
            P.op("pool", lambda e: e.tensor_tensor(out=hr[:], in0=q1[:], in1=q2[:], op=ALU.subtract), reads=[q1, q2], writes=[hr])
            P.op("dve", lambda e: e.scalar_tensor_tensor(out=q3[:], in0=Gr, scalar=cr, in1=Toi, op0=ALU.add, op1=ALU.mult),
                 reads=[bk[3], K.Tout_im] + crt, writes=[q3])
            P.op("dve", lambda e: e.scalar_tensor_tensor(out=q4[:], in0=Gi, scalar=ci_, in1=Tor, op0=ALU.add, op1=ALU.mult),
                 reads=[bk[4], K.Tout_re] + crt, writes=[q4])
            P.op("pool", lambda e: e.tensor_tensor(out=hi[:], in0=q3[:], in1=q4[:], op=ALU.add), reads=[q3, q4], writes=[hi])
            if pl == 3:
                osl, csl, st0 = slice(64, 128), slice(0, 64), True
            elif pl == 2:
                osl, csl, st0 = slice(64, 96), slice(32, 64), False
            else:
                osl, csl, st0 = slice(32 * pl, 32 * pl + 32), slice(32, 64), True
            sgc = pl >= 2
            P.op("pe", lambda e: e.matmul(bk[5][osl, ch * 128:(ch + 1) * 128], lhsT=K.Cre[:, pp, csl], rhs=hr[:],
                                          start=st0, stop=False, skip_group_check=sgc), reads=[K.Cre, hr], writes=[bk[5]], accum=not (ch == 0 and pl == 0))
            P.op("pe", lambda e: e.matmul(bk[5][osl, ch * 128:(ch + 1) * 128], lhsT=K.Cimn[:, pp, csl], rhs=hi[:],
                                          start=False, stop=True, skip_group_check=sgc), reads=[K.Cimn, hi], writes=[bk[5]], accum=True)

    def epilogue(it):
        i = order[it]; par = it % 2
        uTt = uT[par]
        if d == 0:
            yt = ysb[par]
            copy_op(P, "act", yt[:].rearrange("p a b -> p (a b)"), bk[5][:], [bk[5]], [yt])
            P.dma(ysT[:, :, i * 128:(i + 1) * 128], yt[:], reads=[yt], writes=[C.ys_tr[i]], q="act")
        else:
            f = lambda t: t[:].rearrange("p a b -> p (a b)")
            P.op("dve", lambda e: e.tensor_tensor(out=f(yv), in0=bk[5][:], in1=f(yf[par]), op=ALU.add), reads=[bk[5], yf[par]], writes=[yv])
            for ch in range(4):
                P.op("dve", lambda e, ch=ch: e.scalar_tensor_tensor(out=yv[:, ch, :], in0=uTt[:, ch, :], scalar=dcol[:, ch:ch + 1], in1=yv[:, ch, :],
                                                                    op0=ALU.mult, op1=ALU.add), reads=[uTt, dcol, yv], writes=[yv], accum=True)
            P.op("act", lambda e: e.activation(out=f(yg), in_=f(yv), func=AF.Gelu_apprx_tanh), reads=[yv], writes=[yg])
            copy_op(P, "act", f(ygb), f(yg), [yg], [ygb])
            for co in range(4):
                for kc in range(4):
                    P.op("pe", lambda e, co=co, kc=kc: e.matmul(bk[6][:, co * 128:(co + 1) * 128], lhsT=wg[:, kc, co * 128:(co + 1) * 128],
                                                                rhs=ygb[:, kc, :], start=(kc == 0), stop=(kc == 3)),
                         reads=[wg, ygb], writes=[bk[6]], accum=not (co == 0 and kc == 0))
            for co in range(4):
                P.op("act", lambda e, co=co: e.activation(out=sg[:, co, :], in_=bk[6][:, co * 128:(co + 1) * 128], func=AF.Sigmoid,
                                                          bias=gbcol[:, co:co + 1]), reads=[bk[6], gbcol], writes=[sg], accum=(co > 0))
            for c in range(4):
                P.op("pe", lambda e, c=c: e.transpose(out=bk[7][:, c * 128:(c + 1) * 128], in_=zt[par][:, c * 128:(c + 1) * 128], identity=C.ident[:]),
                     reads=[zt[par], C.ident], writes=[bk[7]], accum=(c > 0))
            P.op("act", lambda e: e.activation(out=f(szT), in_=bk[7][:], func=AF.Silu), reads=[bk[7]], writes=[szT])
            P.op("dve", lambda e: e.tensor_tensor(out=f(gs), in0=f(yg), in1=f(sg), op=ALU.mult), reads=[yg, sg], writes=[gs])
            P.op("pool", lambda e: e.tensor_tensor(out=f(ya[par]), in0=f(gs), in1=f(szT), op=ALU.mult), reads=[gs, szT], writes=[ya[par]])
            P.dma(mixT[:, 0:4, i * 128:(i + 1) * 128], ya[par][:], reads=[ya[par]], writes=[C.mix_tr[i]], q="pool")

    units = [(it, ch) for it in range(NT) for ch in range(4)]
    prologue(0); front(0, 0)
    for u, (it, ch) in enumerate(units):
        if u + 1 < len(units):
            it2, ch2 = units[u + 1]
            if ch2 == 0:
                prologue(it2)
            front(it2, ch2)
        back(it, ch)
        if ch == 3:
            epilogue(it)
    if d == 0:
        parl = (NT - 1) % 2
        for pp in range(16):
            copy_op(P, ("dve", "pool")[pp % 2], cbuf[:, pp:pp + 1], hre[parl][pp][:, 127:128], [hre[parl][pp]], [cbuf], accum=(pp > 0))
            copy_op(P, ("pool", "dve")[pp % 2], cbuf[:, 16 + pp:17 + pp], him[parl][pp][:, 127:128], [him[parl][pp]], [cbuf], accum=True)
        P.dma(C.s5src[:, :], cbuf[:], reads=[cbuf], writes=[C.s5src_tr])
        P.collective(C.s5src[:, :], C.s5dst[:, :], reads=[C.s5src_tr], writes=[C.s5dst_tr])
    P.barrier()
    st.close(); P.stack = P.gstack


def even_layer(P, C, l, x_src, x_dst, xs_tr, xd_tr):
    import os
    dbg = int(os.environ.get("EDBG", "9"))
    even_phaseA(P, C, l, x_src, xs_tr)
    if dbg >= 1:
        s5_pass(P, C, l, 0)
        s5_pass(P, C, l, 1)
    if dbg >= 2:
        rwkv_pass(P, C, l, 0, x_src, x_dst, xs_tr, xd_tr)
        rwkv_pass(P, C, l, 1, x_src, x_dst, xs_tr, xd_tr)


def rwkv_pass(P, C, l, d, x_src, x_dst, xs_tr, xd_tr):
    NT = C.NT
    st = ExitStack(); P.stack = st
    I = C.inp
    bk = C.banks
    f32 = lambda shape: P.sb(shape)
    b16 = lambda shape: P.sb(shape, BF16)
    import os
    _rp = os.environ.get("RWPOOL", "dve")
    tt = lambda eng, o, a, b, op, rd, wr, accum=False: P.op(_rp if eng == "pool" else eng, lambda e: e.tensor_tensor(out=o, in0=a, in1=b, op=op), reads=rd, writes=wr, accum=accum)

    mu = f32([128, 1664]); P.dma(mu[:], I["rw_mu_rep"][l], writes=[mu])
    w0 = f32([128, 512]); P.dma(w0[:], I["rw_w0_rep"][l, d], writes=[w0])
    a0 = f32([128, 512]); P.dma(a0[:], I["rw_a0_rep"][l], writes=[a0])
    kkp = f32([128, 512]); P.dma(kkp[:], I["rw_k_k_rep"][l], writes=[kkp])
    kap = f32([128, 512]); P.dma(kap[:], I["rw_k_a_rep"][l], writes=[kap])
    ups = f32([128, 2, 512])
    P.dma(ups[0:64, 0, :], I["rw_w_up"][l, d], writes=[ups])
    P.dma(ups[64:128, 1, :], I["rw_a_up"][l], writes=[ups])
    triI = C.tri[d]; triE = C.triE[d]; triET = C.triE[1 - d]
    eye_b = C.ident
    if d == 1:
        rkp = f32([128, 512]); P.dma(rkp[:], I["rw_r_k_rep"][l], writes=[rkp])
        lng = f32([128, 512]); P.dma(lng[:], I["rw_ln_g_rep"][l], writes=[lng])
        lnb = f32([128, 512]); P.dma(lnb[:], I["rw_ln_b_rep"][l], writes=[lnb])
        wo = b16([128, 8, 1024])
        st2 = ExitStack(); P.stack = st2
        wst = [f32([128, 1024]), f32([128, 1024])]
        load_weight_bf16(P, C, wo, lambda c: I["ev_w_out"][l, c * 128:(c + 1) * 128, :], 8, 1024, wst)
        P.barrier()
        st2.close(); P.stack = st

    if d == 0:
        cur = [f32([128, 1664])] * 2; prv = [f32([128, 1664])] * 2; nxt = [f32([128, 1664])] * 2
        tsum = f32([128, 1664])
    else:
        cur = prv = nxt = [None, None]; tsum = None
    hsr = [f32([128, 1664]) for _ in range(2)]
    twla = f32([128, 128]); twlaT = f32([128, 128])
    a_t = f32([128, 512]); e2 = f32([128, 512]); tmp = f32([128, 512]); tmp2 = f32([128, 512])
    kkn = f32([128, 512]); pss = f32([128, 8]); prn = f32([128, 8])
    p_t = f32([128, 512]); q_t = f32([128, 512]); kpr = [f32([128, 512]) for _ in range(2)]
    GI = f32([128, 512]); GIinv = f32([128, 512]); GE = f32([128, 512])
    Pd = f32([128, 512]); Qd = f32([128, 512]); Kd = f32([128, 512]); Rd = f32([128, 512])
    Pdb = b16([128, 512]); Qdbr = [b16([128, 512]) for _ in range(2)]; Kdb = b16([128, 512]); Vbr = [b16([128, 512]) for _ in range(2)]
    PRr = [b16([64, 8, 2, 128]) for _ in range(2)]; QTt = b16([64, 8, 128]); KTt = b16([64, 8, 128])
    Bm = [b16([128, 8, 128]) for _ in range(2)]; Am = [b16([128, 8, 128]) for _ in range(2)]; Pmr = [b16([128, 8, 128]) for _ in range(2)]
    MqTr = [b16([128, 8, 128]) for _ in range(2)]; LkT = b16([128, 8, 128]); MkTr = [b16([128, 8, 128]) for _ in range(2)]
    LkVr = [f32([128, 512]) for _ in range(2)]; KVr = [f32([64, 8, 64]) for _ in range(2)]
    Z = f32([64, 8, 64]); Zb = b16([64, 8, 64]); ZK = f32([64, 8, 64]); ZKg = f32([64, 8, 64]); Ztmp = f32([64, 8, 64])
    gcolr = [f32([64, 8]) for _ in range(2)]; onescol = C.ones_col
    rhs_sb = b16([128, 512]); U_sb = b16([128, 512])
    ysb = [f32([128, 512]) for _ in range(2)]
    if d == 0:
        P.op("pool", lambda e: e.memset(Z[:], 0.0), writes=[Z])
        P.op("pool", lambda e: e.memset(Zb[:], 0.0), writes=[Zb])
        hb = P.sb([1, 2, 1664])
        P.collective(C.proj[NT * 128 - 1:NT * 128, 1024:2688], C.hdst[:, :], reads=[C.proj_tr[NT - 1]], writes=[C.hdst_tr])
        P.dma(hb[:], C.hdst.ap().rearrange("(o s) n -> o s n", o=1), reads=[C.hdst_tr], writes=[hb])
        P.op("dve", lambda e: e.tensor_scalar(out=C.hrow[:], in0=hb[:, 0, :], scalar1=C.flags[0:1, 0:1], scalar2=None, op0=ALU.mult), reads=[hb, C.flags], writes=[C.hrow])
        P.op("dve", lambda e: e.scalar_tensor_tensor(out=C.hrow[:], in0=hb[:, 1, :], scalar=C.flags[0:1, 1:2], in1=C.hrow[:], op0=ALU.mult, op1=ALU.add),
             reads=[hb, C.flags, C.hrow], writes=[C.hrow])
    else:
        z2 = P.sb([64, 2, 512])
        P.dma(z2[:], C.zdst.ap().rearrange("(s p) n -> p s n", s=2), reads=[C.zdst_tr], writes=[z2])
        zf_ = Z[:].rearrange("p a b -> p (a b)")
        P.op("dve", lambda e: e.tensor_scalar(out=zf_, in0=z2[:, 0, :], scalar1=C.flags[0:64, 0:1], scalar2=None, op0=ALU.mult), reads=[z2, C.flags], writes=[Z])
        P.op("dve", lambda e: e.scalar_tensor_tensor(out=zf_, in0=z2[:, 1, :], scalar=C.flags[0:64, 1:2], in1=zf_, op0=ALU.mult, op1=ALU.add),
             reads=[z2, C.flags, Z], writes=[Z])
        copy_op(P, "dve", Zb[:], Z[:], [Z], [Zb])
    if d == 1:
        yfw = [f32([128, 512]) for _ in range(2)]
        zrw = [f32([128, 512]) for _ in range(2)]
        xres = [f32([128, 1024]) for _ in range(2)]
        mean = f32([128, 8]); var = f32([128, 8]); cent = f32([128, 512]); rkk = f32([128, 512]); bon = f32([128, 8])
        yb = f32([128, 512]); szr = f32([128, 512])
        mixA = [b16([128, 4, 128]) for _ in range(2)]; ybT = b16([128, 4, 128])
        xo = [f32([128, 1024]) for _ in range(2)]
        mixT = C.mixT.ap().rearrange("(k p) t -> p k t", p=128)

    v3 = lambda t, n=8: t[:].rearrange("p (h d) -> p h d", h=n)
    order = list(range(NT)) if d == 0 else list(range(NT - 1, -1, -1))
    last = 127 if d == 0 else 0
    HW = C.hrw
    def front(it):
        i = order[it]; par = it % 2; r0 = i * 128
        PR, Pm, MqT, MkT, LkV, KV, Qdb, Vb, gcol, hs, kp = PRr[par], Pmr[par], MqTr[par], MkTr[par], LkVr[par], KVr[par], Qdbr[par], Vbr[par], gcolr[par], hsr[par], kpr[par]
        r_ap, k_ap, v_ap = hs[:, 0:512], hs[:, 512:1024], hs[:, 1024:1536]
        c_, p_, n_ = cur[par], prv[par], nxt[par]
        r_ap, k_ap, v_ap = hs[:, 0:512], hs[:, 512:1024], hs[:, 1024:1536]
        if d == 0:
            P.dma(c_[:], HW[r0:r0 + 128, 1024:2688], reads=[C.proj_tr[i]], writes=[c_])
            if i == 0:
                P.op("pool", lambda e: e.memset(p_[:], 0.0), writes=[p_])
                P.dma(p_[1:128, :], HW[r0:r0 + 127, 1024:2688], reads=[C.proj_tr[i]], writes=[p_])
            else:
                P.dma(p_[:], HW[r0 - 1:r0 + 127, 1024:2688], reads=[C.proj_tr[i], C.proj_tr[i - 1]], writes=[p_])
            if i == NT - 1:
                P.dma(n_[0:127, :], HW[r0 + 1:r0 + 128, 1024:2688], reads=[C.proj_tr[i]], writes=[n_])
                P.dma(n_[127:128, :], C.hrow[:], reads=[C.hrow], writes=[n_], accum=True)
            else:
                P.dma(n_[:], HW[r0 + 1:r0 + 129, 1024:2688], reads=[C.proj_tr[i], C.proj_tr[i + 1]], writes=[n_])
        if d == 1:
            P.dma(yfw[par][:], C.yrw[r0:r0 + 128, :], reads=[C.yrw_tr[i]], writes=[yfw[par]])
            P.dma(zrw[par][:], HW[r0:r0 + 128, 2688:3200], reads=[C.proj_tr[i]], writes=[zrw[par]])
            P.dma(xres[par][:], x_src[r0:r0 + 128, :], reads=[xs_tr[i]], writes=[xres[par]])
            P.dma(mixA[par][:], mixT[:, 0:4, r0:r0 + 128], reads=[C.mix_tr[i]], writes=[mixA[par]])
        if d == 0:
            tt("pool", tsum[:], p_[:], n_[:], ALU.add, [p_, n_], [tsum])
            P.op("dve", lambda e: e.scalar_tensor_tensor(out=tsum[:], in0=tsum[:], scalar=0.5, in1=c_[:], op0=ALU.mult, op1=ALU.subtract),
                 reads=[tsum, c_], writes=[tsum])
            tt("pool", tsum[:], tsum[:], mu[:], ALU.mult, [tsum, mu], [tsum])
            tt("dve", hs[:], tsum[:], c_[:], ALU.add, [tsum, c_], [hs])
            P.op("act", lambda e: e.activation(out=twla[:, 0:64], in_=hs[:, 1536:1600], func=AF.Tanh), reads=[hs], writes=[twla])
            copy_op(P, "dve", twla[:, 64:128], hs[:, 1600:1664], [hs], [twla], accum=True)
            g0 = C.gbank()
            P.op("pe", lambda e: e.transpose(out=g0[:, 0:128], in_=twla[:], identity=C.ident[:]), reads=[twla, C.ident], writes=[g0])
            copy_op(P, "dve", twlaT[:], g0[:, 0:128], [g0], [twlaT])
            g1 = C.gbank()
            P.op("pe", lambda e: e.matmul(g1[:], lhsT=twlaT[64:128, :], rhs=ups[64:128, 1, :], start=True, stop=True), reads=[twlaT, ups], writes=[g1])
            tt("dve", a_t[:], g1[:], a0[:], ALU.add, [g1, a0], [a_t])
            P.op("act", lambda e: e.activation(out=a_t[:], in_=a_t[:], func=AF.Sigmoid), reads=[a_t], writes=[a_t])
            tt("pool", kkn[:], k_ap, kkp[:], ALU.mult, [hs, kkp], [kkn])
            tt("pool", tmp[:], kkn[:], kkn[:], ALU.mult, [kkn], [tmp])
            P.op("dve", lambda e: e.tensor_reduce(out=pss[:], in_=v3(tmp), axis=AX.X, op=ALU.add), reads=[tmp], writes=[pss])
            P.op("act", lambda e: e.activation(out=pss[:], in_=pss[:], func=AF.Sqrt), reads=[pss], writes=[pss])
            P.op("dve", lambda e: e.tensor_scalar(out=pss[:], in0=pss[:], scalar1=1e-12, scalar2=None, op0=ALU.max), reads=[pss], writes=[pss])
            P.op("dve", lambda e: e.reciprocal(out=prn[:], in_=pss[:]), reads=[pss], writes=[prn])
            tt("dve", v3(p_t), v3(kkn), prn[:].unsqueeze(2).to_broadcast([128, 8, 64]), ALU.mult, [kkn, prn], [p_t])
            tt("pool", q_t[:], p_t[:], a_t[:], ALU.mult, [p_t, a_t], [q_t])
            P.op("dve", lambda e: e.scalar_tensor_tensor(out=tmp[:], in0=a_t[:], scalar=-1.0, in1=kap[:], op0=ALU.add, op1=ALU.mult), reads=[a_t, kap], writes=[tmp])
            P.op("dve", lambda e: e.scalar_tensor_tensor(out=kp[:], in0=tmp[:], scalar=1.0, in1=k_ap, op0=ALU.add, op1=ALU.mult), reads=[tmp, hs], writes=[kp])
            P.dma(C.rwc[r0:r0 + 128, 0:512], hs[:, 0:512], reads=[hs], writes=[C.rwc_tr[i]])
            P.dma(C.rwc[r0:r0 + 128, 512:1024], hs[:, 1024:1536], reads=[hs], writes=[C.rwc_tr[i]], accum=True)
            P.dma(C.rwc[r0:r0 + 128, 1024:1536], p_t[:], reads=[p_t], writes=[C.rwc_tr[i]], accum=True)
            P.dma(C.rwc[r0:r0 + 128, 1536:2048], q_t[:], reads=[q_t], writes=[C.rwc_tr[i]], accum=True)
            P.dma(C.rwc[r0:r0 + 128, 2048:2560], kp[:], reads=[kp], writes=[C.rwc_tr[i]], accum=True)
            P.dma(C.rwt[i], twlaT[0:64, :], reads=[twlaT], writes=[C.rwc_tr[i]], accum=True)
        else:
            P.dma(hs[:, 0:512], C.rwc[r0:r0 + 128, 0:512], reads=[C.rwc_tr[i]], writes=[hs])
            P.dma(hs[:, 1024:1536], C.rwc[r0:r0 + 128, 512:1024], reads=[C.rwc_tr[i]], writes=[hs], accum=True)
            P.dma(p_t[:], C.rwc[r0:r0 + 128, 1024:1536], reads=[C.rwc_tr[i]], writes=[p_t])
            P.dma(q_t[:], C.rwc[r0:r0 + 128, 1536:2048], reads=[C.rwc_tr[i]], writes=[q_t])
            P.dma(kp[:], C.rwc[r0:r0 + 128, 2048:2560], reads=[C.rwc_tr[i]], writes=[kp])
            P.dma(twlaT[0:64, :], C.rwt[i], reads=[C.rwc_tr[i]], writes=[twlaT])
        g2 = C.gbank()
        P.op("pe", lambda e: e.matmul(g2[:], lhsT=twlaT[0:64, :], rhs=ups[0:64, 0, :], start=True, stop=True), reads=[twlaT, ups], writes=[g2])
        tt("dve", e2[:], g2[:], w0[:], ALU.add, [g2, w0], [e2])
        P.op("act", lambda e: e.activation(out=e2[:], in_=e2[:], func=AF.Exp, scale=-1.0), reads=[e2], writes=[e2])
        P.op("act", lambda e: e.activation(out=e2[:], in_=e2[:], func=AF.Ln, bias=1.0), reads=[e2], writes=[e2])
        P.op("act", lambda e: e.activation(out=e2[:], in_=e2[:], func=AF.Exp, scale=-1.0, bias=-0.5), reads=[e2], writes=[e2])
        copy_op(P, "act", Vb[:], v_ap, [hs], [Vb])
        yield
        gI = C.gbank()
        P.op("pe", lambda e: e.matmul(gI[:], lhsT=triI[:], rhs=e2[:], start=True, stop=True), reads=[triI, e2], writes=[gI])
        P.op("act", lambda e: e.activation(out=GI[:], in_=gI[:], func=AF.Exp, scale=-1.0), reads=[gI], writes=[GI])
        P.op("act", lambda e: e.activation(out=GIinv[:], in_=gI[:], func=AF.Exp), reads=[gI], writes=[GIinv])
        gE = C.gbank()
        P.op("pe", lambda e: e.matmul(gE[:], lhsT=triE[:], rhs=e2[:], start=True, stop=True), reads=[triE, e2], writes=[gE])
        P.op("act", lambda e: e.activation(out=GE[:], in_=gE[:], func=AF.Exp, scale=-1.0), reads=[gE], writes=[GE])
        gT = C.gbank()
        for h in range(8):
            P.op("pe", lambda e, h=h: e.matmul(gT[0:64, h:h + 1], lhsT=e2[:, h * 64:(h + 1) * 64], rhs=onescol[:, 0:1], start=True, stop=True),
                 reads=[e2, onescol], writes=[gT], accum=(h > 0))
        P.op("act", lambda e: e.activation(out=gcol[:], in_=gT[0:64, 0:8], func=AF.Exp, scale=-1.0), reads=[gT], writes=[gcol])
        tt("dve", Pd[:], p_t[:], GE[:], ALU.mult, [p_t, GE], [Pd])
        tt("pool", Qd[:], q_t[:], GIinv[:], ALU.mult, [q_t, GIinv], [Qd])
        tt("dve", Kd[:], kp[:], GIinv[:], ALU.mult, [kp, GIinv], [Kd])
        tt("pool", Rd[:], r_ap, GI[:], ALU.mult, [hs, GI], [Rd])
        copy_op(P, "act", Pdb[:], Pd[:], [Pd], [Pdb]); copy_op(P, "dve", Qdb[:], Qd[:], [Qd], [Qdb]); copy_op(P, "act", Kdb[:], Kd[:], [Kd], [Kdb])
        yield
        for (src, dstfn, dstt) in ((Pd, lambda h: PR[:, h, 0, :], PR), (Rd, lambda h: PR[:, h, 1, :], PR), (Qd, lambda h: QTt[:, h, :], QTt), (Kd, lambda h: KTt[:, h, :], KTt)):
            for hb in range(2):
                g = C.gbank()
                for hl in range(4):
                    h = 4 * hb + hl
                    P.op("pe", lambda e, hl=hl, h=h, g=g, src=src: e.transpose(out=g[0:64, hl * 128:(hl + 1) * 128], in_=src[:, h * 64:(h + 1) * 64], identity=C.ident[:]),
                         reads=[src, C.ident], writes=[g], accum=(hl > 0))
                if dstt is PR:
                    which = 0 if src is Pd else 1
                    copy_op(P, ("dve", "act")[hb], PR[:, 4 * hb:4 * hb + 4, which, :], g[0:64, :].rearrange("p (a b) -> p a b", a=4), [g], [PR], accum=True)
                else:
                    copy_op(P, ("act", "dve")[hb], dstt[:, 4 * hb:4 * hb + 4, :], g[0:64, :].rearrange("p (a b) -> p a b", a=4), [g], [dstt], accum=(hb > 0))
        yield
        Bc, Ac = Bm[0], Am[0]
        for hg in range(4):
            gq = C.gbank(); gk = C.gbank()
            for hh in range(2):
                h = 2 * hg + hh
                P.op("pe", lambda e: e.matmul(gq[:, hh * 256:(hh + 1) * 256], lhsT=QTt[:, h, :],
                                              rhs=PR[:, h, :, :].rearrange("p a b -> p (a b)"), start=True, stop=True),
                     reads=[QTt, PR], writes=[gq], accum=(hh > 0))
                P.op("pe", lambda e: e.matmul(gk[:, hh * 256:(hh + 1) * 256], lhsT=KTt[:, h, :],
                                              rhs=PR[:, h, :, :].rearrange("p a b -> p (a b)"), start=True, stop=True),
                     reads=[KTt, PR], writes=[gk], accum=(hh > 0))
            gq4 = gq[:].rearrange("p (h a b) -> p h a b", h=2, a=2)
            gk4 = gk[:].rearrange("p (h a b) -> p h a b", h=2, a=2)
            hs2 = slice(2 * hg, 2 * hg + 2)
            mE = triE[:].unsqueeze(1).to_broadcast([128, 2, 128]); mI = triI[:].unsqueeze(1).to_broadcast([128, 2, 128])
            tt("dve", Bc[:, hs2, :], gq4[:, :, 0, :], mE, ALU.mult, [gq, triE], [Bc], accum=(hg > 0))
            tt("dve", MqT[:, hs2, :], gq4[:, :, 1, :], mI, ALU.mult, [gq, triI], [MqT], accum=(hg > 0))
            tt("dve", LkT[:, hs2, :], gk4[:, :, 0, :], mE, ALU.mult, [gk, triE], [LkT], accum=(hg > 0))
            tt("dve", MkT[:, hs2, :], gk4[:, :, 1, :], mI, ALU.mult, [gk, triI], [MkT], accum=(hg > 0))
        for hb in range(2):
            g = C.gbank()
            for hl in range(4):
                h = 4 * hb + hl
                P.op("pe", lambda e: e.matmul(g[:, hl * 128:(hl + 1) * 128], lhsT=PR[:, h, 0, :], rhs=QTt[:, h, :],
                                              start=True, stop=True), reads=[PR, QTt], writes=[g], accum=(hl > 0))
            tt("dve", Ac[:, 4 * hb:4 * hb + 4, :], g[:].rearrange("p (h b) -> p h b", h=4), triET[:].unsqueeze(1).to_broadcast([128, 4, 128]), ALU.mult,
               [g, triET], [Ac], accum=(hb > 0))
        yield
        P.op("dve", lambda e: e.scalar_tensor_tensor(out=Pm[:], in0=Bc[:], scalar=-1.0, in1=eye_b[:].unsqueeze(1).to_broadcast([128, 8, 128]),
                                                     op0=ALU.mult, op1=ALU.add), reads=[Bc, eye_b], writes=[Pm])
        for lev in range(6):
            Bn, An = Bm[(lev + 1) % 2], Am[(lev + 1) % 2]
            for hb in range(2):
                gA = C.gbank()
                for hl in range(4):
                    h = 4 * hb + hl
                    P.op("pe", lambda e: e.matmul(gA[:, hl * 128:(hl + 1) * 128], lhsT=Bc[:, h, :], rhs=Ac[:, h, :], start=True, stop=True),
                         reads=[Bc, Ac], writes=[gA], accum=(hl > 0))
                copy_op(P, ("act", "dve")[hb], An[:, 4 * hb:4 * hb + 4, :].rearrange("p a b -> p (a b)"), gA[:], [gA], [An], accum=(hb > 0))
                if lev < 5:
                    gB = C.gbank()
                    for hl in range(4):
                        h = 4 * hb + hl
                        P.op("pe", lambda e: e.matmul(gB[:, hl * 128:(hl + 1) * 128], lhsT=Ac[:, h, :], rhs=Bc[:, h, :], start=True, stop=True),
                             reads=[Bc, Ac], writes=[gB], accum=(hl > 0))
                    copy_op(P, ("dve", "act")[hb], Bn[:, 4 * hb:4 * hb + 4, :].rearrange("p a b -> p (a b)"), gB[:], [gB], [Bn], accum=(hb > 0))
            for hb in range(2):
                gP = C.gbank()
                for hl in range(4):
                    h = 4 * hb + hl
                    P.op("pe", lambda e: e.matmul(gP[:, hl * 128:(hl + 1) * 128], lhsT=An[:, h, :], rhs=Pm[:, h, :], start=True, stop=True),
                         reads=[An, Pm], writes=[gP], accum=(hl > 0))
                tt("dve", Pm[:, 4 * hb:4 * hb + 4, :].rearrange("p a b -> p (a b)"), gP[:], Pm[:, 4 * hb:4 * hb + 4, :].rearrange("p a b -> p (a b)"),
                   ALU.add, [gP, Pm], [Pm], accum=True)
            Bc, Ac = Bn, An
            yield
        yield
        g = C.gbank()
        for h in range(8):
            P.op("pe", lambda e: e.matmul(g[:, h * 64:(h + 1) * 64], lhsT=LkT[:, h, :], rhs=Vb[:, h * 64:(h + 1) * 64], start=True, stop=True),
                 reads=[LkT, Vb], writes=[g], accum=(h > 0))
        copy_op(P, "act", LkV[:], g[:], [g], [LkV])
        g = C.gbank()
        for h in range(8):
            P.op("pe", lambda e: e.matmul(g[0:64, h * 64:(h + 1) * 64], lhsT=Kdb[:, h * 64:(h + 1) * 64], rhs=Vb[:, h * 64:(h + 1) * 64],
                                          start=True, stop=True), reads=[Kdb, Vb], writes=[g], accum=(h > 0))
        copy_op(P, "dve", KV[:].rearrange("p a b -> p (a b)"), g[0:64, :], [g], [KV])

    def back(it):
        i = order[it]; par = it % 2; r0 = i * 128
        PR, Pm, MqT, MkT, LkV, KV, Qdb, Vb, gcol, hs, kp = PRr[par], Pmr[par], MqTr[par], MkTr[par], LkVr[par], KVr[par], Qdbr[par], Vbr[par], gcolr[par], hsr[par], kpr[par]
        r_ap, k_ap, v_ap = hs[:, 0:512], hs[:, 512:1024], hs[:, 1024:1536]
        gcb = gcol[:].unsqueeze(2).to_broadcast([64, 8, 64])
        tt("pool", ZK[:], Z[:], KV[:], ALU.add, [Z, KV], [ZK])
        tt("pool", ZKg[:], ZK[:], gcb, ALU.mult, [ZK, gcol], [ZKg])
        gz = bk[3]
        for h in range(8):
            P.op("pe", lambda e: e.matmul(gz[:, h * 64:(h + 1) * 64], lhsT=PR[:, h, 0, :], rhs=Zb[:, h, :], start=True, stop=True),
                 reads=[PR, Zb], writes=[gz], accum=(h > 0))
        tt("dve", rhs_sb[:], gz[:], LkV[:], ALU.add, [gz, LkV], [rhs_sb])
        yield
        gu = bk[4]
        for h in range(8):
            P.op("pe", lambda e: e.matmul(gu[:, h * 64:(h + 1) * 64], lhsT=Pm[:, h, :], rhs=rhs_sb[:, h * 64:(h + 1) * 64], start=True, stop=True),
                 reads=[Pm, rhs_sb], writes=[gu], accum=(h > 0))
        P.op("act", lambda e: e.activation(out=U_sb[:], in_=gu[:], func=AF.Copy, scale=-1.0), reads=[gu], writes=[U_sb])
        yield
        gy = bk[5]
        for h in range(8):
            osl = gy[:, h * 64:(h + 1) * 64]
            P.op("pe", lambda e: e.matmul(osl, lhsT=PR[:, h, 1, :], rhs=Zb[:, h, :], start=True, stop=False),
                 reads=[PR, Zb], writes=[gy], accum=(h > 0))
            P.op("pe", lambda e: e.matmul(osl, lhsT=MqT[:, h, :], rhs=U_sb[:, h * 64:(h + 1) * 64], start=False, stop=False),
                 reads=[MqT, U_sb], writes=[gy], accum=True)
            P.op("pe", lambda e: e.matmul(osl, lhsT=MkT[:, h, :], rhs=Vb[:, h * 64:(h + 1) * 64], start=False, stop=True),
                 reads=[MkT, Vb], writes=[gy], accum=True)
        gq_ = bk[6]
        for h in range(8):
            P.op("pe", lambda e: e.matmul(gq_[0:64, h * 64:(h + 1) * 64], lhsT=Qdb[:, h * 64:(h + 1) * 64], rhs=U_sb[:, h * 64:(h + 1) * 64],
                                          start=True, stop=True), reads=[Qdb, U_sb], writes=[gq_], accum=(h > 0))
        tt("dve", Ztmp[:], gq_[0:64, :].rearrange("p (a b) -> p a b", a=8), gcb, ALU.mult, [gq_, gcol], [Ztmp])
        tt("dve", Z[:], Ztmp[:], ZKg[:], ALU.add, [Ztmp, ZKg], [Z])
        copy_op(P, "dve", Zb[:], Z[:], [Z], [Zb])
        yield
        if d == 0:
            yt = ysb[par]
            copy_op(P, "act", yt[:], gy[:], [gy], [yt])
            P.dma(C.yrw[r0:r0 + 128, :], yt[:], reads=[yt], writes=[C.yrw_tr[i]], q="act")
            if it == NT - 1:
                P.dma(C.zsrc[:, :], Z[:].rearrange("p a b -> p (a b)"), reads=[Z], writes=[C.zsrc_tr])
                P.collective(C.zsrc[:, :], C.zdst[:, :], reads=[C.zsrc_tr], writes=[C.zdst_tr])
        else:
            y = ysb[par]
            tt("dve", y[:], gy[:], yfw[par][:], ALU.add, [gy, yfw[par]], [y])
            P.op("dve", lambda e: e.tensor_reduce(out=mean[:], in_=v3(y), axis=AX.X, op=ALU.add), reads=[y], writes=[mean])
            P.op("dve", lambda e: e.tensor_scalar(out=mean[:], in0=mean[:], scalar1=1.0 / 64, scalar2=None, op0=ALU.mult), reads=[mean], writes=[mean])
            tt("dve", v3(cent), v3(y), mean[:].unsqueeze(2).to_broadcast([128, 8, 64]), ALU.subtract, [y, mean], [cent])
            tt("pool", tmp2[:], cent[:], cent[:], ALU.mult, [cent], [tmp2])
            P.op("dve", lambda e: e.tensor_reduce(out=var[:], in_=v3(tmp2), axis=AX.X, op=ALU.add), reads=[tmp2], writes=[var])
            P.op("act", lambda e: e.activation(out=var[:], in_=var[:], func=AF.Sqrt, scale=1.0 / 64, bias=64e-5), reads=[var], writes=[var])
            P.op("dve", lambda e: e.reciprocal(out=var[:], in_=var[:]), reads=[var], writes=[var])
            tt("dve", v3(cent), v3(cent), var[:].unsqueeze(2).to_broadcast([128, 8, 64]), ALU.mult, [cent, var], [cent])
            tt("pool", cent[:], cent[:], lng[:], ALU.mult, [cent, lng], [cent])
            tt("pool", cent[:], cent[:], lnb[:], ALU.add, [cent, lnb], [cent])
            tt("pool", rkk[:], r_ap, kp[:], ALU.mult, [hs, kp], [rkk])
            tt("pool", rkk[:], rkk[:], rkp[:], ALU.mult, [rkk, rkp], [rkk])
            P.op("dve", lambda e: e.tensor_reduce(out=bon[:], in_=v3(rkk), axis=AX.X, op=ALU.add), reads=[rkk], writes=[bon])
            tt("dve", v3(rkk), hs[:, 1024:1536].rearrange("p (h d) -> p h d", h=8), bon[:].unsqueeze(2).to_broadcast([128, 8, 64]), ALU.mult, [hs, bon], [rkk])
            tt("pool", cent[:], cent[:], rkk[:], ALU.add, [cent, rkk], [cent])
            P.op("act", lambda e: e.activation(out=szr[:], in_=zrw[par][:], func=AF.Silu), reads=[zrw[par]], writes=[szr])
            tt("dve", yb[:], cent[:], szr[:], ALU.mult, [cent, szr], [yb])
            yield
            transpose_to(P, C, yb, 4, ybT)
            xot = xo[par]
            for gcol_i in range(2):
                bank = C.gbank()
                for c in range(8):
                    lhs = mixA[par][:, c, :] if c < 4 else ybT[:, c - 4, :]
                    P.op("pe", lambda e, c=c, lhs=lhs, bank=bank: e.matmul(bank[:], lhsT=lhs, rhs=wo[:, c, gcol_i * 512:(gcol_i + 1) * 512],
                                                                          start=(c == 0), stop=(c == 7)),
                         reads=[mixA[par], ybT, wo], writes=[bank], accum=(c > 0))
                tt("dve", xot[:, gcol_i * 512:(gcol_i + 1) * 512], bank[:], xres[par][:, gcol_i * 512:(gcol_i + 1) * 512], ALU.add,
                   [bank, xres[par]], [xot], accum=(gcol_i > 0))
            P.dma(x_dst[r0:r0 + 128, :], xot[:], reads=[xot], writes=[xd_tr[i]], q="act")

    import os
    if os.environ.get("NOSKEW"):
        for it in range(NT):
            for _ in front(it):
                pass
            for _ in back(it):
                pass
    else:
        for _ in front(0):
            pass
        for it in range(NT):
            gf = front(it + 1) if it + 1 < NT else iter(())
            gb = back(it)
            fdone = bdone = False
            while not (fdone and bdone):
                if not fdone:
                    try:
                        next(gf)
                    except StopIteration:
                        fdone = True
                if not bdone:
                    try:
                        next(gb)
                    except StopIteration:
                        bdone = True
    P.barrier()
    st.close(); P.stack = P.gstack


NT_FULL = 64
LAYERS = [("even", 0), ("odd", 0), ("even", 1), ("odd", 1)]
DIR_KEYS = ("s5_ar_row", "s5_ai_row", "s5_dt_row", "s5_ar_col", "s5_ai_col", "s5_dt_col", "s5_b_col", "s5_c_col", "rw_w0_rep", "rw_w_up")


def core_flags(w0, w1):
    fl = np.zeros((128, 4), np.float32)
    fl[:, 0] = w0
    fl[:, 1] = w1
    fl[:, 2] = 0.0 if (w0 + w1) > 0 else NEG
    return fl


def core_maps(m, streams):
    mrev = dict(m)
    for k in DIR_KEYS:
        mrev[k] = np.ascontiguousarray(m[k][:, ::-1])
    maps = []
    for (x, rev, w0, w1) in streams:
        mm = dict(mrev if rev else m)
        mm["flags"] = core_flags(w0, w1)
        mm["xin"] = np.ascontiguousarray(x[::-1] if rev else x)
        maps.append(mm)
    return maps


def kernel(**inputs):
    xp = np.asarray(inputs["x_prompt"], np.float32)
    xs = np.asarray(inputs["x_sample"], np.float32)
    m = host_layout(inputs, LAYERS)
    nc, gst = build_program(NT_FULL, LAYERS)
    streams = [(xs[0, 0:8192], False, 0.0, 1.0), (xs[0, 8192:16384], True, 1.0, 0.0)]
    for b in range(4):
        streams.append((xp[b], False, 0.0, 0.0))
    streams += [(xp[0], False, 0.0, 0.0), (xp[1], False, 0.0, 0.0)]
    maps = core_maps(m, streams)
    res = run_bass_kernel_spmd(nc, maps, core_ids=list(range(8)))
    outs = [np.asarray(res.results[c]["xout"], np.float32) for c in range(6)]
    y_sample = np.concatenate([outs[0], outs[1][::-1]], axis=0).reshape(1, 16384, D)
    y_prompt = np.stack(outs[2:6], axis=0)
    return (y_prompt, y_sample)
```

```python
import numpy as np
from contextlib import ExitStack
import concourse.bass as bass
import concourse.mybir as mybir
from concourse.bass_utils import run_bass_kernel_spmd

F32 = mybir.dt.float32
BF16 = mybir.dt.bfloat16
AF = mybir.ActivationFunctionType
ALU = mybir.AluOpType
AX = mybir.AxisListType

import os as _os
N_DMA_SLOTS = int(_os.environ.get("NSLOTS", "24"))
D = 1024
EPS = 1e-6
NEG = -30000.0


import types


def _snap(fn):
    if fn.__closure__ is None:
        return fn
    cells = tuple(types.CellType(c.cell_contents) for c in fn.__closure__)
    return types.FunctionType(fn.__code__, fn.__globals__, fn.__name__, fn.__defaults__, cells)


class T:
    __slots__ = ("t", "name", "writers", "readers", "war")

    def __init__(self, t, name=""):
        self.t = t
        self.name = name
        self.writers = []
        self.readers = []
        self.war = []

    def __getitem__(self, idx):
        return self.t[idx]


class Prog:
    ENGS = ("pe", "act", "dve", "pool", "sp")

    def __init__(self, nc, stack):
        self.nc = nc
        self.stack = stack
        self.gstack = stack
        self.ops = {e: [] for e in self.ENGS}
        self.cnt = {e: 0 for e in self.ENGS}
        self.seen = {e: {} for e in self.ENGS}
        self.sems = {e: stack.enter_context(nc.semaphore("s_" + e)) for e in self.ENGS}
        self.dma_sems = [stack.enter_context(nc.semaphore("s_dma%d" % i)) for i in range(N_DMA_SLOTS)]
        self.dma_n = 0
        self.cc_n = 0
        self.sems["cc"] = stack.enter_context(nc.semaphore("s_cc"))
        import os
        self.same_engine_sync = not os.environ.get("NOSES")
        self._uid = 0

    def sb(self, shape, dt=F32, name=None):
        self._uid += 1
        name = "sb%d" % self._uid
        return T(self.stack.enter_context(self.nc.sbuf_tensor(name, list(shape), dt)), name)

    def ps(self, shape, dt=F32):
        self._uid += 1
        name = "ps%d" % self._uid
        return T(self.stack.enter_context(self.nc.psum_tensor(name, list(shape), dt)), name)

    def dram(self, name, shape, dt=F32):
        return self.nc.dram_tensor(name, list(shape), dt, kind="Internal")

    def _need(self, eng, dep, waits):
        key, val, deng = dep
        if deng == eng and (eng == "pe" or not self.same_engine_sync):
            return
        if self.seen[eng].get(key, -1) >= val:
            return
        self.seen[eng][key] = val
        waits.append((key, val))

    def _deps(self, eng, reads, writes, accum):
        waits = []
        for t in reads:
            for w in t.writers:
                self._need(eng, w, waits)
        for t in writes:
            if not accum:
                for w in t.writers:
                    self._need(eng, w, waits)
            else:
                for w in t.war:
                    self._need(eng, w, waits)
            for r in t.readers:
                self._need(eng, r, waits)
        return waits

    def _commit(self, tok, reads, writes, accum):
        for t in reads:
            t.readers.append(tok)
        for t in writes:
            if accum:
                t.writers.append(tok)
                t.war = t.war + t.readers
            else:
                t.war = t.writers + t.readers
                t.writers = [tok]
            t.readers = []

    def _sem(self, key):
        return self.sems[key] if isinstance(key, str) else self.dma_sems[key]

    def op(self, eng, fn, reads=(), writes=(), accum=False):
        import os
        if eng == "pool" and os.environ.get("NOPOOL"):
            eng = "dve"
        kmax = int(os.environ.get("KMAX", "0"))
        if kmax and sum(self.cnt.values()) >= kmax:
            return None
        waits = self._deps(eng, reads, writes, accum)
        self.cnt[eng] += 1
        tok = (eng, self.cnt[eng], eng)
        self._commit(tok, reads, writes, accum)
        self.ops[eng].append((waits, _snap(fn), (eng, 1)))
        return tok

    def dma(self, out_ap, in_ap, reads=(), writes=(), q="sp", accum=False):
        waits = self._deps(q, reads, writes, accum)
        i = self.dma_n
        self.dma_n += 1
        slot = i % N_DMA_SLOTS
        val = 16 * (i // N_DMA_SLOTS + 1)
        if i >= N_DMA_SLOTS and self.seen[q].get(slot, -1) < val - 16:
            self.seen[q][slot] = val - 16
            waits.append((slot, val - 16))
        tok = (slot, val, "dma")
        self._commit(tok, reads, writes, accum)

        def fn(e, out_ap=out_ap, in_ap=in_ap):
            return e.dma_start(out=out_ap, in_=in_ap)
        self.ops[q].append((waits, fn, (slot, 16)))
        return tok

    def collective(self, src_ap, dst_ap, reads=(), writes=(), groups=((0, 1), (2, 3), (4, 5), (6, 7))):
        q = "pool"
        waits = self._deps(q, reads, writes, False)
        self.cc_n += 1
        tok = ("cc", self.cc_n, "cc")
        self._commit(tok, reads, writes, False)
        rg = [list(g) for g in groups]

        def fn(e):
            return e.collective_compute("AllGather", ALU.bypass, replica_groups=rg, ins=[src_ap], outs=[dst_ap])
        self.ops[q].append((waits, fn, ("cc", 1)))
        return tok

    def barrier(self):
        for e in self.ENGS:
            waits = []
            for o in self.ENGS:
                if o != e and self.cnt[o] > 0 and self.seen[e].get(o, -1) < self.cnt[o]:
                    self.seen[e][o] = self.cnt[o]
                    waits.append((o, self.cnt[o]))
            if self.cc_n > 0 and self.seen[e].get("cc", -1) < self.cc_n:
                self.seen[e]["cc"] = self.cc_n
                waits.append(("cc", self.cc_n))
            n = self.dma_n
            for slot in range(min(n, N_DMA_SLOTS)):
                last_i = ((n - 1 - slot) // N_DMA_SLOTS) * N_DMA_SLOTS + slot
                v = 16 * (last_i // N_DMA_SLOTS + 1)
                if self.seen[e].get(slot, -1) < v:
                    self.seen[e][slot] = v
                    waits.append((slot, v))
            if waits:
                self.ops[e].append((waits, None, None))

    def emit(self):
        nc = self.nc
        self.barrier()
        block = self.gstack.enter_context(nc.Block())
        prog = self

        def run(engname, e):
            for waits, fn, inc in prog.ops[engname]:
                for key, val in waits:
                    e.wait_ge(prog._sem(key), val)
                if fn is not None:
                    fn(e).then_inc(prog._sem(inc[0]), inc[1])

        @block.tensor
        def _(e):
            run("pe", e)

        @block.scalar
        def _(e):
            run("act", e)

        @block.vector
        def _(e):
            run("dve", e)

        @block.gpsimd
        def _(e):
            run("pool", e)

        @block.sync
        def _(e):
            run("sp", e)


class Ctx:
    pass


def rr(P, C, key, engs):
    C.rr[key] = C.rr.get(key, -1) + 1
    return engs[C.rr[key] % len(engs)]


def copy_op(P, eng, out_ap, in_ap, reads, writes, accum=False):
    if eng == "act":
        P.op("act", lambda e: e.activation(out=out_ap, in_=in_ap, func=AF.Copy), reads=reads, writes=writes, accum=accum)
    elif eng == "dve":
        P.op("dve", lambda e: e.tensor_copy(out=out_ap, in_=in_ap), reads=reads, writes=writes, accum=accum)
    else:
        P.op("pool", lambda e: e.tensor_copy(out=out_ap, in_=in_ap), reads=reads, writes=writes, accum=accum)


def load_weight_bf16(P, C, dst, src_ap_fn, nchunk, ncols, stage):
    for c in range(nchunk):
        s = stage[c % len(stage)]
        P.dma(s[:, 0:ncols], src_ap_fn(c), reads=[], writes=[s])
        eng = ("act", "dve", "pool")[c % 3]
        copy_op(P, eng, dst[:, c, :], s[:, 0:ncols], [s], [dst], accum=True)


def rmsnorm_T(P, C, xt, gn, hT, S):
    junk, ss, ss2, rs, h = S.junk, S.ss, S.ss2, S.rs, S.h
    P.op("act", lambda e: e.activation(out=junk[:], in_=xt[:], func=AF.Square, accum_out=ss[:]),
         reads=[xt], writes=[junk, ss])
    P.op("act", lambda e: e.activation(out=ss2[:], in_=ss[:], func=AF.Sqrt, scale=1.0 / D, bias=EPS),
         reads=[ss], writes=[ss2])
    P.op("dve", lambda e: e.reciprocal(out=rs[:], in_=ss2[:]), reads=[ss2], writes=[rs])
    P.op("dve", lambda e: e.scalar_tensor_tensor(out=h[:], in0=xt[:], scalar=rs[:, 0:1], in1=gn[:],
                                                 op0=ALU.mult, op1=ALU.mult), reads=[xt, rs, gn], writes=[h])
    transpose_to(P, C, h, 8, hT)


def transpose_to(P, C, src, nblk, dst, src_off=0):
    for g0 in range(0, nblk, 4):
        n = min(4, nblk - g0)
        bank = C.gbank()
        for c in range(n):
            P.op("pe", lambda e, c=c, bank=bank, g0=g0: e.transpose(
                out=bank[:, c * 128:(c + 1) * 128],
                in_=src[:, src_off + (g0 + c) * 128: src_off + (g0 + c + 1) * 128], identity=C.ident[:]),
                reads=[src, C.ident], writes=[bank], accum=(c > 0))
        eng = rr(P, C, "tev", ("act", "dve"))
        copy_op(P, eng, dst[:, g0:g0 + n, :], bank[:, 0:n * 128].rearrange("p (a b) -> p a b", a=n),
                [bank], [dst], accum=(g0 > 0))


def transpose_heads(P, C, src, nheads, dst, src_off=0):
    for g0 in range(0, nheads, 4):
        n = min(4, nheads - g0)
        bank = C.gbank()
        for c in range(n):
            P.op("pe", lambda e, c=c, bank=bank, g0=g0: e.transpose(
                out=bank[0:64, c * 128:(c + 1) * 128],
                in_=src[:, src_off + (g0 + c) * 64: src_off + (g0 + c + 1) * 64], identity=C.ident[:]),
                reads=[src, C.ident], writes=[bank], accum=(c > 0))
        eng = rr(P, C, "tev", ("act", "dve"))
        copy_op(P, eng, dst[:, g0:g0 + n, :], bank[0:64, 0:n * 128].rearrange("p (a b) -> p a b", a=n),
                [bank], [dst], accum=(g0 > 0))


def matmul_group(P, C, bank, ncols, hT, W, col0, nk=8, out_off=0):
    for c in range(nk):
        P.op("pe", lambda e, c=c: e.matmul(bank[:, out_off:out_off + ncols], lhsT=hT[:, c, :],
                                           rhs=W[:, c, col0:col0 + ncols], start=(c == 0), stop=(c == nk - 1)),
             reads=[hT, W], writes=[bank], accum=(c > 0))


def odd_layer(P, C, l, x_src, x_dst, xs_tr, xd_tr):
    NT = C.NT
    st = ExitStack()
    P.stack = st
    I = C.inp
    wq = P.sb([128, 8, 2560], BF16)
    wo = P.sb([128, 8, 1024], BF16)
    st2 = ExitStack(); P.stack = st2
    stage = [P.sb([128, 2560]), P.sb([128, 2560])]
    load_weight_bf16(P, C, wq, lambda c: I["od_w_in"][l, c * 128:(c + 1) * 128, :], 8, 2560, stage)
    load_weight_bf16(P, C, wo, lambda c: I["od_w_out"][l, c * 128:(c + 1) * 128, :], 8, 1024, stage)
    P.barrier()
    st2.close(); P.stack = st
    gn = P.sb([128, 1024]); gq = P.sb([128, 64]); gk = P.sb([128, 64]); esink = P.sb([128, 16])
    P.dma(gn[:], I["od_norm_rep"][l], writes=[gn])
    P.dma(gq[:], I["qg_rep"][l], writes=[gq])
    P.dma(gk[:], I["kg_rep"][l], writes=[gk])
    P.dma(esink[:], I["sink_rep"][l], writes=[esink])
    P.op("act", lambda e: e.activation(out=esink[:], in_=esink[:], func=AF.Exp), reads=[esink], writes=[esink])
    biasT = P.sb([128, 16, 512])
    for j in range(4):
        P.dma(biasT[:, 4 * j:4 * j + 4, :], I["alibi"][j].rearrange("r s q -> s r q"), writes=[biasT], accum=True)
    KTh = P.sb([64, 4, 128], BF16); Vh = P.sb([128, 4, 72], BF16)
    kh2 = P.sb([64, 2, 512], BF16); vh2 = P.sb([128, 2, 288], BF16)

    S = Ctx()
    S.junk = P.sb([128, 1024]); S.ss = P.sb([128, 1]); S.ss2 = P.sb([128, 1]); S.rs = P.sb([128, 1]); S.h = P.sb([128, 1024])
    hT = P.sb([128, 8, 128], BF16)
    xring = [P.sb([128, 1024]) for _ in range(3)]
    qf = P.sb([128, 1024]); qsq = P.sb([128, 1024]); qss = P.sb([128, 16]); qr = P.sb([128, 16]); qn = P.sb([128, 1024])
    kf = P.sb([128, 256]); ksq = P.sb([128, 256]); kss = P.sb([128, 4]); kr = P.sb([128, 4]); kn = P.sb([128, 256])
    QT = [P.sb([64, 16, 128], BF16) for _ in range(3)]
    KT = [P.sb([64, 4, 128], BF16) for _ in range(4)]
    V = [P.sb([128, 4, 72], BF16) for _ in range(4)]
    for v in V:
        P.op("pool", lambda e, v=v: e.memset(v[:], 1.0), writes=[v])
    sz = [P.sb([128, 1024]) for _ in range(3)]
    sring = [P.sb([128, 512]) for _ in range(3)]
    pr = [P.sb([128, 512], BF16) for _ in range(6)]
    den = P.sb([128, 4]); rden = P.sb([128, 4])
    o = P.sb([128, 1024]); og = P.sb([128, 1024]); ogT = P.sb([128, 8, 128], BF16)
    xo = [P.sb([128, 1024]) for _ in range(2)]

    def rms_heads(src, sq, ssum, rinv, nh, g, outs):
        P.op("act", lambda e: e.activation(out=sq[:], in_=src[:], func=AF.Square), reads=[src], writes=[sq])
        P.op("dve", lambda e: e.tensor_reduce(out=ssum[:], in_=sq[:].rearrange("p (h d) -> p h d", h=nh), axis=AX.X, op=ALU.add),
             reads=[sq], writes=[ssum])
        P.op("act", lambda e: e.activation(out=ssum[:], in_=ssum[:], func=AF.Sqrt, scale=1.0 / 64, bias=EPS),
             reads=[ssum], writes=[ssum])
        P.op("dve", lambda e: e.reciprocal(out=rinv[:], in_=ssum[:]), reads=[ssum], writes=[rinv])
        P.op("dve", lambda e: e.tensor_tensor(out=sq[:].rearrange("p (h d) -> p h d", h=nh),
                                              in0=src[:].rearrange("p (h d) -> p h d", h=nh),
                                              in1=rinv[:].unsqueeze(2).to_broadcast([128, nh, 64]), op=ALU.mult),
             reads=[src, rinv], writes=[sq])
        for oi, (oap, ot) in enumerate(outs):
            P.op("dve", lambda e, oap=oap: e.tensor_tensor(out=oap, in0=sq[:].rearrange("p (h d) -> p h d", h=nh),
                                                            in1=g[:].unsqueeze(1).to_broadcast([128, nh, 64]), op=ALU.mult),
                 reads=[sq, g], writes=[ot], accum=(oi > 0))

    def stage1_parts(j):
        xt = xring[j % 3]

        def pa():
            P.dma(xt[:], x_src[j * 128:(j + 1) * 128, :], reads=[xs_tr[j]], writes=[xt])
            rmsnorm_T(P, C, xt, gn, hT, S)

        def pb():
            for g in range(2):
                bank = C.gbank()
                matmul_group(P, C, bank, 512, hT, wq, g * 512)
                copy_op(P, rr(P, C, "qev", ("act", "dve")), qf[:, g * 512:(g + 1) * 512], bank[:], [bank], [qf], accum=(g > 0))
            rms_heads(qf, qsq, qss, qr, 16, gq, [(qn[:].rearrange("p (h d) -> p h d", h=16), qn)])
            transpose_heads(P, C, qn, 16, QT[j % 3])

        def pc():
            bank = C.gbank()
            matmul_group(P, C, bank, 512, hT, wq, 1024)
            copy_op(P, "dve", kf[:], bank[:, 0:256], [bank], [kf])
            Vt = V[j % 4]
            copy_op(P, "dve", Vt[:, :, 0:64], bank[:, 256:512].rearrange("p (h d) -> p h d", h=4), [bank], [Vt])
            rms_heads(kf, ksq, kss, kr, 4, gk, [(kn[:].rearrange("p (h d) -> p h d", h=4), kn)])
            transpose_heads(P, C, kn, 4, KT[j % 4])

        def pd():
            for g in range(2):
                bank = C.gbank()
                matmul_group(P, C, bank, 512, hT, wq, 1536 + g * 512)
                szt = sz[j % 3]
                P.op("act", lambda e, bank=bank, g=g, szt=szt: e.activation(out=szt[:, g * 512:(g + 1) * 512], in_=bank[:], func=AF.Silu),
                     reads=[bank], writes=[szt], accum=(g > 0))

        return [pa, pb, pc, pd]

    def halo_exchange():
        jl = NT - 1
        ktl, vl = KT[jl % 4], V[jl % 4]
        P.dma(C.ksrc[:, :], ktl[:].rearrange("p a b -> p (a b)"), reads=[ktl], writes=[C.ksrc_tr])
        P.dma(C.vsrc[:, :], vl[:].rearrange("p a b -> p (a b)"), reads=[vl], writes=[C.vsrc_tr])
        P.collective(C.ksrc[:, :], C.kdst[:, :], reads=[C.ksrc_tr], writes=[C.kdst_tr])
        P.collective(C.vsrc[:, :], C.vdst[:, :], reads=[C.vsrc_tr], writes=[C.vdst_tr])
        P.dma(kh2[:], C.kdst.ap().rearrange("(s p) n -> p s n", s=2), reads=[C.kdst_tr], writes=[kh2])
        P.dma(vh2[:], C.vdst.ap().rearrange("(s p) n -> p s n", s=2), reads=[C.vdst_tr], writes=[vh2])
        kf_ = KTh[:].rearrange("p a b -> p (a b)"); vf_ = Vh[:].rearrange("p a b -> p (a b)")
        P.op("dve", lambda e: e.tensor_scalar(out=kf_, in0=kh2[:, 0, :], scalar1=C.flags[0:64, 0:1], scalar2=None, op0=ALU.mult), reads=[kh2, C.flags], writes=[KTh])
        P.op("dve", lambda e: e.scalar_tensor_tensor(out=kf_, in0=kh2[:, 1, :], scalar=C.flags[0:64, 1:2], in1=kf_, op0=ALU.mult, op1=ALU.add),
             reads=[kh2, C.flags, KTh], writes=[KTh])
        P.op("dve", lambda e: e.tensor_scalar(out=vf_, in0=vh2[:, 0, :], scalar1=C.flags[:, 0:1], scalar2=None, op0=ALU.mult), reads=[vh2, C.flags], writes=[Vh])
        P.op("dve", lambda e: e.scalar_tensor_tensor(out=vf_, in0=vh2[:, 1, :], scalar=C.flags[:, 1:2], in1=vf_, op0=ALU.mult, op1=ALU.add),
             reads=[vh2, C.flags, Vh], writes=[Vh])

    def stage2_parts(i):
        def pj(jkv):
            blocks = [b for b in (i - 1, i, i + 1) if 0 <= b < NT]
            if i == NT - 1:
                blocks.append(NT)
            for b in blocks:
                rel = b - i + 1
                halo = (b == NT)
                KTb = KTh if halo else KT[b % 4]
                bank = C.sbank[rel]
                for hl in range(4):
                    hq = 4 * jkv + hl
                    P.op("pe", lambda e, bank=bank, hl=hl, b=b, hq=hq: e.matmul(
                        bank[:, hl * 128:(hl + 1) * 128], lhsT=KTb[:, jkv, :],
                        rhs=QT[i % 3][:, hq, :], start=True, stop=True),
                        reads=[KTb, QT[i % 3]], writes=[bank], accum=(hl > 0))
                s_t = sring[rel]
                bidx = 4 * jkv + (3 if halo else rel)
                P.op("dve", lambda e, bank=bank, s_t=s_t, rel=rel: e.scalar_tensor_tensor(
                    out=s_t[:], in0=bank[:], scalar=0.125, in1=biasT[:, bidx, :], op0=ALU.mult, op1=ALU.add),
                    reads=[bank, biasT], writes=[s_t])
                if halo:
                    P.op("dve", lambda e, s_t=s_t: e.tensor_scalar(out=s_t[:], in0=s_t[:], scalar1=C.flags[:, 2:3], scalar2=None,
                                                                   op0=ALU.add), reads=[s_t, C.flags], writes=[s_t])
                pt = pr[(jkv % 2) * 3 + rel]
                P.op("act", lambda e, pt=pt, s_t=s_t: e.activation(out=pt[:], in_=s_t[:], func=AF.Exp), reads=[s_t], writes=[pt])
            pvb = C.pbank[jkv % 2]
            for hl in range(4):
                for bi, b in enumerate(blocks):
                    rel = b - i + 1
                    pt = pr[(jkv % 2) * 3 + rel]
                    Vb_ = Vh if b == NT else V[b % 4]
                    P.op("pe", lambda e, pt=pt, hl=hl, b=b, bi=bi: e.matmul(
                        pvb[:, hl * 65:(hl + 1) * 65], lhsT=pt[:, hl * 128:(hl + 1) * 128], rhs=Vb_[:, jkv, 0:65],
                        start=(bi == 0), stop=(bi == len(blocks) - 1)),
                        reads=[pt, Vb_], writes=[pvb], accum=not (hl == 0 and bi == 0))
            pv3 = pvb[:, 0:260].rearrange("p (h d) -> p h d", h=4)
            P.op("dve", lambda e, pv3=pv3: e.tensor_tensor(out=den[:], in0=pv3[:, :, 64], in1=esink[:, 4 * jkv:4 * jkv + 4], op=ALU.add),
                 reads=[pvb, esink], writes=[den])
            P.op("dve", lambda e: e.reciprocal(out=rden[:], in_=den[:]), reads=[den], writes=[rden])
            P.op("dve", lambda e, pv3=pv3: e.tensor_tensor(
                out=o[:, jkv * 256:(jkv + 1) * 256].rearrange("p (h d) -> p h d", h=4), in0=pv3[:, :, 0:64],
                in1=rden[:].unsqueeze(2).to_broadcast([128, 4, 64]), op=ALU.mult),
                reads=[pvb, rden], writes=[o], accum=(jkv > 0))

        def ptail():
            P.op("dve", lambda e: e.tensor_tensor(out=og[:], in0=o[:], in1=sz[i % 3][:], op=ALU.mult), reads=[o, sz[i % 3]], writes=[og])
            transpose_to(P, C, og, 8, ogT)
            xot = xo[i % 2]
            for g in range(2):
                bank = C.gbank()
                matmul_group(P, C, bank, 512, ogT, wo, g * 512)
                P.op("dve", lambda e, bank=bank, g=g: e.tensor_tensor(out=xot[:, g * 512:(g + 1) * 512], in0=bank[:],
                                                                      in1=xring[i % 3][:, g * 512:(g + 1) * 512], op=ALU.add),
                     reads=[bank, xring[i % 3]], writes=[xot], accum=(g > 0))
            P.dma(x_dst[i * 128:(i + 1) * 128, :], xot[:], reads=[xot], writes=[xd_tr[i]], q="act")

        return [lambda: pj(0), lambda: pj(1), lambda: pj(2), lambda: pj(3), ptail]

    import os
    dbg = int(os.environ.get("KDBG", "9"))
    for t in range(NT + 2):
        p1 = stage1_parts(t) if t < NT else []
        p2 = stage2_parts(t - 2) if t >= 2 else []
        for k in range(max(len(p1), len(p2))):
            if k < len(p2):
                p2[k]()
            if k < len(p1):
                p1[k]()
        if t == NT - 1:
            halo_exchange()
    P.barrier()
    st.close()
    P.stack = P.gstack


INPUT_SHAPES = {
    "od_w_in": [2, 1024, 2560], "od_w_out": [2, 1024, 1024], "od_norm_rep": [2, 128, 1024],
    "qg_rep": [2, 128, 64], "kg_rep": [2, 128, 64], "sink_rep": [2, 128, 16],
    "alibi": [4, 4, 128, 512], "ident": [128, 128], "flags": [128, 4],
    "ev_w_in": [2, 1024, 3200], "ev_norm_rep": [2, 128, 1024], "ev_w_out": [2, 1024, 1024],
    "s5_ar_row": [2, 2, 128, 2048], "s5_ai_row": [2, 2, 128, 2048], "s5_dt_row": [2, 2, 128, 2048],
    "s5_ar_col": [2, 2, 128, 16], "s5_ai_col": [2, 2, 128, 16], "s5_dt_col": [2, 2, 128, 16],
    "s5_b_col": [2, 2, 2, 128, 16, 16], "s5_c_col": [2, 2, 2, 128, 16, 16],
    "s5_d_col": [2, 128, 4], "glu_b_col": [2, 128, 4], "s5_glu_w": [2, 512, 512],
    "iota_col": [128, 2], "iota_row": [2, 128, 128], "tri": [2, 128, 128], "triE": [2, 128, 128],
    "rw_mu_rep": [2, 128, 1664], "rw_w0_rep": [2, 2, 128, 512], "rw_a0_rep": [2, 128, 512], "rw_k_k_rep": [2, 128, 512],
    "rw_k_a_rep": [2, 128, 512], "rw_r_k_rep": [2, 128, 512], "rw_ln_g_rep": [2, 128, 512], "rw_ln_b_rep": [2, 128, 512],
    "rw_w_up": [2, 2, 64, 512], "rw_a_up": [2, 64, 512],
}


def alibi_tables():
    slopes = np.exp2(-8.0 * np.arange(1, 17, dtype=np.float32) / 16).astype(np.float32)
    s = np.arange(128)[:, None]
    t = np.arange(128)[None, :]
    out = np.zeros((4, 4, 128, 4, 128), np.float32)
    for rel in range(4):
        sg = (s + (rel - 1) * 128) if rel < 3 else (255 - s)
        d = np.abs(t - sg).astype(np.float32)
        for j in range(4):
            for hl in range(4):
                out[j, rel, :, hl, :] = np.where(d <= 128, -slopes[4 * j + hl] * d, NEG)
    return out.reshape(4, 4, 128, 512)


def host_layout(inputs, layers):
    f = lambda a: np.ascontiguousarray(np.asarray(a, np.float32))
    rep = lambda a: f(np.broadcast_to(np.asarray(a)[:, None, :], (a.shape[0], 128, a.shape[1])))
    m = {}
    m["od_w_in"] = f(inputs["od_w_in"]); m["od_w_out"] = f(inputs["od_w_out"])
    m["od_norm_rep"] = rep(inputs["od_norm"]); m["qg_rep"] = rep(inputs["at_q_norm"]); m["kg_rep"] = rep(inputs["at_k_norm"])
    m["sink_rep"] = rep(inputs["at_sink"])
    m["alibi"] = alibi_tables(); m["ident"] = np.eye(128, dtype=np.float32)
    m["ev_w_in"] = f(inputs["ev_w_in"]); m["ev_w_out"] = f(inputs["ev_w_out"]); m["ev_norm_rep"] = rep(inputs["ev_norm"])
    NE = 2
    rowrep = lambda a: f(np.broadcast_to(a.reshape(NE, 2, 1, 2048), (NE, 2, 128, 2048)))
    m["s5_ar_row"] = rowrep(np.asarray(inputs["s5_a_re"])); m["s5_ai_row"] = rowrep(np.asarray(inputs["s5_a_im"]))
    m["s5_dt_row"] = rowrep(np.repeat(np.asarray(inputs["s5_log_dt"])[..., None], 64, axis=-1))
    col = lambda a: f(a.reshape(NE, 2, 16, 128).transpose(0, 1, 3, 2))
    m["s5_ar_col"] = col(np.asarray(inputs["s5_a_re"])); m["s5_ai_col"] = col(np.asarray(inputs["s5_a_im"]))
    m["s5_dt_col"] = col(np.repeat(np.asarray(inputs["s5_log_dt"])[..., None], 64, axis=-1))
    bcol = lambda a: np.asarray(a).reshape(NE, 2, 16, 2, 64, 16).transpose(0, 1, 3, 4, 2, 5).reshape(NE, 2, 128, 16, 16)
    m["s5_b_col"] = f(np.stack([bcol(inputs["s5_b_re"]), bcol(inputs["s5_b_im"])], axis=2))
    ccol = lambda a: np.asarray(a).reshape(NE, 2, 16, 2, 16, 64).transpose(0, 1, 3, 5, 2, 4).reshape(NE, 2, 128, 16, 16)
    m["s5_c_col"] = f(np.stack([ccol(inputs["s5_c_re"]), ccol(inputs["s5_c_im"])], axis=2))
    c4 = lambda a: f(np.asarray(a).reshape(NE, 4, 128).transpose(0, 2, 1))
    m["s5_d_col"] = c4(inputs["s5_d"]); m["glu_b_col"] = c4(inputs["s5_glu_b"]); m["s5_glu_w"] = f(inputs["s5_glu_w"])
    ar = np.arange(128, dtype=np.float32)
    m["iota_col"] = f(np.stack([ar + 1, 128 - ar], axis=1))
    m["iota_row"] = f(np.stack([np.broadcast_to(ar + 1, (128, 128)), np.broadcast_to(128 - ar, (128, 128))]))
    s_, t_ = np.arange(128)[:, None], np.arange(128)[None, :]
    m["tri"] = f(np.stack([(s_ <= t_), (s_ >= t_)]).astype(np.float32))
    m["triE"] = f(np.stack([(s_ < t_), (s_ > t_)]).astype(np.float32))
    for k in ("rw_mu", "rw_a0", "rw_k_k", "rw_k_a", "rw_ln_g", "rw_ln_b"):
        m[k + "_rep"] = rep(np.asarray(inputs[k]))
    m["rw_r_k_rep"] = rep(np.asarray(inputs["rw_r_k"]).reshape(NE, 512))
    w0 = np.asarray(inputs["rw_w0"])
    m["rw_w0_rep"] = f(np.broadcast_to(w0[:, :, None, :], (NE, 2, 128, 512)))
    m["rw_w_up"] = f(inputs["rw_w_up"]); m["rw_a_up"] = f(inputs["rw_a_up"])
    return m


def build_program(NT, layers, debug=False):
    nc = bass.Bass("TRN2", target_bir_lowering=False)
    NTOK = NT * 128
    gst = ExitStack()
    P = Prog(nc, gst)
    C = Ctx()
    C.NT = NT
    C.rr = {}
    C.inp = {k: nc.dram_tensor(k, shp, F32, kind="ExternalInput") for k, shp in INPUT_SHAPES.items()}
    xin = nc.dram_tensor("xin", [NTOK, D], F32, kind="ExternalInput")
    xout = nc.dram_tensor("xout", [NTOK, D], F32, kind="ExternalOutput")
    xa = P.dram("xa", [NTOK, D]); xb = P.dram("xb", [NTOK, D])
    banks = [P.ps([128, 512]) for _ in range(8)]
    C.gb = banks[0:3]; C.sbank = banks[3:6]; C.pbank = banks[6:8]
    C.gi = 0

    def gbank():
        C.gi += 1
        return C.gb[C.gi % len(C.gb)]
    C.gbank = gbank
    C.banks = banks
    C.ident = P.sb([128, 128]); C.flags = P.sb([128, 4])
    C.iota_col = P.sb([128, 2]); C.iota_row = [P.sb([128, 128]) for _ in range(2)]; C.tri = [P.sb([128, 128]) for _ in range(2)]
    C.zero_col = P.sb([128, 2]); C.ones_col = P.sb([128, 2]); C.triE = [P.sb([128, 128]) for _ in range(2)]
    P.op("pool", lambda e: e.memset(C.zero_col[:], 0.0), writes=[C.zero_col])
    P.op("pool", lambda e: e.memset(C.ones_col[:], 1.0), writes=[C.ones_col])
    for d in range(2):
        P.dma(C.triE[d][:], C.inp["triE"][d], writes=[C.triE[d]])
    P.dma(C.iota_col[:], C.inp["iota_col"][:, :], writes=[C.iota_col])
    for d in range(2):
        P.dma(C.iota_row[d][:], C.inp["iota_row"][d], writes=[C.iota_row[d]])
        P.dma(C.tri[d][:], C.inp["tri"][d], writes=[C.tri[d]])
    dbgk = "ExternalOutput" if debug else "Internal"
    C.proj = nc.dram_tensor("proj", [NTOK, 3200], F32, kind=dbgk)
    C.ys5T = nc.dram_tensor("ys5T", [512, NTOK], F32, kind="Internal")
    C.mixT = nc.dram_tensor("mixT", [1024, NTOK], BF16, kind=dbgk)
    C.hrw = C.proj
    C.hrow = P.sb([1, 1664])
    for nm, shp, dt in (("ksrc", [64, 512], BF16), ("kdst", [128, 512], BF16), ("vsrc", [128, 288], BF16), ("vdst", [256, 288], BF16),
                        ("s5src", [128, 32], F32), ("s5dst", [256, 32], F32), ("zsrc", [64, 512], F32), ("zdst", [128, 512], F32),
                        ("hdst", [2, 1664], F32)):
        setattr(C, nm, nc.dram_tensor(nm, shp, dt, kind="Internal"))
        setattr(C, nm + "_tr", T(None))
    C.yrw = nc.dram_tensor("yrw", [NTOK, 512], F32, kind="Internal")
    C.yrw_tr = [T(None) for _ in range(NT)]
    C.rwc = nc.dram_tensor("rwc", [NTOK, 2560], F32, kind="Internal")
    C.rwt = nc.dram_tensor("rwt", [NT, 64, 128], F32, kind="Internal")
    C.rwc_tr = [T(None) for _ in range(NT)]
    C.proj_tr = [T(None) for _ in range(NT)]; C.ys_tr = [T(None) for _ in range(NT)]; C.mix_tr = [T(None) for _ in range(NT)]
    P.dma(C.ident[:], C.inp["ident"][:, :], writes=[C.ident])
    P.dma(C.flags[:], C.inp["flags"][:, :], writes=[C.flags])
    bufs = [xin] + [(xa, xb)[i % 2] for i in range(len(layers) - 1)] + [xout]
    trs = [[T(None) for _ in range(NT)] for _ in range(len(layers) + 1)]
    for li, (kind, l) in enumerate(layers):
        if kind == "odd":
            odd_layer(P, C, l, bufs[li], bufs[li + 1], trs[li], trs[li + 1])
        else:
            even_layer(P, C, l, bufs[li], bufs[li + 1], trs[li], trs[li + 1])
    P.emit()
    return nc, gst


MAGIC = 12582912.0
TWO_PI = 2.0 * np.pi


def round_frac(P, eng, out, in_, tmp):
    (o_ap, o_t), (i_ap, i_t), (t_ap, t_t) = out, in_, tmp
    P.op(eng, lambda e: e.tensor_scalar(out=t_ap, in0=i_ap, scalar1=MAGIC, scalar2=MAGIC, op0=ALU.add, op1=ALU.subtract),
         reads=[i_t], writes=[t_t])
    P.op(eng, lambda e: e.tensor_tensor(out=o_ap, in0=i_ap, in1=t_ap, op=ALU.subtract), reads=[i_t, t_t], writes=[o_t])


def even_phaseA(P, C, l, x_src, xs_tr):
    NT = C.NT
    st = ExitStack(); P.stack = st
    I = C.inp
    w = P.sb([128, 8, 3200], BF16)
    stage = [P.sb([128, 3200]), P.sb([128, 3200])]
    load_weight_bf16(P, C, w, lambda c: I["ev_w_in"][l, c * 128:(c + 1) * 128, :], 8, 3200, stage)
    gn = P.sb([128, 1024])
    P.dma(gn[:], I["ev_norm_rep"][l], writes=[gn])
    S = Ctx()
    S.junk = P.sb([128, 1024]); S.ss = P.sb([128, 1]); S.ss2 = P.sb([128, 1]); S.rs = P.sb([128, 1]); S.h = P.sb([128, 1024])
    hT = P.sb([128, 8, 128], BF16)
    xring = [P.sb([128, 1024]) for _ in range(2)]
    for j in range(NT):
        xt = xring[j % 2]
        P.dma(xt[:], x_src[j * 128:(j + 1) * 128, :], reads=[xs_tr[j]], writes=[xt])
        rmsnorm_T(P, C, xt, gn, hT, S)
        pst = stage[j % 2]
        for g in range(7):
            ncol = 512 if g < 6 else 128
            bank = C.gbank()
            matmul_group(P, C, bank, ncol, hT, w, g * 512)
            copy_op(P, rr(P, C, "pev", ("act", "dve")), pst[:, g * 512:g * 512 + ncol], bank[:, 0:ncol], [bank], [pst], accum=(g > 0))
        P.dma(C.proj[j * 128:(j + 1) * 128, :], pst[:], reads=[pst], writes=[C.proj_tr[j]], q="act")
    P.barrier()
    st.close(); P.stack = P.gstack


def s5_tables(P, C, l, d, K):
    I = C.inp
    W = K.work
    a_r, a_i, dtr, t0, t1, t2 = W[0], W[1], W[2], W[3], W[4], W[5]

    def build(shape_is_row, ar_src, ai_src, dt_src, steps_fn, sign, out_re, out_im):
        P.dma(a_r[:], ar_src, writes=[a_r]); P.dma(a_i[:], ai_src, writes=[a_i]); P.dma(dtr[:], dt_src, writes=[dtr])
        P.op("act", lambda e: e.activation(out=dtr[:], in_=dtr[:], func=AF.Exp), reads=[dtr], writes=[dtr])
        P.op("dve", lambda e: e.tensor_tensor(out=a_r[:], in0=a_r[:], in1=dtr[:], op=ALU.mult), reads=[a_r, dtr], writes=[a_r])
        P.op("dve", lambda e: e.scalar_tensor_tensor(out=a_i[:], in0=a_i[:], scalar=1.0 / TWO_PI, in1=dtr[:], op0=ALU.mult, op1=ALU.mult),
             reads=[a_i, dtr], writes=[a_i])
        round_frac(P, "dve", (a_i[:], a_i), (a_i[:], a_i), (t0[:], t0))
        steps_fn(a_r, a_i)
        P.op("act", lambda e: e.activation(out=t1[:], in_=a_r[:], func=AF.Exp, scale=float(sign)), reads=[a_r], writes=[t1])
        round_frac(P, "dve", (t0[:], t0), (a_i[:], a_i), (t2[:], t2))
        P.op("act", lambda e: e.activation(out=t0[:], in_=t0[:], func=AF.Sin, scale=TWO_PI), reads=[t0], writes=[t0])
        P.op("dve", lambda e: e.tensor_scalar(out=a_i[:], in0=a_i[:], scalar1=0.25, scalar2=None, op0=ALU.add), reads=[a_i], writes=[a_i])
        round_frac(P, "dve", (a_i[:], a_i), (a_i[:], a_i), (t2[:], t2))
        P.op("act", lambda e: e.activation(out=a_i[:], in_=a_i[:], func=AF.Sin, scale=TWO_PI), reads=[a_i], writes=[a_i])
        P.op("dve", lambda e: e.tensor_tensor(out=out_re[:].rearrange("p a b -> p (a b)"), in0=t1[:], in1=a_i[:], op=ALU.mult),
             reads=[t1, a_i], writes=[out_re])
        P.op("dve", lambda e: e.scalar_tensor_tensor(out=out_im[:].rearrange("p a b -> p (a b)"), in0=t1[:], scalar=float(sign), in1=t0[:],
                                                     op0=ALU.mult, op1=ALU.mult), reads=[t1, t0], writes=[out_im])

    def steps_row(a_r, a_i):
        for t in (a_r, a_i):
            P.op("dve", lambda e, t=t: e.tensor_scalar(out=t[:], in0=t[:], scalar1=C.iota_col[:, d:d + 1], scalar2=None, op0=ALU.mult),
                 reads=[t, C.iota_col], writes=[t])
    build(True, I["s5_ar_row"][l, d], I["s5_ai_row"][l, d], I["s5_dt_row"][l, d], steps_row, -1, K.Tin_re, K.Tin_im)

    def steps_col(a_r, a_i):
        for t in (a_r, a_i):
            P.op("dve", lambda e, t=t: e.tensor_tensor(out=t[:].rearrange("p (a b) -> p a b", a=16),
                                                       in0=t[:, 0:16].unsqueeze(2).to_broadcast([128, 16, 128]),
                                                       in1=C.iota_row[d][:].unsqueeze(1).to_broadcast([128, 16, 128]), op=ALU.mult),
                 reads=[t, C.iota_row[d]], writes=[t])
    ca, ci, cd = K.col_a, K.col_i, K.col_d

    def build_col():
        P.dma(ca[:], I["s5_ar_col"][l, d], writes=[ca]); P.dma(ci[:], I["s5_ai_col"][l, d], writes=[ci]); P.dma(cd[:], I["s5_dt_col"][l, d], writes=[cd])
        P.op("act", lambda e: e.activation(out=cd[:], in_=cd[:], func=AF.Exp), reads=[cd], writes=[cd])
        P.op("dve", lambda e: e.tensor_tensor(out=K.c_ardt[:], in0=ca[:], in1=cd[:], op=ALU.mult), reads=[ca, cd], writes=[K.c_ardt])
        P.op("dve", lambda e: e.scalar_tensor_tensor(out=K.c_frac[:], in0=ci[:], scalar=1.0 / TWO_PI, in1=cd[:], op0=ALU.mult, op1=ALU.mult),
             reads=[ci, cd], writes=[K.c_frac])
        round_frac(P, "dve", (K.c_frac[:], K.c_frac), (K.c_frac[:], K.c_frac), (K.c_tmp[:], K.c_tmp))
        P.op("dve", lambda e: e.tensor_tensor(out=a_r[:].rearrange("p (a b) -> p a b", a=16),
                                              in0=K.c_ardt[:].unsqueeze(2).to_broadcast([128, 16, 128]),
                                              in1=C.iota_row[d][:].unsqueeze(1).to_broadcast([128, 16, 128]), op=ALU.mult),
             reads=[K.c_ardt, C.iota_row[d]], writes=[a_r])
        P.op("dve", lambda e: e.tensor_tensor(out=a_i[:].rearrange("p (a b) -> p a b", a=16),
                                              in0=K.c_frac[:].unsqueeze(2).to_broadcast([128, 16, 128]),
                                              in1=C.iota_row[d][:].unsqueeze(1).to_broadcast([128, 16, 128]), op=ALU.mult),
             reads=[K.c_frac, C.iota_row[d]], writes=[a_i])
        sign = 1
        P.op("act", lambda e: e.activation(out=t1[:], in_=a_r[:], func=AF.Exp, scale=float(sign)), reads=[a_r], writes=[t1])
        round_frac(P, "dve", (t0[:], t0), (a_i[:], a_i), (t2[:], t2))
        P.op("act", lambda e: e.activation(out=t0[:], in_=t0[:], func=AF.Sin, scale=TWO_PI), reads=[t0], writes=[t0])
        P.op("dve", lambda e: e.tensor_scalar(out=a_i[:], in0=a_i[:], scalar1=0.25, scalar2=None, op0=ALU.add), reads=[a_i], writes=[a_i])
        round_frac(P, "dve", (a_i[:], a_i), (a_i[:], a_i), (t2[:], t2))
        P.op("act", lambda e: e.activation(out=a_i[:], in_=a_i[:], func=AF.Sin, scale=TWO_PI), reads=[a_i], writes=[a_i])
        P.op("dve", lambda e: e.tensor_tensor(out=K.Tout_re[:].rearrange("p a b -> p (a b)"), in0=t1[:], in1=a_i[:], op=ALU.mult),
             reads=[t1, a_i], writes=[K.Tout_re])
        P.op("dve", lambda e: e.tensor_tensor(out=K.Tout_im[:].rearrange("p a b -> p (a b)"), in0=t1[:], in1=t0[:], op=ALU.mult),
             reads=[t1, t0], writes=[K.Tout_im])
    build_col()

    s1, c1, m1, nr, dn, q_r, q_i, u0, u1 = [K.small[i] for i in range(9)]
    P.op("act", lambda e: e.activation(out=m1[:], in_=K.c_ardt[:], func=AF.Exp), reads=[K.c_ardt], writes=[m1])
    P.op("act", lambda e: e.activation(out=s1[:], in_=K.c_frac[:], func=AF.Sin, scale=TWO_PI), reads=[K.c_frac], writes=[s1])
    P.op("dve", lambda e: e.tensor_scalar(out=u0[:], in0=K.c_frac[:], scalar1=0.25, scalar2=None, op0=ALU.add), reads=[K.c_frac], writes=[u0])
    round_frac(P, "dve", (u0[:], u0), (u0[:], u0), (u1[:], u1))
    P.op("act", lambda e: e.activation(out=c1[:], in_=u0[:], func=AF.Sin, scale=TWO_PI), reads=[u0], writes=[c1])
    tt = lambda o, a, b, op, eng="dve": P.op(eng, lambda e: e.tensor_tensor(out=o[:], in0=a[:], in1=b[:], op=op), reads=[a, b], writes=[o])
    tt(c1, c1, m1, ALU.mult)
    tt(s1, s1, m1, ALU.mult)
    P.op("dve", lambda e: e.tensor_scalar(out=nr[:], in0=c1[:], scalar1=-1.0, scalar2=None, op0=ALU.add), reads=[c1], writes=[nr])
    tt(dn, ca, ca, ALU.mult); tt(u0, ci, ci, ALU.mult); tt(dn, dn, u0, ALU.add)
    P.op("dve", lambda e: e.reciprocal(out=dn[:], in_=dn[:]), reads=[dn], writes=[dn])
    tt(u0, nr, ca, ALU.mult); tt(u1, s1, ci, ALU.mult); tt(u0, u0, u1, ALU.add); tt(q_r, u0, dn, ALU.mult)
    tt(u0, s1, ca, ALU.mult); tt(u1, nr, ci, ALU.mult); tt(u0, u0, u1, ALU.subtract); tt(q_i, u0, dn, ALU.mult)
    bre, bim, bbr, bbi, tb = K.bre, K.bim, K.bbr, K.bbi, K.tb
    for (dst, ri) in ((bre, 0), (bim, 1)):
        P.op("pool", lambda e, dst=dst: e.memset(dst[:], 0.0), writes=[dst])
        P.dma(dst[0:64, :, 0:16], I["s5_b_col"][l, d, ri, 0:64], writes=[dst])
        P.dma(dst[64:128, :, 16:32], I["s5_b_col"][l, d, ri, 64:128], writes=[dst])
    bc = lambda q: q[:].unsqueeze(2).to_broadcast([128, 16, 32])
    P.op("dve", lambda e: e.tensor_tensor(out=bbr[:], in0=bre[:], in1=bc(q_r), op=ALU.mult), reads=[bre, q_r], writes=[bbr])
    P.op("dve", lambda e: e.tensor_tensor(out=tb[:], in0=bim[:], in1=bc(q_i), op=ALU.mult), reads=[bim, q_i], writes=[tb])
    tt(bbr, bbr, tb, ALU.subtract)
    P.op("dve", lambda e: e.tensor_tensor(out=bbi[:], in0=bim[:], in1=bc(q_r), op=ALU.mult), reads=[bim, q_r], writes=[bbi])
    P.op("dve", lambda e: e.tensor_tensor(out=tb[:], in0=bre[:], in1=bc(q_i), op=ALU.mult), reads=[bre, q_i], writes=[tb])
    tt(bbi, bbi, tb, ALU.add)
    zp = K.zp
    for z in zp:
        P.op("pool", lambda e, z=z: e.memset(z[:], 0.0), writes=[z])
    for (src, dst) in ((bbr, K.BT_re), (bbi, K.BT_im)):
        for ch in range(4):
            bank = C.gbank()
            for pl in range(4):
                copy_op(P, "dve", zp[pl][:, 32 * pl:32 * pl + 32], src[:, 4 * ch + pl, :], [src], [zp[pl]])
                P.op("pe", lambda e, bank=bank, pl=pl: e.transpose(out=bank[:, pl * 128:(pl + 1) * 128], in_=zp[pl][:], identity=C.ident[:]),
                     reads=[zp[pl], C.ident], writes=[bank], accum=(pl > 0))
            copy_op(P, "act", dst[:, ch, :], bank[:], [bank], [dst], accum=(ch > 0))
    for (dst, ri) in ((K.Cre, 0), (K.Cimn, 1)):
        P.op("pool", lambda e, dst=dst: e.memset(dst[:], 0.0), writes=[dst])
        P.dma(dst[0:64, :, 32:48], I["s5_c_col"][l, d, ri, 0:64], writes=[dst])
        P.dma(dst[64:128, :, 48:64], I["s5_c_col"][l, d, ri, 64:128], writes=[dst])
    P.op("dve", lambda e: e.tensor_scalar(out=K.Cimn[:], in0=K.Cimn[:], scalar1=-1.0, scalar2=None, op0=ALU.mult), reads=[K.Cimn], writes=[K.Cimn])


def s5_pass(P, C, l, d):
    half = -999
    NT = C.NT
    st = ExitStack(); P.stack = st
    I = C.inp
    K = Ctx()
    K.Tin_re = P.sb([128, 4, 512]); K.Tin_im = P.sb([128, 4, 512])
    K.Tout_re = P.sb([128, 16, 128]); K.Tout_im = P.sb([128, 16, 128])
    K.BT_re = P.sb([128, 4, 512], BF16); K.BT_im = P.sb([128, 4, 512], BF16)
    K.Cre = P.sb([128, 16, 64]); K.Cimn = P.sb([128, 16, 64])
    st2 = ExitStack(); P.stack = st2
    K.work = [P.sb([128, 2048]) for _ in range(6)]
    K.col_a = P.sb([128, 16]); K.col_i = P.sb([128, 16]); K.col_d = P.sb([128, 16])
    K.c_ardt = P.sb([128, 16]); K.c_frac = P.sb([128, 16]); K.c_tmp = P.sb([128, 16])
    K.small = [P.sb([128, 16]) for _ in range(9)]
    K.bre = P.sb([128, 16, 32]); K.bim = P.sb([128, 16, 32]); K.bbr = P.sb([128, 16, 32]); K.bbi = P.sb([128, 16, 32]); K.tb = P.sb([128, 16, 32])
    K.zp = [P.sb([128, 128]) for _ in range(4)]
    s5_tables(P, C, l, d, K)
    P.barrier()
    st2.close(); P.stack = st
    tri = C.tri[d]
    zero = C.zero_col
    uring = [P.sb([128, 512]) for _ in range(2)]
    uT = [P.sb([128, 4, 128], BF16) for _ in range(2)]
    g_re = [P.sb([128, 512], BF16) for _ in range(2)]; g_im = [P.sb([128, 512], BF16) for _ in range(2)]
    ta = [P.sb([128, 512]) for _ in range(2)]; tb = [P.sb([128, 512]) for _ in range(2)]
    ta2 = [P.sb([128, 512]) for _ in range(2)]; tb2 = [P.sb([128, 512]) for _ in range(2)]
    hre = [[P.sb([128, 128]) for _ in range(16)] for _ in range(2)]
    him = [[P.sb([128, 128]) for _ in range(16)] for _ in range(2)]
    r1 = [P.sb([128, 128]) for _ in range(2)]; r2 = [P.sb([128, 128]) for _ in range(2)]
    r3 = [P.sb([128, 128]) for _ in range(2)]; r4 = [P.sb([128, 128]) for _ in range(2)]
    cc = [[P.sb([128, 2]) for _ in range(16)] for _ in range(1)][0]
    ysb = [P.sb([128, 4, 128]) for _ in range(2)]
    ysT = C.ys5T.ap().rearrange("(k p) t -> p k t", p=128)
    bk = C.banks
    if d == 1:
        dcol = P.sb([128, 4]); gbcol = P.sb([128, 4])
        P.dma(dcol[:], I["s5_d_col"][l], writes=[dcol]); P.dma(gbcol[:], I["glu_b_col"][l], writes=[gbcol])
        wg = P.sb([128, 4, 512], BF16)
        wgs = [P.sb([128, 512]), P.sb([128, 512])]
        load_weight_bf16(P, C, wg, lambda c: I["s5_glu_w"][l, c * 128:(c + 1) * 128, :], 4, 512, wgs)
        yf = [P.sb([128, 4, 128]) for _ in range(2)]
        zt = [P.sb([128, 512]) for _ in range(2)]
        szT = P.sb([128, 4, 128])
        yv = P.sb([128, 4, 128]); x2 = P.sb([128, 4, 128]); sg = P.sb([128, 4, 128]); yg = P.sb([128, 4, 128]); ygb = P.sb([128, 4, 128], BF16)
        gs = P.sb([128, 4, 128]); ya = [P.sb([128, 4, 128], BF16) for _ in range(2)]
        mixT = C.mixT.ap().rearrange("(k p) t -> p k t", p=128)

    cin = P.sb([128, 32]); cbuf = P.sb([128, 32]); cin2 = P.sb([128, 2, 32])
    if d == 1:
        P.dma(cin2[:], C.s5dst.ap().rearrange("(s p) n -> p s n", s=2), reads=[C.s5dst_tr], writes=[cin2])
        P.op("dve", lambda e: e.tensor_scalar(out=cin[:], in0=cin2[:, 0, :], scalar1=C.flags[:, 0:1], scalar2=None, op0=ALU.mult), reads=[cin2, C.flags], writes=[cin])
        P.op("dve", lambda e: e.scalar_tensor_tensor(out=cin[:], in0=cin2[:, 1, :], scalar=C.flags[:, 1:2], in1=cin[:], op0=ALU.mult, op1=ALU.add),
             reads=[cin2, C.flags, cin], writes=[cin])
    order = list(range(NT)) if d == 0 else list(range(NT - 1, -1, -1))
    last = 127 if d == 0 else 0
    trib = P.sb([128, 128], BF16)
    copy_op(P, "dve", trib[:], tri[:], [tri], [trib])

    def prologue(it):
        i = order[it]; par = it % 2
        ut = uring[par]
        P.dma(ut[:], C.proj[i * 128:(i + 1) * 128, 0:512], reads=[C.proj_tr[i]], writes=[ut])
        if d == 1:
            P.dma(yf[par][:], ysT[:, :, i * 128:(i + 1) * 128], reads=[C.ys_tr[i]], writes=[yf[par]])
            P.dma(zt[par][:], C.proj[i * 128:(i + 1) * 128, 512:1024], reads=[C.proj_tr[i]], writes=[zt[par]])
        uTt = uT[par]
        for c in range(4):
            P.op("pe", lambda e, c=c: e.transpose(out=bk[0][:, c * 128:(c + 1) * 128], in_=ut[:, c * 128:(c + 1) * 128], identity=C.ident[:]),
                 reads=[ut, C.ident], writes=[bk[0]], accum=(c > 0))
        copy_op(P, "act", uTt[:].rearrange("p a b -> p (a b)"), bk[0][:], [bk[0]], [uTt])

    def front(it, ch):
        par = it % 2; cp = ch % 2
        uTt = uT[par]
        b1, b2 = (bk[6], bk[7]) if (d == 0 and ch % 2 == 1) else (bk[1], bk[2])
        P.op("pe", lambda e: e.matmul(b1[:], lhsT=uTt[:, ch, :], rhs=K.BT_re[:, ch, :], start=True, stop=True),
             reads=[uTt, K.BT_re], writes=[b1])
        P.op("pe", lambda e: e.matmul(b2[:], lhsT=uTt[:, ch, :], rhs=K.BT_im[:, ch, :], start=True, stop=True),
             reads=[uTt, K.BT_im], writes=[b2])
        Tr = K.Tin_re[:, ch, :]; Ti = K.Tin_im[:, ch, :]
        gr, gi, a_, b_, a2_, b2_ = g_re[cp], g_im[cp], ta[cp], tb[cp], ta2[cp], tb2[cp]
        P.op("dve", lambda e: e.tensor_tensor(out=a_[:], in0=b1[:], in1=Tr, op=ALU.mult), reads=[b1, K.Tin_re], writes=[a_])
        P.op("dve", lambda e: e.tensor_tensor(out=b_[:], in0=b2[:], in1=Ti, op=ALU.mult), reads=[b2, K.Tin_im], writes=[b_])
        P.op("pool", lambda e: e.tensor_tensor(out=gr[:], in0=a_[:], in1=b_[:], op=ALU.subtract), reads=[a_, b_], writes=[gr])
        P.op("dve", lambda e: e.tensor_tensor(out=a2_[:], in0=b1[:], in1=Ti, op=ALU.mult), reads=[b1, K.Tin_im], writes=[a2_])
        P.op("dve", lambda e: e.tensor_tensor(out=b2_[:], in0=b2[:], in1=Tr, op=ALU.mult), reads=[b2, K.Tin_re], writes=[b2_])
        P.op("pool", lambda e: e.tensor_tensor(out=gi[:], in0=a2_[:], in1=b2_[:], op=ALU.add), reads=[a2_, b2_], writes=[gi])

    def back(it, ch):
        par = it % 2; cp = ch % 2
        gr, gi = g_re[cp], g_im[cp]
        for pl in range(4):
            P.op("pe", lambda e, pl=pl: e.matmul(bk[3][:, pl * 128:(pl + 1) * 128], lhsT=gr[:, pl * 128:(pl + 1) * 128], rhs=trib[:],
                                                 start=True, stop=True), reads=[gr, trib], writes=[bk[3]], accum=(pl > 0))
        for pl in range(4):
            P.op("pe", lambda e, pl=pl: e.matmul(bk[4][:, pl * 128:(pl + 1) * 128], lhsT=gi[:, pl * 128:(pl + 1) * 128], rhs=trib[:],
                                                 start=True, stop=True), reads=[gi, trib], writes=[bk[4]], accum=(pl > 0))
        for pl in (0, 1, 3, 2):
            pp = 4 * ch + pl
            hr_prev, hi_prev = hre[1 - par][pp], him[1 - par][pp]
            hr, hi = hre[par][pp], him[par][pp]
            if it == 0 and d == 0:
                cr, ci_, crt = zero[:, 0:1], zero[:, 0:1], [zero]
            elif it == 0:
                cr, ci_, crt = cin[:, pp:pp + 1], cin[:, 16 + pp:17 + pp], [cin]
            else:
                cr, ci_, crt = hr_prev[:, last:last + 1], hi_prev[:, last:last + 1], [hr_prev, hi_prev]
            Gr = bk[3][:, pl * 128:(pl + 1) * 128]; Gi = bk[4][:, pl * 128:(pl + 1) * 128]
            Tor = K.Tout_re[:, pp, :]; Toi = K.Tout_im[:, pp, :]
            q1, q2, q3, q4 = r1[pl % 2], r2[pl % 2], r3[pl % 2], r4[pl % 2]
            P.op("dve", lambda e: e.scalar_tensor_tensor(out=q1[:], in0=Gr, scalar=cr, in1=Tor, op0=ALU.add, op1=ALU.mult),
                 reads=[bk[3], K.Tout_re] + crt, writes=[q1])
            P.op("dve", lambda e: e.scalar_tensor_tensor(out=q2[:], in0=Gi, scalar=ci_, in1=Toi, op0=ALU.add, op1=ALU.mult),
                 reads=[bk[4], K.Tout_im] + crt, writes=[q2])
            P.op("pool", lambda e: e.tensor_tensor(out=hr[:], in0=q1[:], in1=q2[:], op=ALU.subtract), reads=[q1, q2], writes=[hr])
            P.op("dve", lambda e: e.scalar_tensor_tensor(out=q3[:], in0=Gr, scalar=cr, in1=Toi, op0=ALU.add, op1=ALU.mult),
                 reads=[bk[3], K.Tout_im] + crt, writes=[q3])
            P.op("dve", lambda e: e.scalar_tensor_tensor(out=q4[:], in0=Gi, scalar=ci_, in1=Tor, op0=ALU.add, op1=ALU.mult),
                 reads=[bk[4], K.Tout_re] + crt, writes=[q4])
            P.op("pool", lambda e: e.tensor_tensor(out=hi[:], in0=q3[:], in1=q4[:], op=ALU.add), reads=[q3, q4], writes=[hi])
            if pl == 3:
                osl, csl, st0 = slice(64, 128), slice(0, 64), True
            elif pl == 2:
                osl, csl, st0 = slice(64, 96), slice(32, 64), False
            else:
                osl, csl, st0 = slice(32 * pl, 32 * pl + 32), slice(32, 64), True
            sgc = pl >= 2
            P.op("pe", lambda e: e.matmul(bk[5][osl, ch * 128:(ch + 1) * 128], lhsT=K.Cre[:, pp, csl], rhs=hr[:],
                                          start=st0, stop=False, skip_group_check=sgc), reads=[K.Cre, hr], writes=[bk[5]], accum=not (ch == 0 and pl == 0))
            P.op("pe", lambda e: e.matmul(bk[5][osl, ch * 128:(ch + 1) * 128], lhsT=K.Cimn[:, pp, csl], rhs=hi[:],
                                          start=False, stop=True, skip_group_check=sgc), reads=[K.Cimn, hi], writes=[bk[5]], accum=True)

    def epilogue(it):
        i = order[it]; par = it % 2
        uTt = uT[par]
        if d == 0:
            yt = ysb[par]
            copy_op(P, "act", yt[:].rearrange("p a b -> p (a b)"), bk[5][:], [bk[5]], [yt])
            P.dma(ysT[:, :, i * 128:(i + 1) * 128], yt[:], reads=[yt], writes=[C.ys_tr[i]], q="act")
        else:
            f = lambda t: t[:].rearrange("p a b -> p (a b)")
            P.op("dve", lambda e: e.tensor_tensor(out=f(yv), in0=bk[5][:], in1=f(yf[par]), op=ALU.add), reads=[bk[5], yf[par]], writes=[yv])
            for ch in range(4):
                P.op("dve", lambda e, ch=ch: e.scalar_tensor_tensor(out=yv[:, ch, :], in0=uTt[:, ch, :], scalar=dcol[:, ch:ch + 1], in1=yv[:, ch, :],
                                                                    op0=ALU.mult, op1=ALU.add), reads=[uTt, dcol, yv], writes=[yv], accum=True)
            P.op("act", lambda e: e.activation(out=f(yg), in_=f(yv), func=AF.Gelu_apprx_tanh), reads=[yv], writes=[yg])
            copy_op(P, "act", f(ygb), f(yg), [yg], [ygb])
            for co in range(4):
                for kc in range(4):
                    P.op("pe", lambda e, co=co, kc=kc: e.matmul(bk[6][:, co * 128:(co + 1) * 128], lhsT=wg[:, kc, co * 128:(co + 1) * 128],
                                                                rhs=ygb[:, kc, :], start=(kc == 0), stop=(kc == 3)),
                         reads=[wg, ygb], writes=[bk[6]], accum=not (co == 0 and kc == 0))
            for co in range(4):
                P.op("act", lambda e, co=co: e.activation(out=sg[:, co, :], in_=bk[6][:, co * 128:(co + 1) * 128], func=AF.Sigmoid,
                                                          bias=gbcol[:, co:co + 1]), reads=[bk[6], gbcol], writes=[sg], accum=(co > 0))
            for c in range(4):
                P.op("pe", lambda e, c=c: e.transpose(out=bk[7][:, c * 128:(c + 1) * 128], in_=zt[par][:, c * 128:(c + 1) * 128], identity=C.ident[:]),
                     reads=[zt[par], C.ident], writes=[bk[7]], accum=(c > 0))
            P.op("act", lambda e: e.activation(out=f(szT), in_=bk[7][:], func=AF.Silu), reads=[bk[7]], writes=[szT])
            P.op("dve", lambda e: e.tensor_tensor(out=f(gs), in0=f(yg), in1=f(sg), op=ALU.mult), reads=[yg, sg], writes=[gs])
            P.op("pool", lambda e: e.tensor_tensor(out=f(ya[par]), in0=f(gs), in1=f(szT), op=ALU.mult), reads=[gs, szT], writes=[ya[par]])
            P.dma(mixT[:, 0:4, i * 128:(i + 1) * 128], ya[par][:], reads=[ya[par]], writes=[C.mix_tr[i]], q="act")

    units = [(it, ch) for it in range(NT) for ch in range(4)]
    prologue(0); front(0, 0)
    for u, (it, ch) in enumerate(units):
        if u + 1 < len(units):
            it2, ch2 = units[u + 1]
            if ch2 == 0:
                prologue(it2)
            front(it2, ch2)
        back(it, ch)
        if ch == 3:
            epilogue(it)
    if d == 0:
        parl = (NT - 1) % 2
        for pp in range(16):
            copy_op(P, ("dve", "pool")[pp % 2], cbuf[:, pp:pp + 1], hre[parl][pp][:, 127:128], [hre[parl][pp]], [cbuf], accum=(pp > 0))
            copy_op(P, ("pool", "dve")[pp % 2], cbuf[:, 16 + pp:17 + pp], him[parl][pp][:, 127:128], [him[parl][pp]], [cbuf], accum=True)
        P.dma(C.s5src[:, :], cbuf[:], reads=[cbuf], writes=[C.s5src_tr])
        P.collective(C.s5src[:, :], C.s5dst[:, :], reads=[C.s5src_tr], writes=[C.s5dst_tr])
    P.barrier()
    st.close(); P.stack = P.gstack


def even_layer(P, C, l, x_src, x_dst, xs_tr, xd_tr):
    import os
    dbg = int(os.environ.get("EDBG", "9"))
    even_phaseA(P, C, l, x_src, xs_tr)
    if dbg >= 1:
        s5_pass(P, C, l, 0)
        s5_pass(P, C, l, 1)
    if dbg >= 2:
        rwkv_pass(P, C, l, 0, x_src, x_dst, xs_tr, xd_tr)
        rwkv_pass(P, C, l, 1, x_src, x_dst, xs_tr, xd_tr)


def rwkv_pass(P, C, l, d, x_src, x_dst, xs_tr, xd_tr):
    NT = C.NT
    st = ExitStack(); P.stack = st
    I = C.inp
    bk = C.banks
    f32 = lambda shape: P.sb(shape)
    b16 = lambda shape: P.sb(shape, BF16)
    import os
    _rp = os.environ.get("RWPOOL", "dve")
    tt = lambda eng, o, a, b, op, rd, wr, accum=False: P.op(_rp if eng == "pool" else eng, lambda e: e.tensor_tensor(out=o, in0=a, in1=b, op=op), reads=rd, writes=wr, accum=accum)

    mu = f32([128, 1664]); P.dma(mu[:], I["rw_mu_rep"][l], writes=[mu])
    w0 = f32([128, 512]); P.dma(w0[:], I["rw_w0_rep"][l, d], writes=[w0])
    a0 = f32([128, 512]); P.dma(a0[:], I["rw_a0_rep"][l], writes=[a0])
    kkp = f32([128, 512]); P.dma(kkp[:], I["rw_k_k_rep"][l], writes=[kkp])
    kap = f32([128, 512]); P.dma(kap[:], I["rw_k_a_rep"][l], writes=[kap])
    ups = f32([128, 2, 512])
    P.dma(ups[0:64, 0, :], I["rw_w_up"][l, d], writes=[ups])
    P.dma(ups[64:128, 1, :], I["rw_a_up"][l], writes=[ups])
    triI = C.tri[d]; triE = C.triE[d]; triET = C.triE[1 - d]
    eye_b = C.ident
    if d == 1:
        rkp = f32([128, 512]); P.dma(rkp[:], I["rw_r_k_rep"][l], writes=[rkp])
        lng = f32([128, 512]); P.dma(lng[:], I["rw_ln_g_rep"][l], writes=[lng])
        lnb = f32([128, 512]); P.dma(lnb[:], I["rw_ln_b_rep"][l], writes=[lnb])
        wo = b16([128, 8, 1024])
        st2 = ExitStack(); P.stack = st2
        wst = [f32([128, 1024]), f32([128, 1024])]
        load_weight_bf16(P, C, wo, lambda c: I["ev_w_out"][l, c * 128:(c + 1) * 128, :], 8, 1024, wst)
        P.barrier()
        st2.close(); P.stack = st

    if d == 0:
        cur = [f32([128, 1664])] * 2; prv = [f32([128, 1664])] * 2; nxt = [f32([128, 1664])] * 2
        tsum = f32([128, 1664])
    else:
        cur = prv = nxt = [None, None]; tsum = None
    hsr = [f32([128, 1664]) for _ in range(2)]
    twla = f32([128, 128]); twlaT = f32([128, 128])
    a_t = f32([128, 512]); e2 = f32([128, 512]); tmp = f32([128, 512]); tmp2 = f32([128, 512])
    kkn = f32([128, 512]); pss = f32([128, 8]); prn = f32([128, 8])
    p_t = f32([128, 512]); q_t = f32([128, 512]); kpr = [f32([128, 512]) for _ in range(2)]
    GI = f32([128, 512]); GIinv = f32([128, 512]); GE = f32([128, 512])
    Pd = f32([128, 512]); Qd = f32([128, 512]); Kd = f32([128, 512]); Rd = f32([128, 512])
    Pdb = b16([128, 512]); Qdbr = [b16([128, 512]) for _ in range(2)]; Kdb = b16([128, 512]); Vbr = [b16([128, 512]) for _ in range(2)]
    PRr = [b16([64, 8, 2, 128]) for _ in range(2)]; QTt = b16([64, 8, 128]); KTt = b16([64, 8, 128])
    Bm = [b16([128, 8, 128]) for _ in range(2)]; Am = [b16([128, 8, 128]) for _ in range(2)]; Pmr = [b16([128, 8, 128]) for _ in range(2)]
    MqTr = [b16([128, 8, 128]) for _ in range(2)]; LkT = b16([128, 8, 128]); MkTr = [b16([128, 8, 128]) for _ in range(2)]
    LkVr = [f32([128, 512]) for _ in range(2)]; KVr = [f32([64, 8, 64]) for _ in range(2)]
    Z = f32([64, 8, 64]); Zb = b16([64, 8, 64]); ZK = f32([64, 8, 64]); ZKg = f32([64, 8, 64]); Ztmp = f32([64, 8, 64])
    gcolr = [f32([64, 8]) for _ in range(2)]; onescol = C.ones_col
    rhs_sb = b16([128, 512]); U_sb = b16([128, 512])
    ysb = [f32([128, 512]) for _ in range(2)]
    if d == 0:
        P.op("pool", lambda e: e.memset(Z[:], 0.0), writes=[Z])
        P.op("pool", lambda e: e.memset(Zb[:], 0.0), writes=[Zb])
        hb = P.sb([1, 2, 1664])
        P.collective(C.proj[NT * 128 - 1:NT * 128, 1024:2688], C.hdst[:, :], reads=[C.proj_tr[NT - 1]], writes=[C.hdst_tr])
        P.dma(hb[:], C.hdst.ap().rearrange("(o s) n -> o s n", o=1), reads=[C.hdst_tr], writes=[hb])
        P.op("dve", lambda e: e.tensor_scalar(out=C.hrow[:], in0=hb[:, 0, :], scalar1=C.flags[0:1, 0:1], scalar2=None, op0=ALU.mult), reads=[hb, C.flags], writes=[C.hrow])
        P.op("dve", lambda e: e.scalar_tensor_tensor(out=C.hrow[:], in0=hb[:, 1, :], scalar=C.flags[0:1, 1:2], in1=C.hrow[:], op0=ALU.mult, op1=ALU.add),
             reads=[hb, C.flags, C.hrow], writes=[C.hrow])
    else:
        z2 = P.sb([64, 2, 512])
        P.dma(z2[:], C.zdst.ap().rearrange("(s p) n -> p s n", s=2), reads=[C.zdst_tr], writes=[z2])
        zf_ = Z[:].rearrange("p a b -> p (a b)")
        P.op("dve", lambda e: e.tensor_scalar(out=zf_, in0=z2[:, 0, :], scalar1=C.flags[0:64, 0:1], scalar2=None, op0=ALU.mult), reads=[z2, C.flags], writes=[Z])
        P.op("dve", lambda e: e.scalar_tensor_tensor(out=zf_, in0=z2[:, 1, :], scalar=C.flags[0:64, 1:2], in1=zf_, op0=ALU.mult, op1=ALU.add),
             reads=[z2, C.flags, Z], writes=[Z])
        copy_op(P, "dve", Zb[:], Z[:], [Z], [Zb])
    if d == 1:
        yfw = [f32([128, 512]) for _ in range(2)]
        zrw = [f32([128, 512]) for _ in range(2)]
        xres = [f32([128, 1024]) for _ in range(2)]
        mean = f32([128, 8]); var = f32([128, 8]); cent = f32([128, 512]); rkk = f32([128, 512]); bon = f32([128, 8])
        yb = f32([128, 512]); szr = f32([128, 512])
        mixA = [b16([128, 4, 128]) for _ in range(2)]; ybT = b16([128, 4, 128])
        xo = [f32([128, 1024]) for _ in range(2)]
        mixT = C.mixT.ap().rearrange("(k p) t -> p k t", p=128)

    v3 = lambda t, n=8: t[:].rearrange("p (h d) -> p h d", h=n)
    order = list(range(NT)) if d == 0 else list(range(NT - 1, -1, -1))
    last = 127 if d == 0 else 0
    HW = C.hrw
    def front(it):
        i = order[it]; par = it % 2; r0 = i * 128
        PR, Pm, MqT, MkT, LkV, KV, Qdb, Vb, gcol, hs, kp = PRr[par], Pmr[par], MqTr[par], MkTr[par], LkVr[par], KVr[par], Qdbr[par], Vbr[par], gcolr[par], hsr[par], kpr[par]
        r_ap, k_ap, v_ap = hs[:, 0:512], hs[:, 512:1024], hs[:, 1024:1536]
        c_, p_, n_ = cur[par], prv[par], nxt[par]
        r_ap, k_ap, v_ap = hs[:, 0:512], hs[:, 512:1024], hs[:, 1024:1536]
        if d == 0:
            P.dma(c_[:], HW[r0:r0 + 128, 1024:2688], reads=[C.proj_tr[i]], writes=[c_])
            if i == 0:
                P.op("pool", lambda e: e.memset(p_[:], 0.0), writes=[p_])
                P.dma(p_[1:128, :], HW[r0:r0 + 127, 1024:2688], reads=[C.proj_tr[i]], writes=[p_])
            else:
                P.dma(p_[:], HW[r0 - 1:r0 + 127, 1024:2688], reads=[C.proj_tr[i], C.proj_tr[i - 1]], writes=[p_])
            if i == NT - 1:
                P.dma(n_[0:127, :], HW[r0 + 1:r0 + 128, 1024:2688], reads=[C.proj_tr[i]], writes=[n_])
                P.dma(n_[127:128, :], C.hrow[:], reads=[C.hrow], writes=[n_], accum=True)
            else:
                P.dma(n_[:], HW[r0 + 1:r0 + 129, 1024:2688], reads=[C.proj_tr[i], C.proj_tr[i + 1]], writes=[n_])
        if d == 1:
            P.dma(yfw[par][:], C.yrw[r0:r0 + 128, :], reads=[C.yrw_tr[i]], writes=[yfw[par]])
            P.dma(zrw[par][:], HW[r0:r0 + 128, 2688:3200], reads=[C.proj_tr[i]], writes=[zrw[par]])
            P.dma(xres[par][:], x_src[r0:r0 + 128, :], reads=[xs_tr[i]], writes=[xres[par]])
            P.dma(mixA[par][:], mixT[:, 0:4, r0:r0 + 128], reads=[C.mix_tr[i]], writes=[mixA[par]])
        if d == 0:
            tt("pool", tsum[:], p_[:], n_[:], ALU.add, [p_, n_], [tsum])
            P.op("dve", lambda e: e.scalar_tensor_tensor(out=tsum[:], in0=tsum[:], scalar=0.5, in1=c_[:], op0=ALU.mult, op1=ALU.subtract),
                 reads=[tsum, c_], writes=[tsum])
            tt("pool", tsum[:], tsum[:], mu[:], ALU.mult, [tsum, mu], [tsum])
            tt("dve", hs[:], tsum[:], c_[:], ALU.add, [tsum, c_], [hs])
            P.op("act", lambda e: e.activation(out=twla[:, 0:64], in_=hs[:, 1536:1600], func=AF.Tanh), reads=[hs], writes=[twla])
            copy_op(P, "dve", twla[:, 64:128], hs[:, 1600:1664], [hs], [twla], accum=True)
            g0 = C.gbank()
            P.op("pe", lambda e: e.transpose(out=g0[:, 0:128], in_=twla[:], identity=C.ident[:]), reads=[twla, C.ident], writes=[g0])
            copy_op(P, "dve", twlaT[:], g0[:, 0:128], [g0], [twlaT])
            g1 = C.gbank()
            P.op("pe", lambda e: e.matmul(g1[:], lhsT=twlaT[64:128, :], rhs=ups[64:128, 1, :], start=True, stop=True), reads=[twlaT, ups], writes=[g1])
            tt("dve", a_t[:], g1[:], a0[:], ALU.add, [g1, a0], [a_t])
            P.op("act", lambda e: e.activation(out=a_t[:], in_=a_t[:], func=AF.Sigmoid), reads=[a_t], writes=[a_t])
            tt("pool", kkn[:], k_ap, kkp[:], ALU.mult, [hs, kkp], [kkn])
            tt("pool", tmp[:], kkn[:], kkn[:], ALU.mult, [kkn], [tmp])
            P.op("dve", lambda e: e.tensor_reduce(out=pss[:], in_=v3(tmp), axis=AX.X, op=ALU.add), reads=[tmp], writes=[pss])
            P.op("act", lambda e: e.activation(out=pss[:], in_=pss[:], func=AF.Sqrt), reads=[pss], writes=[pss])
            P.op("dve", lambda e: e.tensor_scalar(out=pss[:], in0=pss[:], scalar1=1e-12, scalar2=None, op0=ALU.max), reads=[pss], writes=[pss])
            P.op("dve", lambda e: e.reciprocal(out=prn[:], in_=pss[:]), reads=[pss], writes=[prn])
            tt("dve", v3(p_t), v3(kkn), prn[:].unsqueeze(2).to_broadcast([128, 8, 64]), ALU.mult, [kkn, prn], [p_t])
            tt("pool", q_t[:], p_t[:], a_t[:], ALU.mult, [p_t, a_t], [q_t])
            P.op("dve", lambda e: e.scalar_tensor_tensor(out=tmp[:], in0=a_t[:], scalar=-1.0, in1=kap[:], op0=ALU.add, op1=ALU.mult), reads=[a_t, kap], writes=[tmp])
            P.op("dve", lambda e: e.scalar_tensor_tensor(out=kp[:], in0=tmp[:], scalar=1.0, in1=k_ap, op0=ALU.add, op1=ALU.mult), reads=[tmp, hs], writes=[kp])
            P.dma(C.rwc[r0:r0 + 128, 0:512], hs[:, 0:512], reads=[hs], writes=[C.rwc_tr[i]])
            P.dma(C.rwc[r0:r0 + 128, 512:1024], hs[:, 1024:1536], reads=[hs], writes=[C.rwc_tr[i]], accum=True)
            P.dma(C.rwc[r0:r0 + 128, 1024:1536], p_t[:], reads=[p_t], writes=[C.rwc_tr[i]], accum=True)
            P.dma(C.rwc[r0:r0 + 128, 1536:2048], q_t[:], reads=[q_t], writes=[C.rwc_tr[i]], accum=True)
            P.dma(C.rwc[r0:r0 + 128, 2048:2560], kp[:], reads=[kp], writes=[C.rwc_tr[i]], accum=True)
            P.dma(C.rwt[i], twlaT[0:64, :], reads=[twlaT], writes=[C.rwc_tr[i]], accum=True)
        else:
            P.dma(hs[:, 0:512], C.rwc[r0:r0 + 128, 0:512], reads=[C.rwc_tr[i]], writes=[hs])
            P.dma(hs[:, 1024:1536], C.rwc[r0:r0 + 128, 512:1024], reads=[C.rwc_tr[i]], writes=[hs], accum=True)
            P.dma(p_t[:], C.rwc[r0:r0 + 128, 1024:1536], reads=[C.rwc_tr[i]], writes=[p_t])
            P.dma(q_t[:], C.rwc[r0:r0 + 128, 1536:2048], reads=[C.rwc_tr[i]], writes=[q_t])
            P.dma(kp[:], C.rwc[r0:r0 + 128, 2048:2560], reads=[C.rwc_tr[i]], writes=[kp])
            P.dma(twlaT[0:64, :], C.rwt[i], reads=[C.rwc_tr[i]], writes=[twlaT])
        g2 = C.gbank()
        P.op("pe", lambda e: e.matmul(g2[:], lhsT=twlaT[0:64, :], rhs=ups[0:64, 0, :], start=True, stop=True), reads=[twlaT, ups], writes=[g2])
        tt("dve", e2[:], g2[:], w0[:], ALU.add, [g2, w0], [e2])
        P.op("act", lambda e: e.activation(out=e2[:], in_=e2[:], func=AF.Exp, scale=-1.0), reads=[e2], writes=[e2])
        P.op("act", lambda e: e.activation(out=e2[:], in_=e2[:], func=AF.Ln, bias=1.0), reads=[e2], writes=[e2])
        P.op("act", lambda e: e.activation(out=e2[:], in_=e2[:], func=AF.Exp, scale=-1.0, bias=-0.5), reads=[e2], writes=[e2])
        copy_op(P, "act", Vb[:], v_ap, [hs], [Vb])
        yield
        gI = C.gbank()
        P.op("pe", lambda e: e.matmul(gI[:], lhsT=triI[:], rhs=e2[:], start=True, stop=True), reads=[triI, e2], writes=[gI])
        P.op("act", lambda e: e.activation(out=GI[:], in_=gI[:], func=AF.Exp, scale=-1.0), reads=[gI], writes=[GI])
        P.op("act", lambda e: e.activation(out=GIinv[:], in_=gI[:], func=AF.Exp), reads=[gI], writes=[GIinv])
        gE = C.gbank()
        P.op("pe", lambda e: e.matmul(gE[:], lhsT=triE[:], rhs=e2[:], start=True, stop=True), reads=[triE, e2], writes=[gE])
        P.op("act", lambda e: e.activation(out=GE[:], in_=gE[:], func=AF.Exp, scale=-1.0), reads=[gE], writes=[GE])
        gT = C.gbank()
        for h in range(8):
            P.op("pe", lambda e, h=h: e.matmul(gT[0:64, h:h + 1], lhsT=e2[:, h * 64:(h + 1) * 64], rhs=onescol[:, 0:1], start=True, stop=True),
                 reads=[e2, onescol], writes=[gT], accum=(h > 0))
        P.op("act", lambda e: e.activation(out=gcol[:], in_=gT[0:64, 0:8], func=AF.Exp, scale=-1.0), reads=[gT], writes=[gcol])
        tt("dve", Pd[:], p_t[:], GE[:], ALU.mult, [p_t, GE], [Pd])
        tt("pool", Qd[:], q_t[:], GIinv[:], ALU.mult, [q_t, GIinv], [Qd])
        tt("dve", Kd[:], kp[:], GIinv[:], ALU.mult, [kp, GIinv], [Kd])
        tt("pool", Rd[:], r_ap, GI[:], ALU.mult, [hs, GI], [Rd])
        copy_op(P, "act", Pdb[:], Pd[:], [Pd], [Pdb]); copy_op(P, "dve", Qdb[:], Qd[:], [Qd], [Qdb]); copy_op(P, "act", Kdb[:], Kd[:], [Kd], [Kdb])
        yield
        for (src, dstfn, dstt) in ((Pd, lambda h: PR[:, h, 0, :], PR), (Rd, lambda h: PR[:, h, 1, :], PR), (Qd, lambda h: QTt[:, h, :], QTt), (Kd, lambda h: KTt[:, h, :], KTt)):
            for hb in range(2):
                g = C.gbank()
                for hl in range(4):
                    h = 4 * hb + hl
                    P.op("pe", lambda e, hl=hl, h=h, g=g, src=src: e.transpose(out=g[0:64, hl * 128:(hl + 1) * 128], in_=src[:, h * 64:(h + 1) * 64], identity=C.ident[:]),
                         reads=[src, C.ident], writes=[g], accum=(hl > 0))
                if dstt is PR:
                    which = 0 if src is Pd else 1
                    copy_op(P, ("dve", "act")[hb], PR[:, 4 * hb:4 * hb + 4, which, :], g[0:64, :].rearrange("p (a b) -> p a b", a=4), [g], [PR], accum=True)
                else:
                    copy_op(P, ("act", "dve")[hb], dstt[:, 4 * hb:4 * hb + 4, :], g[0:64, :].rearrange("p (a b) -> p a b", a=4), [g], [dstt], accum=(hb > 0))
        yield
        Bc, Ac = Bm[0], Am[0]
        for hg in range(4):
            gq = C.gbank(); gk = C.gbank()
            for hh in range(2):
                h = 2 * hg + hh
                P.op("pe", lambda e: e.matmul(gq[:, hh * 256:(hh + 1) * 256], lhsT=QTt[:, h, :],
                                              rhs=PR[:, h, :, :].rearrange("p a b -> p (a b)"), start=True, stop=True),
                     reads=[QTt, PR], writes=[gq], accum=(hh > 0))
                P.op("pe", lambda e: e.matmul(gk[:, hh * 256:(hh + 1) * 256], lhsT=KTt[:, h, :],
                                              rhs=PR[:, h, :, :].rearrange("p a b -> p (a b)"), start=True, stop=True),
                     reads=[KTt, PR], writes=[gk], accum=(hh > 0))
            gq4 = gq[:].rearrange("p (h a b) -> p h a b", h=2, a=2)
            gk4 = gk[:].rearrange("p (h a b) -> p h a b", h=2, a=2)
            hs2 = slice(2 * hg, 2 * hg + 2)
            mE = triE[:].unsqueeze(1).to_broadcast([128, 2, 128]); mI = triI[:].unsqueeze(1).to_broadcast([128, 2, 128])
            tt("dve", Bc[:, hs2, :], gq4[:, :, 0, :], mE, ALU.mult, [gq, triE], [Bc], accum=(hg > 0))
            tt("dve", MqT[:, hs2, :], gq4[:, :, 1, :], mI, ALU.mult, [gq, triI], [MqT], accum=(hg > 0))
            tt("dve", LkT[:, hs2, :], gk4[:, :, 0, :], mE, ALU.mult, [gk, triE], [LkT], accum=(hg > 0))
            tt("dve", MkT[:, hs2, :], gk4[:, :, 1, :], mI, ALU.mult, [gk, triI], [MkT], accum=(hg > 0))
        for hb in range(2):
            g = C.gbank()
            for hl in range(4):
                h = 4 * hb + hl
                P.op("pe", lambda e: e.matmul(g[:, hl * 128:(hl + 1) * 128], lhsT=PR[:, h, 0, :], rhs=QTt[:, h, :],
                                              start=True, stop=True), reads=[PR, QTt], writes=[g], accum=(hl > 0))
            tt("dve", Ac[:, 4 * hb:4 * hb + 4, :], g[:].rearrange("p (h b) -> p h b", h=4), triET[:].unsqueeze(1).to_broadcast([128, 4, 128]), ALU.mult,
               [g, triET], [Ac], accum=(hb > 0))
        yield
        P.op("dve", lambda e: e.scalar_tensor_tensor(out=Pm[:], in0=Bc[:], scalar=-1.0, in1=eye_b[:].unsqueeze(1).to_broadcast([128, 8, 128]),
                                                     op0=ALU.mult, op1=ALU.add), reads=[Bc, eye_b], writes=[Pm])
        for lev in range(6):
            Bn, An = Bm[(lev + 1) % 2], Am[(lev + 1) % 2]
            for hb in range(2):
                gA = C.gbank()
                for hl in range(4):
                    h = 4 * hb + hl
                    P.op("pe", lambda e: e.matmul(gA[:, hl * 128:(hl + 1) * 128], lhsT=Bc[:, h, :], rhs=Ac[:, h, :], start=True, stop=True),
                         reads=[Bc, Ac], writes=[gA], accum=(hl > 0))
                copy_op(P, ("act", "dve")[hb], An[:, 4 * hb:4 * hb + 4, :].rearrange("p a b -> p (a b)"), gA[:], [gA], [An], accum=(hb > 0))
                if lev < 5:
                    gB = C.gbank()
                    for hl in range(4):
                        h = 4 * hb + hl
                        P.op("pe", lambda e: e.matmul(gB[:, hl * 128:(hl + 1) * 128], lhsT=Ac[:, h, :], rhs=Bc[:, h, :], start=True, stop=True),
                             reads=[Bc, Ac], writes=[gB], accum=(hl > 0))
                    copy_op(P, ("dve", "act")[hb], Bn[:, 4 * hb:4 * hb + 4, :].rearrange("p a b -> p (a b)"), gB[:], [gB], [Bn], accum=(hb > 0))
            for hb in range(2):
                gP = C.gbank()
                for hl in range(4):
                    h = 4 * hb + hl
                    P.op("pe", lambda e: e.matmul(gP[:, hl * 128:(hl + 1) * 128], lhsT=An[:, h, :], rhs=Pm[:, h, :], start=True, stop=True),
                         reads=[An, Pm], writes=[gP], accum=(hl > 0))
                tt("dve", Pm[:, 4 * hb:4 * hb + 4, :].rearrange("p a b -> p (a b)"), gP[:], Pm[:, 4 * hb:4 * hb + 4, :].rearrange("p a b -> p (a b)"),
                   ALU.add, [gP, Pm], [Pm], accum=True)
            Bc, Ac = Bn, An
            yield
        yield
        g = C.gbank()
        for h in range(8):
            P.op("pe", lambda e: e.matmul(g[:, h * 64:(h + 1) * 64], lhsT=LkT[:, h, :], rhs=Vb[:, h * 64:(h + 1) * 64], start=True, stop=True),
                 reads=[LkT, Vb], writes=[g], accum=(h > 0))
        copy_op(P, "act", LkV[:], g[:], [g], [LkV])
        g = C.gbank()
        for h in range(8):
            P.op("pe", lambda e: e.matmul(g[0:64, h * 64:(h + 1) * 64], lhsT=Kdb[:, h * 64:(h + 1) * 64], rhs=Vb[:, h * 64:(h + 1) * 64],
                                          start=True, stop=True), reads=[Kdb, Vb], writes=[g], accum=(h > 0))
        copy_op(P, "dve", KV[:].rearrange("p a b -> p (a b)"), g[0:64, :], [g], [KV])

    def back(it):
        i = order[it]; par = it % 2; r0 = i * 128
        PR, Pm, MqT, MkT, LkV, KV, Qdb, Vb, gcol, hs, kp = PRr[par], Pmr[par], MqTr[par], MkTr[par], LkVr[par], KVr[par], Qdbr[par], Vbr[par], gcolr[par], hsr[par], kpr[par]
        r_ap, k_ap, v_ap = hs[:, 0:512], hs[:, 512:1024], hs[:, 1024:1536]
        gcb = gcol[:].unsqueeze(2).to_broadcast([64, 8, 64])
        tt("pool", ZK[:], Z[:], KV[:], ALU.add, [Z, KV], [ZK])
        tt("pool", ZKg[:], ZK[:], gcb, ALU.mult, [ZK, gcol], [ZKg])
        gz = bk[3]
        for h in range(8):
            P.op("pe", lambda e: e.matmul(gz[:, h * 64:(h + 1) * 64], lhsT=PR[:, h, 0, :], rhs=Zb[:, h, :], start=True, stop=True),
                 reads=[PR, Zb], writes=[gz], accum=(h > 0))
        tt("dve", rhs_sb[:], gz[:], LkV[:], ALU.add, [gz, LkV], [rhs_sb])
        yield
        gu = bk[4]
        for h in range(8):
            P.op("pe", lambda e: e.matmul(gu[:, h * 64:(h + 1) * 64], lhsT=Pm[:, h, :], rhs=rhs_sb[:, h * 64:(h + 1) * 64], start=True, stop=True),
                 reads=[Pm, rhs_sb], writes=[gu], accum=(h > 0))
        P.op("act", lambda e: e.activation(out=U_sb[:], in_=gu[:], func=AF.Copy, scale=-1.0), reads=[gu], writes=[U_sb])
        yield
        gy = bk[5]
        for h in range(8):
            osl = gy[:, h * 64:(h + 1) * 64]
            P.op("pe", lambda e: e.matmul(osl, lhsT=PR[:, h, 1, :], rhs=Zb[:, h, :], start=True, stop=False),
                 reads=[PR, Zb], writes=[gy], accum=(h > 0))
            P.op("pe", lambda e: e.matmul(osl, lhsT=MqT[:, h, :], rhs=U_sb[:, h * 64:(h + 1) * 64], start=False, stop=False),
                 reads=[MqT, U_sb], writes=[gy], accum=True)
            P.op("pe", lambda e: e.matmul(osl, lhsT=MkT[:, h, :], rhs=Vb[:, h * 64:(h + 1) * 64], start=False, stop=True),
                 reads=[MkT, Vb], writes=[gy], accum=True)
        gq_ = bk[6]
        for h in range(8):
            P.op("pe", lambda e: e.matmul(gq_[0:64, h * 64:(h + 1) * 64], lhsT=Qdb[:, h * 64:(h + 1) * 64], rhs=U_sb[:, h * 64:(h + 1) * 64],
                                          start=True, stop=True), reads=[Qdb, U_sb], writes=[gq_], accum=(h > 0))
        tt("dve", Ztmp[:], gq_[0:64, :].rearrange("p (a b) -> p a b", a=8), gcb, ALU.mult, [gq_, gcol], [Ztmp])
        tt("dve", Z[:], Ztmp[:], ZKg[:], ALU.add, [Ztmp, ZKg], [Z])
        copy_op(P, "dve", Zb[:], Z[:], [Z], [Zb])
        yield
        if d == 0:
            yt = ysb[par]
            copy_op(P, "act", yt[:], gy[:], [gy], [yt])
            P.dma(C.yrw[r0:r0 + 128, :], yt[:], reads=[yt], writes=[C.yrw_tr[i]], q="act")
            if it == NT - 1:
                P.dma(C.zsrc[:, :], Z[:].rearrange("p a b -> p (a b)"), reads=[Z], writes=[C.zsrc_tr])
                P.collective(C.zsrc[:, :], C.zdst[:, :], reads=[C.zsrc_tr], writes=[C.zdst_tr])
        else:
            y = ysb[par]
            tt("dve", y[:], gy[:], yfw[par][:], ALU.add, [gy, yfw[par]], [y])
            P.op("dve", lambda e: e.tensor_reduce(out=mean[:], in_=v3(y), axis=AX.X, op=ALU.add), reads=[y], writes=[mean])
            P.op("dve", lambda e: e.tensor_scalar(out=mean[:], in0=mean[:], scalar1=1.0 / 64, scalar2=None, op0=ALU.mult), reads=[mean], writes=[mean])
            tt("dve", v3(cent), v3(y), mean[:].unsqueeze(2).to_broadcast([128, 8, 64]), ALU.subtract, [y, mean], [cent])
            tt("pool", tmp2[:], cent[:], cent[:], ALU.mult, [cent], [tmp2])
            P.op("dve", lambda e: e.tensor_reduce(out=var[:], in_=v3(tmp2), axis=AX.X, op=ALU.add), reads=[tmp2], writes=[var])
            P.op("act", lambda e: e.activation(out=var[:], in_=var[:], func=AF.Sqrt, scale=1.0 / 64, bias=64e-5), reads=[var], writes=[var])
            P.op("dve", lambda e: e.reciprocal(out=var[:], in_=var[:]), reads=[var], writes=[var])
            tt("dve", v3(cent), v3(cent), var[:].unsqueeze(2).to_broadcast([128, 8, 64]), ALU.mult, [cent, var], [cent])
            tt("pool", cent[:], cent[:], lng[:], ALU.mult, [cent, lng], [cent])
            tt("pool", cent[:], cent[:], lnb[:], ALU.add, [cent, lnb], [cent])
            tt("pool", rkk[:], r_ap, kp[:], ALU.mult, [hs, kp], [rkk])
            tt("pool", rkk[:], rkk[:], rkp[:], ALU.mult, [rkk, rkp], [rkk])
            P.op("dve", lambda e: e.tensor_reduce(out=bon[:], in_=v3(rkk), axis=AX.X, op=ALU.add), reads=[rkk], writes=[bon])
            tt("dve", v3(rkk), hs[:, 1024:1536].rearrange("p (h d) -> p h d", h=8), bon[:].unsqueeze(2).to_broadcast([128, 8, 64]), ALU.mult, [hs, bon], [rkk])
            tt("pool", cent[:], cent[:], rkk[:], ALU.add, [cent, rkk], [cent])
            P.op("act", lambda e: e.activation(out=szr[:], in_=zrw[par][:], func=AF.Silu), reads=[zrw[par]], writes=[szr])
            tt("dve", yb[:], cent[:], szr[:], ALU.mult, [cent, szr], [yb])
            yield
            transpose_to(P, C, yb, 4, ybT)
            xot = xo[par]
            for gcol_i in range(2):
                bank = C.gbank()
                for c in range(8):
                    lhs = mixA[par][:, c, :] if c < 4 else ybT[:, c - 4, :]
                    P.op("pe", lambda e, c=c, lhs=lhs, bank=bank: e.matmul(bank[:], lhsT=lhs, rhs=wo[:, c, gcol_i * 512:(gcol_i + 1) * 512],
                                                                          start=(c == 0), stop=(c == 7)),
                         reads=[mixA[par], ybT, wo], writes=[bank], accum=(c > 0))
                tt("dve", xot[:, gcol_i * 512:(gcol_i + 1) * 512], bank[:], xres[par][:, gcol_i * 512:(gcol_i + 1) * 512], ALU.add,
                   [bank, xres[par]], [xot], accum=(gcol_i > 0))
            P.dma(x_dst[r0:r0 + 128, :], xot[:], reads=[xot], writes=[xd_tr[i]], q="act")

    import os
    if os.environ.get("NOSKEW"):
        for it in range(NT):
            for _ in front(it):
                pass
            for _ in back(it):
                pass
    else:
        for _ in front(0):
            pass
        for it in range(NT):
            gf = front(it + 1) if it + 1 < NT else iter(())
            gb = back(it)
            fdone = bdone = False
            while not (fdone and bdone):
                if not fdone:
                    try:
                        next(gf)
                    except StopIteration:
                        fdone = True
                if not bdone:
                    try:
                        next(gb)
                    except StopIteration:
                        bdone = True
    P.barrier()
    st.close(); P.stack = P.gstack


NT_FULL = 64
LAYERS = [("even", 0), ("odd", 0), ("even", 1), ("odd", 1)]
DIR_KEYS = ("s5_ar_row", "s5_ai_row", "s5_dt_row", "s5_ar_col", "s5_ai_col", "s5_dt_col", "s5_b_col", "s5_c_col", "rw_w0_rep", "rw_w_up")


def core_flags(w0, w1):
    fl = np.zeros((128, 4), np.float32)
    fl[:, 0] = w0
    fl[:, 1] = w1
    fl[:, 2] = 0.0 if (w0 + w1) > 0 else NEG
    return fl


def core_maps(m, streams):
    mrev = dict(m)
    for k in DIR_KEYS:
        mrev[k] = np.ascontiguousarray(m[k][:, ::-1])
    maps = []
    for (x, rev, w0, w1) in streams:
        mm = dict(mrev if rev else m)
        mm["flags"] = core_flags(w0, w1)
        mm["xin"] = np.ascontiguousarray(x[::-1] if rev else x)
        maps.append(mm)
    return maps


def kernel(**inputs):
    xp = np.asarray(inputs["x_prompt"], np.float32)
    xs = np.asarray(inputs["x_sample"], np.float32)
    m = host_layout(inputs, LAYERS)
    nc, gst = build_program(NT_FULL, LAYERS)
    streams = [(xs[0, 0:8192], False, 0.0, 1.0), (xs[0, 8192:16384], True, 1.0, 0.0)]
    for b in range(4):
        streams.append((xp[b], False, 0.0, 0.0))
    streams += [(xp[0], False, 0.0, 0.0), (xp[1], False, 0.0, 0.0)]
    maps = core_maps(m, streams)
    res = run_bass_kernel_spmd(nc, maps, core_ids=list(range(8)))
    outs = [np.asarray(res.results[c]["xout"], np.float32) for c in range(6)]
    y_sample = np.concatenate([outs[0], outs[1][::-1]], axis=0).reshape(1, 16384, D)
    y_prompt = np.stack(outs[2:6], axis=0)
    return (y_prompt, y_sample)
```
